# Optimizing a Trainium2 kernel written in Bass

```python
import math
import jax, jax.numpy as jnp
from jax import lax
import numpy as np

D_MODEL = 1024
BATCH = 4
SEQ = 8192
DEPTH = 2

PLE_DIM = 256
NSA_HEADS = 8
NSA_HEAD_DIM = 64
NSA_GROUPS = 2
NSA_Q_PER_GROUP = NSA_HEADS // NSA_GROUPS
CMP_BLOCK = 32
CMP_STRIDE = 16
CMP_HIDDEN = 4 * NSA_HEAD_DIM
SLC_BLOCK = 64
N_SELECT = 16
WINDOW = 512
Q_BLOCK = 128
RET_HEADS = 4
RET_HEAD_DIM = 128
RET_CHUNK = 128
ROPE_BASE = 10000.0
D_FF = 4 * D_MODEL
EPS = 1e-6
MASK_VALUE = -1e30
FORCE_SCORE = 1e4

NSA_Q_W = NSA_HEADS * NSA_HEAD_DIM
NSA_KV_W = NSA_GROUPS * NSA_HEAD_DIM
NSA_GATE_W = 3 * NSA_HEADS
RET_W = RET_HEADS * RET_HEAD_DIM
IN_SPLITS = (NSA_Q_W, NSA_KV_W, NSA_KV_W, NSA_KV_W, NSA_KV_W, NSA_KV_W, NSA_KV_W, NSA_GATE_W,
             RET_W, RET_W, RET_W, RET_W, D_MODEL, D_MODEL)
D_IN = sum(IN_SPLITS)

kernel_name = "hybrid_nsa_retention_griffin_merge"


def split_cols(z, sizes):
    out = []
    off = 0
    for s in sizes:
        out.append(z[..., off:off + s])
        off += s
    return out


def rmsnorm(x, g):
    xf = x.astype(jnp.float32)
    y = xf * lax.rsqrt(jnp.mean(xf * xf, axis=-1, keepdims=True) + EPS)
    return (y * g.astype(jnp.float32)).astype(x.dtype)


def rms_noaffine(x):
    xf = x.astype(jnp.float32)
    return xf * lax.rsqrt(jnp.mean(xf * xf, axis=-1, keepdims=True) + EPS)


def masked_softmax(s, mask):
    s = jnp.where(mask, s, MASK_VALUE)
    m = jnp.max(s, axis=-1, keepdims=True)
    e = jnp.exp(s - m) * mask
    return e / jnp.maximum(jnp.sum(e, axis=-1, keepdims=True), 1e-30)


def rope(x, pos):
    half = x.shape[-1] // 2
    inv = ROPE_BASE ** (-jnp.arange(half, dtype=jnp.float32) / half)
    ang = pos[:, None] * inv[None, :]
    cos = jnp.cos(ang)[:, None, :]
    sin = jnp.sin(ang)[:, None, :]
    x1 = x[..., :half].astype(jnp.float32)
    x2 = x[..., half:].astype(jnp.float32)
    return jnp.concatenate([x1 * cos - x2 * sin, x1 * sin + x2 * cos], axis=-1)


def nsa_mixer(q_in, kc_in, vc_in, ks_in, vs_in, kw_in, vw_in, gate_logits,
              qn_g, kn_g, pos_k, pos_v, w1_k, w2_k, w1_v, w2_v):
    B, S = q_in.shape[:2]
    G, Hg, dh = NSA_GROUPS, NSA_Q_PER_GROUP, NSA_HEAD_DIM
    dtype = q_in.dtype
    scale = dh ** -0.5
    q = rmsnorm(q_in.reshape(B, S, G, Hg, dh).transpose(0, 2, 3, 1, 4), qn_g)

    def kv_heads(z):
        return z.reshape(B, S, G, dh).transpose(0, 2, 1, 3)

    k_c, v_c = kv_heads(kc_in), kv_heads(vc_in)
    k_s, v_s = rmsnorm(kv_heads(ks_in), kn_g), kv_heads(vs_in)
    k_w, v_w = rmsnorm(kv_heads(kw_in), kn_g), kv_heads(vw_in)
    gates = jax.nn.sigmoid(gate_logits.astype(jnp.float32)).reshape(B, S, 3, G, Hg).transpose(0, 3, 4, 1, 2)

    nc = S // CMP_STRIDE
    starts = jnp.arange(nc) * CMP_STRIDE
    idx = jnp.minimum(starts[:, None] + jnp.arange(CMP_BLOCK)[None, :], S - 1)
    cmp_end = starts + CMP_BLOCK - 1

    def compress(z, pos, w1, w2):
        zb = (z[:, :, idx] + pos).reshape(B, G, nc, CMP_BLOCK * dh)
        return jax.nn.gelu(zb @ w1) @ w2

    k_cmp = rmsnorm(compress(k_c, pos_k, w1_k, w2_k), kn_g)
    v_cmp = compress(v_c, pos_v, w1_v, w2_v)

    ns = S // SLC_BLOCK
    n_sel = min(N_SELECT, ns)
    ratio = SLC_BLOCK // CMP_STRIDE
    k_blk = k_s.reshape(B, G, ns, SLC_BLOCK, dh)
    v_blk = v_s.reshape(B, G, ns, SLC_BLOCK, dh)
    b_ix = jnp.arange(B)[:, None, None, None]
    g_ix = jnp.arange(G)[None, :, None, None]

    pad = ((0, 0), (0, 0), (WINDOW, 0), (0, 0))
    k_w_pad = jnp.pad(k_w, pad)
    v_w_pad = jnp.pad(v_w, pad)

    def query_block(c):
        q0 = c * Q_BLOCK
        qc = lax.dynamic_slice_in_dim(q, q0, Q_BLOCK, axis=3)
        gc = lax.dynamic_slice_in_dim(gates, q0, Q_BLOCK, axis=3)
        t = q0 + jnp.arange(Q_BLOCK)

        s = jnp.einsum('bghqd,bgnd->bghqn', qc, k_cmp).astype(jnp.float32) * scale
        p_cmp = masked_softmax(s, cmp_end[None, :] <= t[:, None])
        o_cmp = jnp.einsum('bghqn,bgnd->bghqd', p_cmp.astype(dtype), v_cmp)

        imp = jnp.sum(p_cmp, axis=2).reshape(B, G, Q_BLOCK, ns, ratio)
        imp_blk = jnp.sum(imp, axis=-1) + jnp.pad(imp[..., :-1, -1], ((0, 0), (0, 0), (0, 0), (1, 0)))
        blk = jnp.arange(ns)[None, :]
        cur = (t // SLC_BLOCK)[:, None]
        forced = (blk == 0) | (blk == cur) | (blk == cur - 1)
        score = jnp.where(forced, FORCE_SCORE, imp_blk)
        score = jnp.where(blk > cur, MASK_VALUE, score)
        _, sel = lax.top_k(score, n_sel)
        ks = k_blk[b_ix, g_ix, sel]
        vs = v_blk[b_ix, g_ix, sel]
        kpos = sel[..., None] * SLC_BLOCK + jnp.arange(SLC_BLOCK)
        smask = (kpos <= t[:, None, None]).reshape(B, G, Q_BLOCK, n_sel * SLC_BLOCK)[:, :, None]
        s = jnp.einsum('bghqd,bgqnkd->bghqnk', qc, ks).astype(jnp.float32) * scale
        p_slc = masked_softmax(s.reshape(B, G, Hg, Q_BLOCK, n_sel * SLC_BLOCK), smask)
        o_slc = jnp.einsum('bghqnk,bgqnkd->bghqd',
                           p_slc.reshape(B, G, Hg, Q_BLOCK, n_sel, SLC_BLOCK).astype(dtype), vs)

        kw = lax.dynamic_slice_in_dim(k_w_pad, q0, Q_BLOCK + WINDOW, axis=2)
        vw = lax.dynamic_slice_in_dim(v_w_pad, q0, Q_BLOCK + WINDOW, axis=2)
        wpos = q0 - WINDOW + jnp.arange(Q_BLOCK + WINDOW)
        dist = t[:, None] - wpos[None, :]
        wmask = (dist >= 0) & (dist < WINDOW) & (wpos[None, :] >= 0)
        s = jnp.einsum('bghqd,bgkd->bghqk', qc, kw).astype(jnp.float32) * scale
        p_win = masked_softmax(s, wmask)
        o_win = jnp.einsum('bghqk,bgkd->bghqd', p_win.astype(dtype), vw)

        return gc[..., 0:1] * o_cmp + gc[..., 1:2] * o_slc + gc[..., 2:3] * o_win

    out = lax.map(query_block, jnp.arange(S // Q_BLOCK))
    return out.transpose(1, 0, 4, 2, 3, 5).reshape(B, S, NSA_Q_W).astype(dtype)


def retention(q_in, k_in, v_in, g_in):
    B, S = q_in.shape[:2]
    H, dk, C = RET_HEADS, RET_HEAD_DIM, RET_CHUNK
    dtype = q_in.dtype
    pos = jnp.arange(S, dtype=jnp.float32)
    q = rope(q_in.reshape(B, S, H, dk), pos)
    k = rope(k_in.reshape(B, S, H, dk), pos) * (dk ** -0.5)
    v = v_in.reshape(B, S, H, dk).astype(jnp.float32)
    nch = S // C

    def chunks(z):
        return z.reshape(B, nch, C, H, dk).transpose(0, 3, 1, 2, 4)

    qc, kc, vc = chunks(q), chunks(k), chunks(v)
    gamma = 1.0 - 2.0 ** (-5.0 - jnp.arange(H, dtype=jnp.float32))
    lg = jnp.log(gamma)
    n = jnp.arange(C, dtype=jnp.float32)
    diff = n[:, None] - n[None, :]
    decay = jnp.where(diff >= 0, jnp.exp(lg[:, None, None] * jnp.maximum(diff, 0.0)), 0.0)
    xi = jnp.exp(lg[:, None] * (n + 1.0))[None, :, None, :, None]
    zeta = jnp.exp(lg[:, None] * (C - 1.0 - n))[None, :, None, :, None]
    g_chunk = jnp.exp(lg * C)[None, :, None, None]

    inner = jnp.einsum('bhcnd,bhcmd->bhcnm', qc, kc) * decay[None, :, None]
    inner = jnp.einsum('bhcnm,bhcme->bhcne', inner, vc)
    kv = jnp.einsum('bhcmd,bhcme->cbhde', kc * zeta, vc)

    def step(R, kv_c):
        return g_chunk * R + kv_c, R

    _, R_prev = lax.scan(step, jnp.zeros((B, H, dk, dk), jnp.float32), kv)
    cross = jnp.einsum('bhcnd,cbhde->bhcne', qc * xi, R_prev)
    y = rms_noaffine(inner + cross)
    y = y.transpose(0, 2, 3, 1, 4).reshape(B, S, RET_W)
    return (jax.nn.silu(g_in.astype(jnp.float32)) * y).astype(dtype)


def setup_inputs(seed: int = 0) -> dict:
    key = jax.random.key(seed)
    ks = jax.random.split(key, 21)
    f32 = jnp.float32

    def nrm(k, shape, scale):
        return jax.random.normal(k, shape, f32) * scale

    def gain(k, shape):
        return 1.0 + 0.1 * jax.random.normal(k, shape, f32)

    dh = NSA_HEAD_DIM
    return {
        "x": nrm(ks[0], (BATCH, SEQ, D_MODEL), 1.0),
        "p": nrm(ks[1], (DEPTH, BATCH, SEQ, PLE_DIM), 1.0),
        "norm_mix": gain(ks[2], (DEPTH, D_MODEL)),
        "w_in": nrm(ks[3], (DEPTH, D_MODEL, D_IN), D_MODEL ** -0.5),
        "nsa_q_norm": gain(ks[4], (DEPTH, dh)),
        "nsa_k_norm": gain(ks[5], (DEPTH, dh)),
        "cmp_pos_k": nrm(ks[6], (DEPTH, CMP_BLOCK, dh), 0.1),
        "cmp_pos_v": nrm(ks[7], (DEPTH, CMP_BLOCK, dh), 0.1),
        "cmp_w1_k": nrm(ks[8], (DEPTH, CMP_BLOCK * dh, CMP_HIDDEN), (CMP_BLOCK * dh) ** -0.5),
        "cmp_w2_k": nrm(ks[9], (DEPTH, CMP_HIDDEN, dh), CMP_HIDDEN ** -0.5),
        "cmp_w1_v": nrm(ks[10], (DEPTH, CMP_BLOCK * dh, CMP_HIDDEN), (CMP_BLOCK * dh) ** -0.5),
        "cmp_w2_v": nrm(ks[11], (DEPTH, CMP_HIDDEN, dh), CMP_HIDDEN ** -0.5),
        "w_up_nsa": nrm(ks[12], (DEPTH, NSA_Q_W, D_MODEL), NSA_Q_W ** -0.5),
        "w_up_ret": nrm(ks[13], (DEPTH, RET_W, D_MODEL), RET_W ** -0.5),
        "w_out": nrm(ks[14], (DEPTH, D_MODEL, D_MODEL), D_MODEL ** -0.5),
        "norm_mlp": gain(ks[15], (DEPTH, D_MODEL)),
        "w_ff1": nrm(ks[16], (DEPTH, D_MODEL, D_FF), D_MODEL ** -0.5),
        "w_ff2": nrm(ks[17], (DEPTH, D_FF, D_MODEL), D_FF ** -0.5),
        "norm_ple": gain(ks[18], (DEPTH, D_MODEL)),
        "w_ple": nrm(ks[19], (DEPTH, PLE_DIM, D_MODEL), PLE_DIM ** -0.5),
        "w_ple_gate": nrm(ks[20], (DEPTH, D_MODEL, D_MODEL), D_MODEL ** -0.5),
    }


def reference(x, p, norm_mix, w_in, nsa_q_norm, nsa_k_norm, cmp_pos_k, cmp_pos_v,
              cmp_w1_k, cmp_w2_k, cmp_w1_v, cmp_w2_v, w_up_nsa, w_up_ret, w_out,
              norm_mlp, w_ff1, w_ff2, norm_ple, w_ple, w_ple_gate):
    for i in range(DEPTH):
        h = rmsnorm(x, norm_mix[i])
        z = h @ w_in[i]
        (q_a, kc, vc, ks_, vs_, kw, vw, g_nsa, rq, rk, rv, rg, merge_a, merge_b) = split_cols(z, IN_SPLITS)
        y_a = nsa_mixer(q_a, kc, vc, ks_, vs_, kw, vw, g_nsa, nsa_q_norm[i], nsa_k_norm[i],
                        cmp_pos_k[i], cmp_pos_v[i], cmp_w1_k[i], cmp_w2_k[i], cmp_w1_v[i], cmp_w2_v[i]) @ w_up_nsa[i]
        y_b = retention(rq, rk, rv, rg) @ w_up_ret[i]
        mix = jax.nn.sigmoid(merge_a) * y_a + jax.nn.sigmoid(merge_b) * y_b
        x = x + mix @ w_out[i]
        h = rmsnorm(x, norm_mlp[i])
        x = x + jnp.square(jax.nn.relu(h @ w_ff1[i])) @ w_ff2[i]
        gate = jax.nn.sigmoid(rmsnorm(x, norm_ple[i]) @ w_ple_gate[i])
        x = x + gate * (p[i] @ w_ple[i])
    return x
```

```python
from contextlib import ExitStack
import numpy as np
import concourse.bass as bass
import concourse.mybir as mybir

F32 = mybir.dt.float32
BF16 = mybir.dt.bfloat16
I32 = mybir.dt.int32
AF = mybir.ActivationFunctionType
ALU = mybir.AluOpType
AX = mybir.AxisListType

ENGS = ("pe", "act", "dve", "pool", "sp")
N_DMA_SEMS = 24


class Tok:
    __slots__ = ("lws", "rs", "base", "name", "excl", "accgrp")

    def __init__(self, name="", excl=False):
        self.excl = excl
        self.accgrp = False
        self.lws = []
        self.rs = []
        self.base = []
        self.name = name


class Op:
    __slots__ = ("eng", "fn", "deps", "dma", "idx", "sig", "dma_n", "cc")

    def __init__(self, eng, fn, deps, dma, idx):
        self.eng = eng
        self.fn = fn
        self.deps = deps
        self.dma = dma
        self.idx = idx
        self.sig = None
        self.dma_n = None
        self.cc = None


class _Scope:
    def __init__(self, S):
        self.S = S

    def __enter__(self):
        self.saved = self.S.stack
        self.S.stack = ExitStack()
        return self

    def __exit__(self, *a):
        self.S.barrier()
        self.S.stack.close()
        self.S.stack = self.saved
        return False


class Sched:
    def __init__(self, nc):
        self.nc = nc
        self.ops = {e: [] for e in ENGS}
        self.ndma = {e: 0 for e in ENGS}
        self.final_waits = []
        self.all_dma = []
        self.ncc = 0
        self.prefix = ""
        self.stack = ExitStack()

    def sbuf(self, name, shape, dtype):
        return self.stack.enter_context(self.nc.sbuf_tensor("sb_" + self.prefix + name, list(shape), dtype))

    def psum(self, name, shape, dtype):
        return self.stack.enter_context(self.nc.psum_tensor("pp_" + name, list(shape), dtype))

    def add(self, eng, fn, reads=(), writes=(), dma=False, accw=(), extra=()):
        deps = []
        seen = set()

        def push(d):
            if d is not None and d not in seen:
                seen.add(d)
                deps.append(d)

        for d in extra:
            push(d)
        for t in reads:
            for w in t.lws:
                push(w)
            if t.excl:
                for r in t.rs:
                    if r[0] != eng:
                        push(r)
        for t in writes:
            for w in t.lws:
                push(w)
            for r in t.rs:
                push(r)
        for t in accw:
            if t.rs or not t.lws or not t.accgrp:
                for w in t.lws:
                    push(w)
                for r in t.rs:
                    push(r)
            else:
                for d in t.base:
                    push(d)
        lst = self.ops[eng]
        op = Op(eng, fn, deps, dma, len(lst))
        if dma:
            op.dma_n = self.ndma[eng]
            self.ndma[eng] += 1
            self.all_dma.append((eng, op.idx))
        lst.append(op)
        me = (eng, op.idx)
        for t in reads:
            t.rs.append(me)
        for t in writes:
            t.lws = [me]
            t.rs = []
            t.base = []
            t.accgrp = False
        for t in accw:
            if t.rs or not t.lws or not t.accgrp:
                t.base = list(t.lws) + list(t.rs)
                t.lws = [me]
                t.rs = []
                t.accgrp = True
            else:
                t.lws.append(me)
        return op

    def collective(self, kind, src, dst, groups, reads=(), writes=(), accw=()):
        op = self.add("pool", lambda e: e.collective_compute(kind, ALU.bypass, replica_groups=groups,
                                                             ins=[src], outs=[dst]), reads, writes, accw=accw)
        self.ncc += 1
        op.cc = self.ncc
        return op

    def barrier(self):
        extra = list(self.all_dma)
        for e in ENGS:
            if self.ops[e]:
                extra.append((e, len(self.ops[e]) - 1))
        self.all_dma = []
        b0 = self.add("sp", lambda e: e.nop(), extra=extra)
        me = ("sp", b0.idx)
        for e in ("pe", "act", "dve", "pool"):
            self.add(e, lambda eng: eng.nop(), extra=[me])

    def dma(self, out, in_, reads=(), writes=(), q="sp", accw=(), **kw):
        return self.add(q, lambda e: e.dma_start(out=out, in_=in_, **kw), reads, writes, dma=True, accw=accw)

    def scope(self):
        return _Scope(self)

    def emit(self):
        nc = self.nc
        ops = self.ops
        needed = {e: set() for e in ENGS}
        waits = {e: [] for e in ENGS}
        for e in ENGS:
            maxw = {d: -1 for d in ENGS}
            dma_waited = set()
            for op in ops[e]:
                keep = []
                for (de, di) in op.deps:
                    dop = ops[de][di]
                    if dop.dma or dop.cc:
                        if (de, di) in dma_waited:
                            continue
                        dma_waited.add((de, di))
                        keep.append((de, di))
                    else:
                        if de == e and e == "pe":
                            continue
                        if de == e and di == op.idx:
                            continue
                        if di <= maxw[de]:
                            continue
                        maxw[de] = di
                        keep.append((de, di))
                        needed[de].add(di)
                waits[e].append(keep)
        for e in ENGS:
            c = 0
            for op in ops[e]:
                if (not op.dma) and (not op.cc) and op.idx in needed[e]:
                    c += 1
                    op.sig = c
        st = self.stack
        csem = {e: st.enter_context(nc.semaphore("c_" + e)) for e in ENGS}
        ccsem = st.enter_context(nc.semaphore("cc_sem"))
        dsem = {e: [st.enter_context(nc.semaphore("d_%s_%d" % (e, i))) for i in range(N_DMA_SEMS)]
                for e in ENGS if self.ndma[e] > 0}
        block = st.enter_context(nc.Block())

        def gen(e, eng):
            for op, keep in zip(ops[e], waits[e]):
                if op.dma:
                    n = op.dma_n
                    if n >= N_DMA_SEMS:
                        eng.wait_ge(dsem[e][n % N_DMA_SEMS], 16 * (n // N_DMA_SEMS))
                for (de, di) in keep:
                    dop = ops[de][di]
                    if dop.dma:
                        n = dop.dma_n
                        eng.wait_ge(dsem[de][n % N_DMA_SEMS], 16 * (n // N_DMA_SEMS + 1))
                    elif dop.cc:
                        eng.wait_ge(ccsem, dop.cc)
                    else:
                        eng.wait_ge(csem[de], dop.sig)
                ins = op.fn(eng)
                if op.dma:
                    n = op.dma_n
                    ins.then_inc(dsem[e][n % N_DMA_SEMS], 16)
                elif op.cc:
                    ins.then_inc(ccsem, 1)
                elif op.sig is not None:
                    ins.then_inc(csem[e], 1)
            if e == "pool" and self.ncc:
                eng.wait_ge(ccsem, self.ncc)
            nd = self.ndma[e]
            for i in range(min(nd, N_DMA_SEMS)):
                cnt = (nd - 1 - i) // N_DMA_SEMS + 1
                eng.wait_ge(dsem[e][i], 16 * cnt)

        @block.tensor
        def _(eng):
            gen("pe", eng)

        @block.scalar
        def _(eng):
            gen("act", eng)

        @block.vector
        def _(eng):
            gen("dve", eng)

        @block.gpsimd
        def _(eng):
            gen("pool", eng)

        @block.sync
        def _(eng):
            gen("sp", eng)

    def close(self):
        self.stack.close()


D_MODEL = 1024
EPS = 1e-6
WU_ELEMS = 4096


class WSpec:
    def __init__(self, S, name, w_ap, K, N, kind):
        self.name, self.K, self.N, self.kind = name, K, N, kind
        self.KT = K // 128
        self.w = w_ap
        nc = S.nc
        if kind == "S":
            assert N % 512 == 0
            self.nunits = N // 512
            self.uelems = 4 * self.KT * 128
        else:
            assert N % 512 == 0
            self.KTU = min(8, self.KT)
            self.nv = self.KT // self.KTU
            self.nunits = (N // 512) * self.nv
            self.uelems = self.KTU * 512
        assert self.uelems <= WU_ELEMS
        self.scr = nc.dram_tensor("scr_" + S.prefix + name, [self.nunits, 128, self.uelems], BF16, kind="Internal").ap()
        self.tok = Tok("scr_" + name)

    def unit_src(self, u):
        return self.scr[u]

    def view(self, buf):
        b = buf[:, : self.uelems]
        if self.kind == "S":
            return b.rearrange("p (f k c) -> p f k c", f=4, k=self.KT)
        return b.rearrange("p (k c) -> p k c", k=self.KTU)


class Ctx:
    pass


def make_pools(S, n_wbuf=5, n_ps=6):
    C = Ctx()
    C.S = S
    C.wbuf = [S.sbuf("wbuf%d" % i, [128, WU_ELEMS], BF16) for i in range(n_wbuf)]
    C.wtok = [Tok("wbuf%d" % i) for i in range(n_wbuf)]
    C.wn = 0
    C.ps = [S.psum("ps%d" % i, [128, 512], F32) for i in range(n_ps)]
    C.pstok = [Tok("ps%d" % i, excl=True) for i in range(n_ps)]
    C.pn = 0
    C.psb = [S.psum("psb%d" % i, [128, 1024], BF16) for i in range(2)]
    C.psbtok = [Tok("psb0", excl=True), Tok("psb1", excl=True)]
    C.pbn = 0
    C.stg = [S.sbuf("stg%d" % i, [128, 512], F32) for i in range(3)]
    C.stgtok = [Tok() for _ in range(3)]
    C.stgb = [S.sbuf("stgb%d" % i, [128, 512], BF16) for i in range(3)]
    C.stgbtok = [Tok() for _ in range(3)]
    C.sn = 0
    C.cast_rr = 0
    return C


def next_ps(C):
    i = C.pn % getattr(C, "ps_active", len(C.ps))
    C.pn += 1
    return C.ps[i], C.pstok[i]


def next_psb(C):
    i = C.pbn % 2
    C.pbn += 1
    return C.psb[i][:, 0:512], C.psbtok[i]


def load_unit(C, ws, u, q="sp"):
    i = C.wn % len(C.wbuf)
    C.wn += 1
    buf, tok = C.wbuf[i], C.wtok[i]
    C.S.dma(buf[:, : ws.uelems], ws.unit_src(u), reads=[ws.tok], writes=[tok], q=q)
    return ws.view(buf), tok


def prep_weight(C, ws):
    S = C.S
    K, N, KT = ws.K, ws.N, ws.KT
    for kt in range(KT):
        for c0 in range(0, N, 512):
            cw = min(512, N - c0)
            i = C.sn % 3
            C.sn += 1
            st, stt, sb, sbt = C.stg[i], C.stgtok[i], C.stgb[i], C.stgbtok[i]
            S.dma(st[:, :cw], ws.w[kt * 128:(kt + 1) * 128, c0:c0 + cw], writes=[stt])
            eng = ("dve", "pool")[C.cast_rr % 2]
            C.cast_rr += 1
            S.add(eng, lambda e, sb=sb, st=st, cw=cw: e.tensor_copy(out=sb[:, :cw], in_=st[:, :cw]), [stt], [sbt])
            if ws.kind == "S":
                u0, nu = c0 // 512, cw // 512
                dst = ws.scr[u0:u0 + nu].rearrange("u p (f k c) -> p u f k c", f=4, k=KT)[:, :, :, kt, :]
                src = sb[:, :cw].rearrange("p (u f c) -> p u f c", u=nu, f=4)
                for uu in range(nu):
                    S.dma(dst[:, uu], src[:, uu], reads=[sbt], accw=[ws.tok], q="act")
            else:
                v, kk = kt // ws.KTU, kt % ws.KTU
                n0, nn = c0 // 512, cw // 512
                for n in range(nn):
                    u = (n0 + n) * ws.nv + v
                    dst = ws.scr[u].rearrange("p (k c) -> p k c", k=ws.KTU)[:, kk, :]
                    S.dma(dst, sb[:, n * 512:(n + 1) * 512], reads=[sbt], accw=[ws.tok], q="act")


def rms_to_featmajor(C, xt, xtok, gains, gtok, gcol0, hT, htok, ident, itok, tmp):
    S = C.S
    ss, sstok, xs, xstok, junk, jtok, rstd, rtok = tmp
    for j in range(4):
        S.add("act", lambda e, j=j: e.activation(
            out=xs[:, j, :], in_=xt[:, j, :], func=AF.Square, accum_out=ss[:, j:j + 1]), [xtok], [xstok, sstok])
    S.add("dve", lambda e: e.tensor_scalar(out=rstd[:], in0=ss[:], scalar1=1.0 / D_MODEL, scalar2=EPS,
                                           op0=ALU.mult, op1=ALU.add), [sstok], [rtok])
    S.add("act", lambda e: e.activation(out=rstd[:], in_=rstd[:], func=AF.Sqrt), [rtok], [rtok])
    S.add("dve", lambda e: e.reciprocal(out=rstd[:], in_=rstd[:]), [rtok], [rtok])
    for j in range(4):
        S.add("act", lambda e, j=j: e.activation(out=xs[:, j, :], in_=xt[:, j, :], func=AF.Copy,
                                                 scale=rstd[:, j:j + 1]), [xtok, rtok], [xstok])
    for kt in range(8):
        ps, pt = next_ps(C)
        for j in range(4):
            S.add("pe", lambda e, ps=ps, j=j, kt=kt: e.transpose(
                out=ps[:, j * 128:(j + 1) * 128], in_=xs[:, j, kt * 128:(kt + 1) * 128], identity=ident[:]),
                [xstok, itok], [pt])
        if kt % 2 == 0:
            S.add("dve", lambda e, ps=ps, kt=kt: e.tensor_scalar(
                out=hT[:, kt, :], in0=ps[:], scalar1=gains[:, gcol0 + kt:gcol0 + kt + 1], scalar2=None,
                op0=ALU.mult), [pt, gtok], accw=[htok])
        else:
            S.add("act", lambda e, ps=ps, kt=kt: e.activation(
                out=hT[:, kt, :], in_=ps[:], func=AF.Copy, scale=gains[:, gcol0 + kt:gcol0 + kt + 1]),
                [pt, gtok], accw=[htok])


def build_phaseB(T=4096):
    nc = bass.Bass("TRN2", target_bir_lowering=False)
    dt = nc.dram_tensor
    x = dt("x", [T, 1024], F32, kind="ExternalInput").ap()
    attn = dt("attn", [T, 1024], BF16, kind="ExternalInput").ap()
    pin = dt("p", [T, 256], F32, kind="ExternalInput").ap()
    gains_d = dt("gains", [128, 24], F32, kind="ExternalInput").ap()
    ident_d = dt("ident", [128, 128], F32, kind="ExternalInput").ap()
    wd = {}
    for name, K, N in (("w_merge", 1024, 2048), ("w_up_nsa", 512, 1024), ("w_up_ret", 512, 1024),
                       ("w_out", 1024, 1024), ("w_ff1", 1024, 4096), ("w_ff2", 4096, 1024),
                       ("w_gate", 1024, 1024), ("w_ple", 256, 1024)):
        wd[name] = dt(name, [K, N], F32, kind="ExternalInput").ap()
    xo = dt("xo", [T, 1024], F32, kind="ExternalOutput").ap()
    S = Sched(nc)
    C = make_pools(S)
    emit_phaseB(C, T, x, attn, pin, gains_d, ident_d, wd, xo)
    S.emit()
    S.close()
    return nc


DBG_STAGE = 99
DBG_R = 99
DBG_SUB = 0
DBG_Q = "act"
DBG_PREP = True


def emit_phaseB(C, T, x, attn, pin, gains_d, ident_d, wd, xo, xin_tok=None, attn_tok=None, xo_tok=None,
                gathered=None):
    S = C.S
    kinds = {"w_merge": "S", "w_up_nsa": "S", "w_up_ret": "S", "w_out": "M", "w_ff1": "S", "w_ff2": "M",
             "w_gate": "M", "w_ple": "M"}
    W = {}
    for name, ap in wd.items():
        K, N = ap.shape
        W[name] = WSpec(S, name, ap, K, N, kinds[name])
    gains = S.sbuf("gains", [128, 24], F32)
    gtok = Tok()
    ident = S.sbuf("ident", [128, 128], F32)
    identb = S.sbuf("identb", [128, 128], BF16)
    itok, ibtok = Tok(), Tok()
    S.dma(gains[:], gains_d, writes=[gtok])
    S.dma(ident[:], ident_d, writes=[itok])
    S.add("dve", lambda e: e.tensor_copy(out=identb[:], in_=ident[:]), [itok], [ibtok])
    for name in ("w_merge", "w_up_nsa", "w_up_ret", "w_out", "w_ff1", "w_ff2", "w_gate", "w_ple"):
        if DBG_PREP:
            prep_weight(C, W[name])
    xt = S.sbuf("xt", [128, 4, 1024], F32)
    at = S.sbuf("at", [128, 4, 1024], BF16)
    ptm = S.sbuf("ptm", [128, 4, 256], F32)
    xs = S.sbuf("xs", [128, 4, 1024], F32)
    junk = None
    ss = S.sbuf("ss", [128, 4], F32)
    rstd = S.sbuf("rstd", [128, 4], F32)
    hT = S.sbuf("hT", [128, 8, 512], BF16)
    aT = S.sbuf("aT", [128, 8, 512], BF16)
    sgT = S.sbuf("sgT", [128, 16, 512], BF16)
    mixT = S.sbuf("mixT", [128, 8, 512], BF16)
    uT = S.sbuf("uT", [128, 32, 512], BF16)
    pT = S.sbuf("pT", [128, 2, 512], BF16)
    tmpf = [S.sbuf("tmpf%d" % i, [128, 512], F32) for i in range(2)]
    tmpft = [Tok(), Tok()]
    gsb = S.sbuf("gsb", [128, 512], F32)
    xtok, atok, ptok, xstok, jtok, sstok, rtok = [Tok() for _ in range(7)]
    htok, aTtok, sgtok, mixtok, utok, pTtok, gsbtok = [Tok() for _ in range(7)]
    tmp = (ss, sstok, xs, xstok, junk, jtok, rstd, rtok)
    xin_tok = xin_tok or Tok()
    attn_tok = attn_tok or Tok()
    xo_tok = xo_tok or Tok()
    nchunk = T // 512
    tn = 0
    for c in range(nchunk):
        t0 = c * 512
        S.dma(xt[:], x[t0:t0 + 512, :].rearrange("(j p) d -> p j d", p=128), reads=[xin_tok], writes=[xtok])
        if gathered is None:
            S.dma(at[:], attn[t0:t0 + 512, :].rearrange("(j p) d -> p j d", p=128), reads=[attn_tok], writes=[atok])
        else:
            SLg, hm, hmt, atAB, atABt = gathered
            for hf in range(2):
                for g in range(2):
                    RKg = min(2048, SLg)
                    tk_ = hf * T + t0
                    r0 = 2 * (tk_ // RKg) * RKg + g * RKg + tk_ % RKg
                    srcv = attn[r0:r0 + 512, :].rearrange("(j p) d -> p j d", p=128)
                    S.dma(atAB[hf][:, :, g * 256:(g + 1) * 256], srcv[:, :, 0:256], reads=[attn_tok], accw=[atABt[hf]])
                    S.dma(atAB[hf][:, :, 512 + g * 256:512 + (g + 1) * 256], srcv[:, :, 256:512], reads=[attn_tok],
                          accw=[atABt[hf]])
            S.add("dve", lambda e: e.tensor_scalar(out=at[:], in0=atAB[0][:], scalar1=hm[:, 0:1], scalar2=None,
                                                   op0=ALU.mult), [atABt[0], hmt], [atok])
            S.add("dve", lambda e: e.scalar_tensor_tensor(out=at[:], in0=atAB[1][:], scalar=hm[:, 1:2], in1=at[:],
                                                          op0=ALU.mult, op1=ALU.add), [atABt[1], hmt, atok], [atok])
        S.dma(ptm[:], pin[t0:t0 + 512, :].rearrange("(j p) d -> p j d", p=128), writes=[ptok])
        def _store(t0=t0):
            S.dma(xo[t0:t0 + 512, :].rearrange("(j p) d -> p j d", p=128), xt[:], reads=[xtok], writes=[xo_tok],
                  q=DBG_Q)
        if DBG_STAGE <= 0:
            _store()
            continue
        rms_to_featmajor(C, xt, xtok, gains, gtok, 0, hT, htok, ident, itok, tmp)
        if DBG_STAGE <= 1:
            _store()
            continue
        ws = W["w_merge"]
        for u in range(ws.nunits):
            wv, wt = load_unit(C, ws, u)
            for f in range(4):
                ps, pt = next_ps(C)
                if DBG_SUB == 1:
                    continue
                for kt in range(8):
                    S.add("pe", lambda e, ps=ps, wv=wv, f=f, kt=kt: e.matmul(
                        ps[:], lhsT=wv[:, f, kt, :], rhs=hT[:, kt, :], start=(kt == 0), stop=(kt == 7)),
                        [wt, htok], [pt])
                if DBG_SUB == 2:
                    continue
                S.add("act", lambda e, ps=ps, ft=u * 4 + f: e.activation(
                    out=sgT[:, ft, :], in_=ps[:], func=AF.Sigmoid), [pt], accw=[sgtok])
        if DBG_STAGE <= 2:
            _store()
            continue
        for ft in range(8):
            pb, pbt = next_psb(C)
            for j in range(4):
                S.add("pe", lambda e, pb=pb, j=j, ft=ft: e.transpose(
                    out=pb[:, j * 128:(j + 1) * 128], in_=at[:, j, ft * 128:(ft + 1) * 128], identity=identb[:]),
                    [atok, ibtok], [pbt])
            S.add("dve", lambda e, pb=pb, ft=ft: e.tensor_copy(out=aT[:, ft, :], in_=pb), [pbt], accw=[aTtok])
        if DBG_SUB == 3:
            _store()
            continue
        wsa, wsb = W["w_up_nsa"], W["w_up_ret"]
        for u in range(2):
            wva, wta = load_unit(C, wsa, u)
            wvb, wtb = load_unit(C, wsb, u)
            for f in range(4):
                ft = u * 4 + f
                psa, pta = next_ps(C)
                for kt in range(4):
                    S.add("pe", lambda e, psa=psa, wva=wva, f=f, kt=kt: e.matmul(
                        psa[:], lhsT=wva[:, f, kt, :], rhs=aT[:, kt, :], start=(kt == 0), stop=(kt == 3)),
                        [wta, aTtok], [pta])
                psb_, ptb = next_ps(C)
                for kt in range(4):
                    S.add("pe", lambda e, psb_=psb_, wvb=wvb, f=f, kt=kt: e.matmul(
                        psb_[:], lhsT=wvb[:, f, kt, :], rhs=aT[:, 4 + kt, :], start=(kt == 0), stop=(kt == 3)),
                        [wtb, aTtok], [ptb])
                if DBG_SUB == 4:
                    continue
                tf, tft = tmpf[tn % 2], tmpft[tn % 2]
                tn += 1
                S.add("dve", lambda e, tf=tf, psa=psa, ft=ft: e.tensor_tensor(
                    out=tf[:], in0=psa[:], in1=sgT[:, ft, :], op=ALU.mult), [pta, sgtok], [tft])
                tf2, tft2 = tmpf[tn % 2], tmpft[tn % 2]
                tn += 1
                S.add("dve", lambda e, tf2=tf2, psb_=psb_, ft=ft: e.tensor_tensor(
                    out=tf2[:], in0=psb_[:], in1=sgT[:, 8 + ft, :], op=ALU.mult), [ptb, sgtok], [tft2])
                if DBG_SUB == 5:
                    continue
                S.add("pool", lambda e, tf=tf, tf2=tf2, ft=ft: e.tensor_tensor(
                    out=mixT[:, ft, :], in0=tf[:], in1=tf2[:], op=ALU.add), [tft, tft2], accw=[mixtok])
        if DBG_STAGE <= 3:
            _store()
            continue
        ws = W["w_out"]
        for n in range(2):
            wv, wt = load_unit(C, ws, n)
            for j in range(4):
                ps, pt = next_ps(C)
                for kt in range(8):
                    S.add("pe", lambda e, ps=ps, wv=wv, j=j, kt=kt: e.matmul(
                        ps[:], lhsT=mixT[:, kt, j * 128:(j + 1) * 128], rhs=wv[:, kt, :],
                        start=(kt == 0), stop=(kt == 7)), [wt, mixtok], [pt])
                S.add("dve", lambda e, ps=ps, j=j, n=n: e.tensor_tensor(
                    out=xt[:, j, n * 512:(n + 1) * 512], in0=ps[:], in1=xt[:, j, n * 512:(n + 1) * 512],
                    op=ALU.add), [pt, xtok], [xtok])
        if DBG_STAGE <= 4:
            _store()
            continue
        rms_to_featmajor(C, xt, xtok, gains, gtok, 8, hT, htok, ident, itok, tmp)
        ws = W["w_ff1"]
        for u in range(ws.nunits):
            wv, wt = load_unit(C, ws, u)
            for f in range(4):
                ft = u * 4 + f
                ps, pt = next_ps(C)
                for kt in range(8):
                    S.add("pe", lambda e, ps=ps, wv=wv, f=f, kt=kt: e.matmul(
                        ps[:], lhsT=wv[:, f, kt, :], rhs=hT[:, kt, :], start=(kt == 0), stop=(kt == 7)),
                        [wt, htok], [pt])
                tf, tft = tmpf[tn % 2], tmpft[tn % 2]
                tn += 1
                S.add("act", lambda e, ps=ps, tf=tf: e.activation(out=tf[:], in_=ps[:], func=AF.Relu),
                      [pt], [tft])
                S.add("pool", lambda e, tf=tf, ft=ft: e.tensor_tensor(
                    out=uT[:, ft, :], in0=tf[:], in1=tf[:], op=ALU.mult), [tft], accw=[utok])
        ws = W["w_ff2"]
        for n in range(2):
            pss = [next_ps(C) for _ in range(4)]
            for v in range(ws.nv):
                wv, wt = load_unit(C, ws, n * ws.nv + v)
                for j in range(4):
                    ps, pt = pss[j]
                    for kk in range(8):
                        kt = v * 8 + kk
                        S.add("pe", lambda e, ps=ps, wv=wv, j=j, kk=kk, kt=kt: e.matmul(
                            ps[:], lhsT=uT[:, kt, j * 128:(j + 1) * 128], rhs=wv[:, kk, :],
                            start=(kt == 0), stop=(kt == 31)), [wt, utok], [pt])
            for j in range(4):
                ps, pt = pss[j]
                S.add("dve", lambda e, ps=ps, j=j, n=n: e.tensor_tensor(
                    out=xt[:, j, n * 512:(n + 1) * 512], in0=ps[:], in1=xt[:, j, n * 512:(n + 1) * 512],
                    op=ALU.add), [pt, xtok], [xtok])
        if DBG_STAGE <= 5:
            _store()
            continue
        rms_to_featmajor(C, xt, xtok, gains, gtok, 16, hT, htok, ident, itok, tmp)
        for kt in range(2):
            ps, pt = next_ps(C)
            for j in range(4):
                S.add("pe", lambda e, ps=ps, j=j, kt=kt: e.transpose(
                    out=ps[:, j * 128:(j + 1) * 128], in_=ptm[:, j, kt * 128:(kt + 1) * 128], identity=ident[:]),
                    [ptok, itok], [pt])
            S.add("dve", lambda e, ps=ps, kt=kt: e.tensor_copy(out=pT[:, kt, :], in_=ps[:]), [pt], accw=[pTtok])
        wsg, wsp = W["w_gate"], W["w_ple"]
        for n in range(2):
            wvg, wtg = load_unit(C, wsg, n)
            wvp, wtp = load_unit(C, wsp, n)
            for j in range(4):
                ps, pt = next_ps(C)
                for kt in range(8):
                    S.add("pe", lambda e, ps=ps, wvg=wvg, j=j, kt=kt: e.matmul(
                        ps[:], lhsT=hT[:, kt, j * 128:(j + 1) * 128], rhs=wvg[:, kt, :],
                        start=(kt == 0), stop=(kt == 7)), [wtg, htok], [pt])
                S.add("act", lambda e, ps=ps: e.activation(out=gsb[:], in_=ps[:], func=AF.Sigmoid),
                      [pt], [gsbtok])
                ps2, pt2 = next_ps(C)
                for kt in range(2):
                    S.add("pe", lambda e, ps2=ps2, wvp=wvp, j=j, kt=kt: e.matmul(
                        ps2[:], lhsT=pT[:, kt, j * 128:(j + 1) * 128], rhs=wvp[:, kt, :],
                        start=(kt == 0), stop=(kt == 1)), [wtp, pTtok], [pt2])
                tf, tft = tmpf[tn % 2], tmpft[tn % 2]
                tn += 1
                S.add("dve", lambda e, tf=tf, ps2=ps2: e.tensor_tensor(
                    out=tf[:], in0=ps2[:], in1=gsb[:], op=ALU.mult), [pt2, gsbtok], [tft])
                S.add("pool", lambda e, tf=tf, j=j, n=n: e.tensor_tensor(
                    out=xt[:, j, n * 512:(n + 1) * 512], in0=tf[:], in1=xt[:, j, n * 512:(n + 1) * 512],
                    op=ALU.add), [tft, xtok], [xtok])
        _store()


IN_SPLITS = (512, 128, 128, 128, 128, 128, 128, 24, 512, 512, 512, 512, 1024, 1024)
NEG = -30000.0


def hostA_weights(w_in, g):
    offs = np.cumsum([0] + list(IN_SPLITS))

    def col(i, a, b):
        return w_in[:, offs[i] + a: offs[i] + b]

    def swap(x):
        return np.concatenate([x[:, 64:], x[:, :64]], 1)

    q = [col(0, (g * 4 + h) * 64, (g * 4 + h + 1) * 64) for h in range(4)]
    kc, vc = col(1, g * 64, g * 64 + 64), col(2, g * 64, g * 64 + 64)
    ks, vs = col(3, g * 64, g * 64 + 64), col(4, g * 64, g * 64 + 64)
    kw, vw = col(5, g * 64, g * 64 + 64), col(6, g * 64, g * 64 + 64)
    gates = np.stack([w_in[:, offs[7] + br * 8 + g * 4 + h] for br in range(3) for h in range(4)], 1)
    rq = [col(8, (2 * g + h) * 128, (2 * g + h + 1) * 128) for h in range(2)]
    rk = [col(9, (2 * g + h) * 128, (2 * g + h + 1) * 128) for h in range(2)]
    rv = col(10, 2 * g * 128, (2 * g + 2) * 128)
    rg = col(11, 2 * g * 128, (2 * g + 2) * 128)
    z128 = np.zeros((1024, 128), np.float32)
    WS = np.concatenate([q[0], q[1], q[2], q[3], ks, ks, kw, kw, kc, vc, z128, z128, z128,
                         rq[0], swap(rq[0]), rq[1], swap(rq[1]), rk[0], swap(rk[0]), rk[1], swap(rk[1])], 1)
    WM = np.concatenate([rk[0], rk[1], rv, rg, np.zeros((1024, 256), np.float32),
                         vs, vw, gates, np.zeros((1024, 512 - 140), np.float32)], 1)
    return np.ascontiguousarray(WS, np.float32), np.ascontiguousarray(WM, np.float32)


def hostA_consts(g, S):
    import ml_dtypes
    bf = ml_dtypes.bfloat16
    c = {}
    c["ident"] = np.eye(128, dtype=np.float32)
    bd = np.zeros((128, 128), np.float32)
    bd[:64, :64] = 1
    bd[64:, 64:] = 1
    c["onesbd"] = bd.astype(bf)
    half = 64
    inv = (10000.0 ** (-np.arange(half, dtype=np.float32) / half)).astype(np.float32)
    pos = np.arange(S, dtype=np.float32)
    ang = (pos[:, None] * inv[None, :]).astype(np.float32)
    cos, sin = np.cos(ang.astype(np.float64)), np.sin(ang.astype(np.float64))
    cosT = np.concatenate([cos.T, cos.T], 0)
    sinsT = np.concatenate([-sin.T, sin.T], 0)
    ksc = 128.0 ** -0.5
    c["ropeq"] = np.stack([cosT, sinsT], 1).astype(np.float32)
    c["ropek"] = (np.stack([cosT, sinsT], 1) * ksc).astype(np.float32)
    hh = np.array([2 * g, 2 * g + 1], np.float64)
    gamma = 1.0 - 2.0 ** (-5.0 - hh)
    lg = np.log(gamma)
    n = np.arange(128, dtype=np.float64)
    xi = np.exp(lg[:, None] * (n + 1.0))
    zeta = np.exp(lg[:, None] * (127.0 - n))
    c["xi"] = np.broadcast_to(np.tile(xi, (1, 4))[None], (128, 2, 512)).astype(np.float32).copy()
    zt = zeta[:, np.arange(S) % 128]
    c["ctk"] = (cos[:, None, :] * zt.T[:, :, None] * ksc).astype(np.float32)
    c["stk"] = (sin[:, None, :] * zt.T[:, :, None] * ksc).astype(np.float32)
    diff = n[None, :] - n[:, None]
    dec = np.where(diff[None] >= 0, np.exp(lg[:, None, None] * np.maximum(diff[None], 0)), 0.0)
    c["decayT"] = np.ascontiguousarray(dec.transpose(1, 0, 2)).astype(np.float32)
    c["gch"] = np.broadcast_to(np.exp(lg * 128.0)[None], (128, 2)).astype(np.float32).copy()
    kk, qq = np.arange(128)[:, None], np.arange(128)[None, :]
    c["triT"] = np.where(kk <= qq, 0.0, NEG).astype(bf)
    c["triU"] = np.where(kk > qq, 0.0, NEG).astype(bf)
    r = np.arange(16)[None, :, None]
    ql, il = np.arange(128)[:, None, None], np.arange(128)[None, None, :]
    c["cmaskQ"] = np.where(128 * r + ql - 16 * il - 31 >= 0, 0.0, NEG).astype(bf)
    c["cmaskT"] = np.ascontiguousarray(np.transpose(np.where(128 * r + ql - 16 * il - 31 >= 0, 0.0, NEG), (2, 1, 0))).astype(bf)
    c["wexp"] = (np.arange(S)[None, :] // 64 == np.arange(128)[:, None]).astype(bf)
    return c


A_CONST_SHAPES = lambda S: {
    "ident": ([128, 128], F32), "onesbd": ([128, 128], BF16), "ropeq": ([128, 2, S], F32),
    "ropek": ([128, 2, S], F32), "xi": ([128, 2, 512], F32), "ctk": ([S, 2, 64], F32), "stk": ([S, 2, 64], F32),
    "decayT": ([128, 2, 128], F32), "gch": ([128, 2], F32), "triT": ([128, 128], BF16), "triU": ([128, 128], BF16),
    "cmaskQ": ([128, 16, 128], BF16), "cmaskT": ([128, 16, 128], BF16), "wexp": ([128, S], BF16)}


def hostA_params(z, L, g):
    p = {}

    def gl(v):
        return np.ascontiguousarray(v.reshape(8, 128).T)
    qg, kg = z["nsa_q_norm"][L], z["nsa_k_norm"][L]
    p["gainsA"] = np.concatenate([gl(z["norm_mix"][L]), np.tile(qg, 2)[:, None], np.tile(kg, 2)[:, None]], 1).astype(np.float32)
    p["posT"] = np.ascontiguousarray(np.concatenate([z["cmp_pos_k"][L].T, z["cmp_pos_v"][L].T], 0), np.float32)
    w1k = z["cmp_w1_k"][L].reshape(32, 64, 256).transpose(1, 0, 2)
    w1v = z["cmp_w1_v"][L].reshape(32, 64, 256).transpose(1, 0, 2)
    zz = np.zeros_like(w1k)
    p["w1k"] = np.ascontiguousarray(np.concatenate([w1k, zz], 0), np.float32)
    p["w1v"] = np.ascontiguousarray(np.concatenate([zz, w1v], 0), np.float32)
    w2k = z["cmp_w2_k"][L].reshape(2, 128, 64).transpose(1, 0, 2)
    w2v = z["cmp_w2_v"][L].reshape(2, 128, 64).transpose(1, 0, 2)
    p["w2"] = np.ascontiguousarray(np.stack([np.concatenate([w2k, w2k], 2), np.concatenate([w2v, np.zeros_like(w2v)], 2)], 2), np.float32)
    return p


A_PARAM_SHAPES = {"gainsA": [128, 10], "posT": [128, 32], "w1k": [128, 32, 256], "w1v": [128, 32, 256],
                  "w2": [128, 2, 2, 128]}


def build_phaseA(SL=8192, parts=("ret", "nsa")):
    nc = bass.Bass("TRN2", target_bir_lowering=False)
    dt = nc.dram_tensor
    x = dt("x", [SL, 1024], F32, kind="ExternalInput").ap()
    WS_d = dt("WS", [1024, 2048], F32, kind="ExternalInput").ap()
    WM_d = dt("WM", [1024, 1536], F32, kind="ExternalInput").ap()
    cd = {k: dt(k, sh, ty, kind="ExternalInput").ap() for k, (sh, ty) in A_CONST_SHAPES(SL).items()}
    pd = {k: dt(k, sh, F32, kind="ExternalInput").ap() for k, sh in A_PARAM_SHAPES.items()}
    ao = dt("ao", [SL, 512], BF16, kind="ExternalOutput").ap()
    S = Sched(nc)
    C = make_pools(S, n_wbuf=3)
    emit_phaseA(C, SL, x, WS_d, WM_d, cd, pd, ao, parts)
    S.emit()
    S.close()
    return nc


def norm_evac(C, ps, pt, gains, gtok, gcol, onesbd, otok, tmps, dsts):
    S = C.S
    qf, qft, sq, sqt, rs, rst = tmps
    N = ps.shape[-1]
    S.add("act", lambda e: e.activation(out=qf[:, :N], in_=ps, func=AF.Copy), [pt], [qft])
    S.add("act", lambda e: e.activation(out=sq[:, :N], in_=ps, func=AF.Square), [pt], [sqt])
    p2, pt2 = next_ps(C)
    S.add("pe", lambda e: e.matmul(p2[:, :N], lhsT=onesbd[:], rhs=sq[:, :N], start=True, stop=True), [sqt, otok], [pt2])
    S.add("act", lambda e: e.activation(out=rs[:, :N], in_=p2[:, :N], func=AF.Ln, scale=1.0 / 64, bias=C.epsb[:, 0:1]), [pt2, C.epst], [rst])
    S.add("act", lambda e: e.activation(out=rs[:, :N], in_=rs[:, :N], func=AF.Exp, scale=-0.5), [rst], [rst])
    for (dst, lo, hi, tok) in dsts:
        S.add("dve", lambda e, dst=dst, lo=lo, hi=hi: e.scalar_tensor_tensor(
            out=dst, in0=qf[lo:hi, :N], scalar=gains[lo:hi, gcol:gcol + 1], in1=rs[lo:hi, :N],
            op0=ALU.mult, op1=ALU.mult), [qft, rst, gtok], accw=[tok])


def emit_phaseA(C, SL, x, WS_d, WM_d, cd, pd, ao, parts=("ret", "nsa"), xin_tok=None, ao_tok=None, xmap=None):
    C.xmap = xmap or (lambda t: t)
    S = C.S
    nchunk = SL // 512
    xin_tok = xin_tok or Tok()
    ao_tok = ao_tok or Tok()
    WSs = WSpec(S, "WSs", WS_d, 1024, 2048, "S")
    WMs = WSpec(S, "WMs", WM_d, 1024, 1536, "M")
    gains = S.sbuf("gainsA", [128, 10], F32)
    gtok = Tok()
    ident = S.sbuf("identA", [128, 128], F32)
    itok = Tok()
    C.epsb = S.sbuf("epsb", [128, 1], F32)
    C.epst = Tok()
    S.dma(gains[:], pd["gainsA"], writes=[gtok])
    S.dma(ident[:], cd["ident"], writes=[itok])
    S.add("dve", lambda e: e.memset(C.epsb[:], EPS), [], [C.epst])
    prep_weight(C, WSs)
    prep_weight(C, WMs)
    if "ret" in parts:
        with S.scope():
            emit_ret_pass(C, SL, x, xin_tok, WSs, WMs, cd, gains, gtok, ident, itok, ao, ao_tok)
    if "nsa" in parts:
        emit_nsa(C, SL, x, xin_tok, WSs, WMs, cd, pd, gains, gtok, ident, itok, ao, ao_tok)


def load_x_rms(C, x, xin_tok, t0, xt, xtok, junk, jtok, ss, sstok, rstd, rtok, gains, gtok, hT, htok, ident, itok):
    S = C.S
    xr0 = C.xmap(t0)
    S.dma(xt[:], x[xr0:xr0 + 512, :].rearrange("(j p) d -> p j d", p=128), reads=[xin_tok], writes=[xtok])
    for j in range(4):
        S.add("act", lambda e, j=j: e.activation(out=junk[:], in_=xt[:, j, :], func=AF.Square,
                                                 accum_out=ss[:, j:j + 1]), [xtok], [jtok, sstok])
    S.add("dve", lambda e: e.tensor_scalar(out=rstd[:], in0=ss[:], scalar1=1.0 / D_MODEL, scalar2=EPS,
                                           op0=ALU.mult, op1=ALU.add), [sstok], [rtok])
    S.add("act", lambda e: e.activation(out=rstd[:], in_=rstd[:], func=AF.Sqrt), [rtok], [rtok])
    S.add("dve", lambda e: e.reciprocal(out=rstd[:], in_=rstd[:]), [rtok], [rtok])
    for j in range(4):
        S.add("act", lambda e, j=j: e.activation(out=xt[:, j, :], in_=xt[:, j, :], func=AF.Copy,
                                                 scale=rstd[:, j:j + 1]), [xtok, rtok], [xtok])
    for kt in range(8):
        ps, pt = next_ps(C)
        for j in range(4):
            S.add("pe", lambda e, ps=ps, j=j, kt=kt: e.transpose(
                out=ps[:, j * 128:(j + 1) * 128], in_=xt[:, j, kt * 128:(kt + 1) * 128], identity=ident[:]),
                [xtok, itok], [pt])
        if kt % 2 == 0:
            S.add("dve", lambda e, ps=ps, kt=kt: e.tensor_scalar(
                out=hT[:, kt, :], in0=ps[:], scalar1=gains[:, kt:kt + 1], scalar2=None, op0=ALU.mult),
                [pt, gtok], accw=[htok])
        else:
            S.add("act", lambda e, ps=ps, kt=kt: e.activation(
                out=hT[:, kt, :], in_=ps[:], func=AF.Copy, scale=gains[:, kt:kt + 1]), [pt, gtok], accw=[htok])


def emit_ret_pass(C, SL, x, xin_tok, WSs, WMs, cd, gains, gtok, ident, itok, ao, ao_tok):
    S = C.S
    sb = S.sbuf
    xt = sb("r_xt", [128, 4, 1024], F32)
    junk = sb("r_junk", [128, 1024], F32)
    ss = sb("r_ss", [128, 4], F32)
    rstd = sb("r_rstd", [128, 4], F32)
    hT = sb("r_hT", [128, 8, 512], BF16)
    rq_tab = sb("r_rqtab", [128, 2, 512], F32)
    rk_tab = sb("r_rktab", [128, 2, 512], F32)
    ctk_t = sb("r_ctk", [128, 4, 2, 64], F32)
    stk_t = sb("r_stk", [128, 4, 2, 64], F32)
    xi_t = sb("r_xi", [128, 2, 512], F32)
    decT = sb("r_dec", [128, 2, 128], F32)
    gch = sb("r_gch", [128, 2], F32)
    t1 = [sb("r_t1_%d" % i, [128, 512], F32) for i in range(2)]
    t2 = [sb("r_t2_%d" % i, [128, 512], F32) for i in range(2)]
    tmpq = sb("r_tmpq", [128, 512], F32)
    QrT = sb("r_QrT", [128, 2, 512], BF16)
    QrxT = sb("r_QrxT", [128, 2, 512], BF16)
    KrT = sb("r_KrT", [128, 2, 512], BF16)
    Vr = sb("r_Vr", [128, 4, 256], BF16)
    kz = sb("r_kz", [128, 4, 2, 128], BF16)
    sg = sb("r_sg", [128, 4, 256], F32)
    tabcd = [sb("r_tabcd%d" % i, [128, 2, 64], F32) for i in range(4)]
    IT = [sb("r_IT%d" % i, [128, 128], BF16) for i in range(2)]
    yr = sb("r_yr", [128, 4, 2, 128], F32)
    ssr = sb("r_ssr", [128, 8], F32)
    rr = sb("r_rr", [128, 8], F32)
    ro = sb("r_ro", [128, 4, 256], BF16)
    R = sb("r_R", [128, 2, 128], F32)
    Rb = sb("r_Rb", [128, 2, 128], BF16)
    junkb = sb("r_junkb", [128, 128], BF16)
    (xtok, jtok, sstok, rtok, htok, rqt, rkt, ctt, stt, xit, dect, gcht, tmpqt, qrt, qrxt, krt, vrt, kzt, sgt,
     yrt, ssrt, rrt, rot, jbt) = [Tok() for _ in range(24)]
    t1t, t2t = [Tok(), Tok()], [Tok(), Tok()]
    tabt = [Tok() for _ in range(4)]
    ITt = [Tok(), Tok()]
    Rt, Rbt = [Tok(), Tok()], [Tok(), Tok()]
    S.dma(xi_t[:], cd["xi"], writes=[xit])
    S.dma(decT[:], cd["decayT"], writes=[dect])
    S.dma(gch[:], cd["gch"], writes=[gcht])
    for h in range(2):
        S.add("dve", lambda e, h=h: e.memset(R[:, h, :], 0.0), [], [Rt[h]])
        S.add("pool", lambda e, h=h: e.memset(Rb[:, h, :], 0.0), [], [Rbt[h]])
    nchunk = SL // 512
    tn = 0
    itn = 0
    for c in range(nchunk):
        t0 = c * 512
        load_x_rms(C, x, xin_tok, t0, xt, xtok, junk, jtok, ss, sstok, rstd, rtok, gains, gtok, hT, htok, ident, itok)
        S.dma(rq_tab[:], cd["ropeq"][:, :, t0:t0 + 512], writes=[rqt])
        S.dma(rk_tab[:], cd["ropek"][:, :, t0:t0 + 512], writes=[rkt])
        S.dma(ctk_t[:], cd["ctk"][t0:t0 + 512].rearrange("(j p) h i -> p j h i", p=128), writes=[ctt])
        S.dma(stk_t[:], cd["stk"][t0:t0 + 512].rearrange("(j p) h i -> p j h i", p=128), writes=[stt])
        if DBG_R <= 1:
            continue
        for u in (2, 3):
            wv, wt = load_unit(C, WSs, u)
            isq = (u == 2)
            tab, tabt_ = (rq_tab, rqt) if isq else (rk_tab, rkt)
            for h in range(2):
                a, at_ = t1[tn % 2], t1t[tn % 2]
                b, bt_ = t2[tn % 2], t2t[tn % 2]
                tn += 1
                for half, (dstb, dtok) in enumerate(((a, at_), (b, bt_))):
                    f = 2 * h + half
                    ps, pt = next_ps(C)
                    for kt in range(8):
                        S.add("pe", lambda e, ps=ps, wv=wv, f=f, kt=kt: e.matmul(
                            ps[:], lhsT=wv[:, f, kt, :], rhs=hT[:, kt, :], start=(kt == 0), stop=(kt == 7)),
                            [wt, htok], [pt])
                    S.add("dve", lambda e, ps=ps, dstb=dstb, half=half, tab=tab: e.tensor_tensor(
                        out=dstb[:], in0=ps[:], in1=tab[:, half, :], op=ALU.mult), [pt, tabt_], [dtok])
                if isq:
                    S.add("pool", lambda e, a=a, b=b: e.tensor_tensor(out=tmpq[:], in0=a[:], in1=b[:], op=ALU.add),
                          [at_, bt_], [tmpqt])
                    S.add("act", lambda e, h=h: e.activation(out=QrT[:, h, :], in_=tmpq[:], func=AF.Copy),
                          [tmpqt], accw=[qrt])
                    S.add("pool", lambda e, h=h: e.tensor_tensor(out=QrxT[:, h, :], in0=tmpq[:], in1=xi_t[:, h, :],
                                                                 op=ALU.mult), [tmpqt, xit], accw=[qrxt])
                else:
                    S.add("pool", lambda e, a=a, b=b, h=h: e.tensor_tensor(out=KrT[:, h, :], in0=a[:], in1=b[:],
                                                                           op=ALU.add), [at_, bt_], accw=[krt])
        if DBG_R <= 2:
            continue
        wv, wt = load_unit(C, WMs, 0)
        for j in range(4):
            ps, pt = next_ps(C)
            for kt in range(8):
                S.add("pe", lambda e, ps=ps, wv=wv, j=j, kt=kt: e.matmul(
                    ps[:], lhsT=hT[:, kt, j * 128:(j + 1) * 128], rhs=wv[:, kt, :], start=(kt == 0), stop=(kt == 7)),
                    [wt, htok], [pt])
            if DBG_SUB == 1:
                S.add("act", lambda e, ps=ps, j=j: e.activation(out=Vr[:, j, :], in_=ps[:, 256:512], func=AF.Copy),
                      [pt], accw=[vrt])
                continue
            pv = ps[:, 0:256].rearrange("p (h t i) -> p h t i", h=2, t=2)
            x1, x2 = pv[:, :, 0, :], pv[:, :, 1, :]
            kzv = kz[:, j].rearrange("p h (t i) -> p h t i", t=2)
            ta, tb, tc, td = tabcd
            S.add("dve", lambda e, x1=x1, j=j: e.tensor_tensor(out=ta[:], in0=x1, in1=ctk_t[:, j], op=ALU.mult),
                  [pt, ctt], [tabt[0]])
            S.add("dve", lambda e, x2=x2, j=j: e.tensor_tensor(out=tb[:], in0=x2, in1=stk_t[:, j], op=ALU.mult),
                  [pt, stt], [tabt[1]])
            S.add("dve", lambda e, x1=x1, j=j: e.tensor_tensor(out=tc[:], in0=x1, in1=stk_t[:, j], op=ALU.mult),
                  [pt, stt], [tabt[2]])
            S.add("dve", lambda e, x2=x2, j=j: e.tensor_tensor(out=td[:], in0=x2, in1=ctk_t[:, j], op=ALU.mult),
                  [pt, ctt], [tabt[3]])
            if DBG_SUB == 2:
                continue
            S.add("dve", lambda e, kzv=kzv: e.tensor_tensor(out=kzv[:, :, 0, :], in0=ta[:], in1=tb[:], op=ALU.subtract),
                  [tabt[0], tabt[1]] if DBG_SUB != 3 else [], accw=[kzt])
            S.add("dve", lambda e, kzv=kzv: e.tensor_tensor(out=kzv[:, :, 1, :], in0=tc[:], in1=td[:], op=ALU.add),
                  [tabt[2], tabt[3]] if DBG_SUB != 3 else [], accw=[kzt])
            S.add("act", lambda e, ps=ps, j=j: e.activation(out=Vr[:, j, :], in_=ps[:, 256:512], func=AF.Copy),
                  [pt], accw=[vrt])
        if DBG_R <= 3:
            continue
        wv, wt = load_unit(C, WMs, 1)
        for j in range(4):
            ps, pt = next_ps(C)
            for kt in range(8):
                S.add("pe", lambda e, ps=ps, wv=wv, j=j, kt=kt: e.matmul(
                    ps[:, 0:256], lhsT=hT[:, kt, j * 128:(j + 1) * 128], rhs=wv[:, kt, 0:256],
                    start=(kt == 0), stop=(kt == 7)), [wt, htok], [pt])
            S.add("act", lambda e, ps=ps, j=j: e.activation(out=sg[:, j, :], in_=ps[:, 0:256], func=AF.Silu),
                  [pt], accw=[sgt])
        if DBG_R <= 4:
            continue
        for j in range(4):
            js = slice(j * 128, (j + 1) * 128)
            for h in range(2):
                hs = slice(h * 128, (h + 1) * 128)
                psI, ptI = next_ps(C)
                S.add("pe", lambda e, psI=psI, h=h, js=js: e.matmul(
                    psI[:, 0:128], lhsT=KrT[:, h, js], rhs=QrT[:, h, js], start=True, stop=True), [krt, qrt], [ptI])
                it_, itt = IT[itn % 2], ITt[itn % 2]
                itn += 1
                S.add("dve", lambda e, psI=psI, it_=it_, h=h: e.tensor_tensor(
                    out=it_[:], in0=psI[:, 0:128], in1=decT[:, h, :], op=ALU.mult), [ptI, dect], [itt])
                psO, ptO = next_ps(C)
                S.add("pe", lambda e, psO=psO, it_=it_, j=j, hs=hs: e.matmul(
                    psO[:, 0:128], lhsT=it_[:], rhs=Vr[:, j, hs], start=True, stop=False), [itt, vrt], [ptO])
                S.add("pe", lambda e, psO=psO, h=h, js=js: e.matmul(
                    psO[:, 0:128], lhsT=QrxT[:, h, js], rhs=Rb[:, h, :], start=False, stop=True), [qrxt, Rbt[h]], [ptO])
                S.add("act", lambda e, psO=psO, j=j, h=h: e.activation(
                    out=junkb[:], in_=psO[:, 0:128], func=AF.Square, accum_out=ssr[:, j * 2 + h:j * 2 + h + 1]),
                    [ptO], [jbt], accw=[ssrt])
                S.add("dve", lambda e, psO=psO, j=j, h=h: e.tensor_copy(out=yr[:, j, h, :], in_=psO[:, 0:128]),
                      [ptO], accw=[yrt])
                psK, ptK = next_ps(C)
                S.add("pe", lambda e, psK=psK, j=j, h=h, hs=hs: e.matmul(
                    psK[:, 0:128], lhsT=kz[:, j, h, :], rhs=Vr[:, j, hs], start=True, stop=True), [kzt, vrt], [ptK])
                S.add("dve", lambda e, psK=psK, h=h: e.scalar_tensor_tensor(
                    out=R[:, h, :], in0=R[:, h, :], scalar=gch[:, h:h + 1], in1=psK[:, 0:128],
                    op0=ALU.mult, op1=ALU.add), [ptK, gcht, Rt[h]], [Rt[h]])
                S.add("pool", lambda e, h=h: e.tensor_copy(out=Rb[:, h, :], in_=R[:, h, :]), [Rt[h]], [Rbt[h]])
        if DBG_R <= 5:
            continue
        S.add("dve", lambda e: e.tensor_scalar(out=rr[:], in0=ssr[:], scalar1=1.0 / 128, scalar2=EPS,
                                               op0=ALU.mult, op1=ALU.add), [ssrt], [rrt])
        S.add("act", lambda e: e.activation(out=rr[:], in_=rr[:], func=AF.Sqrt), [rrt], [rrt])
        S.add("dve", lambda e: e.reciprocal(out=rr[:], in_=rr[:]), [rrt], [rrt])
        for j in range(4):
            for h in range(2):
                hs = slice(h * 128, (h + 1) * 128)
                S.add("dve", lambda e, j=j, h=h, hs=hs: e.scalar_tensor_tensor(
                    out=ro[:, j, hs], in0=yr[:, j, h, :], scalar=rr[:, j * 2 + h:j * 2 + h + 1], in1=sg[:, j, hs],
                    op0=ALU.mult, op1=ALU.mult), [yrt, rrt, sgt], accw=[rot])
        S.dma(ao[t0:t0 + 512, 256:512].rearrange("(j p) d -> p j d", p=128), ro[:], reads=[rot], accw=[ao_tok], q="act")


HORD = (0, 2, 1, 3)


def emit_nsa(C, SL, x, xin_tok, WSs, WMs, cd, pd, gains, gtok, ident, itok, ao, ao_tok):
    S = C.S
    sb = S.sbuf
    NT = SL // 128
    nb = SL // 16
    assert nb <= 512
    NCT = max(1, nb // 128)
    QT = sb("n_QT", [128, 2, SL], BF16)
    Kslo, Kshi = sb("n_Kslo", [128, SL], BF16), sb("n_Kshi", [128, SL], BF16)
    Kwlo, Kwhi = sb("n_Kwlo", [128, SL], BF16), sb("n_Kwhi", [128, SL], BF16)
    V1 = sb("n_V1", [128, NT, 2, 65], BF16)
    Gt = sb("n_Gt", [128, NT, 12], F32)
    kclo, kchi = sb("n_kclo", [128, 512], BF16), sb("n_kchi", [128, 512], BF16)
    Vc1 = sb("n_Vc1", [128, 4, 65], BF16)
    onesbd = sb("n_onesbd", [128, 128], BF16)
    identb = sb("n_identb", [128, 128], BF16)
    qf = sb("n_qf", [128, 512], F32)
    sq = sb("n_sq", [128, 512], BF16)
    rs = sb("n_rs", [128, 512], F32)
    qtok, kst, kwt, kcvt, v1t, gtt, kct, vct, onest, ibt, qft, sqt, rst = [Tok() for _ in range(13)]
    tmps = (qf, qft, sq, sqt, rs, rst)
    S.dma(onesbd[:], cd["onesbd"], writes=[onest])
    S.add("dve", lambda e: e.tensor_copy(out=identb[:], in_=ident[:]), [itok], [ibt])
    for (t_, tk) in ((Kslo, kst), (Kshi, kst), (Kwlo, kwt), (Kwhi, kwt), (kclo, kct), (kchi, kct)):
        S.add("pool", lambda e, t_=t_: e.memset(t_[:], 0.0), [], [tk])
    S.add("pool", lambda e: e.memset(V1[:], 1.0), [], [v1t])
    S.add("pool", lambda e: e.memset(Vc1[:], 0.0), [], [vct])
    S.add("pool", lambda e: e.memset(Vc1[:, :, 64:65], 1.0), [vct], [vct])
    kc_scope = S.scope()
    kc_scope.__enter__()
    KcVcT = sb("n_KcVcT", [128, SL + 16], BF16)
    with S.scope():
        xt = sb("n_xt", [128, 4, 1024], F32)
        junk = sb("n_junk", [128, 1024], F32)
        ss = sb("n_ss", [128, 4], F32)
        rstd = sb("n_rstd", [128, 4], F32)
        hT = sb("n_hT", [128, 8, 512], BF16)
        xtok, jtok, sstok, rtok, htok = [Tok() for _ in range(5)]
        for c in range(SL // 512):
            t0 = c * 512
            cs = slice(t0, t0 + 512)
            load_x_rms(C, x, xin_tok, t0, xt, xtok, junk, jtok, ss, sstok, rstd, rtok, gains, gtok, hT, htok, ident, itok)
            for u in (0, 1):
                wv, wt = load_unit(C, WSs, u)
                for f in range(4 if u == 0 else 1):
                    ft = u * 4 + f
                    ps, pt = next_ps(C)
                    for kt in range(8):
                        S.add("pe", lambda e, ps=ps, wv=wv, f=f, kt=kt: e.matmul(
                            ps[:], lhsT=wv[:, f, kt, :], rhs=hT[:, kt, :], start=(kt == 0), stop=(kt == 7)),
                            [wt, htok], [pt])
                    if ft < 2:
                        norm_evac(C, ps[:], pt, gains, gtok, 8, onesbd, onest, tmps, [(QT[:, ft, cs], 0, 128, qtok)])
                    elif ft == 2:
                        norm_evac(C, ps[:], pt, gains, gtok, 9, onesbd, onest, tmps,
                                  [(Kslo[0:64, cs], 0, 64, kst), (Kshi[64:128, cs], 64, 128, kst)])
                    elif ft == 3:
                        norm_evac(C, ps[:], pt, gains, gtok, 9, onesbd, onest, tmps,
                                  [(Kwlo[0:64, cs], 0, 64, kwt), (Kwhi[64:128, cs], 64, 128, kwt)])
                    else:
                        S.add("act", lambda e, ps=ps, cs=cs: e.activation(out=KcVcT[:, cs], in_=ps[:], func=AF.Copy),
                              [pt], accw=[kcvt])
            wv, wt = load_unit(C, WMs, 2)
            for j in range(4):
                tile_i = c * 4 + j
                ps, pt = next_ps(C)
                for kt in range(8):
                    S.add("pe", lambda e, ps=ps, wv=wv, j=j, kt=kt: e.matmul(
                        ps[:, 0:140], lhsT=hT[:, kt, j * 128:(j + 1) * 128], rhs=wv[:, kt, 0:140],
                        start=(kt == 0), stop=(kt == 7)), [wt, htok], [pt])
                S.add("dve", lambda e, ps=ps, tile_i=tile_i: e.tensor_copy(
                    out=V1[:, tile_i, :, 0:64], in_=ps[:, 0:128].rearrange("p (b d) -> p b d", b=2)), [pt], accw=[v1t])
                S.add("act", lambda e, ps=ps, tile_i=tile_i: e.activation(
                    out=Gt[:, tile_i, :], in_=ps[:, 128:140], func=AF.Sigmoid), [pt], accw=[gtt])
    with S.scope():
        W1b = sb("n_W1b", [128, 32, 256], BF16)
        posT = sb("n_posT", [128, 32], F32)
        w2f = sb("n_w2f", [128, 2, 2, 128], F32)
        w2b = sb("n_w2b", [128, 2, 2, 128], BF16)
        zr = [sb("n_zr%d" % i, [128, 512], BF16) for i in range(4)]
        zrt = [Tok() for _ in range(4)]
        GT = sb("n_GT", [128, 4, 512], BF16)
        ga = sb("n_ga", [128, 512], F32)
        gb = sb("n_gb", [128, 512], F32)
        w1t, post, w2t, w2bt, GTt, gat, gbt = [Tok() for _ in range(7)]
        S.dma(posT[:], pd["posT"], writes=[post])
        S.dma(w2f[:], pd["w2"], writes=[w2t])
        S.add("dve", lambda e: e.tensor_copy(out=w2b[:], in_=w2f[:]), [w2t], [w2bt])
        S.add("dve", lambda e: e.tensor_copy(out=KcVcT[:, SL:SL + 16], in_=KcVcT[:, SL - 1:SL].to_broadcast([128, 16])),
              [kcvt], [kcvt])
        accs = [next_ps(C) for _ in range(4)]
        zn = 0
        for kv, srcw in enumerate((pd["w1k"], pd["w1v"])):
            for r0 in range(0, 32, 2):
                i = C.sn % 3
                C.sn += 1
                st, stt = C.stg[i], C.stgtok[i]
                S.dma(st[:, :512], srcw[:, r0:r0 + 2, :].rearrange("p r h -> p (r h)"), writes=[stt])
                eng = ("dve", "pool")[C.cast_rr % 2]
                C.cast_rr += 1
                S.add(eng, lambda e, st=st, r0=r0: e.tensor_copy(
                    out=W1b[:, r0:r0 + 2, :].rearrange("p r h -> p (r h)"), in_=st[:, :512]), [stt], accw=[w1t])
            for r in range(32):
                z, zt = zr[zn % 4], zrt[zn % 4]
                zn += 1
                if r < 16:
                    src = KcVcT[:, 0:16 * nb].rearrange("p (i s) -> p i s", s=16)[:, :, r]
                else:
                    src = KcVcT[:, 16:16 + 16 * nb].rearrange("p (i s) -> p i s", s=16)[:, :, r - 16]
                eng = ("dve", "pool")[r % 2]
                S.add(eng, lambda e, z=z, src=src, r=r: e.tensor_scalar(
                    out=z[:, :nb], in0=src, scalar1=posT[:, r:r + 1], scalar2=None, op0=ALU.add), [kcvt, post], [zt])
                for hid in range(2):
                    ps, pt = accs[kv * 2 + hid]
                    S.add("pe", lambda e, ps=ps, hid=hid, r=r, z=z: e.matmul(
                        ps[:, :nb], lhsT=W1b[:, r, hid * 128:(hid + 1) * 128], rhs=z[:, :nb],
                        start=(r == 0), stop=(r == 31)), [w1t, zt], [pt])
        for a in range(4):
            ps, pt = accs[a]
            S.add("act", lambda e, ps=ps: e.activation(out=ga[:, :nb], in_=ps[:, :nb], func=AF.Square), [pt], [gat])
            S.add("dve", lambda e: e.tensor_scalar(out=ga[:, :nb], in0=ga[:, :nb], scalar1=0.044715, scalar2=1.0,
                                                   op0=ALU.mult, op1=ALU.add), [gat], [gat])
            S.add("dve", lambda e, ps=ps: e.tensor_tensor(out=ga[:, :nb], in0=ga[:, :nb], in1=ps[:, :nb], op=ALU.mult),
                  [gat, pt], [gat])
            S.add("act", lambda e: e.activation(out=gb[:, :nb], in_=ga[:, :nb], func=AF.Sigmoid, scale=1.5957691216057308),
                  [gat], [gbt])
            S.add("dve", lambda e, ps=ps, a=a: e.tensor_tensor(out=GT[:, a, :nb], in0=gb[:, :nb], in1=ps[:, :nb],
                                                               op=ALU.mult), [gbt, pt], accw=[GTt])
        ps, pt = next_ps(C)
        for t in range(2):
            S.add("pe", lambda e, ps=ps, t=t: e.matmul(ps[:, :nb], lhsT=w2b[:, t, 0, :], rhs=GT[:, t, :nb],
                                                       start=(t == 0), stop=(t == 1)), [w2bt, GTt], [pt])
        norm_evac(C, ps[:, :nb], pt, gains, gtok, 9, onesbd, onest, tmps,
                  [(kclo[0:64, :nb], 0, 64, kct), (kchi[64:128, :nb], 64, 128, kct)])
        for ct in range(NCT):
            ps, pt = next_ps(C)
            wdt = min(128, nb)
            for t in range(2):
                S.add("pe", lambda e, ps=ps, t=t, ct=ct, wdt=wdt: e.matmul(
                    ps[:wdt, 0:64], lhsT=GT[:, 2 + t, ct * 128:ct * 128 + wdt], rhs=w2b[:, t, 1, 0:64],
                    start=(t == 0), stop=(t == 1)), [w2bt, GTt], [pt])
            S.add("dve", lambda e, ps=ps, ct=ct, wdt=wdt: e.tensor_copy(out=Vc1[:wdt, ct, 0:64], in_=ps[:wdt, 0:64]),
                  [pt], accw=[vct])
    kc_scope.__exit__(None, None, None)
    with S.scope():
        wexp = sb("n_wexp", [128, SL], BF16)
        triT = sb("n_triT", [128, 128], BF16)
        triU = sb("n_triU", [128, 128], BF16)
        cmQ = sb("n_cmQ", [128, 16, 128], BF16)
        cmT = sb("n_cmT", [128, 16, 128], BF16)
        wet, trt, trut, cmqt, cmtt = [Tok() for _ in range(5)]
        S.dma(wexp[:], cd["wexp"], writes=[wet])
        S.dma(triT[:], cd["triT"], writes=[trt])
        S.dma(triU[:], cd["triU"], writes=[trut])
        S.dma(cmQ[:], cd["cmaskQ"], writes=[cmqt])
        S.dma(cmT[:], cd["cmaskT"], writes=[cmtt])
        E4 = sb("n_E4", [128, 4, 512], F32)
        rsum = sb("n_rsum", [128, 4], F32)
        rinv = sb("n_rinv", [128, 4], F32)
        imp = sb("n_imp", [128, 512], F32)
        ib = sb("n_ib", [128, 128], F32)
        sc = sb("n_sc", [128, 128], F32)
        sc2 = sb("n_sc2", [128, 128], F32)
        m8a = sb("n_m8a", [128, 8], F32)
        m8b = sb("n_m8b", [128, 8], F32)
        nmf = sb("n_nmf", [128, 128], F32)
        nmb = sb("n_nmb", [128, 128], BF16)
        nmT = sb("n_nmT", [128, 128], BF16)
        PT = [sb("n_PT%d" % i, [128, 512], BF16) for i in range(3)]
        PTt = [Tok() for _ in range(3)]
        den = sb("n_den", [128, 4], F32)
        coef = sb("n_coef", [128, 4], F32)
        acc = sb("n_acc", [128, 4, 64], F32)
        ob = [sb("n_ob%d" % i, [128, 256], BF16) for i in range(2)]
        obt = [Tok(), Tok()]
        e4t, rsumt, rinvt, impt, ibt_, sct, sc2t, m8at, m8bt, nmft, nmbt, nmTt, dent, coeft, acct = [Tok() for _ in range(15)]
        npool = len(C.ps)
        Obank = [(C.ps[npool - 2], C.pstok[npool - 2]), (C.ps[npool - 1], C.pstok[npool - 1])]
        C.ps_active = npool - 2
        on = 0
        ptn = 0

        def att_branch(tiles, br, qt, first_branch):
            nonlocal on, ptn
            qs = slice(qt * 128, (qt + 1) * 128)
            O, Ot = Obank[on % 2]
            on += 1
            nt = len(tiles)
            for idx, (klo, khi, ktoks, v, vtok, masks) in enumerate(tiles):
                psT, ptT = next_ps(C)
                first = True
                for (ml, mlt, mr, mrt) in masks:
                    for cb in range(4):
                        S.add("pe", lambda e, psT=psT, ml=ml, mr=mr, cb=cb, first=first: e.matmul(
                            psT[:, cb * 128:(cb + 1) * 128], lhsT=ml, rhs=mr, start=first, stop=False,
                            skip_group_check=True), [mlt, mrt], [ptT])
                        first = False
                S.add("pe", lambda e, psT=psT, klo=klo, qs=qs, first=first: e.matmul(
                    psT[:, 0:256], lhsT=klo, rhs=QT[:, :, qs], start=first, stop=False, skip_group_check=True),
                    [ktoks, qtok], [ptT])
                S.add("pe", lambda e, psT=psT, khi=khi, qs=qs: e.matmul(
                    psT[:, 256:512], lhsT=khi, rhs=QT[:, :, qs], start=False, stop=True, skip_group_check=True),
                    [ktoks, qtok], [ptT])
                P, Pt_ = PT[ptn % 3], PTt[ptn % 3]
                ptn += 1
                S.add("act", lambda e, psT=psT, P=P: e.activation(out=P[:], in_=psT[:], func=AF.Exp, scale=0.125),
                      [ptT], [Pt_])
                for cb in range(4):
                    S.add("pe", lambda e, O=O, P=P, v=v, cb=cb, idx=idx: e.matmul(
                        O[:, cb * 65:(cb + 1) * 65], lhsT=P[:, cb * 128:(cb + 1) * 128], rhs=v,
                        start=(idx == 0 and cb == 0), stop=(idx == nt - 1), skip_group_check=True), [Pt_, vtok], [Ot])
            Ov = O[:, 0:260].rearrange("p (c d) -> p c d", c=4)
            S.add("dve", lambda e, Ov=Ov: e.tensor_scalar(out=den[:], in0=Ov[:, :, 64], scalar1=1e-30, scalar2=None,
                                                          op0=ALU.max), [Ot], [dent])
            S.add("dve", lambda e: e.reciprocal(out=den[:], in_=den[:]), [dent], [dent])
            gv = Gt[:, qt, br * 4:(br + 1) * 4].rearrange("p (a b) -> p b a", a=2)
            S.add("dve", lambda e, gv=gv: e.tensor_tensor(out=coef[:].rearrange("p (b a) -> p b a", b=2),
                                                          in0=den[:].rearrange("p (b a) -> p b a", b=2), in1=gv,
                                                          op=ALU.mult), [dent, gtt], [coeft])
            for cb in range(4):
                h = HORD[cb]
                if first_branch:
                    S.add("dve", lambda e, Ov=Ov, cb=cb, h=h: e.tensor_scalar(
                        out=acc[:, h, :], in0=Ov[:, cb, 0:64], scalar1=coef[:, cb:cb + 1], scalar2=None,
                        op0=ALU.mult), [Ot, coeft], [acct])
                else:
                    S.add("dve", lambda e, Ov=Ov, cb=cb, h=h: e.scalar_tensor_tensor(
                        out=acc[:, h, :], in0=Ov[:, cb, 0:64], scalar=coef[:, cb:cb + 1], in1=acc[:, h, :],
                        op0=ALU.mult, op1=ALU.add), [Ot, coeft, acct], [acct])

        for qt in range(NT):
            qs = slice(qt * 128, (qt + 1) * 128)
            ctl = (8 * qt + 6) // 128
            ncol = 128 * (ctl + 1)
            r16 = qt % 16
            for cb, (p, Kc) in enumerate(((0, kclo), (1, kclo), (0, kchi), (1, kchi))):
                psS, ptS = next_ps(C)
                first = True
                if ctl > 0:
                    S.add("pe", lambda e, psS=psS, p=p, Kc=Kc, qs=qs, ctl=ctl: e.matmul(
                        psS[:, 0:ctl * 128], lhsT=QT[:, p, qs], rhs=Kc[:, 0:ctl * 128], start=True, stop=False,
                        skip_group_check=True), [qtok, kct], [ptS])
                    first = False
                S.add("pe", lambda e, psS=psS, ctl=ctl, ncol=ncol, r16=r16, first=first: e.matmul(
                    psS[:, ctl * 128:ncol], lhsT=identb[:], rhs=cmQ[:, r16, :], start=first, stop=False,
                    skip_group_check=True), [ibt, cmqt], [ptS])
                S.add("pe", lambda e, psS=psS, p=p, Kc=Kc, qs=qs, ctl=ctl, ncol=ncol: e.matmul(
                    psS[:, ctl * 128:ncol], lhsT=QT[:, p, qs], rhs=Kc[:, ctl * 128:ncol], start=False, stop=True,
                    skip_group_check=True), [qtok, kct], [ptS])
                S.add("act", lambda e, psS=psS, cb=cb, ncol=ncol: e.activation(
                    out=E4[:, cb, :ncol], in_=psS[:, :ncol], func=AF.Exp, scale=0.125, accum_out=rsum[:, cb:cb + 1]),
                    [ptS], accw=[e4t, rsumt])
            S.add("dve", lambda e: e.tensor_scalar(out=rinv[:], in0=rsum[:], scalar1=1e-30, scalar2=None, op0=ALU.max),
                  [rsumt], [rinvt])
            S.add("dve", lambda e: e.reciprocal(out=rinv[:], in_=rinv[:]), [rinvt], [rinvt])
            S.add("dve", lambda e, ncol=ncol: e.tensor_scalar(out=imp[:, :ncol], in0=E4[:, 0, :ncol], scalar1=rinv[:, 0:1],
                                                             scalar2=None, op0=ALU.mult), [e4t, rinvt], [impt])
            for cb in range(1, 4):
                S.add("dve", lambda e, cb=cb, ncol=ncol: e.scalar_tensor_tensor(
                    out=imp[:, :ncol], in0=E4[:, cb, :ncol], scalar=rinv[:, cb:cb + 1], in1=imp[:, :ncol],
                    op0=ALU.mult, op1=ALU.add), [e4t, rinvt, impt], [impt])
            nblk = ncol // 4
            S.add("dve", lambda e, ncol=ncol, nblk=nblk: e.tensor_reduce(
                out=ib[:, :nblk], in_=imp[:, :ncol].rearrange("p (j r) -> p j r", r=4), axis=AX.X, op=ALU.add),
                [impt], [ibt_])
            S.add("dve", lambda e, nblk=nblk: e.tensor_tensor(
                out=ib[:, 1:nblk], in0=ib[:, 1:nblk],
                in1=imp[:, 0:4 * (nblk - 1)].rearrange("p (j r) -> p j r", r=4)[:, :, 3], op=ALU.add),
                [impt, ibt_], [ibt_])
            S.add("pool", lambda e: e.memset(sc[:], -1e30), [], [sct])
            if qt > 0:
                S.add("dve", lambda e, qt=qt: e.tensor_copy(out=sc[:, 0:2 * qt], in_=ib[:, 0:2 * qt]), [ibt_, sct], [sct])
                S.add("dve", lambda e, qt=qt: e.memset(sc[0:64, 2 * qt - 1:2 * qt], 1e4), [sct], [sct])
            S.add("dve", lambda e: e.memset(sc[:, 0:1], 1e4), [sct], [sct])
            S.add("dve", lambda e, qt=qt: e.memset(sc[:, 2 * qt:2 * qt + 1], 1e4), [sct], [sct])
            S.add("dve", lambda e, qt=qt: e.memset(sc[64:128, 2 * qt + 1:2 * qt + 2], 1e4), [sct], [sct])
            S.add("dve", lambda e: e.max(out=m8a[:], in_=sc[:]), [sct], [m8at])
            S.add("dve", lambda e: e.match_replace(out=sc2[:], in_to_replace=m8a[:], in_values=sc[:], imm_value=-1e30),
                  [sct, m8at], [sc2t])
            S.add("dve", lambda e: e.max(out=m8b[:], in_=sc2[:]), [sc2t], [m8bt])
            S.add("dve", lambda e: e.tensor_scalar(out=nmf[:], in0=sc[:], scalar1=m8b[:, 7:8], scalar2=None,
                                                   op0=ALU.is_ge), [sct, m8bt], [nmft])
            S.add("dve", lambda e: e.tensor_scalar(out=nmb[:], in0=nmf[:], scalar1=-1.0, scalar2=-NEG,
                                                   op0=ALU.add, op1=ALU.mult), [nmft], [nmbt])
            pb, pbt = next_psb(C)
            S.add("pe", lambda e, pb=pb: e.transpose(out=pb[:, 0:128], in_=nmb[:], identity=identb[:]), [nmbt, ibt], [pbt])
            S.add("dve", lambda e, pb=pb: e.tensor_copy(out=nmT[:], in_=pb[:, 0:128]), [pbt], [nmTt])
            tiles = []
            for ct in range(ctl + 1):
                cs = slice(ct * 128, (ct + 1) * 128)
                masks = [(identb[:], ibt, cmT[:, r16, :], cmtt)] if ct == ctl else []
                tiles.append((kclo[:, cs], kchi[:, cs], kct, Vc1[:, ct, :], vct, masks))
            att_branch(tiles, 0, qt, True)
            tiles = []
            for kt in range(qt + 1):
                ks_ = slice(kt * 128, (kt + 1) * 128)
                masks = [(wexp[:, ks_], wet, nmT[:], nmTt)]
                if kt == qt:
                    masks.append((identb[:], ibt, triT[:], trt))
                tiles.append((Kslo[:, ks_], Kshi[:, ks_], kst, V1[:, kt, 0, :], v1t, masks))
            att_branch(tiles, 1, qt, False)
            tiles = []
            for kt in range(max(0, qt - 4), qt + 1):
                ks_ = slice(kt * 128, (kt + 1) * 128)
                masks = []
                if kt == qt:
                    masks.append((identb[:], ibt, triT[:], trt))
                if kt == qt - 4:
                    masks.append((identb[:], ibt, triU[:], trut))
                tiles.append((Kwlo[:, ks_], Kwhi[:, ks_], kwt, V1[:, kt, 1, :], v1t, masks))
            att_branch(tiles, 2, qt, False)
            o_, ot_ = ob[qt % 2], obt[qt % 2]
            S.add("act", lambda e, o_=o_: e.activation(out=o_[:], in_=acc[:].rearrange("p h d -> p (h d)"), func=AF.Copy),
                  [acct], [ot_])
            S.dma(ao[qs, 0:256], o_[:], reads=[ot_], accw=[ao_tok], q="act")
        C.ps_active = npool


from concourse.bass_utils import run_bass_kernel_spmd

_NC_CACHE = {}


def _get_nc(key, builder):
    if key not in _NC_CACHE:
        _NC_CACHE[key] = builder()
    return _NC_CACHE[key]


def kernel(**inputs):
    z = {k: np.asarray(v) for k, v in inputs.items()}
    x = np.ascontiguousarray(z["x"], np.float32)
    B, SL, D = x.shape
    depth = z["w_in"].shape[0]
    ncA = _get_nc("A", lambda: build_phaseA(SL))
    ncB = _get_nc("B", lambda: build_phaseB(SL // 2))
    constsA = [hostA_consts(g, SL) for g in range(2)]
    ident = np.eye(128, dtype=np.float32)
    for L in range(depth):
        insA = []
        wsm = [hostA_weights(z["w_in"][L], g) for g in range(2)]
        prm = [hostA_params(z, L, g) for g in range(2)]
        for c in range(8):
            b, g = c // 2, c % 2
            d = dict(x=np.ascontiguousarray(x[b]), WS=wsm[g][0], WM=wsm[g][1])
            d.update(constsA[g])
            d.update(prm[g])
            insA.append(d)
        resA = run_bass_kernel_spmd(ncA, insA, core_ids=list(range(8)))
        ao = [np.asarray(resA.results[c]["ao"]) for c in range(8)]

        def gl(v):
            return np.ascontiguousarray(np.asarray(v, np.float32).reshape(8, 128).T)
        gains = np.ascontiguousarray(np.concatenate(
            [gl(z["norm_mix"][L]), gl(z["norm_mlp"][L]), gl(z["norm_ple"][L])], 1), np.float32)
        w_merge = np.ascontiguousarray(z["w_in"][L][:, 3352:5400], np.float32)
        insB = []
        for c in range(8):
            b, hf = c // 2, c % 2
            sl = slice(hf * (SL // 2), (hf + 1) * (SL // 2))
            attn = np.concatenate([ao[2 * b][sl, :256], ao[2 * b + 1][sl, :256],
                                   ao[2 * b][sl, 256:], ao[2 * b + 1][sl, 256:]], 1)
            insB.append(dict(
                x=np.ascontiguousarray(x[b, sl]), attn=np.ascontiguousarray(attn),
                p=np.ascontiguousarray(z["p"][L, b, sl], np.float32), gains=gains, ident=ident,
                w_merge=w_merge, w_up_nsa=np.ascontiguousarray(z["w_up_nsa"][L], np.float32),
                w_up_ret=np.ascontiguousarray(z["w_up_ret"][L], np.float32),
                w_out=np.ascontiguousarray(z["w_out"][L], np.float32),
                w_ff1=np.ascontiguousarray(z["w_ff1"][L], np.float32),
                w_ff2=np.ascontiguousarray(z["w_ff2"][L], np.float32),
                w_gate=np.ascontiguousarray(z["w_ple_gate"][L], np.float32),
                w_ple=np.ascontiguousarray(z["w_ple"][L], np.float32)))
        resB = run_bass_kernel_spmd(ncB, insB, core_ids=list(range(8)))
        xn = np.empty_like(x)
        for c in range(8):
            b, hf = c // 2, c % 2
            xn[b, hf * (SL // 2):(hf + 1) * (SL // 2)] = np.asarray(resB.results[c]["xo"])
        x = xn
    return x


B_WNAMES = (("w_merge", 1024, 2048), ("w_up_nsa", 512, 1024), ("w_up_ret", 512, 1024), ("w_out", 1024, 1024),
            ("w_ff1", 1024, 4096), ("w_ff2", 4096, 1024), ("w_gate", 1024, 1024), ("w_ple", 256, 1024))
PAIR_GROUPS = [[0, 1], [2, 3], [4, 5], [6, 7]]


def build_fused(SL=8192, depth=2):
    nc = bass.Bass("TRN2", target_bir_lowering=False)
    dt = nc.dram_tensor
    T = SL // 2
    x_full = dt("x", [SL, 1024], F32, kind="ExternalInput").ap()
    xh = dt("xh", [T, 1024], F32, kind="ExternalInput").ap()
    hmask_d = dt("hmask", [128, 2], F32, kind="ExternalInput").ap()
    cd = {k: dt(k, sh, ty, kind="ExternalInput").ap() for k, (sh, ty) in A_CONST_SHAPES(SL).items()}
    WS_d, WM_d, pd, p_d, gB_d, wd = [], [], [], [], [], []
    for L in range(depth):
        WS_d.append(dt("WS%d" % L, [1024, 2048], F32, kind="ExternalInput").ap())
        WM_d.append(dt("WM%d" % L, [1024, 1536], F32, kind="ExternalInput").ap())
        pd.append({k: dt("%s%d" % (k, L), sh, F32, kind="ExternalInput").ap() for k, sh in A_PARAM_SHAPES.items()})
        p_d.append(dt("p%d" % L, [T, 256], F32, kind="ExternalInput").ap())
        gB_d.append(dt("gainsB%d" % L, [128, 24], F32, kind="ExternalInput").ap())
        wd.append({n: dt("%s%d" % (n, L), [K, N], F32, kind="ExternalInput").ap() for n, K, N in B_WNAMES})
    out = dt("xo", [T, 1024], F32, kind="ExternalOutput").ap()
    ao = [dt("ao%d" % L, [SL, 512], BF16, kind="Internal").ap() for L in range(depth)]
    aog = [dt("aog%d" % L, [2 * SL, 512], BF16, kind="Internal").ap() for L in range(depth)]
    xmid = [dt("xmid%d" % L, [T, 1024], F32, kind="Internal").ap() for L in range(depth - 1)]
    xg = [dt("xg%d" % L, [SL, 1024], F32, kind="Internal").ap() for L in range(depth - 1)]
    S = Sched(nc)
    C = make_pools(S, n_wbuf=3)
    xg_tok = None
    xmid_tok = None
    for L in range(depth):
        S.prefix = "L%dA_" % L
        aot, aogt = Tok(), Tok()
        with S.scope():
            XK = min(512, T)
            xmap = None if L == 0 else (lambda t: 2 * ((t % T) // XK) * XK + (t // T) * XK + (t % T) % XK)
            emit_phaseA(C, SL, x_full if L == 0 else xg[L - 1], WS_d[L], WM_d[L], cd, pd[L], ao[L],
                        xin_tok=xg_tok, ao_tok=aot, xmap=xmap)
        RK = min(2048, SL)
        for k in range(SL // RK):
            S.collective("AllGather", ao[L][k * RK:(k + 1) * RK, :].opt(), aog[L][2 * k * RK:2 * (k + 1) * RK, :].opt(),
                         PAIR_GROUPS, reads=[aot], accw=[aogt])
        S.prefix = "L%dB_" % L
        with S.scope():
            hm = S.sbuf("hm", [128, 2], F32)
            hmt = Tok()
            S.dma(hm[:], hmask_d, writes=[hmt])
            atAB = [S.sbuf("atAB%d" % i, [128, 4, 1024], BF16) for i in range(2)]
            atABt = [Tok(), Tok()]
            nw = len(C.wbuf)
            C.wbuf = C.wbuf + [S.sbuf("wbufx%d" % i, [128, WU_ELEMS], BF16) for i in range(1)]
            C.wtok = C.wtok + [Tok() for _ in range(1)]
            last = (L == depth - 1)
            xo_tok = Tok()
            emit_phaseB(C, T, xh if L == 0 else xmid[L - 1], aog[L], p_d[L], gB_d[L], cd["ident"], wd[L],
                        out if last else xmid[L], xin_tok=xmid_tok, attn_tok=aogt, xo_tok=xo_tok,
                        gathered=(SL, hm, hmt, atAB, atABt))
            C.wbuf = C.wbuf[:nw]
            C.wtok = C.wtok[:nw]
        if not last:
            xmid_tok = xo_tok
            xg_tok = Tok()
            XK = min(512, T)
            for k in range(T // XK):
                S.collective("AllGather", xmid[L][k * XK:(k + 1) * XK, :].opt(),
                             xg[L][2 * k * XK:2 * (k + 1) * XK, :].opt(), PAIR_GROUPS, reads=[xo_tok], accw=[xg_tok])
    S.emit()
    S.close()
    return nc


def fused_inputs(z, SL, depth):
    import ml_dtypes
    x = np.ascontiguousarray(z["x"], np.float32)
    T = SL // 2
    consts = [hostA_consts(g, SL) for g in range(2)]

    def gl(v):
        return np.ascontiguousarray(np.asarray(v, np.float32).reshape(8, 128).T)
    per_layer = []
    for L in range(depth):
        d = {}
        d["wsm"] = [hostA_weights(z["w_in"][L], g) for g in range(2)]
        d["prm"] = [hostA_params(z, L, g) for g in range(2)]
        d["gainsB"] = np.ascontiguousarray(np.concatenate(
            [gl(z["norm_mix"][L]), gl(z["norm_mlp"][L]), gl(z["norm_ple"][L])], 1), np.float32)
        d["w"] = dict(
            w_merge=np.ascontiguousarray(z["w_in"][L][:, 3352:5400], np.float32),
            w_up_nsa=np.ascontiguousarray(z["w_up_nsa"][L], np.float32),
            w_up_ret=np.ascontiguousarray(z["w_up_ret"][L], np.float32),
            w_out=np.ascontiguousarray(z["w_out"][L], np.float32),
            w_ff1=np.ascontiguousarray(z["w_ff1"][L], np.float32),
            w_ff2=np.ascontiguousarray(z["w_ff2"][L], np.float32),
            w_gate=np.ascontiguousarray(z["w_ple_gate"][L], np.float32),
            w_ple=np.ascontiguousarray(z["w_ple"][L], np.float32))
        per_layer.append(d)
    ins = []
    for c in range(8):
        b, r = c // 2, c % 2
        sl = slice(r * T, (r + 1) * T)
        d = dict(x=np.ascontiguousarray(x[b, :SL]), xh=np.ascontiguousarray(x[b, sl]))
        hm = np.zeros((128, 2), np.float32)
        hm[:, r] = 1.0
        d["hmask"] = hm
        d.update(consts[r])
        for L in range(depth):
            pl = per_layer[L]
            d["WS%d" % L], d["WM%d" % L] = pl["wsm"][r]
            for k, v in pl["prm"][r].items():
                d["%s%d" % (k, L)] = v
            d["p%d" % L] = np.ascontiguousarray(z["p"][L, b, sl], np.float32)
            d["gainsB%d" % L] = pl["gainsB"]
            for k, v in pl["w"].items():
                d["%s%d" % (k, L)] = v
        ins.append(d)
    return ins


def kernel(**inputs):
    z = {k: np.asarray(v) for k, v in inputs.items()}
    B, SL, D = z["x"].shape
    depth = z["w_in"].shape[0]
    nc = _get_nc(("F", SL, depth), lambda: build_fused(SL, depth))
    ins = fused_inputs(z, SL, depth)
    res = run_bass_kernel_spmd(nc, ins, core_ids=list(range(8)))
    T = SL // 2
    out = np.empty((B, SL, D), np.float32)
    for c in range(8):
        b, r = c // 2, c % 2
        out[b, r * T:(r + 1) * T] = np.asarray(res.results[c]["xo"])
    return out
```

```python
from contextlib import ExitStack
import numpy as np
import concourse.bass as bass
import concourse.mybir as mybir

F32 = mybir.dt.float32
BF16 = mybir.dt.bfloat16
I32 = mybir.dt.int32
AF = mybir.ActivationFunctionType
ALU = mybir.AluOpType
AX = mybir.AxisListType

ENGS = ("pe", "act", "dve", "pool", "sp")
N_DMA_SEMS = 24


class Tok:
    __slots__ = ("lws", "rs", "base", "name", "excl", "accgrp")

    def __init__(self, name="", excl=False):
        self.excl = excl
        self.accgrp = False
        self.lws = []
        self.rs = []
        self.base = []
        self.name = name


class Op:
    __slots__ = ("eng", "fn", "deps", "dma", "idx", "sig", "dma_n", "cc")

    def __init__(self, eng, fn, deps, dma, idx):
        self.eng = eng
        self.fn = fn
        self.deps = deps
        self.dma = dma
        self.idx = idx
        self.sig = None
        self.dma_n = None
        self.cc = None


class _Scope:
    def __init__(self, S):
        self.S = S

    def __enter__(self):
        self.saved = self.S.stack
        self.S.stack = ExitStack()
        return self

    def __exit__(self, *a):
        self.S.barrier()
        self.S.stack.close()
        self.S.stack = self.saved
        return False


class Sched:
    def __init__(self, nc):
        self.nc = nc
        self.ops = {e: [] for e in ENGS}
        self.ndma = {e: 0 for e in ENGS}
        self.final_waits = []
        self.all_dma = []
        self.ncc = 0
        self.prefix = ""
        self.stack = ExitStack()

    def sbuf(self, name, shape, dtype):
        return self.stack.enter_context(self.nc.sbuf_tensor("sb_" + self.prefix + name, list(shape), dtype))

    def psum(self, name, shape, dtype):
        return self.stack.enter_context(self.nc.psum_tensor("pp_" + name, list(shape), dtype))

    def add(self, eng, fn, reads=(), writes=(), dma=False, accw=(), extra=()):
        deps = []
        seen = set()

        def push(d):
            if d is not None and d not in seen:
                seen.add(d)
                deps.append(d)

        for d in extra:
            push(d)
        for t in reads:
            for w in t.lws:
                push(w)
            if t.excl:
                for r in t.rs:
                    if r[0] != eng:
                        push(r)
        for t in writes:
            for w in t.lws:
                push(w)
            for r in t.rs:
                push(r)
        for t in accw:
            if t.rs or not t.lws or not t.accgrp:
                for w in t.lws:
                    push(w)
                for r in t.rs:
                    push(r)
            else:
                for d in t.base:
                    push(d)
        lst = self.ops[eng]
        op = Op(eng, fn, deps, dma, len(lst))
        if dma:
            op.dma_n = self.ndma[eng]
            self.ndma[eng] += 1
            self.all_dma.append((eng, op.idx))
        lst.append(op)
        me = (eng, op.idx)
        for t in reads:
            t.rs.append(me)
        for t in writes:
            t.lws = [me]
            t.rs = []
            t.base = []
            t.accgrp = False
        for t in accw:
            if t.rs or not t.lws or not t.accgrp:
                t.base = list(t.lws) + list(t.rs)
                t.lws = [me]
                t.rs = []
                t.accgrp = True
            else:
                t.lws.append(me)
        return op

    def collective(self, kind, src, dst, groups, reads=(), writes=(), accw=()):
        op = self.add("pool", lambda e: e.collective_compute(kind, ALU.bypass, replica_groups=groups,
                                                             ins=[src], outs=[dst]), reads, writes, accw=accw)
        self.ncc += 1
        op.cc = self.ncc
        return op

    def barrier(self):
        extra = list(self.all_dma)
        for e in ENGS:
            if self.ops[e]:
                extra.append((e, len(self.ops[e]) - 1))
        self.all_dma = []
        b0 = self.add("sp", lambda e: e.nop(), extra=extra)
        me = ("sp", b0.idx)
        for e in ("pe", "act", "dve", "pool"):
            self.add(e, lambda eng: eng.nop(), extra=[me])

    def dma(self, out, in_, reads=(), writes=(), q="sp", accw=(), **kw):
        return self.add(q, lambda e: e.dma_start(out=out, in_=in_, **kw), reads, writes, dma=True, accw=accw)

    def scope(self):
        return _Scope(self)

    def emit(self):
        nc = self.nc
        ops = self.ops
        needed = {e: set() for e in ENGS}
        waits = {e: [] for e in ENGS}
        for e in ENGS:
            maxw = {d: -1 for d in ENGS}
            dma_waited = set()
            for op in ops[e]:
                keep = []
                for (de, di) in op.deps:
                    dop = ops[de][di]
                    if dop.dma or dop.cc:
                        if (de, di) in dma_waited:
                            continue
                        dma_waited.add((de, di))
                        keep.append((de, di))
                    else:
                        if de == e and e == "pe":
                            continue
                        if de == e and di == op.idx:
                            continue
                        if di <= maxw[de]:
                            continue
                        maxw[de] = di
                        keep.append((de, di))
                        needed[de].add(di)
                waits[e].append(keep)
        for e in ENGS:
            c = 0
            for op in ops[e]:
                if (not op.dma) and (not op.cc) and op.idx in needed[e]:
                    c += 1
                    op.sig = c
        st = self.stack
        csem = {e: st.enter_context(nc.semaphore("c_" + e)) for e in ENGS}
        ccsem = st.enter_context(nc.semaphore("cc_sem"))
        dsem = {e: [st.enter_context(nc.semaphore("d_%s_%d" % (e, i))) for i in range(N_DMA_SEMS)]
                for e in ENGS if self.ndma[e] > 0}
        block = st.enter_context(nc.Block())

        def gen(e, eng):
            for op, keep in zip(ops[e], waits[e]):
                if op.dma:
                    n = op.dma_n
                    if n >= N_DMA_SEMS:
                        eng.wait_ge(dsem[e][n % N_DMA_SEMS], 16 * (n // N_DMA_SEMS))
                for (de, di) in keep:
                    dop = ops[de][di]
                    if dop.dma:
                        n = dop.dma_n
                        eng.wait_ge(dsem[de][n % N_DMA_SEMS], 16 * (n // N_DMA_SEMS + 1))
                    elif dop.cc:
                        eng.wait_ge(ccsem, dop.cc)
                    else:
                        eng.wait_ge(csem[de], dop.sig)
                ins = op.fn(eng)
                if op.dma:
                    n = op.dma_n
                    ins.then_inc(dsem[e][n % N_DMA_SEMS], 16)
                elif op.cc:
                    ins.then_inc(ccsem, 1)
                elif op.sig is not None:
                    ins.then_inc(csem[e], 1)
            if e == "pool" and self.ncc:
                eng.wait_ge(ccsem, self.ncc)
            nd = self.ndma[e]
            for i in range(min(nd, N_DMA_SEMS)):
                cnt = (nd - 1 - i) // N_DMA_SEMS + 1
                eng.wait_ge(dsem[e][i], 16 * cnt)

        @block.tensor
        def _(eng):
            gen("pe", eng)

        @block.scalar
        def _(eng):
            gen("act", eng)

        @block.vector
        def _(eng):
            gen("dve", eng)

        @block.gpsimd
        def _(eng):
            gen("pool", eng)

        @block.sync
        def _(eng):
            gen("sp", eng)

    def close(self):
        self.stack.close()


D_MODEL = 1024
EPS = 1e-6
WU_ELEMS = 4096


class WSpec:
    def __init__(self, S, name, w_ap, K, N, kind):
        self.name, self.K, self.N, self.kind = name, K, N, kind
        self.KT = K // 128
        self.w = w_ap
        nc = S.nc
        if kind == "S":
            assert N % 512 == 0
            self.nunits = N // 512
            self.uelems = 4 * self.KT * 128
        else:
            assert N % 512 == 0
            self.KTU = min(8, self.KT)
            self.nv = self.KT // self.KTU
            self.nunits = (N // 512) * self.nv
            self.uelems = self.KTU * 512
        assert self.uelems <= WU_ELEMS
        self.scr = nc.dram_tensor("scr_" + S.prefix + name, [self.nunits, 128, self.uelems], BF16, kind="Internal").ap()
        self.tok = Tok("scr_" + name)

    def unit_src(self, u):
        return self.scr[u]

    def view(self, buf):
        b = buf[:, : self.uelems]
        if self.kind == "S":
            return b.rearrange("p (f k c) -> p f k c", f=4, k=self.KT)
        return b.rearrange("p (k c) -> p k c", k=self.KTU)


class Ctx:
    pass


def make_pools(S, n_wbuf=5, n_ps=6):
    C = Ctx()
    C.S = S
    C.wbuf = [S.sbuf("wbuf%d" % i, [128, WU_ELEMS], BF16) for i in range(n_wbuf)]
    C.wtok = [Tok("wbuf%d" % i) for i in range(n_wbuf)]
    C.wn = 0
    C.ps = [S.psum("ps%d" % i, [128, 512], F32) for i in range(n_ps)]
    C.pstok = [Tok("ps%d" % i, excl=True) for i in range(n_ps)]
    C.pn = 0
    C.psb = [S.psum("psb%d" % i, [128, 1024], BF16) for i in range(2)]
    C.psbtok = [Tok("psb0", excl=True), Tok("psb1", excl=True)]
    C.pbn = 0
    C.stg = [S.sbuf("stg%d" % i, [128, 512], F32) for i in range(3)]
    C.stgtok = [Tok() for _ in range(3)]
    C.stgb = [S.sbuf("stgb%d" % i, [128, 512], BF16) for i in range(3)]
    C.stgbtok = [Tok() for _ in range(3)]
    C.sn = 0
    C.cast_rr = 0
    return C


def next_ps(C):
    i = C.pn % getattr(C, "ps_active", len(C.ps))
    C.pn += 1
    return C.ps[i], C.pstok[i]


def next_psb(C):
    i = C.pbn % 2
    C.pbn += 1
    return C.psb[i][:, 0:512], C.psbtok[i]


def load_unit(C, ws, u, q="sp"):
    i = C.wn % len(C.wbuf)
    C.wn += 1
    buf, tok = C.wbuf[i], C.wtok[i]
    C.S.dma(buf[:, : ws.uelems], ws.unit_src(u), reads=[ws.tok], writes=[tok], q=q)
    return ws.view(buf), tok


def prep_weight(C, ws):
    S = C.S
    K, N, KT = ws.K, ws.N, ws.KT
    for kt in range(KT):
        for c0 in range(0, N, 512):
            cw = min(512, N - c0)
            i = C.sn % 3
            C.sn += 1
            st, stt, sb, sbt = C.stg[i], C.stgtok[i], C.stgb[i], C.stgbtok[i]
            S.dma(st[:, :cw], ws.w[kt * 128:(kt + 1) * 128, c0:c0 + cw], writes=[stt])
            eng = ("dve", "pool")[C.cast_rr % 2]
            C.cast_rr += 1
            S.add(eng, lambda e, sb=sb, st=st, cw=cw: e.tensor_copy(out=sb[:, :cw], in_=st[:, :cw]), [stt], [sbt])
            if ws.kind == "S":
                u0, nu = c0 // 512, cw // 512
                dst = ws.scr[u0:u0 + nu].rearrange("u p (f k c) -> p u f k c", f=4, k=KT)[:, :, :, kt, :]
                src = sb[:, :cw].rearrange("p (u f c) -> p u f c", u=nu, f=4)
                for uu in range(nu):
                    S.dma(dst[:, uu], src[:, uu], reads=[sbt], accw=[ws.tok], q="act")
            else:
                v, kk = kt // ws.KTU, kt % ws.KTU
                n0, nn = c0 // 512, cw // 512
                for n in range(nn):
                    u = (n0 + n) * ws.nv + v
                    dst = ws.scr[u].rearrange("p (k c) -> p k c", k=ws.KTU)[:, kk, :]
                    S.dma(dst, sb[:, n * 512:(n + 1) * 512], reads=[sbt], accw=[ws.tok], q="act")


def rms_to_featmajor(C, xt, xtok, gains, gtok, gcol0, hT, htok, ident, itok, tmp):
    S = C.S
    ss, sstok, xs, xstok, junk, jtok, rstd, rtok = tmp
    for j in range(4):
        S.add("act", lambda e, j=j: e.activation(
            out=xs[:, j, :], in_=xt[:, j, :], func=AF.Square, accum_out=ss[:, j:j + 1]), [xtok], [xstok, sstok])
    S.add("dve", lambda e: e.tensor_scalar(out=rstd[:], in0=ss[:], scalar1=1.0 / D_MODEL, scalar2=EPS,
                                           op0=ALU.mult, op1=ALU.add), [sstok], [rtok])
    S.add("act", lambda e: e.activation(out=rstd[:], in_=rstd[:], func=AF.Sqrt), [rtok], [rtok])
    S.add("dve", lambda e: e.reciprocal(out=rstd[:], in_=rstd[:]), [rtok], [rtok])
    for j in range(4):
        S.add("act", lambda e, j=j: e.activation(out=xs[:, j, :], in_=xt[:, j, :], func=AF.Copy,
                                                 scale=rstd[:, j:j + 1]), [xtok, rtok], [xstok])
    for kt in range(8):
        ps, pt = next_ps(C)
        for j in range(4):
            S.add("pe", lambda e, ps=ps, j=j, kt=kt: e.transpose(
                out=ps[:, j * 128:(j + 1) * 128], in_=xs[:, j, kt * 128:(kt + 1) * 128], identity=ident[:]),
                [xstok, itok], [pt])
        if kt % 2 == 0:
            S.add("dve", lambda e, ps=ps, kt=kt: e.tensor_scalar(
                out=hT[:, kt, :], in0=ps[:], scalar1=gains[:, gcol0 + kt:gcol0 + kt + 1], scalar2=None,
                op0=ALU.mult), [pt, gtok], accw=[htok])
        else:
            S.add("act", lambda e, ps=ps, kt=kt: e.activation(
                out=hT[:, kt, :], in_=ps[:], func=AF.Copy, scale=gains[:, gcol0 + kt:gcol0 + kt + 1]),
                [pt, gtok], accw=[htok])


def build_phaseB(T=4096):
    nc = bass.Bass("TRN2", target_bir_lowering=False)
    dt = nc.dram_tensor
    x = dt("x", [T, 1024], F32, kind="ExternalInput").ap()
    attn = dt("attn", [T, 1024], BF16, kind="ExternalInput").ap()
    pin = dt("p", [T, 256], F32, kind="ExternalInput").ap()
    gains_d = dt("gains", [128, 24], F32, kind="ExternalInput").ap()
    ident_d = dt("ident", [128, 128], F32, kind="ExternalInput").ap()
    wd = {}
    for name, K, N in (("w_merge", 1024, 2048), ("w_up_nsa", 512, 1024), ("w_up_ret", 512, 1024),
                       ("w_out", 1024, 1024), ("w_ff1", 1024, 4096), ("w_ff2", 4096, 1024),
                       ("w_gate", 1024, 1024), ("w_ple", 256, 1024)):
        wd[name] = dt(name, [K, N], F32, kind="ExternalInput").ap()
    xo = dt("xo", [T, 1024], F32, kind="ExternalOutput").ap()
    S = Sched(nc)
    C = make_pools(S)
    emit_phaseB(C, T, x, attn, pin, gains_d, ident_d, wd, xo)
    S.emit()
    S.close()
    return nc


DBG_STAGE = 99
DBG_R = 99
DBG_SUB = 0
DBG_Q = "act"
DBG_PREP = True


def emit_phaseB(C, T, x, attn, pin, gains_d, ident_d, wd, xo, xin_tok=None, attn_tok=None, xo_tok=None,
                gathered=None):
    S = C.S
    kinds = {"w_merge": "S", "w_up_nsa": "S", "w_up_ret": "S", "w_out": "M", "w_ff1": "S", "w_ff2": "M",
             "w_gate": "M", "w_ple": "M"}
    W = {}
    for name, ap in wd.items():
        K, N = ap.shape
        W[name] = WSpec(S, name, ap, K, N, kinds[name])
    gains = S.sbuf("gains", [128, 24], F32)
    gtok = Tok()
    ident = S.sbuf("ident", [128, 128], F32)
    identb = S.sbuf("identb", [128, 128], BF16)
    itok, ibtok = Tok(), Tok()
    S.dma(gains[:], gains_d, writes=[gtok])
    S.dma(ident[:], ident_d, writes=[itok])
    S.add("dve", lambda e: e.tensor_copy(out=identb[:], in_=ident[:]), [itok], [ibtok])
    for name in ("w_merge", "w_up_nsa", "w_up_ret", "w_out", "w_ff1", "w_ff2", "w_gate", "w_ple"):
        if DBG_PREP:
            prep_weight(C, W[name])
    xt = S.sbuf("xt", [128, 4, 1024], F32)
    at = S.sbuf("at", [128, 4, 1024], BF16)
    ptm = S.sbuf("ptm", [128, 4, 256], F32)
    xs = S.sbuf("xs", [128, 4, 1024], F32)
    junk = None
    ss = S.sbuf("ss", [128, 4], F32)
    rstd = S.sbuf("rstd", [128, 4], F32)
    hT = S.sbuf("hT", [128, 8, 512], BF16)
    aT = S.sbuf("aT", [128, 8, 512], BF16)
    sgT = S.sbuf("sgT", [128, 16, 512], BF16)
    mixT = S.sbuf("mixT", [128, 8, 512], BF16)
    uT = S.sbuf("uT", [128, 32, 512], BF16)
    pT = S.sbuf("pT", [128, 2, 512], BF16)
    tmpf = [S.sbuf("tmpf%d" % i, [128, 512], F32) for i in range(2)]
    tmpft = [Tok(), Tok()]
    gsb = S.sbuf("gsb", [128, 512], F32)
    xtok, atok, ptok, xstok, jtok, sstok, rtok = [Tok() for _ in range(7)]
    htok, aTtok, sgtok, mixtok, utok, pTtok, gsbtok = [Tok() for _ in range(7)]
    tmp = (ss, sstok, xs, xstok, junk, jtok, rstd, rtok)
    xin_tok = xin_tok or Tok()
    attn_tok = attn_tok or Tok()
    xo_tok = xo_tok or Tok()
    nchunk = T // 512
    tn = 0
    for c in range(nchunk):
        t0 = c * 512
        S.dma(xt[:], x[t0:t0 + 512, :].rearrange("(j p) d -> p j d", p=128), reads=[xin_tok], writes=[xtok])
        if gathered is None:
            S.dma(at[:], attn[t0:t0 + 512, :].rearrange("(j p) d -> p j d", p=128), reads=[attn_tok], writes=[atok])
        else:
            SLg, hm, hmt, atAB, atABt = gathered
            for hf in range(2):
                for g in range(2):
                    RKg = min(2048, SLg)
                    tk_ = hf * T + t0
                    r0 = 2 * (tk_ // RKg) * RKg + g * RKg + tk_ % RKg
                    srcv = attn[r0:r0 + 512, :].rearrange("(j p) d -> p j d", p=128)
                    S.dma(atAB[hf][:, :, g * 256:(g + 1) * 256], srcv[:, :, 0:256], reads=[attn_tok], accw=[atABt[hf]])
                    S.dma(atAB[hf][:, :, 512 + g * 256:512 + (g + 1) * 256], srcv[:, :, 256:512], reads=[attn_tok],
                          accw=[atABt[hf]])
            S.add("dve", lambda e: e.tensor_scalar(out=at[:], in0=atAB[0][:], scalar1=hm[:, 0:1], scalar2=None,
                                                   op0=ALU.mult), [atABt[0], hmt], [atok])
            S.add("dve", lambda e: e.scalar_tensor_tensor(out=at[:], in0=atAB[1][:], scalar=hm[:, 1:2], in1=at[:],
                                                          op0=ALU.mult, op1=ALU.add), [atABt[1], hmt, atok], [atok])
        S.dma(ptm[:], pin[t0:t0 + 512, :].rearrange("(j p) d -> p j d", p=128), writes=[ptok])
        def _store(t0=t0):
            S.dma(xo[t0:t0 + 512, :].rearrange("(j p) d -> p j d", p=128), xt[:], reads=[xtok], writes=[xo_tok],
                  q=DBG_Q)
        if DBG_STAGE <= 0:
            _store()
            continue
        rms_to_featmajor(C, xt, xtok, gains, gtok, 0, hT, htok, ident, itok, tmp)
        if DBG_STAGE <= 1:
            _store()
            continue
        ws = W["w_merge"]
        for u in range(ws.nunits):
            wv, wt = load_unit(C, ws, u)
            for f in range(4):
                ps, pt = next_ps(C)
                if DBG_SUB == 1:
                    continue
                for kt in range(8):
                    S.add("pe", lambda e, ps=ps, wv=wv, f=f, kt=kt: e.matmul(
                        ps[:], lhsT=wv[:, f, kt, :], rhs=hT[:, kt, :], start=(kt == 0), stop=(kt == 7)),
                        [wt, htok], [pt])
                if DBG_SUB == 2:
                    continue
                S.add("act", lambda e, ps=ps, ft=u * 4 + f: e.activation(
                    out=sgT[:, ft, :], in_=ps[:], func=AF.Sigmoid), [pt], accw=[sgtok])
        if DBG_STAGE <= 2:
            _store()
            continue
        for ft in range(8):
            pb, pbt = next_psb(C)
            for j in range(4):
                S.add("pe", lambda e, pb=pb, j=j, ft=ft: e.transpose(
                    out=pb[:, j * 128:(j + 1) * 128], in_=at[:, j, ft * 128:(ft + 1) * 128], identity=identb[:]),
                    [atok, ibtok], [pbt])
            S.add("dve", lambda e, pb=pb, ft=ft: e.tensor_copy(out=aT[:, ft, :], in_=pb), [pbt], accw=[aTtok])
        if DBG_SUB == 3:
            _store()
            continue
        wsa, wsb = W["w_up_nsa"], W["w_up_ret"]
        for u in range(2):
            wva, wta = load_unit(C, wsa, u)
            wvb, wtb = load_unit(C, wsb, u)
            for f in range(4):
                ft = u * 4 + f
                psa, pta = next_ps(C)
                for kt in range(4):
                    S.add("pe", lambda e, psa=psa, wva=wva, f=f, kt=kt: e.matmul(
                        psa[:], lhsT=wva[:, f, kt, :], rhs=aT[:, kt, :], start=(kt == 0), stop=(kt == 3)),
                        [wta, aTtok], [pta])
                psb_, ptb = next_ps(C)
                for kt in range(4):
                    S.add("pe", lambda e, psb_=psb_, wvb=wvb, f=f, kt=kt: e.matmul(
                        psb_[:], lhsT=wvb[:, f, kt, :], rhs=aT[:, 4 + kt, :], start=(kt == 0), stop=(kt == 3)),
                        [wtb, aTtok], [ptb])
                if DBG_SUB == 4:
                    continue
                tf, tft = tmpf[tn % 2], tmpft[tn % 2]
                tn += 1
                S.add("dve", lambda e, tf=tf, psa=psa, ft=ft: e.tensor_tensor(
                    out=tf[:], in0=psa[:], in1=sgT[:, ft, :], op=ALU.mult), [pta, sgtok], [tft])
                tf2, tft2 = tmpf[tn % 2], tmpft[tn % 2]
                tn += 1
                S.add("dve", lambda e, tf2=tf2, psb_=psb_, ft=ft: e.tensor_tensor(
                    out=tf2[:], in0=psb_[:], in1=sgT[:, 8 + ft, :], op=ALU.mult), [ptb, sgtok], [tft2])
                if DBG_SUB == 5:
                    continue
                S.add("pool", lambda e, tf=tf, tf2=tf2, ft=ft: e.tensor_tensor(
                    out=mixT[:, ft, :], in0=tf[:], in1=tf2[:], op=ALU.add), [tft, tft2], accw=[mixtok])
        if DBG_STAGE <= 3:
            _store()
            continue
        ws = W["w_out"]
        for n in range(2):
            wv, wt = load_unit(C, ws, n)
            for j in range(4):
                ps, pt = next_ps(C)
                for kt in range(8):
                    S.add("pe", lambda e, ps=ps, wv=wv, j=j, kt=kt: e.matmul(
                        ps[:], lhsT=mixT[:, kt, j * 128:(j + 1) * 128], rhs=wv[:, kt, :],
                        start=(kt == 0), stop=(kt == 7)), [wt, mixtok], [pt])
                S.add("dve", lambda e, ps=ps, j=j, n=n: e.tensor_tensor(
                    out=xt[:, j, n * 512:(n + 1) * 512], in0=ps[:], in1=xt[:, j, n * 512:(n + 1) * 512],
                    op=ALU.add), [pt, xtok], [xtok])
        if DBG_STAGE <= 4:
            _store()
            continue
        rms_to_featmajor(C, xt, xtok, gains, gtok, 8, hT, htok, ident, itok, tmp)
        ws = W["w_ff1"]
        for u in range(ws.nunits):
            wv, wt = load_unit(C, ws, u)
            for f in range(4):
                ft = u * 4 + f
                ps, pt = next_ps(C)
                for kt in range(8):
                    S.add("pe", lambda e, ps=ps, wv=wv, f=f, kt=kt: e.matmul(
                        ps[:], lhsT=wv[:, f, kt, :], rhs=hT[:, kt, :], start=(kt == 0), stop=(kt == 7)),
                        [wt, htok], [pt])
                tf, tft = tmpf[tn % 2], tmpft[tn % 2]
                tn += 1
                S.add("act", lambda e, ps=ps, tf=tf: e.activation(out=tf[:], in_=ps[:], func=AF.Relu),
                      [pt], [tft])
                S.add("pool", lambda e, tf=tf, ft=ft: e.tensor_tensor(
                    out=uT[:, ft, :], in0=tf[:], in1=tf[:], op=ALU.mult), [tft], accw=[utok])
        ws = W["w_ff2"]
        for n in range(2):
            pss = [next_ps(C) for _ in range(4)]
            for v in range(ws.nv):
                wv, wt = load_unit(C, ws, n * ws.nv + v)
                for j in range(4):
                    ps, pt = pss[j]
                    for kk in range(8):
                        kt = v * 8 + kk
                        S.add("pe", lambda e, ps=ps, wv=wv, j=j, kk=kk, kt=kt: e.matmul(
                            ps[:], lhsT=uT[:, kt, j * 128:(j + 1) * 128], rhs=wv[:, kk, :],
                            start=(kt == 0), stop=(kt == 31)), [wt, utok], [pt])
            for j in range(4):
                ps, pt = pss[j]
                S.add("dve", lambda e, ps=ps, j=j, n=n: e.tensor_tensor(
                    out=xt[:, j, n * 512:(n + 1) * 512], in0=ps[:], in1=xt[:, j, n * 512:(n + 1) * 512],
                    op=ALU.add), [pt, xtok], [xtok])
        if DBG_STAGE <= 5:
            _store()
            continue
        rms_to_featmajor(C, xt, xtok, gains, gtok, 16, hT, htok, ident, itok, tmp)
        for kt in range(2):
            ps, pt = next_ps(C)
            for j in range(4):
                S.add("pe", lambda e, ps=ps, j=j, kt=kt: e.transpose(
                    out=ps[:, j * 128:(j + 1) * 128], in_=ptm[:, j, kt * 128:(kt + 1) * 128], identity=ident[:]),
                    [ptok, itok], [pt])
            S.add("dve", lambda e, ps=ps, kt=kt: e.tensor_copy(out=pT[:, kt, :], in_=ps[:]), [pt], accw=[pTtok])
        wsg, wsp = W["w_gate"], W["w_ple"]
        for n in range(2):
            wvg, wtg = load_unit(C, wsg, n)
            wvp, wtp = load_unit(C, wsp, n)
            for j in range(4):
                ps, pt = next_ps(C)
                for kt in range(8):
                    S.add("pe", lambda e, ps=ps, wvg=wvg, j=j, kt=kt: e.matmul(
                        ps[:], lhsT=hT[:, kt, j * 128:(j + 1) * 128], rhs=wvg[:, kt, :],
                        start=(kt == 0), stop=(kt == 7)), [wtg, htok], [pt])
                S.add("act", lambda e, ps=ps: e.activation(out=gsb[:], in_=ps[:], func=AF.Sigmoid),
                      [pt], [gsbtok])
                ps2, pt2 = next_ps(C)
                for kt in range(2):
                    S.add("pe", lambda e, ps2=ps2, wvp=wvp, j=j, kt=kt: e.matmul(
                        ps2[:], lhsT=pT[:, kt, j * 128:(j + 1) * 128], rhs=wvp[:, kt, :],
                        start=(kt == 0), stop=(kt == 1)), [wtp, pTtok], [pt2])
                tf, tft = tmpf[tn % 2], tmpft[tn % 2]
                tn += 1
                S.add("dve", lambda e, tf=tf, ps2=ps2: e.tensor_tensor(
                    out=tf[:], in0=ps2[:], in1=gsb[:], op=ALU.mult), [pt2, gsbtok], [tft])
                S.add("pool", lambda e, tf=tf, j=j, n=n: e.tensor_tensor(
                    out=xt[:, j, n * 512:(n + 1) * 512], in0=tf[:], in1=xt[:, j, n * 512:(n + 1) * 512],
                    op=ALU.add), [tft, xtok], [xtok])
        _store()


IN_SPLITS = (512, 128, 128, 128, 128, 128, 128, 24, 512, 512, 512, 512, 1024, 1024)
NEG = -30000.0


def hostA_weights(w_in, g):
    offs = np.cumsum([0] + list(IN_SPLITS))

    def col(i, a, b):
        return w_in[:, offs[i] + a: offs[i] + b]

    def swap(x):
        return np.concatenate([x[:, 64:], x[:, :64]], 1)

    q = [col(0, (g * 4 + h) * 64, (g * 4 + h + 1) * 64) for h in range(4)]
    kc, vc = col(1, g * 64, g * 64 + 64), col(2, g * 64, g * 64 + 64)
    ks, vs = col(3, g * 64, g * 64 + 64), col(4, g * 64, g * 64 + 64)
    kw, vw = col(5, g * 64, g * 64 + 64), col(6, g * 64, g * 64 + 64)
    gates = np.stack([w_in[:, offs[7] + br * 8 + g * 4 + h] for br in range(3) for h in range(4)], 1)
    rq = [col(8, (2 * g + h) * 128, (2 * g + h + 1) * 128) for h in range(2)]
    rk = [col(9, (2 * g + h) * 128, (2 * g + h + 1) * 128) for h in range(2)]
    rv = col(10, 2 * g * 128, (2 * g + 2) * 128)
    rg = col(11, 2 * g * 128, (2 * g + 2) * 128)
    z128 = np.zeros((1024, 128), np.float32)
    WS = np.concatenate([q[0], q[1], q[2], q[3], ks, ks, kw, kw, kc, vc, z128, z128, z128,
                         rq[0], swap(rq[0]), rq[1], swap(rq[1]), rk[0], swap(rk[0]), rk[1], swap(rk[1])], 1)
    WM = np.concatenate([rk[0], rk[1], rv, rg, np.zeros((1024, 256), np.float32),
                         vs, vw, gates, np.zeros((1024, 512 - 140), np.float32)], 1)
    return np.ascontiguousarray(WS, np.float32), np.ascontiguousarray(WM, np.float32)


def hostA_consts(g, S):
    import ml_dtypes
    bf = ml_dtypes.bfloat16
    c = {}
    c["ident"] = np.eye(128, dtype=np.float32)
    bd = np.zeros((128, 128), np.float32)
    bd[:64, :64] = 1
    bd[64:, 64:] = 1
    c["onesbd"] = bd.astype(bf)
    half = 64
    inv = (10000.0 ** (-np.arange(half, dtype=np.float32) / half)).astype(np.float32)
    pos = np.arange(S, dtype=np.float32)
    ang = (pos[:, None] * inv[None, :]).astype(np.float32)
    cos, sin = np.cos(ang.astype(np.float64)), np.sin(ang.astype(np.float64))
    cosT = np.concatenate([cos.T, cos.T], 0)
    sinsT = np.concatenate([-sin.T, sin.T], 0)
    ksc = 128.0 ** -0.5
    c["ropeq"] = np.stack([cosT, sinsT], 1).astype(np.float32)
    c["ropek"] = (np.stack([cosT, sinsT], 1) * ksc).astype(np.float32)
    hh = np.array([2 * g, 2 * g + 1], np.float64)
    gamma = 1.0 - 2.0 ** (-5.0 - hh)
    lg = np.log(gamma)
    n = np.arange(128, dtype=np.float64)
    xi = np.exp(lg[:, None] * (n + 1.0))
    zeta = np.exp(lg[:, None] * (127.0 - n))
    c["xi"] = np.broadcast_to(np.tile(xi, (1, 4))[None], (128, 2, 512)).astype(np.float32).copy()
    zt = zeta[:, np.arange(S) % 128]
    c["ctk"] = (cos[:, None, :] * zt.T[:, :, None] * ksc).astype(np.float32)
    c["stk"] = (sin[:, None, :] * zt.T[:, :, None] * ksc).astype(np.float32)
    diff = n[None, :] - n[:, None]
    dec = np.where(diff[None] >= 0, np.exp(lg[:, None, None] * np.maximum(diff[None], 0)), 0.0)
    c["decayT"] = np.ascontiguousarray(dec.transpose(1, 0, 2)).astype(np.float32)
    c["gch"] = np.broadcast_to(np.exp(lg * 128.0)[None], (128, 2)).astype(np.float32).copy()
    kk, qq = np.arange(128)[:, None], np.arange(128)[None, :]
    c["triT"] = np.tile(np.where(kk <= qq, 0.0, NEG), (1, 4)).astype(bf)
    c["triU"] = np.tile(np.where(kk > qq, 0.0, NEG), (1, 4)).astype(bf)
    r = np.arange(16)[None, :, None]
    ql, il = np.arange(128)[:, None, None], np.arange(128)[None, None, :]
    c["cmaskQ"] = np.where(128 * r + ql - 16 * il - 31 >= 0, 0.0, NEG).astype(bf)
    c["cmaskT"] = np.ascontiguousarray(np.transpose(np.where(128 * r + ql - 16 * il - 31 >= 0, 0.0, NEG), (2, 1, 0))).astype(bf)
    c["wexp"] = (np.arange(S)[None, :] // 64 == np.arange(128)[:, None]).astype(bf)
    return c


A_CONST_SHAPES = lambda S: {
    "ident": ([128, 128], F32), "onesbd": ([128, 128], BF16), "ropeq": ([128, 2, S], F32),
    "ropek": ([128, 2, S], F32), "xi": ([128, 2, 512], F32), "ctk": ([S, 2, 64], F32), "stk": ([S, 2, 64], F32),
    "decayT": ([128, 2, 128], F32), "gch": ([128, 2], F32), "triT": ([128, 512], BF16), "triU": ([128, 512], BF16),
    "cmaskQ": ([128, 16, 128], BF16), "cmaskT": ([128, 16, 128], BF16), "wexp": ([128, S], BF16)}


def hostA_params(z, L, g):
    p = {}

    def gl(v):
        return np.ascontiguousarray(v.reshape(8, 128).T)
    qg, kg = z["nsa_q_norm"][L], z["nsa_k_norm"][L]
    p["gainsA"] = np.concatenate([gl(z["norm_mix"][L]), np.tile(qg, 2)[:, None], np.tile(kg, 2)[:, None]], 1).astype(np.float32)
    p["posT"] = np.ascontiguousarray(np.concatenate([z["cmp_pos_k"][L].T, z["cmp_pos_v"][L].T], 0), np.float32)
    w1k = z["cmp_w1_k"][L].reshape(32, 64, 256).transpose(1, 0, 2)
    w1v = z["cmp_w1_v"][L].reshape(32, 64, 256).transpose(1, 0, 2)
    zz = np.zeros_like(w1k)
    p["w1k"] = np.ascontiguousarray(np.concatenate([w1k, zz], 0), np.float32)
    p["w1v"] = np.ascontiguousarray(np.concatenate([zz, w1v], 0), np.float32)
    w2k = z["cmp_w2_k"][L].reshape(2, 128, 64).transpose(1, 0, 2)
    w2v = z["cmp_w2_v"][L].reshape(2, 128, 64).transpose(1, 0, 2)
    p["w2"] = np.ascontiguousarray(np.stack([np.concatenate([w2k, w2k], 2), np.concatenate([w2v, np.zeros_like(w2v)], 2)], 2), np.float32)
    return p


A_PARAM_SHAPES = {"gainsA": [128, 10], "posT": [128, 32], "w1k": [128, 32, 256], "w1v": [128, 32, 256],
                  "w2": [128, 2, 2, 128]}


def build_phaseA(SL=8192, parts=("ret", "nsa")):
    nc = bass.Bass("TRN2", target_bir_lowering=False)
    dt = nc.dram_tensor
    x = dt("x", [SL, 1024], F32, kind="ExternalInput").ap()
    WS_d = dt("WS", [1024, 2048], F32, kind="ExternalInput").ap()
    WM_d = dt("WM", [1024, 1536], F32, kind="ExternalInput").ap()
    cd = {k: dt(k, sh, ty, kind="ExternalInput").ap() for k, (sh, ty) in A_CONST_SHAPES(SL).items()}
    pd = {k: dt(k, sh, F32, kind="ExternalInput").ap() for k, sh in A_PARAM_SHAPES.items()}
    ao = dt("ao", [SL, 512], BF16, kind="ExternalOutput").ap()
    S = Sched(nc)
    C = make_pools(S, n_wbuf=3)
    emit_phaseA(C, SL, x, WS_d, WM_d, cd, pd, ao, parts)
    S.emit()
    S.close()
    return nc


def norm_evac(C, ps, pt, gains, gtok, gcol, onesbd, otok, tmps, dsts):
    S = C.S
    qf, qft, sq, sqt, rs, rst = tmps
    N = ps.shape[-1]
    S.add("act", lambda e: e.activation(out=qf[:, :N], in_=ps, func=AF.Copy), [pt], [qft])
    S.add("act", lambda e: e.activation(out=sq[:, :N], in_=ps, func=AF.Square), [pt], [sqt])
    p2, pt2 = next_ps(C)
    S.add("pe", lambda e: e.matmul(p2[:, :N], lhsT=onesbd[:], rhs=sq[:, :N], start=True, stop=True), [sqt, otok], [pt2])
    S.add("act", lambda e: e.activation(out=rs[:, :N], in_=p2[:, :N], func=AF.Ln, scale=1.0 / 64, bias=C.epsb[:, 0:1]), [pt2, C.epst], [rst])
    S.add("act", lambda e: e.activation(out=rs[:, :N], in_=rs[:, :N], func=AF.Exp, scale=-0.5), [rst], [rst])
    for (dst, lo, hi, tok) in dsts:
        S.add("dve", lambda e, dst=dst, lo=lo, hi=hi: e.scalar_tensor_tensor(
            out=dst, in0=qf[lo:hi, :N], scalar=gains[lo:hi, gcol:gcol + 1], in1=rs[lo:hi, :N],
            op0=ALU.mult, op1=ALU.mult), [qft, rst, gtok], accw=[tok])


def emit_phaseA(C, SL, x, WS_d, WM_d, cd, pd, ao, parts=("ret", "nsa"), xin_tok=None, ao_tok=None, xmap=None):
    C.xmap = xmap or (lambda t: t)
    S = C.S
    nchunk = SL // 512
    xin_tok = xin_tok or Tok()
    ao_tok = ao_tok or Tok()
    WSs = WSpec(S, "WSs", WS_d, 1024, 2048, "S")
    WMs = WSpec(S, "WMs", WM_d, 1024, 1536, "M")
    gains = S.sbuf("gainsA", [128, 10], F32)
    gtok = Tok()
    ident = S.sbuf("identA", [128, 128], F32)
    itok = Tok()
    C.epsb = S.sbuf("epsb", [128, 1], F32)
    C.epst = Tok()
    S.dma(gains[:], pd["gainsA"], writes=[gtok])
    S.dma(ident[:], cd["ident"], writes=[itok])
    S.add("dve", lambda e: e.memset(C.epsb[:], EPS), [], [C.epst])
    prep_weight(C, WSs)
    prep_weight(C, WMs)
    if "ret" in parts:
        with S.scope():
            emit_ret_pass(C, SL, x, xin_tok, WSs, WMs, cd, gains, gtok, ident, itok, ao, ao_tok)
    if "nsa" in parts:
        emit_nsa(C, SL, x, xin_tok, WSs, WMs, cd, pd, gains, gtok, ident, itok, ao, ao_tok)


def load_x_rms(C, x, xin_tok, t0, xt, xtok, junk, jtok, ss, sstok, rstd, rtok, gains, gtok, hT, htok, ident, itok):
    S = C.S
    xr0 = C.xmap(t0)
    S.dma(xt[:], x[xr0:xr0 + 512, :].rearrange("(j p) d -> p j d", p=128), reads=[xin_tok], writes=[xtok])
    for j in range(4):
        S.add("act", lambda e, j=j: e.activation(out=junk[:], in_=xt[:, j, :], func=AF.Square,
                                                 accum_out=ss[:, j:j + 1]), [xtok], [jtok, sstok])
    S.add("dve", lambda e: e.tensor_scalar(out=rstd[:], in0=ss[:], scalar1=1.0 / D_MODEL, scalar2=EPS,
                                           op0=ALU.mult, op1=ALU.add), [sstok], [rtok])
    S.add("act", lambda e: e.activation(out=rstd[:], in_=rstd[:], func=AF.Sqrt), [rtok], [rtok])
    S.add("dve", lambda e: e.reciprocal(out=rstd[:], in_=rstd[:]), [rtok], [rtok])
    for j in range(4):
        S.add("act", lambda e, j=j: e.activation(out=xt[:, j, :], in_=xt[:, j, :], func=AF.Copy,
                                                 scale=rstd[:, j:j + 1]), [xtok, rtok], [xtok])
    for kt in range(8):
        ps, pt = next_ps(C)
        for j in range(4):
            S.add("pe", lambda e, ps=ps, j=j, kt=kt: e.transpose(
                out=ps[:, j * 128:(j + 1) * 128], in_=xt[:, j, kt * 128:(kt + 1) * 128], identity=ident[:]),
                [xtok, itok], [pt])
        if kt % 2 == 0:
            S.add("dve", lambda e, ps=ps, kt=kt: e.tensor_scalar(
                out=hT[:, kt, :], in0=ps[:], scalar1=gains[:, kt:kt + 1], scalar2=None, op0=ALU.mult),
                [pt, gtok], accw=[htok])
        else:
            S.add("act", lambda e, ps=ps, kt=kt: e.activation(
                out=hT[:, kt, :], in_=ps[:], func=AF.Copy, scale=gains[:, kt:kt + 1]), [pt, gtok], accw=[htok])


def emit_ret_pass(C, SL, x, xin_tok, WSs, WMs, cd, gains, gtok, ident, itok, ao, ao_tok):
    S = C.S
    sb = S.sbuf
    xt = sb("r_xt", [128, 4, 1024], F32)
    junk = sb("r_junk", [128, 1024], F32)
    ss = sb("r_ss", [128, 4], F32)
    rstd = sb("r_rstd", [128, 4], F32)
    hT = sb("r_hT", [128, 8, 512], BF16)
    rq_tab = sb("r_rqtab", [128, 2, 512], F32)
    rk_tab = sb("r_rktab", [128, 2, 512], F32)
    ctk_t = sb("r_ctk", [128, 4, 2, 64], F32)
    stk_t = sb("r_stk", [128, 4, 2, 64], F32)
    xi_t = sb("r_xi", [128, 2, 512], F32)
    decT = sb("r_dec", [128, 2, 128], F32)
    gch = sb("r_gch", [128, 2], F32)
    t1 = [sb("r_t1_%d" % i, [128, 512], F32) for i in range(2)]
    t2 = [sb("r_t2_%d" % i, [128, 512], F32) for i in range(2)]
    tmpq = sb("r_tmpq", [128, 512], F32)
    QrT = sb("r_QrT", [128, 2, 512], BF16)
    QrxT = sb("r_QrxT", [128, 2, 512], BF16)
    KrT = sb("r_KrT", [128, 2, 512], BF16)
    Vr = sb("r_Vr", [128, 4, 256], BF16)
    kz = sb("r_kz", [128, 4, 2, 128], BF16)
    sg = sb("r_sg", [128, 4, 256], F32)
    tabcd = [sb("r_tabcd%d" % i, [128, 2, 64], F32) for i in range(4)]
    IT = [sb("r_IT%d" % i, [128, 128], BF16) for i in range(2)]
    yr = sb("r_yr", [128, 4, 2, 128], F32)
    ssr = sb("r_ssr", [128, 8], F32)
    rr = sb("r_rr", [128, 8], F32)
    ro = sb("r_ro", [128, 4, 256], BF16)
    R = sb("r_R", [128, 2, 128], F32)
    Rb = sb("r_Rb", [128, 2, 128], BF16)
    junkb = sb("r_junkb", [128, 128], BF16)
    (xtok, jtok, sstok, rtok, htok, rqt, rkt, ctt, stt, xit, dect, gcht, tmpqt, qrt, qrxt, krt, vrt, kzt, sgt,
     yrt, ssrt, rrt, rot, jbt) = [Tok() for _ in range(24)]
    t1t, t2t = [Tok(), Tok()], [Tok(), Tok()]
    tabt = [Tok() for _ in range(4)]
    ITt = [Tok(), Tok()]
    Rt, Rbt = [Tok(), Tok()], [Tok(), Tok()]
    S.dma(xi_t[:], cd["xi"], writes=[xit])
    S.dma(decT[:], cd["decayT"], writes=[dect])
    S.dma(gch[:], cd["gch"], writes=[gcht])
    for h in range(2):
        S.add("dve", lambda e, h=h: e.memset(R[:, h, :], 0.0), [], [Rt[h]])
        S.add("pool", lambda e, h=h: e.memset(Rb[:, h, :], 0.0), [], [Rbt[h]])
    nchunk = SL // 512
    tn = 0
    itn = 0
    for c in range(nchunk):
        t0 = c * 512
        load_x_rms(C, x, xin_tok, t0, xt, xtok, junk, jtok, ss, sstok, rstd, rtok, gains, gtok, hT, htok, ident, itok)
        S.dma(rq_tab[:], cd["ropeq"][:, :, t0:t0 + 512], writes=[rqt])
        S.dma(rk_tab[:], cd["ropek"][:, :, t0:t0 + 512], writes=[rkt])
        S.dma(ctk_t[:], cd["ctk"][t0:t0 + 512].rearrange("(j p) h i -> p j h i", p=128), writes=[ctt])
        S.dma(stk_t[:], cd["stk"][t0:t0 + 512].rearrange("(j p) h i -> p j h i", p=128), writes=[stt])
        if DBG_R <= 1:
            continue
        for u in (2, 3):
            wv, wt = load_unit(C, WSs, u)
            isq = (u == 2)
            tab, tabt_ = (rq_tab, rqt) if isq else (rk_tab, rkt)
            for h in range(2):
                a, at_ = t1[tn % 2], t1t[tn % 2]
                b, bt_ = t2[tn % 2], t2t[tn % 2]
                tn += 1
                for half, (dstb, dtok) in enumerate(((a, at_), (b, bt_))):
                    f = 2 * h + half
                    ps, pt = next_ps(C)
                    for kt in range(8):
                        S.add("pe", lambda e, ps=ps, wv=wv, f=f, kt=kt: e.matmul(
                            ps[:], lhsT=wv[:, f, kt, :], rhs=hT[:, kt, :], start=(kt == 0), stop=(kt == 7)),
                            [wt, htok], [pt])
                    S.add("dve", lambda e, ps=ps, dstb=dstb, half=half, tab=tab: e.tensor_tensor(
                        out=dstb[:], in0=ps[:], in1=tab[:, half, :], op=ALU.mult), [pt, tabt_], [dtok])
                if isq:
                    S.add("pool", lambda e, a=a, b=b: e.tensor_tensor(out=tmpq[:], in0=a[:], in1=b[:], op=ALU.add),
                          [at_, bt_], [tmpqt])
                    S.add("act", lambda e, h=h: e.activation(out=QrT[:, h, :], in_=tmpq[:], func=AF.Copy),
                          [tmpqt], accw=[qrt])
                    S.add("pool", lambda e, h=h: e.tensor_tensor(out=QrxT[:, h, :], in0=tmpq[:], in1=xi_t[:, h, :],
                                                                 op=ALU.mult), [tmpqt, xit], accw=[qrxt])
                else:
                    S.add("pool", lambda e, a=a, b=b, h=h: e.tensor_tensor(out=KrT[:, h, :], in0=a[:], in1=b[:],
                                                                           op=ALU.add), [at_, bt_], accw=[krt])
        if DBG_R <= 2:
            continue
        wv, wt = load_unit(C, WMs, 0)
        for j in range(4):
            ps, pt = next_ps(C)
            for kt in range(8):
                S.add("pe", lambda e, ps=ps, wv=wv, j=j, kt=kt: e.matmul(
                    ps[:], lhsT=hT[:, kt, j * 128:(j + 1) * 128], rhs=wv[:, kt, :], start=(kt == 0), stop=(kt == 7)),
                    [wt, htok], [pt])
            if DBG_SUB == 1:
                S.add("act", lambda e, ps=ps, j=j: e.activation(out=Vr[:, j, :], in_=ps[:, 256:512], func=AF.Copy),
                      [pt], accw=[vrt])
                continue
            pv = ps[:, 0:256].rearrange("p (h t i) -> p h t i", h=2, t=2)
            x1, x2 = pv[:, :, 0, :], pv[:, :, 1, :]
            kzv = kz[:, j].rearrange("p h (t i) -> p h t i", t=2)
            ta, tb, tc, td = tabcd
            S.add("dve", lambda e, x1=x1, j=j: e.tensor_tensor(out=ta[:], in0=x1, in1=ctk_t[:, j], op=ALU.mult),
                  [pt, ctt], [tabt[0]])
            S.add("dve", lambda e, x2=x2, j=j: e.tensor_tensor(out=tb[:], in0=x2, in1=stk_t[:, j], op=ALU.mult),
                  [pt, stt], [tabt[1]])
            S.add("dve", lambda e, x1=x1, j=j: e.tensor_tensor(out=tc[:], in0=x1, in1=stk_t[:, j], op=ALU.mult),
                  [pt, stt], [tabt[2]])
            S.add("dve", lambda e, x2=x2, j=j: e.tensor_tensor(out=td[:], in0=x2, in1=ctk_t[:, j], op=ALU.mult),
                  [pt, ctt], [tabt[3]])
            if DBG_SUB == 2:
                continue
            S.add("dve", lambda e, kzv=kzv: e.tensor_tensor(out=kzv[:, :, 0, :], in0=ta[:], in1=tb[:], op=ALU.subtract),
                  [tabt[0], tabt[1]] if DBG_SUB != 3 else [], accw=[kzt])
            S.add("dve", lambda e, kzv=kzv: e.tensor_tensor(out=kzv[:, :, 1, :], in0=tc[:], in1=td[:], op=ALU.add),
                  [tabt[2], tabt[3]] if DBG_SUB != 3 else [], accw=[kzt])
            S.add("act", lambda e, ps=ps, j=j: e.activation(out=Vr[:, j, :], in_=ps[:, 256:512], func=AF.Copy),
                  [pt], accw=[vrt])
        if DBG_R <= 3:
            continue
        wv, wt = load_unit(C, WMs, 1)
        for j in range(4):
            ps, pt = next_ps(C)
            for kt in range(8):
                S.add("pe", lambda e, ps=ps, wv=wv, j=j, kt=kt: e.matmul(
                    ps[:, 0:256], lhsT=hT[:, kt, j * 128:(j + 1) * 128], rhs=wv[:, kt, 0:256],
                    start=(kt == 0), stop=(kt == 7)), [wt, htok], [pt])
            S.add("act", lambda e, ps=ps, j=j: e.activation(out=sg[:, j, :], in_=ps[:, 0:256], func=AF.Silu),
                  [pt], accw=[sgt])
        if DBG_R <= 4:
            continue
        for j in range(4):
            js = slice(j * 128, (j + 1) * 128)
            for h in range(2):
                hs = slice(h * 128, (h + 1) * 128)
                psI, ptI = next_ps(C)
                S.add("pe", lambda e, psI=psI, h=h, js=js: e.matmul(
                    psI[:, 0:128], lhsT=KrT[:, h, js], rhs=QrT[:, h, js], start=True, stop=True), [krt, qrt], [ptI])
                it_, itt = IT[itn % 2], ITt[itn % 2]
                itn += 1
                S.add("dve", lambda e, psI=psI, it_=it_, h=h: e.tensor_tensor(
                    out=it_[:], in0=psI[:, 0:128], in1=decT[:, h, :], op=ALU.mult), [ptI, dect], [itt])
                psO, ptO = next_ps(C)
                S.add("pe", lambda e, psO=psO, it_=it_, j=j, hs=hs: e.matmul(
                    psO[:, 0:128], lhsT=it_[:], rhs=Vr[:, j, hs], start=True, stop=False), [itt, vrt], [ptO])
                S.add("pe", lambda e, psO=psO, h=h, js=js: e.matmul(
                    psO[:, 0:128], lhsT=QrxT[:, h, js], rhs=Rb[:, h, :], start=False, stop=True), [qrxt, Rbt[h]], [ptO])
                S.add("act", lambda e, psO=psO, j=j, h=h: e.activation(
                    out=junkb[:], in_=psO[:, 0:128], func=AF.Square, accum_out=ssr[:, j * 2 + h:j * 2 + h + 1]),
                    [ptO], [jbt], accw=[ssrt])
                S.add("dve", lambda e, psO=psO, j=j, h=h: e.tensor_copy(out=yr[:, j, h, :], in_=psO[:, 0:128]),
                      [ptO], accw=[yrt])
                psK, ptK = next_ps(C)
                S.add("pe", lambda e, psK=psK, j=j, h=h, hs=hs: e.matmul(
                    psK[:, 0:128], lhsT=kz[:, j, h, :], rhs=Vr[:, j, hs], start=True, stop=True), [kzt, vrt], [ptK])
                S.add("dve", lambda e, psK=psK, h=h: e.scalar_tensor_tensor(
                    out=R[:, h, :], in0=R[:, h, :], scalar=gch[:, h:h + 1], in1=psK[:, 0:128],
                    op0=ALU.mult, op1=ALU.add), [ptK, gcht, Rt[h]], [Rt[h]])
                S.add("pool", lambda e, h=h: e.tensor_copy(out=Rb[:, h, :], in_=R[:, h, :]), [Rt[h]], [Rbt[h]])
        if DBG_R <= 5:
            continue
        S.add("dve", lambda e: e.tensor_scalar(out=rr[:], in0=ssr[:], scalar1=1.0 / 128, scalar2=EPS,
                                               op0=ALU.mult, op1=ALU.add), [ssrt], [rrt])
        S.add("act", lambda e: e.activation(out=rr[:], in_=rr[:], func=AF.Sqrt), [rrt], [rrt])
        S.add("dve", lambda e: e.reciprocal(out=rr[:], in_=rr[:]), [rrt], [rrt])
        for j in range(4):
            for h in range(2):
                hs = slice(h * 128, (h + 1) * 128)
                S.add("dve", lambda e, j=j, h=h, hs=hs: e.scalar_tensor_tensor(
                    out=ro[:, j, hs], in0=yr[:, j, h, :], scalar=rr[:, j * 2 + h:j * 2 + h + 1], in1=sg[:, j, hs],
                    op0=ALU.mult, op1=ALU.mult), [yrt, rrt, sgt], accw=[rot])
        S.dma(ao[t0:t0 + 512, 256:512].rearrange("(j p) d -> p j d", p=128), ro[:], reads=[rot], accw=[ao_tok], q="act")


HORD = (0, 2, 1, 3)


def emit_nsa(C, SL, x, xin_tok, WSs, WMs, cd, pd, gains, gtok, ident, itok, ao, ao_tok):
    S = C.S
    sb = S.sbuf
    NT = SL // 128
    nb = SL // 16
    assert nb <= 512
    NCT = max(1, nb // 128)
    QT = sb("n_QT", [128, 2, SL], BF16)
    Kslo, Kshi = sb("n_Kslo", [128, SL], BF16), sb("n_Kshi", [128, SL], BF16)
    Kwlo, Kwhi = sb("n_Kwlo", [128, SL], BF16), sb("n_Kwhi", [128, SL], BF16)
    V1 = sb("n_V1", [128, NT, 2, 65], BF16)
    Gt = sb("n_Gt", [128, NT, 12], F32)
    kclo, kchi = sb("n_kclo", [128, 512], BF16), sb("n_kchi", [128, 512], BF16)
    Vc1 = sb("n_Vc1", [128, 4, 65], BF16)
    onesbd = sb("n_onesbd", [128, 128], BF16)
    identb = sb("n_identb", [128, 128], BF16)
    qf = sb("n_qf", [128, 512], F32)
    sq = sb("n_sq", [128, 512], BF16)
    rs = sb("n_rs", [128, 512], F32)
    qtok, kst, kwt, kcvt, v1t, gtt, kct, vct, onest, ibt, qft, sqt, rst = [Tok() for _ in range(13)]
    tmps = (qf, qft, sq, sqt, rs, rst)
    S.dma(onesbd[:], cd["onesbd"], writes=[onest])
    S.add("dve", lambda e: e.tensor_copy(out=identb[:], in_=ident[:]), [itok], [ibt])
    for (t_, tk) in ((Kslo, kst), (Kshi, kst), (Kwlo, kwt), (Kwhi, kwt), (kclo, kct), (kchi, kct)):
        S.add("pool", lambda e, t_=t_: e.memset(t_[:], 0.0), [], [tk])
    S.add("pool", lambda e: e.memset(V1[:], 1.0), [], [v1t])
    S.add("pool", lambda e: e.memset(Vc1[:], 0.0), [], [vct])
    S.add("pool", lambda e: e.memset(Vc1[:, :, 64:65], 1.0), [vct], [vct])
    kc_scope = S.scope()
    kc_scope.__enter__()
    KcVcT = sb("n_KcVcT", [128, SL + 16], BF16)
    with S.scope():
        xt = sb("n_xt", [128, 4, 1024], F32)
        junk = sb("n_junk", [128, 1024], F32)
        ss = sb("n_ss", [128, 4], F32)
        rstd = sb("n_rstd", [128, 4], F32)
        hT = sb("n_hT", [128, 8, 512], BF16)
        xtok, jtok, sstok, rtok, htok = [Tok() for _ in range(5)]
        for c in range(SL // 512):
            t0 = c * 512
            cs = slice(t0, t0 + 512)
            load_x_rms(C, x, xin_tok, t0, xt, xtok, junk, jtok, ss, sstok, rstd, rtok, gains, gtok, hT, htok, ident, itok)
            for u in (0, 1):
                wv, wt = load_unit(C, WSs, u)
                for f in range(4 if u == 0 else 1):
                    ft = u * 4 + f
                    ps, pt = next_ps(C)
                    for kt in range(8):
                        S.add("pe", lambda e, ps=ps, wv=wv, f=f, kt=kt: e.matmul(
                            ps[:], lhsT=wv[:, f, kt, :], rhs=hT[:, kt, :], start=(kt == 0), stop=(kt == 7)),
                            [wt, htok], [pt])
                    if ft < 2:
                        norm_evac(C, ps[:], pt, gains, gtok, 8, onesbd, onest, tmps, [(QT[:, ft, cs], 0, 128, qtok)])
                    elif ft == 2:
                        norm_evac(C, ps[:], pt, gains, gtok, 9, onesbd, onest, tmps,
                                  [(Kslo[0:64, cs], 0, 64, kst), (Kshi[64:128, cs], 64, 128, kst)])
                    elif ft == 3:
                        norm_evac(C, ps[:], pt, gains, gtok, 9, onesbd, onest, tmps,
                                  [(Kwlo[0:64, cs], 0, 64, kwt), (Kwhi[64:128, cs], 64, 128, kwt)])
                    else:
                        S.add("act", lambda e, ps=ps, cs=cs: e.activation(out=KcVcT[:, cs], in_=ps[:], func=AF.Copy),
                              [pt], accw=[kcvt])
            wv, wt = load_unit(C, WMs, 2)
            for j in range(4):
                tile_i = c * 4 + j
                ps, pt = next_ps(C)
                for kt in range(8):
                    S.add("pe", lambda e, ps=ps, wv=wv, j=j, kt=kt: e.matmul(
                        ps[:, 0:140], lhsT=hT[:, kt, j * 128:(j + 1) * 128], rhs=wv[:, kt, 0:140],
                        start=(kt == 0), stop=(kt == 7)), [wt, htok], [pt])
                S.add("dve", lambda e, ps=ps, tile_i=tile_i: e.tensor_copy(
                    out=V1[:, tile_i, :, 0:64], in_=ps[:, 0:128].rearrange("p (b d) -> p b d", b=2)), [pt], accw=[v1t])
                S.add("act", lambda e, ps=ps, tile_i=tile_i: e.activation(
                    out=Gt[:, tile_i, :], in_=ps[:, 128:140], func=AF.Sigmoid), [pt], accw=[gtt])
    with S.scope():
        W1b = sb("n_W1b", [128, 32, 256], BF16)
        posT = sb("n_posT", [128, 32], F32)
        w2f = sb("n_w2f", [128, 2, 2, 128], F32)
        w2b = sb("n_w2b", [128, 2, 2, 128], BF16)
        zr = [sb("n_zr%d" % i, [128, 512], BF16) for i in range(4)]
        zrt = [Tok() for _ in range(4)]
        GT = sb("n_GT", [128, 4, 512], BF16)
        ga = sb("n_ga", [128, 512], F32)
        gb = sb("n_gb", [128, 512], F32)
        w1t, post, w2t, w2bt, GTt, gat, gbt = [Tok() for _ in range(7)]
        S.dma(posT[:], pd["posT"], writes=[post])
        S.dma(w2f[:], pd["w2"], writes=[w2t])
        S.add("dve", lambda e: e.tensor_copy(out=w2b[:], in_=w2f[:]), [w2t], [w2bt])
        S.add("dve", lambda e: e.tensor_copy(out=KcVcT[:, SL:SL + 16], in_=KcVcT[:, SL - 1:SL].to_broadcast([128, 16])),
              [kcvt], [kcvt])
        accs = [next_ps(C) for _ in range(4)]
        zn = 0
        for kv, srcw in enumerate((pd["w1k"], pd["w1v"])):
            for r0 in range(0, 32, 2):
                i = C.sn % 3
                C.sn += 1
                st, stt = C.stg[i], C.stgtok[i]
                S.dma(st[:, :512], srcw[:, r0:r0 + 2, :].rearrange("p r h -> p (r h)"), writes=[stt])
                eng = ("dve", "pool")[C.cast_rr % 2]
                C.cast_rr += 1
                S.add(eng, lambda e, st=st, r0=r0: e.tensor_copy(
                    out=W1b[:, r0:r0 + 2, :].rearrange("p r h -> p (r h)"), in_=st[:, :512]), [stt], accw=[w1t])
            for r in range(32):
                z, zt = zr[zn % 4], zrt[zn % 4]
                zn += 1
                if r < 16:
                    src = KcVcT[:, 0:16 * nb].rearrange("p (i s) -> p i s", s=16)[:, :, r]
                else:
                    src = KcVcT[:, 16:16 + 16 * nb].rearrange("p (i s) -> p i s", s=16)[:, :, r - 16]
                eng = ("dve", "pool")[r % 2]
                S.add(eng, lambda e, z=z, src=src, r=r: e.tensor_scalar(
                    out=z[:, :nb], in0=src, scalar1=posT[:, r:r + 1], scalar2=None, op0=ALU.add), [kcvt, post], [zt])
                for hid in range(2):
                    ps, pt = accs[kv * 2 + hid]
                    S.add("pe", lambda e, ps=ps, hid=hid, r=r, z=z: e.matmul(
                        ps[:, :nb], lhsT=W1b[:, r, hid * 128:(hid + 1) * 128], rhs=z[:, :nb],
                        start=(r == 0), stop=(r == 31)), [w1t, zt], [pt])
        for a in range(4):
            ps, pt = accs[a]
            S.add("act", lambda e, ps=ps: e.activation(out=ga[:, :nb], in_=ps[:, :nb], func=AF.Square), [pt], [gat])
            S.add("dve", lambda e: e.tensor_scalar(out=ga[:, :nb], in0=ga[:, :nb], scalar1=0.044715, scalar2=1.0,
                                                   op0=ALU.mult, op1=ALU.add), [gat], [gat])
            S.add("dve", lambda e, ps=ps: e.tensor_tensor(out=ga[:, :nb], in0=ga[:, :nb], in1=ps[:, :nb], op=ALU.mult),
                  [gat, pt], [gat])
            S.add("act", lambda e: e.activation(out=gb[:, :nb], in_=ga[:, :nb], func=AF.Sigmoid, scale=1.5957691216057308),
                  [gat], [gbt])
            S.add("dve", lambda e, ps=ps, a=a: e.tensor_tensor(out=GT[:, a, :nb], in0=gb[:, :nb], in1=ps[:, :nb],
                                                               op=ALU.mult), [gbt, pt], accw=[GTt])
        ps, pt = next_ps(C)
        for t in range(2):
            S.add("pe", lambda e, ps=ps, t=t: e.matmul(ps[:, :nb], lhsT=w2b[:, t, 0, :], rhs=GT[:, t, :nb],
                                                       start=(t == 0), stop=(t == 1)), [w2bt, GTt], [pt])
        norm_evac(C, ps[:, :nb], pt, gains, gtok, 9, onesbd, onest, tmps,
                  [(kclo[0:64, :nb], 0, 64, kct), (kchi[64:128, :nb], 64, 128, kct)])
        for ct in range(NCT):
            ps, pt = next_ps(C)
            wdt = min(128, nb)
            for t in range(2):
                S.add("pe", lambda e, ps=ps, t=t, ct=ct, wdt=wdt: e.matmul(
                    ps[:wdt, 0:64], lhsT=GT[:, 2 + t, ct * 128:ct * 128 + wdt], rhs=w2b[:, t, 1, 0:64],
                    start=(t == 0), stop=(t == 1)), [w2bt, GTt], [pt])
            S.add("dve", lambda e, ps=ps, ct=ct, wdt=wdt: e.tensor_copy(out=Vc1[:wdt, ct, 0:64], in_=ps[:wdt, 0:64]),
                  [pt], accw=[vct])
    kc_scope.__exit__(None, None, None)
    with S.scope():
        wexp = sb("n_wexp", [128, SL], BF16)
        triT4 = sb("n_triT4", [128, 512], BF16)
        triU4 = sb("n_triU4", [128, 512], BF16)
        cmQ = sb("n_cmQ", [128, 16, 128], BF16)
        cmT = sb("n_cmT", [128, 16, 128], BF16)
        wet, trt, trut, cmqt, cmtt = [Tok() for _ in range(5)]
        S.dma(wexp[:], cd["wexp"], writes=[wet])
        S.dma(triT4[:], cd["triT"], writes=[trt])
        S.dma(triU4[:], cd["triU"], writes=[trut])
        S.dma(cmQ[:], cd["cmaskQ"], writes=[cmqt])
        S.dma(cmT[:], cd["cmaskT"], writes=[cmtt])
        E4 = sb("n_E4", [128, 4, 512], F32)
        rsum = sb("n_rsum", [128, 4], F32)
        rinv = sb("n_rinv", [128, 4], F32)
        imp = sb("n_imp", [128, 512], F32)
        ib = sb("n_ib", [128, 128], F32)
        sc = sb("n_sc", [128, 128], F32)
        sc2 = sb("n_sc2", [128, 128], F32)
        m8a = sb("n_m8a", [128, 8], F32)
        m8b = sb("n_m8b", [128, 8], F32)
        nmf = sb("n_nmf", [128, 128], F32)
        nmb = sb("n_nmb", [128, 128], BF16)
        nmT4 = sb("n_nmT4", [128, 512], BF16)
        PT = [sb("n_PT%d" % i, [128, 512], BF16) for i in range(3)]
        PTt = [Tok() for _ in range(3)]
        den = sb("n_den", [128, 4], F32)
        coef = sb("n_coef", [128, 4], F32)
        acc = sb("n_acc", [128, 4, 64], F32)
        ob = [sb("n_ob%d" % i, [128, 256], BF16) for i in range(2)]
        obt = [Tok(), Tok()]
        e4t, rsumt, rinvt, impt, ibt_, sct, sc2t, m8at, m8bt, nmft, nmbt, nmTt, dent, coeft, acct = [Tok() for _ in range(15)]
        npool = len(C.ps)
        Obank = [(C.ps[npool - 2], C.pstok[npool - 2]), (C.ps[npool - 1], C.pstok[npool - 1])]
        C.ps_active = npool - 2
        on = 0
        ptn = 0

        def att_branch(tiles, br, qt, first_branch):
            nonlocal on, ptn
            qs = slice(qt * 128, (qt + 1) * 128)
            O, Ot = Obank[on % 2]
            on += 1
            nt = len(tiles)
            def scores(idx):
                klo, khi, ktoks, v, vtok, masks = tiles[idx]
                psT, ptT = next_ps(C)
                first = True
                for (ml, mlt, mr, mrt, wide) in masks:
                    if wide:
                        S.add("pe", lambda e, psT=psT, ml=ml, mr=mr, first=first: e.matmul(
                            psT[:, 0:512], lhsT=ml, rhs=mr, start=first, stop=False, skip_group_check=True),
                            [mlt, mrt], [ptT])
                        first = False
                    else:
                        for cb in range(4):
                            S.add("pe", lambda e, psT=psT, ml=ml, mr=mr, cb=cb, first=first: e.matmul(
                                psT[:, cb * 128:(cb + 1) * 128], lhsT=ml, rhs=mr, start=first, stop=False,
                                skip_group_check=True), [mlt, mrt], [ptT])
                            first = False
                S.add("pe", lambda e, psT=psT, klo=klo, qs=qs, first=first: e.matmul(
                    psT[:, 0:256], lhsT=klo, rhs=QT[:, :, qs], start=first, stop=False, skip_group_check=True),
                    [ktoks, qtok], [ptT])
                S.add("pe", lambda e, psT=psT, khi=khi, qs=qs: e.matmul(
                    psT[:, 256:512], lhsT=khi, rhs=QT[:, :, qs], start=False, stop=True, skip_group_check=True),
                    [ktoks, qtok], [ptT])
                return psT, ptT

            pend = scores(0)
            for idx in range(nt):
                psT, ptT = pend
                if idx + 1 < nt:
                    pend = scores(idx + 1)
                v, vtok = tiles[idx][3], tiles[idx][4]
                P, Pt_ = PT[ptn % 3], PTt[ptn % 3]
                ptn += 1
                S.add("act", lambda e, psT=psT, P=P: e.activation(out=P[:], in_=psT[:], func=AF.Exp, scale=0.125),
                      [ptT], [Pt_])
                for cb in range(4):
                    S.add("pe", lambda e, O=O, P=P, v=v, cb=cb, idx=idx: e.matmul(
                        O[:, cb * 65:(cb + 1) * 65], lhsT=P[:, cb * 128:(cb + 1) * 128], rhs=v,
                        start=(idx == 0 and cb == 0), stop=(idx == nt - 1), skip_group_check=True), [Pt_, vtok], [Ot])
            Ov = O[:, 0:260].rearrange("p (c d) -> p c d", c=4)
            S.add("dve", lambda e, Ov=Ov: e.tensor_scalar(out=den[:], in0=Ov[:, :, 64], scalar1=1e-30, scalar2=None,
                                                          op0=ALU.max), [Ot], [dent])
            S.add("dve", lambda e: e.reciprocal(out=den[:], in_=den[:]), [dent], [dent])
            gv = Gt[:, qt, br * 4:(br + 1) * 4].rearrange("p (a b) -> p b a", a=2)
            S.add("dve", lambda e, gv=gv: e.tensor_tensor(out=coef[:].rearrange("p (b a) -> p b a", b=2),
                                                          in0=den[:].rearrange("p (b a) -> p b a", b=2), in1=gv,
                                                          op=ALU.mult), [dent, gtt], [coeft])
            for cb in range(4):
                h = HORD[cb]
                if first_branch:
                    S.add("dve", lambda e, Ov=Ov, cb=cb, h=h: e.tensor_scalar(
                        out=acc[:, h, :], in0=Ov[:, cb, 0:64], scalar1=coef[:, cb:cb + 1], scalar2=None,
                        op0=ALU.mult), [Ot, coeft], [acct])
                else:
                    S.add("dve", lambda e, Ov=Ov, cb=cb, h=h: e.scalar_tensor_tensor(
                        out=acc[:, h, :], in0=Ov[:, cb, 0:64], scalar=coef[:, cb:cb + 1], in1=acc[:, h, :],
                        op0=ALU.mult, op1=ALU.add), [Ot, coeft, acct], [acct])

        for qt in range(NT):
            qs = slice(qt * 128, (qt + 1) * 128)
            ctl = (8 * qt + 6) // 128
            ncol = 128 * (ctl + 1)
            r16 = qt % 16
            for cb, (p, Kc) in enumerate(((0, kclo), (1, kclo), (0, kchi), (1, kchi))):
                psS, ptS = next_ps(C)
                first = True
                if ctl > 0:
                    S.add("pe", lambda e, psS=psS, p=p, Kc=Kc, qs=qs, ctl=ctl: e.matmul(
                        psS[:, 0:ctl * 128], lhsT=QT[:, p, qs], rhs=Kc[:, 0:ctl * 128], start=True, stop=False,
                        skip_group_check=True), [qtok, kct], [ptS])
                    first = False
                S.add("pe", lambda e, psS=psS, ctl=ctl, ncol=ncol, r16=r16, first=first: e.matmul(
                    psS[:, ctl * 128:ncol], lhsT=identb[:], rhs=cmQ[:, r16, :], start=first, stop=False,
                    skip_group_check=True), [ibt, cmqt], [ptS])
                S.add("pe", lambda e, psS=psS, p=p, Kc=Kc, qs=qs, ctl=ctl, ncol=ncol: e.matmul(
                    psS[:, ctl * 128:ncol], lhsT=QT[:, p, qs], rhs=Kc[:, ctl * 128:ncol], start=False, stop=True,
                    skip_group_check=True), [qtok, kct], [ptS])
                S.add("act", lambda e, psS=psS, cb=cb, ncol=ncol: e.activation(
                    out=E4[:, cb, :ncol], in_=psS[:, :ncol], func=AF.Exp, scale=0.125, accum_out=rsum[:, cb:cb + 1]),
                    [ptS], accw=[e4t, rsumt])
            S.add("dve", lambda e: e.tensor_scalar(out=rinv[:], in0=rsum[:], scalar1=1e-30, scalar2=None, op0=ALU.max),
                  [rsumt], [rinvt])
            S.add("dve", lambda e: e.reciprocal(out=rinv[:], in_=rinv[:]), [rinvt], [rinvt])
            S.add("dve", lambda e, ncol=ncol: e.tensor_scalar(out=imp[:, :ncol], in0=E4[:, 0, :ncol], scalar1=rinv[:, 0:1],
                                                             scalar2=None, op0=ALU.mult), [e4t, rinvt], [impt])
            for cb in range(1, 4):
                S.add("dve", lambda e, cb=cb, ncol=ncol: e.scalar_tensor_tensor(
                    out=imp[:, :ncol], in0=E4[:, cb, :ncol], scalar=rinv[:, cb:cb + 1], in1=imp[:, :ncol],
                    op0=ALU.mult, op1=ALU.add), [e4t, rinvt, impt], [impt])
            nblk = ncol // 4
            S.add("dve", lambda e, ncol=ncol, nblk=nblk: e.tensor_reduce(
                out=ib[:, :nblk], in_=imp[:, :ncol].rearrange("p (j r) -> p j r", r=4), axis=AX.X, op=ALU.add),
                [impt], [ibt_])
            S.add("dve", lambda e, nblk=nblk: e.tensor_tensor(
                out=ib[:, 1:nblk], in0=ib[:, 1:nblk],
                in1=imp[:, 0:4 * (nblk - 1)].rearrange("p (j r) -> p j r", r=4)[:, :, 3], op=ALU.add),
                [impt, ibt_], [ibt_])
            S.add("pool", lambda e: e.memset(sc[:], -1e30), [], [sct])
            if qt > 0:
                S.add("dve", lambda e, qt=qt: e.tensor_copy(out=sc[:, 0:2 * qt], in_=ib[:, 0:2 * qt]), [ibt_, sct], [sct])
                S.add("dve", lambda e, qt=qt: e.memset(sc[0:64, 2 * qt - 1:2 * qt], 1e4), [sct], [sct])
            S.add("dve", lambda e: e.memset(sc[:, 0:1], 1e4), [sct], [sct])
            S.add("dve", lambda e, qt=qt: e.memset(sc[:, 2 * qt:2 * qt + 1], 1e4), [sct], [sct])
            S.add("dve", lambda e, qt=qt: e.memset(sc[64:128, 2 * qt + 1:2 * qt + 2], 1e4), [sct], [sct])
            S.add("dve", lambda e: e.max(out=m8a[:], in_=sc[:]), [sct], [m8at])
            S.add("dve", lambda e: e.match_replace(out=sc2[:], in_to_replace=m8a[:], in_values=sc[:], imm_value=-1e30),
                  [sct, m8at], [sc2t])
            S.add("dve", lambda e: e.max(out=m8b[:], in_=sc2[:]), [sc2t], [m8bt])
            S.add("dve", lambda e: e.tensor_scalar(out=nmf[:], in0=sc[:], scalar1=m8b[:, 7:8], scalar2=None,
                                                   op0=ALU.is_ge), [sct, m8bt], [nmft])
            S.add("dve", lambda e: e.tensor_scalar(out=nmb[:], in0=nmf[:], scalar1=-1.0, scalar2=-NEG,
                                                   op0=ALU.add, op1=ALU.mult), [nmft], [nmbt])
            tiles = []
            for ct in range(ctl + 1):
                cs = slice(ct * 128, (ct + 1) * 128)
                masks = [(identb[:], ibt, cmT[:, r16, :], cmtt, False)] if ct == ctl else []
                tiles.append((kclo[:, cs], kchi[:, cs], kct, Vc1[:, ct, :], vct, masks))
            att_branch(tiles, 0, qt, True)
            tiles = []
            for kt in range(max(0, qt - 4), qt + 1):
                ks_ = slice(kt * 128, (kt + 1) * 128)
                masks = []
                if kt == qt:
                    masks.append((identb[:], ibt, triT4[:], trt, True))
                if kt == qt - 4:
                    masks.append((identb[:], ibt, triU4[:], trut, True))
                tiles.append((Kwlo[:, ks_], Kwhi[:, ks_], kwt, V1[:, kt, 1, :], v1t, masks))
            att_branch(tiles, 2, qt, False)
            pb, pbt = next_psb(C)
            S.add("pe", lambda e, pb=pb: e.transpose(out=pb[:, 0:128], in_=nmb[:], identity=identb[:]), [nmbt, ibt], [pbt])
            S.add("dve", lambda e, pb=pb: e.tensor_copy(out=nmT4[:, 0:128], in_=pb[:, 0:128]), [pbt], [nmTt])
            S.add("dve", lambda e: e.tensor_copy(out=nmT4[:, 128:256], in_=nmT4[:, 0:128]), [nmTt], [nmTt])
            S.add("dve", lambda e: e.tensor_copy(out=nmT4[:, 256:512], in_=nmT4[:, 0:256]), [nmTt], [nmTt])
            tiles = []
            for kt in range(qt + 1):
                ks_ = slice(kt * 128, (kt + 1) * 128)
                masks = [(wexp[:, ks_], wet, nmT4[:], nmTt, True)]
                if kt == qt:
                    masks.append((identb[:], ibt, triT4[:], trt, True))
                tiles.append((Kslo[:, ks_], Kshi[:, ks_], kst, V1[:, kt, 0, :], v1t, masks))
            att_branch(tiles, 1, qt, False)
            o_, ot_ = ob[qt % 2], obt[qt % 2]
            S.add("act", lambda e, o_=o_: e.activation(out=o_[:], in_=acc[:].rearrange("p h d -> p (h d)"), func=AF.Copy),
                  [acct], [ot_])
            S.dma(ao[qs, 0:256], o_[:], reads=[ot_], accw=[ao_tok], q="act")
        C.ps_active = npool


from concourse.bass_utils import run_bass_kernel_spmd

_NC_CACHE = {}


def _get_nc(key, builder):
    if key not in _NC_CACHE:
        _NC_CACHE[key] = builder()
    return _NC_CACHE[key]


def kernel(**inputs):
    z = {k: np.asarray(v) for k, v in inputs.items()}
    x = np.ascontiguousarray(z["x"], np.float32)
    B, SL, D = x.shape
    depth = z["w_in"].shape[0]
    ncA = _get_nc("A", lambda: build_phaseA(SL))
    ncB = _get_nc("B", lambda: build_phaseB(SL // 2))
    constsA = [hostA_consts(g, SL) for g in range(2)]
    ident = np.eye(128, dtype=np.float32)
    for L in range(depth):
        insA = []
        wsm = [hostA_weights(z["w_in"][L], g) for g in range(2)]
        prm = [hostA_params(z, L, g) for g in range(2)]
        for c in range(8):
            b, g = c // 2, c % 2
            d = dict(x=np.ascontiguousarray(x[b]), WS=wsm[g][0], WM=wsm[g][1])
            d.update(constsA[g])
            d.update(prm[g])
            insA.append(d)
        resA = run_bass_kernel_spmd(ncA, insA, core_ids=list(range(8)))
        ao = [np.asarray(resA.results[c]["ao"]) for c in range(8)]

        def gl(v):
            return np.ascontiguousarray(np.asarray(v, np.float32).reshape(8, 128).T)
        gains = np.ascontiguousarray(np.concatenate(
            [gl(z["norm_mix"][L]), gl(z["norm_mlp"][L]), gl(z["norm_ple"][L])], 1), np.float32)
        w_merge = np.ascontiguousarray(z["w_in"][L][:, 3352:5400], np.float32)
        insB = []
        for c in range(8):
            b, hf = c // 2, c % 2
            sl = slice(hf * (SL // 2), (hf + 1) * (SL // 2))
            attn = np.concatenate([ao[2 * b][sl, :256], ao[2 * b + 1][sl, :256],
                                   ao[2 * b][sl, 256:], ao[2 * b + 1][sl, 256:]], 1)
            insB.append(dict(
                x=np.ascontiguousarray(x[b, sl]), attn=np.ascontiguousarray(attn),
                p=np.ascontiguousarray(z["p"][L, b, sl], np.float32), gains=gains, ident=ident,
                w_merge=w_merge, w_up_nsa=np.ascontiguousarray(z["w_up_nsa"][L], np.float32),
                w_up_ret=np.ascontiguousarray(z["w_up_ret"][L], np.float32),
                w_out=np.ascontiguousarray(z["w_out"][L], np.float32),
                w_ff1=np.ascontiguousarray(z["w_ff1"][L], np.float32),
                w_ff2=np.ascontiguousarray(z["w_ff2"][L], np.float32),
                w_gate=np.ascontiguousarray(z["w_ple_gate"][L], np.float32),
                w_ple=np.ascontiguousarray(z["w_ple"][L], np.float32)))
        resB = run_bass_kernel_spmd(ncB, insB, core_ids=list(range(8)))
        xn = np.empty_like(x)
        for c in range(8):
            b, hf = c // 2, c % 2
            xn[b, hf * (SL // 2):(hf + 1) * (SL // 2)] = np.asarray(resB.results[c]["xo"])
        x = xn
    return x


B_WNAMES = (("w_merge", 1024, 2048), ("w_up_nsa", 512, 1024), ("w_up_ret", 512, 1024), ("w_out", 1024, 1024),
            ("w_ff1", 1024, 4096), ("w_ff2", 4096, 1024), ("w_gate", 1024, 1024), ("w_ple", 256, 1024))
PAIR_GROUPS = [[0, 1], [2, 3], [4, 5], [6, 7]]


def build_fused(SL=8192, depth=2):
    nc = bass.Bass("TRN2", target_bir_lowering=False)
    dt = nc.dram_tensor
    T = SL // 2
    x_full = dt("x", [SL, 1024], F32, kind="ExternalInput").ap()
    xh = dt("xh", [T, 1024], F32, kind="ExternalInput").ap()
    hmask_d = dt("hmask", [128, 2], F32, kind="ExternalInput").ap()
    cd = {k: dt(k, sh, ty, kind="ExternalInput").ap() for k, (sh, ty) in A_CONST_SHAPES(SL).items()}
    WS_d, WM_d, pd, p_d, gB_d, wd = [], [], [], [], [], []
    for L in range(depth):
        WS_d.append(dt("WS%d" % L, [1024, 2048], F32, kind="ExternalInput").ap())
        WM_d.append(dt("WM%d" % L, [1024, 1536], F32, kind="ExternalInput").ap())
        pd.append({k: dt("%s%d" % (k, L), sh, F32, kind="ExternalInput").ap() for k, sh in A_PARAM_SHAPES.items()})
        p_d.append(dt("p%d" % L, [T, 256], F32, kind="ExternalInput").ap())
        gB_d.append(dt("gainsB%d" % L, [128, 24], F32, kind="ExternalInput").ap())
        wd.append({n: dt("%s%d" % (n, L), [K, N], F32, kind="ExternalInput").ap() for n, K, N in B_WNAMES})
    out = dt("xo", [T, 1024], F32, kind="ExternalOutput").ap()
    ao = [dt("ao%d" % L, [SL, 512], BF16, kind="Internal").ap() for L in range(depth)]
    aog = [dt("aog%d" % L, [2 * SL, 512], BF16, kind="Internal").ap() for L in range(depth)]
    xmid = [dt("xmid%d" % L, [T, 1024], F32, kind="Internal").ap() for L in range(depth - 1)]
    xg = [dt("xg%d" % L, [SL, 1024], F32, kind="Internal").ap() for L in range(depth - 1)]
    S = Sched(nc)
    C = make_pools(S, n_wbuf=3)
    xg_tok = None
    xmid_tok = None
    for L in range(depth):
        S.prefix = "L%dA_" % L
        aot, aogt = Tok(), Tok()
        with S.scope():
            XK = min(512, T)
            xmap = None if L == 0 else (lambda t: 2 * ((t % T) // XK) * XK + (t // T) * XK + (t % T) % XK)
            emit_phaseA(C, SL, x_full if L == 0 else xg[L - 1], WS_d[L], WM_d[L], cd, pd[L], ao[L],
                        xin_tok=xg_tok, ao_tok=aot, xmap=xmap)
        RK = min(2048, SL)
        for k in range(SL // RK):
            S.collective("AllGather", ao[L][k * RK:(k + 1) * RK, :].opt(), aog[L][2 * k * RK:2 * (k + 1) * RK, :].opt(),
                         PAIR_GROUPS, reads=[aot], accw=[aogt])
        S.prefix = "L%dB_" % L
        with S.scope():
            hm = S.sbuf("hm", [128, 2], F32)
            hmt = Tok()
            S.dma(hm[:], hmask_d, writes=[hmt])
            atAB = [S.sbuf("atAB%d" % i, [128, 4, 1024], BF16) for i in range(2)]
            atABt = [Tok(), Tok()]
            nw = len(C.wbuf)
            C.wbuf = C.wbuf + [S.sbuf("wbufx%d" % i, [128, WU_ELEMS], BF16) for i in range(1)]
            C.wtok = C.wtok + [Tok() for _ in range(1)]
            last = (L == depth - 1)
            xo_tok = Tok()
            emit_phaseB(C, T, xh if L == 0 else xmid[L - 1], aog[L], p_d[L], gB_d[L], cd["ident"], wd[L],
                        out if last else xmid[L], xin_tok=xmid_tok, attn_tok=aogt, xo_tok=xo_tok,
                        gathered=(SL, hm, hmt, atAB, atABt))
            C.wbuf = C.wbuf[:nw]
            C.wtok = C.wtok[:nw]
        if not last:
            xmid_tok = xo_tok
            xg_tok = Tok()
            XK = min(512, T)
            for k in range(T // XK):
                S.collective("AllGather", xmid[L][k * XK:(k + 1) * XK, :].opt(),
                             xg[L][2 * k * XK:2 * (k + 1) * XK, :].opt(), PAIR_GROUPS, reads=[xo_tok], accw=[xg_tok])
    S.emit()
    S.close()
    return nc


def fused_inputs(z, SL, depth):
    import ml_dtypes
    x = np.ascontiguousarray(z["x"], np.float32)
    T = SL // 2
    consts = [hostA_consts(g, SL) for g in range(2)]

    def gl(v):
        return np.ascontiguousarray(np.asarray(v, np.float32).reshape(8, 128).T)
    per_layer = []
    for L in range(depth):
        d = {}
        d["wsm"] = [hostA_weights(z["w_in"][L], g) for g in range(2)]
        d["prm"] = [hostA_params(z, L, g) for g in range(2)]
        d["gainsB"] = np.ascontiguousarray(np.concatenate(
            [gl(z["norm_mix"][L]), gl(z["norm_mlp"][L]), gl(z["norm_ple"][L])], 1), np.float32)
        d["w"] = dict(
            w_merge=np.ascontiguousarray(z["w_in"][L][:, 3352:5400], np.float32),
            w_up_nsa=np.ascontiguousarray(z["w_up_nsa"][L], np.float32),
            w_up_ret=np.ascontiguousarray(z["w_up_ret"][L], np.float32),
            w_out=np.ascontiguousarray(z["w_out"][L], np.float32),
            w_ff1=np.ascontiguousarray(z["w_ff1"][L], np.float32),
            w_ff2=np.ascontiguousarray(z["w_ff2"][L], np.float32),
            w_gate=np.ascontiguousarray(z["w_ple_gate"][L], np.float32),
            w_ple=np.ascontiguousarray(z["w_ple"][L], np.float32))
        per_layer.append(d)
    ins = []
    for c in range(8):
        b, r = c // 2, c % 2
        sl = slice(r * T, (r + 1) * T)
        d = dict(x=np.ascontiguousarray(x[b, :SL]), xh=np.ascontiguousarray(x[b, sl]))
        hm = np.zeros((128, 2), np.float32)
        hm[:, r] = 1.0
        d["hmask"] = hm
        d.update(consts[r])
        for L in range(depth):
            pl = per_layer[L]
            d["WS%d" % L], d["WM%d" % L] = pl["wsm"][r]
            for k, v in pl["prm"][r].items():
                d["%s%d" % (k, L)] = v
            d["p%d" % L] = np.ascontiguousarray(z["p"][L, b, sl], np.float32)
            d["gainsB%d" % L] = pl["gainsB"]
            for k, v in pl["w"].items():
                d["%s%d" % (k, L)] = v
        ins.append(d)
    return ins


def kernel(**inputs):
    z = {k: np.asarray(v) for k, v in inputs.items()}
    B, SL, D = z["x"].shape
    depth = z["w_in"].shape[0]
    nc = _get_nc(("F", SL, depth), lambda: build_fused(SL, depth))
    ins = fused_inputs(z, SL, depth)
    res = run_bass_kernel_spmd(nc, ins, core_ids=list(range(8)))
    T = SL // 2
    out = np.empty((B, SL, D), np.float32)
    for c in range(8):
        b, r = c // 2, c % 2
        out[b, r * T:(r + 1) * T] = np.asarray(res.results[c]["xo"])
    return out
```

```python
from contextlib import ExitStack
import numpy as np
import concourse.bass as bass
import concourse.mybir as mybir

F32 = mybir.dt.float32
BF16 = mybir.dt.bfloat16
I32 = mybir.dt.int32
AF = mybir.ActivationFunctionType
ALU = mybir.AluOpType
AX = mybir.AxisListType

ENGS = ("pe", "act", "dve", "pool", "sp")
N_DMA_SEMS = 24


class Tok:
    __slots__ = ("lws", "rs", "base", "name", "excl", "accgrp")

    def __init__(self, name="", excl=False):
        self.excl = excl
        self.accgrp = False
        self.lws = []
        self.rs = []
        self.base = []
        self.name = name


class Op:
    __slots__ = ("eng", "fn", "deps", "dma", "idx", "sig", "dma_n", "cc")

    def __init__(self, eng, fn, deps, dma, idx):
        self.eng = eng
        self.fn = fn
        self.deps = deps
        self.dma = dma
        self.idx = idx
        self.sig = None
        self.dma_n = None
        self.cc = None


class _Scope:
    def __init__(self, S):
        self.S = S

    def __enter__(self):
        self.saved = self.S.stack
        self.S.stack = ExitStack()
        return self

    def __exit__(self, *a):
        self.S.barrier()
        self.S.stack.close()
        self.S.stack = self.saved
        return False


class Sched:
    def __init__(self, nc):
        self.nc = nc
        self.ops = {e: [] for e in ENGS}
        self.ndma = {e: 0 for e in ENGS}
        self.final_waits = []
        self.all_dma = []
        self.ncc = 0
        self.prefix = ""
        self.stack = ExitStack()

    def sbuf(self, name, shape, dtype):
        return self.stack.enter_context(self.nc.sbuf_tensor("sb_" + self.prefix + name, list(shape), dtype))

    def psum(self, name, shape, dtype):
        return self.stack.enter_context(self.nc.psum_tensor("pp_" + name, list(shape), dtype))

    def add(self, eng, fn, reads=(), writes=(), dma=False, accw=(), extra=()):
        deps = []
        seen = set()

        def push(d):
            if d is not None and d not in seen:
                seen.add(d)
                deps.append(d)

        for d in extra:
            push(d)
        for t in reads:
            for w in t.lws:
                push(w)
            if t.excl:
                for r in t.rs:
                    if r[0] != eng:
                        push(r)
        for t in writes:
            for w in t.lws:
                push(w)
            for r in t.rs:
                push(r)
        for t in accw:
            if t.rs or not t.lws or not t.accgrp:
                for w in t.lws:
                    push(w)
                for r in t.rs:
                    push(r)
            else:
                for d in t.base:
                    push(d)
        lst = self.ops[eng]
        op = Op(eng, fn, deps, dma, len(lst))
        if dma:
            op.dma_n = self.ndma[eng]
            self.ndma[eng] += 1
            self.all_dma.append((eng, op.idx))
        lst.append(op)
        me = (eng, op.idx)
        for t in reads:
            t.rs.append(me)
        for t in writes:
            t.lws = [me]
            t.rs = []
            t.base = []
            t.accgrp = False
        for t in accw:
            if t.rs or not t.lws or not t.accgrp:
                t.base = list(t.lws) + list(t.rs)
                t.lws = [me]
                t.rs = []
                t.accgrp = True
            else:
                t.lws.append(me)
        return op

    def collective(self, kind, src, dst, groups, reads=(), writes=(), accw=()):
        op = self.add("pool", lambda e: e.collective_compute(kind, ALU.bypass, replica_groups=groups,
                                                             ins=[src], outs=[dst]), reads, writes, accw=accw)
        self.ncc += 1
        op.cc = self.ncc
        return op

    def barrier(self):
        extra = list(self.all_dma)
        for e in ENGS:
            if self.ops[e]:
                extra.append((e, len(self.ops[e]) - 1))
        self.all_dma = []
        b0 = self.add("sp", lambda e: e.nop(), extra=extra)
        me = ("sp", b0.idx)
        for e in ("pe", "act", "dve", "pool"):
            self.add(e, lambda eng: eng.nop(), extra=[me])

    def dma(self, out, in_, reads=(), writes=(), q="sp", accw=(), **kw):
        return self.add(q, lambda e: e.dma_start(out=out, in_=in_, **kw), reads, writes, dma=True, accw=accw)

    def scope(self):
        return _Scope(self)

    def emit(self):
        nc = self.nc
        ops = self.ops
        needed = {e: set() for e in ENGS}
        waits = {e: [] for e in ENGS}
        for e in ENGS:
            maxw = {d: -1 for d in ENGS}
            dma_waited = set()
            for op in ops[e]:
                keep = []
                for (de, di) in op.deps:
                    dop = ops[de][di]
                    if dop.dma or dop.cc:
                        if (de, di) in dma_waited:
                            continue
                        dma_waited.add((de, di))
                        keep.append((de, di))
                    else:
                        if de == e and e == "pe":
                            continue
                        if de == e and di == op.idx:
                            continue
                        if di <= maxw[de]:
                            continue
                        maxw[de] = di
                        keep.append((de, di))
                        needed[de].add(di)
                waits[e].append(keep)
        for e in ENGS:
            c = 0
            for op in ops[e]:
                if (not op.dma) and (not op.cc) and op.idx in needed[e]:
                    c += 1
                    op.sig = c
        st = self.stack
        csem = {e: st.enter_context(nc.semaphore("c_" + e)) for e in ENGS}
        ccsem = st.enter_context(nc.semaphore("cc_sem"))
        dsem = {e: [st.enter_context(nc.semaphore("d_%s_%d" % (e, i))) for i in range(N_DMA_SEMS)]
                for e in ENGS if self.ndma[e] > 0}
        block = st.enter_context(nc.Block())

        def gen(e, eng):
            for op, keep in zip(ops[e], waits[e]):
                if op.dma:
                    n = op.dma_n
                    if n >= N_DMA_SEMS:
                        eng.wait_ge(dsem[e][n % N_DMA_SEMS], 16 * (n // N_DMA_SEMS))
                for (de, di) in keep:
                    dop = ops[de][di]
                    if dop.dma:
                        n = dop.dma_n
                        eng.wait_ge(dsem[de][n % N_DMA_SEMS], 16 * (n // N_DMA_SEMS + 1))
                    elif dop.cc:
                        eng.wait_ge(ccsem, dop.cc)
                    else:
                        eng.wait_ge(csem[de], dop.sig)
                ins = op.fn(eng)
                if op.dma:
                    n = op.dma_n
                    ins.then_inc(dsem[e][n % N_DMA_SEMS], 16)
                elif op.cc:
                    ins.then_inc(ccsem, 1)
                elif op.sig is not None:
                    ins.then_inc(csem[e], 1)
            if e == "pool" and self.ncc:
                eng.wait_ge(ccsem, self.ncc)
            nd = self.ndma[e]
            for i in range(min(nd, N_DMA_SEMS)):
                cnt = (nd - 1 - i) // N_DMA_SEMS + 1
                eng.wait_ge(dsem[e][i], 16 * cnt)

        @block.tensor
        def _(eng):
            gen("pe", eng)

        @block.scalar
        def _(eng):
            gen("act", eng)

        @block.vector
        def _(eng):
            gen("dve", eng)

        @block.gpsimd
        def _(eng):
            gen("pool", eng)

        @block.sync
        def _(eng):
            gen("sp", eng)

    def close(self):
        self.stack.close()


D_MODEL = 1024
EPS = 1e-6
WU_ELEMS = 4096


class WSpec:
    def __init__(self, S, name, w_ap, K, N, kind):
        self.name, self.K, self.N, self.kind = name, K, N, kind
        self.KT = K // 128
        self.w = w_ap
        nc = S.nc
        if kind == "S":
            assert N % 512 == 0
            self.nunits = N // 512
            self.uelems = 4 * self.KT * 128
        else:
            assert N % 512 == 0
            self.KTU = min(8, self.KT)
            self.nv = self.KT // self.KTU
            self.nunits = (N // 512) * self.nv
            self.uelems = self.KTU * 512
        assert self.uelems <= WU_ELEMS
        self.scr = nc.dram_tensor("scr_" + S.prefix + name, [self.nunits, 128, self.uelems], BF16, kind="Internal").ap()
        self.tok = Tok("scr_" + name)

    def unit_src(self, u):
        return self.scr[u]

    def view(self, buf):
        b = buf[:, : self.uelems]
        if self.kind == "S":
            return b.rearrange("p (f k c) -> p f k c", f=4, k=self.KT)
        return b.rearrange("p (k c) -> p k c", k=self.KTU)


class Ctx:
    pass


def make_pools(S, n_wbuf=5, n_ps=6):
    C = Ctx()
    C.S = S
    C.wbuf = [S.sbuf("wbuf%d" % i, [128, WU_ELEMS], BF16) for i in range(n_wbuf)]
    C.wtok = [Tok("wbuf%d" % i) for i in range(n_wbuf)]
    C.wn = 0
    C.ps = [S.psum("ps%d" % i, [128, 512], F32) for i in range(n_ps)]
    C.pstok = [Tok("ps%d" % i, excl=True) for i in range(n_ps)]
    C.pn = 0
    C.psb = [S.psum("psb%d" % i, [128, 1024], BF16) for i in range(2)]
    C.psbtok = [Tok("psb0", excl=True), Tok("psb1", excl=True)]
    C.pbn = 0
    C.stg = [S.sbuf("stg%d" % i, [128, 512], F32) for i in range(3)]
    C.stgtok = [Tok() for _ in range(3)]
    C.stgb = [S.sbuf("stgb%d" % i, [128, 512], BF16) for i in range(3)]
    C.stgbtok = [Tok() for _ in range(3)]
    C.sn = 0
    C.cast_rr = 0
    return C


def next_ps(C):
    i = C.pn % getattr(C, "ps_active", len(C.ps))
    C.pn += 1
    return C.ps[i], C.pstok[i]


def next_psb(C):
    i = C.pbn % 2
    C.pbn += 1
    return C.psb[i][:, 0:512], C.psbtok[i]


def load_unit(C, ws, u, q="sp"):
    i = C.wn % len(C.wbuf)
    C.wn += 1
    buf, tok = C.wbuf[i], C.wtok[i]
    C.S.dma(buf[:, : ws.uelems], ws.unit_src(u), reads=[ws.tok], writes=[tok], q=q)
    return ws.view(buf), tok


def prep_weight_gen(C, ws, q="act"):
    S = C.S
    K, N, KT = ws.K, ws.N, ws.KT
    for kt in range(KT):
        for c0 in range(0, N, 512):
            cw = min(512, N - c0)
            i = C.sn % 3
            C.sn += 1
            st, stt, sb, sbt = C.stg[i], C.stgtok[i], C.stgb[i], C.stgbtok[i]
            S.dma(st[:, :cw], ws.w[kt * 128:(kt + 1) * 128, c0:c0 + cw], writes=[stt])
            eng = ("dve", "pool")[C.cast_rr % 2] if q == "act" else "pool"
            C.cast_rr += 1
            S.add(eng, lambda e, sb=sb, st=st, cw=cw: e.tensor_copy(out=sb[:, :cw], in_=st[:, :cw]), [stt], [sbt])
            if ws.kind == "S":
                u0, nu = c0 // 512, cw // 512
                dst = ws.scr[u0:u0 + nu].rearrange("u p (f k c) -> p u f k c", f=4, k=KT)[:, :, :, kt, :]
                src = sb[:, :cw].rearrange("p (u f c) -> p u f c", u=nu, f=4)
                for uu in range(nu):
                    S.dma(dst[:, uu], src[:, uu], reads=[sbt], accw=[ws.tok], q=q)
            else:
                v, kk = kt // ws.KTU, kt % ws.KTU
                n0, nn = c0 // 512, cw // 512
                for n in range(nn):
                    u = (n0 + n) * ws.nv + v
                    dst = ws.scr[u].rearrange("p (k c) -> p k c", k=ws.KTU)[:, kk, :]
                    S.dma(dst, sb[:, n * 512:(n + 1) * 512], reads=[sbt], accw=[ws.tok], q=q)
            yield


def prep_weight(C, ws):
    for _ in prep_weight_gen(C, ws):
        pass


def rms_to_featmajor(C, xt, xtok, gains, gtok, gcol0, hT, htok, ident, itok, tmp):
    S = C.S
    ss, sstok, xs, xstok, junk, jtok, rstd, rtok = tmp
    for j in range(4):
        S.add("act", lambda e, j=j: e.activation(
            out=xs[:, j, :], in_=xt[:, j, :], func=AF.Square, accum_out=ss[:, j:j + 1]), [xtok], [xstok, sstok])
    S.add("dve", lambda e: e.tensor_scalar(out=rstd[:], in0=ss[:], scalar1=1.0 / D_MODEL, scalar2=EPS,
                                           op0=ALU.mult, op1=ALU.add), [sstok], [rtok])
    S.add("act", lambda e: e.activation(out=rstd[:], in_=rstd[:], func=AF.Sqrt), [rtok], [rtok])
    S.add("dve", lambda e: e.reciprocal(out=rstd[:], in_=rstd[:]), [rtok], [rtok])
    for j in range(4):
        S.add("act", lambda e, j=j: e.activation(out=xs[:, j, :], in_=xt[:, j, :], func=AF.Copy,
                                                 scale=rstd[:, j:j + 1]), [xtok, rtok], [xstok])
    for kt in range(8):
        ps, pt = next_ps(C)
        for j in range(4):
            S.add("pe", lambda e, ps=ps, j=j, kt=kt: e.transpose(
                out=ps[:, j * 128:(j + 1) * 128], in_=xs[:, j, kt * 128:(kt + 1) * 128], identity=ident[:]),
                [xstok, itok], [pt])
        if kt % 2 == 0:
            S.add("dve", lambda e, ps=ps, kt=kt: e.tensor_scalar(
                out=hT[:, kt, :], in0=ps[:], scalar1=gains[:, gcol0 + kt:gcol0 + kt + 1], scalar2=None,
                op0=ALU.mult), [pt, gtok], accw=[htok])
        else:
            S.add("act", lambda e, ps=ps, kt=kt: e.activation(
                out=hT[:, kt, :], in_=ps[:], func=AF.Copy, scale=gains[:, gcol0 + kt:gcol0 + kt + 1]),
                [pt, gtok], accw=[htok])


def build_phaseB(T=4096):
    nc = bass.Bass("TRN2", target_bir_lowering=False)
    dt = nc.dram_tensor
    x = dt("x", [T, 1024], F32, kind="ExternalInput").ap()
    attn = dt("attn", [T, 1024], BF16, kind="ExternalInput").ap()
    pin = dt("p", [T, 256], F32, kind="ExternalInput").ap()
    gains_d = dt("gains", [128, 24], F32, kind="ExternalInput").ap()
    ident_d = dt("ident", [128, 128], F32, kind="ExternalInput").ap()
    wd = {}
    for name, K, N in (("w_merge", 1024, 2048), ("w_up_nsa", 512, 1024), ("w_up_ret", 512, 1024),
                       ("w_out", 1024, 1024), ("w_ff1", 1024, 4096), ("w_ff2", 4096, 1024),
                       ("w_gate", 1024, 1024), ("w_ple", 256, 1024)):
        wd[name] = dt(name, [K, N], F32, kind="ExternalInput").ap()
    xo = dt("xo", [T, 1024], F32, kind="ExternalOutput").ap()
    S = Sched(nc)
    C = make_pools(S)
    emit_phaseB(C, T, x, attn, pin, gains_d, ident_d, wd, xo)
    S.emit()
    S.close()
    return nc


B_KINDS = {"w_merge": "S", "w_up_nsa": "S", "w_up_ret": "S", "w_out": "M", "w_ff1": "S", "w_ff2": "M",
           "w_gate": "M", "w_ple": "M"}


def make_B_wspecs(S, wd):
    W = {}
    for name, ap in wd.items():
        K, N = ap.shape
        W[name] = WSpec(S, name, ap, K, N, B_KINDS[name])
    return W


def prep_B_gen(C, W):
    for name in ("w_merge", "w_up_nsa", "w_up_ret", "w_out", "w_ff1", "w_ff2", "w_gate", "w_ple"):
        for _ in prep_weight_gen(C, W[name], q="sp"):
            yield


DBG_STAGE = 99
DBG_R = 99
DBG_SUB = 0
DBG_Q = "act"
DBG_PREP = True


def emit_phaseB(C, T, x, attn, pin, gains_d, ident_d, wd, xo, xin_tok=None, attn_tok=None, xo_tok=None,
                gathered=None, W=None):
    S = C.S
    kinds = {"w_merge": "S", "w_up_nsa": "S", "w_up_ret": "S", "w_out": "M", "w_ff1": "S", "w_ff2": "M",
             "w_gate": "M", "w_ple": "M"}
    preW = W is not None
    if not preW:
        W = make_B_wspecs(S, wd)
    gains = S.sbuf("gains", [128, 24], F32)
    gtok = Tok()
    ident = S.sbuf("ident", [128, 128], F32)
    identb = S.sbuf("identb", [128, 128], BF16)
    itok, ibtok = Tok(), Tok()
    S.dma(gains[:], gains_d, writes=[gtok])
    S.dma(ident[:], ident_d, writes=[itok])
    S.add("dve", lambda e: e.tensor_copy(out=identb[:], in_=ident[:]), [itok], [ibtok])
    for name in ("w_merge", "w_up_nsa", "w_up_ret", "w_out", "w_ff1", "w_ff2", "w_gate", "w_ple"):
        if DBG_PREP and not preW:
            prep_weight(C, W[name])
    xt = S.sbuf("xt", [128, 4, 1024], F32)
    at = S.sbuf("at", [128, 4, 1024], BF16)
    ptm = S.sbuf("ptm", [128, 4, 256], F32)
    xs = S.sbuf("xs", [128, 4, 1024], F32)
    junk = None
    ss = S.sbuf("ss", [128, 4], F32)
    rstd = S.sbuf("rstd", [128, 4], F32)
    hT = S.sbuf("hT", [128, 8, 512], BF16)
    aT = S.sbuf("aT", [128, 8, 512], BF16)
    sgT = S.sbuf("sgT", [128, 16, 512], BF16)
    mixT = S.sbuf("mixT", [128, 8, 512], BF16)
    uT = S.sbuf("uT", [128, 32, 512], BF16)
    pT = S.sbuf("pT", [128, 2, 512], BF16)
    tmpf = [S.sbuf("tmpf%d" % i, [128, 512], F32) for i in range(2)]
    tmpft = [Tok(), Tok()]
    gsb = S.sbuf("gsb", [128, 512], F32)
    xtok, atok, ptok, xstok, jtok, sstok, rtok = [Tok() for _ in range(7)]
    htok, aTtok, sgtok, mixtok, utok, pTtok, gsbtok = [Tok() for _ in range(7)]
    tmp = (ss, sstok, xs, xstok, junk, jtok, rstd, rtok)
    xin_tok = xin_tok or Tok()
    attn_tok = attn_tok or Tok()
    xo_tok = xo_tok or Tok()
    nchunk = T // 512
    tn = 0
    for c in range(nchunk):
        t0 = c * 512
        S.dma(xt[:], x[t0:t0 + 512, :].rearrange("(j p) d -> p j d", p=128), reads=[xin_tok], writes=[xtok])
        if gathered is None:
            S.dma(at[:], attn[t0:t0 + 512, :].rearrange("(j p) d -> p j d", p=128), reads=[attn_tok], writes=[atok])
        else:
            SLg, hm, hmt, atAB, atABt = gathered
            for hf in range(2):
                for g in range(2):
                    RKg = min(2048, SLg)
                    tk_ = hf * T + t0
                    r0 = 2 * (tk_ // RKg) * RKg + g * RKg + tk_ % RKg
                    srcv = attn[r0:r0 + 512, :].rearrange("(j p) d -> p j d", p=128)
                    S.dma(atAB[hf][:, :, g * 256:(g + 1) * 256], srcv[:, :, 0:256], reads=[attn_tok], accw=[atABt[hf]])
                    S.dma(atAB[hf][:, :, 512 + g * 256:512 + (g + 1) * 256], srcv[:, :, 256:512], reads=[attn_tok],
                          accw=[atABt[hf]])
            S.add("dve", lambda e: e.tensor_scalar(out=at[:], in0=atAB[0][:], scalar1=hm[:, 0:1], scalar2=None,
                                                   op0=ALU.mult), [atABt[0], hmt], [atok])
            S.add("dve", lambda e: e.scalar_tensor_tensor(out=at[:], in0=atAB[1][:], scalar=hm[:, 1:2], in1=at[:],
                                                          op0=ALU.mult, op1=ALU.add), [atABt[1], hmt, atok], [atok])
        S.dma(ptm[:], pin[t0:t0 + 512, :].rearrange("(j p) d -> p j d", p=128), writes=[ptok])
        def _store(t0=t0):
            S.dma(xo[t0:t0 + 512, :].rearrange("(j p) d -> p j d", p=128), xt[:], reads=[xtok], writes=[xo_tok],
                  q=DBG_Q)
        if DBG_STAGE <= 0:
            _store()
            continue
        rms_to_featmajor(C, xt, xtok, gains, gtok, 0, hT, htok, ident, itok, tmp)
        if DBG_STAGE <= 1:
            _store()
            continue
        ws = W["w_merge"]
        for u in range(ws.nunits):
            wv, wt = load_unit(C, ws, u)
            for f in range(4):
                ps, pt = next_ps(C)
                if DBG_SUB == 1:
                    continue
                for kt in range(8):
                    S.add("pe", lambda e, ps=ps, wv=wv, f=f, kt=kt: e.matmul(
                        ps[:], lhsT=wv[:, f, kt, :], rhs=hT[:, kt, :], start=(kt == 0), stop=(kt == 7)),
                        [wt, htok], [pt])
                if DBG_SUB == 2:
                    continue
                S.add("act", lambda e, ps=ps, ft=u * 4 + f: e.activation(
                    out=sgT[:, ft, :], in_=ps[:], func=AF.Sigmoid), [pt], accw=[sgtok])
        if DBG_STAGE <= 2:
            _store()
            continue
        for ft in range(8):
            pb, pbt = next_psb(C)
            for j in range(4):
                S.add("pe", lambda e, pb=pb, j=j, ft=ft: e.transpose(
                    out=pb[:, j * 128:(j + 1) * 128], in_=at[:, j, ft * 128:(ft + 1) * 128], identity=identb[:]),
                    [atok, ibtok], [pbt])
            S.add("dve", lambda e, pb=pb, ft=ft: e.tensor_copy(out=aT[:, ft, :], in_=pb), [pbt], accw=[aTtok])
        if DBG_SUB == 3:
            _store()
            continue
        wsa, wsb = W["w_up_nsa"], W["w_up_ret"]
        for u in range(2):
            wva, wta = load_unit(C, wsa, u)
            wvb, wtb = load_unit(C, wsb, u)
            for f in range(4):
                ft = u * 4 + f
                psa, pta = next_ps(C)
                for kt in range(4):
                    S.add("pe", lambda e, psa=psa, wva=wva, f=f, kt=kt: e.matmul(
                        psa[:], lhsT=wva[:, f, kt, :], rhs=aT[:, kt, :], start=(kt == 0), stop=(kt == 3)),
                        [wta, aTtok], [pta])
                psb_, ptb = next_ps(C)
                for kt in range(4):
                    S.add("pe", lambda e, psb_=psb_, wvb=wvb, f=f, kt=kt: e.matmul(
                        psb_[:], lhsT=wvb[:, f, kt, :], rhs=aT[:, 4 + kt, :], start=(kt == 0), stop=(kt == 3)),
                        [wtb, aTtok], [ptb])
                if DBG_SUB == 4:
                    continue
                tf, tft = tmpf[tn % 2], tmpft[tn % 2]
                tn += 1
                S.add("dve", lambda e, tf=tf, psa=psa, ft=ft: e.tensor_tensor(
                    out=tf[:], in0=psa[:], in1=sgT[:, ft, :], op=ALU.mult), [pta, sgtok], [tft])
                tf2, tft2 = tmpf[tn % 2], tmpft[tn % 2]
                tn += 1
                S.add("dve", lambda e, tf2=tf2, psb_=psb_, ft=ft: e.tensor_tensor(
                    out=tf2[:], in0=psb_[:], in1=sgT[:, 8 + ft, :], op=ALU.mult), [ptb, sgtok], [tft2])
                if DBG_SUB == 5:
                    continue
                S.add("pool", lambda e, tf=tf, tf2=tf2, ft=ft: e.tensor_tensor(
                    out=mixT[:, ft, :], in0=tf[:], in1=tf2[:], op=ALU.add), [tft, tft2], accw=[mixtok])
        if DBG_STAGE <= 3:
            _store()
            continue
        ws = W["w_out"]
        for n in range(2):
            wv, wt = load_unit(C, ws, n)
            for j in range(4):
                ps, pt = next_ps(C)
                for kt in range(8):
                    S.add("pe", lambda e, ps=ps, wv=wv, j=j, kt=kt: e.matmul(
                        ps[:], lhsT=mixT[:, kt, j * 128:(j + 1) * 128], rhs=wv[:, kt, :],
                        start=(kt == 0), stop=(kt == 7)), [wt, mixtok], [pt])
                S.add("dve", lambda e, ps=ps, j=j, n=n: e.tensor_tensor(
                    out=xt[:, j, n * 512:(n + 1) * 512], in0=ps[:], in1=xt[:, j, n * 512:(n + 1) * 512],
                    op=ALU.add), [pt, xtok], [xtok])
        if DBG_STAGE <= 4:
            _store()
            continue
        rms_to_featmajor(C, xt, xtok, gains, gtok, 8, hT, htok, ident, itok, tmp)
        ws = W["w_ff1"]
        for u in range(ws.nunits):
            wv, wt = load_unit(C, ws, u)
            for f in range(4):
                ft = u * 4 + f
                ps, pt = next_ps(C)
                for kt in range(8):
                    S.add("pe", lambda e, ps=ps, wv=wv, f=f, kt=kt: e.matmul(
                        ps[:], lhsT=wv[:, f, kt, :], rhs=hT[:, kt, :], start=(kt == 0), stop=(kt == 7)),
                        [wt, htok], [pt])
                tf, tft = tmpf[tn % 2], tmpft[tn % 2]
                tn += 1
                S.add("act", lambda e, ps=ps, tf=tf: e.activation(out=tf[:], in_=ps[:], func=AF.Relu),
                      [pt], [tft])
                S.add("pool", lambda e, tf=tf, ft=ft: e.tensor_tensor(
                    out=uT[:, ft, :], in0=tf[:], in1=tf[:], op=ALU.mult), [tft], accw=[utok])
        ws = W["w_ff2"]
        for n in range(2):
            pss = [next_ps(C) for _ in range(4)]
            for v in range(ws.nv):
                wv, wt = load_unit(C, ws, n * ws.nv + v)
                for j in range(4):
                    ps, pt = pss[j]
                    for kk in range(8):
                        kt = v * 8 + kk
                        S.add("pe", lambda e, ps=ps, wv=wv, j=j, kk=kk, kt=kt: e.matmul(
                            ps[:], lhsT=uT[:, kt, j * 128:(j + 1) * 128], rhs=wv[:, kk, :],
                            start=(kt == 0), stop=(kt == 31)), [wt, utok], [pt])
            for j in range(4):
                ps, pt = pss[j]
                S.add("dve", lambda e, ps=ps, j=j, n=n: e.tensor_tensor(
                    out=xt[:, j, n * 512:(n + 1) * 512], in0=ps[:], in1=xt[:, j, n * 512:(n + 1) * 512],
                    op=ALU.add), [pt, xtok], [xtok])
        if DBG_STAGE <= 5:
            _store()
            continue
        rms_to_featmajor(C, xt, xtok, gains, gtok, 16, hT, htok, ident, itok, tmp)
        for kt in range(2):
            ps, pt = next_ps(C)
            for j in range(4):
                S.add("pe", lambda e, ps=ps, j=j, kt=kt: e.transpose(
                    out=ps[:, j * 128:(j + 1) * 128], in_=ptm[:, j, kt * 128:(kt + 1) * 128], identity=ident[:]),
                    [ptok, itok], [pt])
            S.add("dve", lambda e, ps=ps, kt=kt: e.tensor_copy(out=pT[:, kt, :], in_=ps[:]), [pt], accw=[pTtok])
        wsg, wsp = W["w_gate"], W["w_ple"]
        for n in range(2):
            wvg, wtg = load_unit(C, wsg, n)
            wvp, wtp = load_unit(C, wsp, n)
            for j in range(4):
                ps, pt = next_ps(C)
                for kt in range(8):
                    S.add("pe", lambda e, ps=ps, wvg=wvg, j=j, kt=kt: e.matmul(
                        ps[:], lhsT=hT[:, kt, j * 128:(j + 1) * 128], rhs=wvg[:, kt, :],
                        start=(kt == 0), stop=(kt == 7)), [wtg, htok], [pt])
                S.add("act", lambda e, ps=ps: e.activation(out=gsb[:], in_=ps[:], func=AF.Sigmoid),
                      [pt], [gsbtok])
                ps2, pt2 = next_ps(C)
                for kt in range(2):
                    S.add("pe", lambda e, ps2=ps2, wvp=wvp, j=j, kt=kt: e.matmul(
                        ps2[:], lhsT=pT[:, kt, j * 128:(j + 1) * 128], rhs=wvp[:, kt, :],
                        start=(kt == 0), stop=(kt == 1)), [wtp, pTtok], [pt2])
                tf, tft = tmpf[tn % 2], tmpft[tn % 2]
                tn += 1
                S.add("dve", lambda e, tf=tf, ps2=ps2: e.tensor_tensor(
                    out=tf[:], in0=ps2[:], in1=gsb[:], op=ALU.mult), [pt2, gsbtok], [tft])
                S.add("pool", lambda e, tf=tf, j=j, n=n: e.tensor_tensor(
                    out=xt[:, j, n * 512:(n + 1) * 512], in0=tf[:], in1=xt[:, j, n * 512:(n + 1) * 512],
                    op=ALU.add), [tft, xtok], [xtok])
        _store()


IN_SPLITS = (512, 128, 128, 128, 128, 128, 128, 24, 512, 512, 512, 512, 1024, 1024)
NEG = -30000.0


def hostA_weights(w_in, g):
    offs = np.cumsum([0] + list(IN_SPLITS))

    def col(i, a, b):
        return w_in[:, offs[i] + a: offs[i] + b]

    def swap(x):
        return np.concatenate([x[:, 64:], x[:, :64]], 1)

    q = [col(0, (g * 4 + h) * 64, (g * 4 + h + 1) * 64) for h in range(4)]
    kc, vc = col(1, g * 64, g * 64 + 64), col(2, g * 64, g * 64 + 64)
    ks, vs = col(3, g * 64, g * 64 + 64), col(4, g * 64, g * 64 + 64)
    kw, vw = col(5, g * 64, g * 64 + 64), col(6, g * 64, g * 64 + 64)
    gates = np.stack([w_in[:, offs[7] + br * 8 + g * 4 + h] for br in range(3) for h in range(4)], 1)
    rq = [col(8, (2 * g + h) * 128, (2 * g + h + 1) * 128) for h in range(2)]
    rk = [col(9, (2 * g + h) * 128, (2 * g + h + 1) * 128) for h in range(2)]
    rv = col(10, 2 * g * 128, (2 * g + 2) * 128)
    rg = col(11, 2 * g * 128, (2 * g + 2) * 128)
    z128 = np.zeros((1024, 128), np.float32)
    WS = np.concatenate([q[0], q[1], q[2], q[3], ks, ks, kw, kw, kc, vc, z128, z128, z128,
                         rq[0], swap(rq[0]), rq[1], swap(rq[1]), rk[0], swap(rk[0]), rk[1], swap(rk[1])], 1)
    WM = np.concatenate([rk[0], rk[1], rv, rg, np.zeros((1024, 256), np.float32),
                         vs, vw, gates, np.zeros((1024, 512 - 140), np.float32)], 1)
    return np.ascontiguousarray(WS, np.float32), np.ascontiguousarray(WM, np.float32)


def hostA_consts(g, S):
    import ml_dtypes
    bf = ml_dtypes.bfloat16
    c = {}
    c["ident"] = np.eye(128, dtype=np.float32)
    bd = np.zeros((128, 128), np.float32)
    bd[:64, :64] = 1
    bd[64:, 64:] = 1
    c["onesbd"] = bd.astype(bf)
    half = 64
    inv = (10000.0 ** (-np.arange(half, dtype=np.float32) / half)).astype(np.float32)
    pos = np.arange(S, dtype=np.float32)
    ang = (pos[:, None] * inv[None, :]).astype(np.float32)
    cos, sin = np.cos(ang.astype(np.float64)), np.sin(ang.astype(np.float64))
    cosT = np.concatenate([cos.T, cos.T], 0)
    sinsT = np.concatenate([-sin.T, sin.T], 0)
    ksc = 128.0 ** -0.5
    c["ropeq"] = np.stack([cosT, sinsT], 1).astype(np.float32)
    c["ropek"] = (np.stack([cosT, sinsT], 1) * ksc).astype(np.float32)
    hh = np.array([2 * g, 2 * g + 1], np.float64)
    gamma = 1.0 - 2.0 ** (-5.0 - hh)
    lg = np.log(gamma)
    n = np.arange(128, dtype=np.float64)
    xi = np.exp(lg[:, None] * (n + 1.0))
    zeta = np.exp(lg[:, None] * (127.0 - n))
    c["xi"] = np.broadcast_to(np.tile(xi, (1, 4))[None], (128, 2, 512)).astype(np.float32).copy()
    zt = zeta[:, np.arange(S) % 128]
    c["ctk"] = (cos[:, None, :] * zt.T[:, :, None] * ksc).astype(np.float32)
    c["stk"] = (sin[:, None, :] * zt.T[:, :, None] * ksc).astype(np.float32)
    diff = n[None, :] - n[:, None]
    dec = np.where(diff[None] >= 0, np.exp(lg[:, None, None] * np.maximum(diff[None], 0)), 0.0)
    c["decayT"] = np.ascontiguousarray(dec.transpose(1, 0, 2)).astype(np.float32)
    c["gch"] = np.broadcast_to(np.exp(lg * 128.0)[None], (128, 2)).astype(np.float32).copy()
    kk, qq = np.arange(128)[:, None], np.arange(128)[None, :]
    c["triT"] = np.tile(np.where(kk <= qq, 0.0, NEG), (1, 4)).astype(bf)
    c["triU"] = np.tile(np.where(kk > qq, 0.0, NEG), (1, 4)).astype(bf)
    r = np.arange(16)[None, :, None]
    ql, il = np.arange(128)[:, None, None], np.arange(128)[None, None, :]
    c["cmaskQ"] = np.where(128 * r + ql - 16 * il - 31 >= 0, 0.0, NEG).astype(bf)
    c["cmaskT"] = np.ascontiguousarray(np.transpose(np.where(128 * r + ql - 16 * il - 31 >= 0, 0.0, NEG), (2, 1, 0))).astype(bf)
    c["wexp"] = (np.arange(S)[None, :] // 64 == np.arange(128)[:, None]).astype(bf)
    return c


A_CONST_SHAPES = lambda S: {
    "ident": ([128, 128], F32), "onesbd": ([128, 128], BF16), "ropeq": ([128, 2, S], F32),
    "ropek": ([128, 2, S], F32), "xi": ([128, 2, 512], F32), "ctk": ([S, 2, 64], F32), "stk": ([S, 2, 64], F32),
    "decayT": ([128, 2, 128], F32), "gch": ([128, 2], F32), "triT": ([128, 512], BF16), "triU": ([128, 512], BF16),
    "cmaskQ": ([128, 16, 128], BF16), "cmaskT": ([128, 16, 128], BF16), "wexp": ([128, S], BF16)}


def hostA_params(z, L, g):
    p = {}

    def gl(v):
        return np.ascontiguousarray(v.reshape(8, 128).T)
    qg, kg = z["nsa_q_norm"][L], z["nsa_k_norm"][L]
    p["gainsA"] = np.concatenate([gl(z["norm_mix"][L]), np.tile(qg, 2)[:, None], np.tile(kg, 2)[:, None]], 1).astype(np.float32)
    p["posT"] = np.ascontiguousarray(np.concatenate([z["cmp_pos_k"][L].T, z["cmp_pos_v"][L].T], 0), np.float32)
    w1k = z["cmp_w1_k"][L].reshape(32, 64, 256).transpose(1, 0, 2)
    w1v = z["cmp_w1_v"][L].reshape(32, 64, 256).transpose(1, 0, 2)
    zz = np.zeros_like(w1k)
    p["w1k"] = np.ascontiguousarray(np.concatenate([w1k, zz], 0), np.float32)
    p["w1v"] = np.ascontiguousarray(np.concatenate([zz, w1v], 0), np.float32)
    w2k = z["cmp_w2_k"][L].reshape(2, 128, 64).transpose(1, 0, 2)
    w2v = z["cmp_w2_v"][L].reshape(2, 128, 64).transpose(1, 0, 2)
    p["w2"] = np.ascontiguousarray(np.stack([np.concatenate([w2k, w2k], 2), np.concatenate([w2v, np.zeros_like(w2v)], 2)], 2), np.float32)
    return p


A_PARAM_SHAPES = {"gainsA": [128, 10], "posT": [128, 32], "w1k": [128, 32, 256], "w1v": [128, 32, 256],
                  "w2": [128, 2, 2, 128]}


def build_phaseA(SL=8192, parts=("ret", "nsa")):
    nc = bass.Bass("TRN2", target_bir_lowering=False)
    dt = nc.dram_tensor
    x = dt("x", [SL, 1024], F32, kind="ExternalInput").ap()
    WS_d = dt("WS", [1024, 2048], F32, kind="ExternalInput").ap()
    WM_d = dt("WM", [1024, 1536], F32, kind="ExternalInput").ap()
    cd = {k: dt(k, sh, ty, kind="ExternalInput").ap() for k, (sh, ty) in A_CONST_SHAPES(SL).items()}
    pd = {k: dt(k, sh, F32, kind="ExternalInput").ap() for k, sh in A_PARAM_SHAPES.items()}
    ao = dt("ao", [SL, 512], BF16, kind="ExternalOutput").ap()
    S = Sched(nc)
    C = make_pools(S, n_wbuf=3)
    emit_phaseA(C, SL, x, WS_d, WM_d, cd, pd, ao, parts)
    S.emit()
    S.close()
    return nc


def norm_evac(C, ps, pt, gains, gtok, gcol, onesbd, otok, tmps, dsts):
    S = C.S
    qf, qft, sq, sqt, rs, rst = tmps
    N = ps.shape[-1]
    S.add("act", lambda e: e.activation(out=qf[:, :N], in_=ps, func=AF.Copy), [pt], [qft])
    S.add("act", lambda e: e.activation(out=sq[:, :N], in_=ps, func=AF.Square), [pt], [sqt])
    p2, pt2 = next_ps(C)
    S.add("pe", lambda e: e.matmul(p2[:, :N], lhsT=onesbd[:], rhs=sq[:, :N], start=True, stop=True), [sqt, otok], [pt2])
    S.add("act", lambda e: e.activation(out=rs[:, :N], in_=p2[:, :N], func=AF.Ln, scale=1.0 / 64, bias=C.epsb[:, 0:1]), [pt2, C.epst], [rst])
    S.add("act", lambda e: e.activation(out=rs[:, :N], in_=rs[:, :N], func=AF.Exp, scale=-0.5), [rst], [rst])
    for (dst, lo, hi, tok) in dsts:
        S.add("dve", lambda e, dst=dst, lo=lo, hi=hi: e.scalar_tensor_tensor(
            out=dst, in0=qf[lo:hi, :N], scalar=gains[lo:hi, gcol:gcol + 1], in1=rs[lo:hi, :N],
            op0=ALU.mult, op1=ALU.mult), [qft, rst, gtok], accw=[tok])


def emit_phaseA(C, SL, x, WS_d, WM_d, cd, pd, ao, parts=("ret", "nsa"), xin_tok=None, ao_tok=None, xmap=None):
    C.xmap = xmap or (lambda t: t)
    S = C.S
    nchunk = SL // 512
    xin_tok = xin_tok or Tok()
    ao_tok = ao_tok or Tok()
    WSs = WSpec(S, "WSs", WS_d, 1024, 2048, "S")
    WMs = WSpec(S, "WMs", WM_d, 1024, 1536, "M")
    gains = S.sbuf("gainsA", [128, 10], F32)
    gtok = Tok()
    ident = S.sbuf("identA", [128, 128], F32)
    itok = Tok()
    C.epsb = S.sbuf("epsb", [128, 1], F32)
    C.epst = Tok()
    S.dma(gains[:], pd["gainsA"], writes=[gtok])
    S.dma(ident[:], cd["ident"], writes=[itok])
    S.add("dve", lambda e: e.memset(C.epsb[:], EPS), [], [C.epst])
    prep_weight(C, WSs)
    prep_weight(C, WMs)
    if "ret" in parts:
        with S.scope():
            emit_ret_pass(C, SL, x, xin_tok, WSs, WMs, cd, gains, gtok, ident, itok, ao, ao_tok)
    if "nsa" in parts:
        emit_nsa(C, SL, x, xin_tok, WSs, WMs, cd, pd, gains, gtok, ident, itok, ao, ao_tok)


def load_x_rms(C, x, xin_tok, t0, xt, xtok, junk, jtok, ss, sstok, rstd, rtok, gains, gtok, hT, htok, ident, itok):
    S = C.S
    xr0 = C.xmap(t0)
    S.dma(xt[:], x[xr0:xr0 + 512, :].rearrange("(j p) d -> p j d", p=128), reads=[xin_tok], writes=[xtok])
    for j in range(4):
        S.add("act", lambda e, j=j: e.activation(out=junk[:], in_=xt[:, j, :], func=AF.Square,
                                                 accum_out=ss[:, j:j + 1]), [xtok], [jtok, sstok])
    S.add("dve", lambda e: e.tensor_scalar(out=rstd[:], in0=ss[:], scalar1=1.0 / D_MODEL, scalar2=EPS,
                                           op0=ALU.mult, op1=ALU.add), [sstok], [rtok])
    S.add("act", lambda e: e.activation(out=rstd[:], in_=rstd[:], func=AF.Sqrt), [rtok], [rtok])
    S.add("dve", lambda e: e.reciprocal(out=rstd[:], in_=rstd[:]), [rtok], [rtok])
    for j in range(4):
        S.add("act", lambda e, j=j: e.activation(out=xt[:, j, :], in_=xt[:, j, :], func=AF.Copy,
                                                 scale=rstd[:, j:j + 1]), [xtok, rtok], [xtok])
    for kt in range(8):
        ps, pt = next_ps(C)
        for j in range(4):
            S.add("pe", lambda e, ps=ps, j=j, kt=kt: e.transpose(
                out=ps[:, j * 128:(j + 1) * 128], in_=xt[:, j, kt * 128:(kt + 1) * 128], identity=ident[:]),
                [xtok, itok], [pt])
        if kt % 2 == 0:
            S.add("dve", lambda e, ps=ps, kt=kt: e.tensor_scalar(
                out=hT[:, kt, :], in0=ps[:], scalar1=gains[:, kt:kt + 1], scalar2=None, op0=ALU.mult),
                [pt, gtok], accw=[htok])
        else:
            S.add("act", lambda e, ps=ps, kt=kt: e.activation(
                out=hT[:, kt, :], in_=ps[:], func=AF.Copy, scale=gains[:, kt:kt + 1]), [pt, gtok], accw=[htok])


def emit_ret_pass(C, SL, x, xin_tok, WSs, WMs, cd, gains, gtok, ident, itok, ao, ao_tok):
    S = C.S
    sb = S.sbuf
    xt = sb("r_xt", [128, 4, 1024], F32)
    junk = sb("r_junk", [128, 1024], F32)
    ss = sb("r_ss", [128, 4], F32)
    rstd = sb("r_rstd", [128, 4], F32)
    hT = sb("r_hT", [128, 8, 512], BF16)
    rq_tab = sb("r_rqtab", [128, 2, 512], F32)
    rk_tab = sb("r_rktab", [128, 2, 512], F32)
    ctk_t = sb("r_ctk", [128, 4, 2, 64], F32)
    stk_t = sb("r_stk", [128, 4, 2, 64], F32)
    xi_t = sb("r_xi", [128, 2, 512], F32)
    decT = sb("r_dec", [128, 2, 128], F32)
    gch = sb("r_gch", [128, 2], F32)
    t1 = [sb("r_t1_%d" % i, [128, 512], F32) for i in range(2)]
    t2 = [sb("r_t2_%d" % i, [128, 512], F32) for i in range(2)]
    tmpq = sb("r_tmpq", [128, 512], F32)
    QrT = sb("r_QrT", [128, 2, 512], BF16)
    QrxT = sb("r_QrxT", [128, 2, 512], BF16)
    KrT = sb("r_KrT", [128, 2, 512], BF16)
    Vr = sb("r_Vr", [128, 4, 256], BF16)
    kz = sb("r_kz", [128, 4, 2, 128], BF16)
    sg = sb("r_sg", [128, 4, 256], F32)
    tabcd = [sb("r_tabcd%d" % i, [128, 2, 64], F32) for i in range(4)]
    IT = [sb("r_IT%d" % i, [128, 128], BF16) for i in range(2)]
    yr = sb("r_yr", [128, 4, 2, 128], F32)
    ssr = sb("r_ssr", [128, 8], F32)
    rr = sb("r_rr", [128, 8], F32)
    ro = sb("r_ro", [128, 4, 256], BF16)
    R = sb("r_R", [128, 2, 128], F32)
    Rb = sb("r_Rb", [128, 2, 128], BF16)
    junkb = sb("r_junkb", [128, 128], BF16)
    (xtok, jtok, sstok, rtok, htok, rqt, rkt, ctt, stt, xit, dect, gcht, tmpqt, qrt, qrxt, krt, vrt, kzt, sgt,
     yrt, ssrt, rrt, rot, jbt) = [Tok() for _ in range(24)]
    t1t, t2t = [Tok(), Tok()], [Tok(), Tok()]
    tabt = [Tok() for _ in range(4)]
    ITt = [Tok(), Tok()]
    Rt, Rbt = [Tok(), Tok()], [Tok(), Tok()]
    S.dma(xi_t[:], cd["xi"], writes=[xit])
    S.dma(decT[:], cd["decayT"], writes=[dect])
    S.dma(gch[:], cd["gch"], writes=[gcht])
    for h in range(2):
        S.add("dve", lambda e, h=h: e.memset(R[:, h, :], 0.0), [], [Rt[h]])
        S.add("pool", lambda e, h=h: e.memset(Rb[:, h, :], 0.0), [], [Rbt[h]])
    nchunk = SL // 512
    tn = 0
    itn = 0
    for c in range(nchunk):
        t0 = c * 512
        load_x_rms(C, x, xin_tok, t0, xt, xtok, junk, jtok, ss, sstok, rstd, rtok, gains, gtok, hT, htok, ident, itok)
        S.dma(rq_tab[:], cd["ropeq"][:, :, t0:t0 + 512], writes=[rqt])
        S.dma(rk_tab[:], cd["ropek"][:, :, t0:t0 + 512], writes=[rkt])
        S.dma(ctk_t[:], cd["ctk"][t0:t0 + 512].rearrange("(j p) h i -> p j h i", p=128), writes=[ctt])
        S.dma(stk_t[:], cd["stk"][t0:t0 + 512].rearrange("(j p) h i -> p j h i", p=128), writes=[stt])
        if DBG_R <= 1:
            continue
        for u in (2, 3):
            wv, wt = load_unit(C, WSs, u)
            isq = (u == 2)
            tab, tabt_ = (rq_tab, rqt) if isq else (rk_tab, rkt)
            for h in range(2):
                a, at_ = t1[tn % 2], t1t[tn % 2]
                b, bt_ = t2[tn % 2], t2t[tn % 2]
                tn += 1
                for half, (dstb, dtok) in enumerate(((a, at_), (b, bt_))):
                    f = 2 * h + half
                    ps, pt = next_ps(C)
                    for kt in range(8):
                        S.add("pe", lambda e, ps=ps, wv=wv, f=f, kt=kt: e.matmul(
                            ps[:], lhsT=wv[:, f, kt, :], rhs=hT[:, kt, :], start=(kt == 0), stop=(kt == 7)),
                            [wt, htok], [pt])
                    S.add("dve", lambda e, ps=ps, dstb=dstb, half=half, tab=tab: e.tensor_tensor(
                        out=dstb[:], in0=ps[:], in1=tab[:, half, :], op=ALU.mult), [pt, tabt_], [dtok])
                if isq:
                    S.add("pool", lambda e, a=a, b=b: e.tensor_tensor(out=tmpq[:], in0=a[:], in1=b[:], op=ALU.add),
                          [at_, bt_], [tmpqt])
                    S.add("act", lambda e, h=h: e.activation(out=QrT[:, h, :], in_=tmpq[:], func=AF.Copy),
                          [tmpqt], accw=[qrt])
                    S.add("pool", lambda e, h=h: e.tensor_tensor(out=QrxT[:, h, :], in0=tmpq[:], in1=xi_t[:, h, :],
                                                                 op=ALU.mult), [tmpqt, xit], accw=[qrxt])
                else:
                    S.add("pool", lambda e, a=a, b=b, h=h: e.tensor_tensor(out=KrT[:, h, :], in0=a[:], in1=b[:],
                                                                           op=ALU.add), [at_, bt_], accw=[krt])
        if DBG_R <= 2:
            continue
        wv, wt = load_unit(C, WMs, 0)
        for j in range(4):
            ps, pt = next_ps(C)
            for kt in range(8):
                S.add("pe", lambda e, ps=ps, wv=wv, j=j, kt=kt: e.matmul(
                    ps[:], lhsT=hT[:, kt, j * 128:(j + 1) * 128], rhs=wv[:, kt, :], start=(kt == 0), stop=(kt == 7)),
                    [wt, htok], [pt])
            if DBG_SUB == 1:
                S.add("act", lambda e, ps=ps, j=j: e.activation(out=Vr[:, j, :], in_=ps[:, 256:512], func=AF.Copy),
                      [pt], accw=[vrt])
                continue
            pv = ps[:, 0:256].rearrange("p (h t i) -> p h t i", h=2, t=2)
            x1, x2 = pv[:, :, 0, :], pv[:, :, 1, :]
            kzv = kz[:, j].rearrange("p h (t i) -> p h t i", t=2)
            ta, tb, tc, td = tabcd
            S.add("dve", lambda e, x1=x1, j=j: e.tensor_tensor(out=ta[:], in0=x1, in1=ctk_t[:, j], op=ALU.mult),
                  [pt, ctt], [tabt[0]])
            S.add("dve", lambda e, x2=x2, j=j: e.tensor_tensor(out=tb[:], in0=x2, in1=stk_t[:, j], op=ALU.mult),
                  [pt, stt], [tabt[1]])
            S.add("dve", lambda e, x1=x1, j=j: e.tensor_tensor(out=tc[:], in0=x1, in1=stk_t[:, j], op=ALU.mult),
                  [pt, stt], [tabt[2]])
            S.add("dve", lambda e, x2=x2, j=j: e.tensor_tensor(out=td[:], in0=x2, in1=ctk_t[:, j], op=ALU.mult),
                  [pt, ctt], [tabt[3]])
            if DBG_SUB == 2:
                continue
            S.add("dve", lambda e, kzv=kzv: e.tensor_tensor(out=kzv[:, :, 0, :], in0=ta[:], in1=tb[:], op=ALU.subtract),
                  [tabt[0], tabt[1]] if DBG_SUB != 3 else [], accw=[kzt])
            S.add("dve", lambda e, kzv=kzv: e.tensor_tensor(out=kzv[:, :, 1, :], in0=tc[:], in1=td[:], op=ALU.add),
                  [tabt[2], tabt[3]] if DBG_SUB != 3 else [], accw=[kzt])
            S.add("act", lambda e, ps=ps, j=j: e.activation(out=Vr[:, j, :], in_=ps[:, 256:512], func=AF.Copy),
                  [pt], accw=[vrt])
        if DBG_R <= 3:
            continue
        wv, wt = load_unit(C, WMs, 1)
        for j in range(4):
            ps, pt = next_ps(C)
            for kt in range(8):
                S.add("pe", lambda e, ps=ps, wv=wv, j=j, kt=kt: e.matmul(
                    ps[:, 0:256], lhsT=hT[:, kt, j * 128:(j + 1) * 128], rhs=wv[:, kt, 0:256],
                    start=(kt == 0), stop=(kt == 7)), [wt, htok], [pt])
            S.add("act", lambda e, ps=ps, j=j: e.activation(out=sg[:, j, :], in_=ps[:, 0:256], func=AF.Silu),
                  [pt], accw=[sgt])
        if DBG_R <= 4:
            continue
        for j in range(4):
            js = slice(j * 128, (j + 1) * 128)
            for h in range(2):
                hs = slice(h * 128, (h + 1) * 128)
                psI, ptI = next_ps(C)
                S.add("pe", lambda e, psI=psI, h=h, js=js: e.matmul(
                    psI[:, 0:128], lhsT=KrT[:, h, js], rhs=QrT[:, h, js], start=True, stop=True), [krt, qrt], [ptI])
                it_, itt = IT[itn % 2], ITt[itn % 2]
                itn += 1
                S.add("dve", lambda e, psI=psI, it_=it_, h=h: e.tensor_tensor(
                    out=it_[:], in0=psI[:, 0:128], in1=decT[:, h, :], op=ALU.mult), [ptI, dect], [itt])
                psO, ptO = next_ps(C)
                S.add("pe", lambda e, psO=psO, it_=it_, j=j, hs=hs: e.matmul(
                    psO[:, 0:128], lhsT=it_[:], rhs=Vr[:, j, hs], start=True, stop=False), [itt, vrt], [ptO])
                S.add("pe", lambda e, psO=psO, h=h, js=js: e.matmul(
                    psO[:, 0:128], lhsT=QrxT[:, h, js], rhs=Rb[:, h, :], start=False, stop=True), [qrxt, Rbt[h]], [ptO])
                S.add("act", lambda e, psO=psO, j=j, h=h: e.activation(
                    out=junkb[:], in_=psO[:, 0:128], func=AF.Square, accum_out=ssr[:, j * 2 + h:j * 2 + h + 1]),
                    [ptO], [jbt], accw=[ssrt])
                S.add("dve", lambda e, psO=psO, j=j, h=h: e.tensor_copy(out=yr[:, j, h, :], in_=psO[:, 0:128]),
                      [ptO], accw=[yrt])
                psK, ptK = next_ps(C)
                S.add("pe", lambda e, psK=psK, j=j, h=h, hs=hs: e.matmul(
                    psK[:, 0:128], lhsT=kz[:, j, h, :], rhs=Vr[:, j, hs], start=True, stop=True), [kzt, vrt], [ptK])
                S.add("dve", lambda e, psK=psK, h=h: e.scalar_tensor_tensor(
                    out=R[:, h, :], in0=R[:, h, :], scalar=gch[:, h:h + 1], in1=psK[:, 0:128],
                    op0=ALU.mult, op1=ALU.add), [ptK, gcht, Rt[h]], [Rt[h]])
                S.add("pool", lambda e, h=h: e.tensor_copy(out=Rb[:, h, :], in_=R[:, h, :]), [Rt[h]], [Rbt[h]])
        if DBG_R <= 5:
            continue
        S.add("dve", lambda e: e.tensor_scalar(out=rr[:], in0=ssr[:], scalar1=1.0 / 128, scalar2=EPS,
                                               op0=ALU.mult, op1=ALU.add), [ssrt], [rrt])
        S.add("act", lambda e: e.activation(out=rr[:], in_=rr[:], func=AF.Sqrt), [rrt], [rrt])
        S.add("dve", lambda e: e.reciprocal(out=rr[:], in_=rr[:]), [rrt], [rrt])
        for j in range(4):
            for h in range(2):
                hs = slice(h * 128, (h + 1) * 128)
                S.add("dve", lambda e, j=j, h=h, hs=hs: e.scalar_tensor_tensor(
                    out=ro[:, j, hs], in0=yr[:, j, h, :], scalar=rr[:, j * 2 + h:j * 2 + h + 1], in1=sg[:, j, hs],
                    op0=ALU.mult, op1=ALU.mult), [yrt, rrt, sgt], accw=[rot])
        S.dma(ao[t0:t0 + 512, 256:512].rearrange("(j p) d -> p j d", p=128), ro[:], reads=[rot], accw=[ao_tok], q="act")


HORD = (0, 2, 1, 3)


def emit_nsa(C, SL, x, xin_tok, WSs, WMs, cd, pd, gains, gtok, ident, itok, ao, ao_tok):
    S = C.S
    sb = S.sbuf
    NT = SL // 128
    nb = SL // 16
    assert nb <= 512
    NCT = max(1, nb // 128)
    QT = sb("n_QT", [128, 2, SL], BF16)
    Kslo, Kshi = sb("n_Kslo", [128, SL], BF16), sb("n_Kshi", [128, SL], BF16)
    Kwlo, Kwhi = sb("n_Kwlo", [128, SL], BF16), sb("n_Kwhi", [128, SL], BF16)
    V1 = sb("n_V1", [128, NT, 2, 65], BF16)
    Gt = sb("n_Gt", [128, NT, 12], F32)
    kclo, kchi = sb("n_kclo", [128, 512], BF16), sb("n_kchi", [128, 512], BF16)
    Vc1 = sb("n_Vc1", [128, 4, 65], BF16)
    onesbd = sb("n_onesbd", [128, 128], BF16)
    identb = sb("n_identb", [128, 128], BF16)
    qf = sb("n_qf", [128, 512], F32)
    sq = sb("n_sq", [128, 512], BF16)
    rs = sb("n_rs", [128, 512], F32)
    qtok, kst, kwt, kcvt, v1t, gtt, kct, vct, onest, ibt, qft, sqt, rst = [Tok() for _ in range(13)]
    tmps = (qf, qft, sq, sqt, rs, rst)
    S.dma(onesbd[:], cd["onesbd"], writes=[onest])
    S.add("dve", lambda e: e.tensor_copy(out=identb[:], in_=ident[:]), [itok], [ibt])
    for (t_, tk) in ((Kslo, kst), (Kshi, kst), (Kwlo, kwt), (Kwhi, kwt), (kclo, kct), (kchi, kct)):
        S.add("pool", lambda e, t_=t_: e.memset(t_[:], 0.0), [], [tk])
    S.add("pool", lambda e: e.memset(V1[:], 1.0), [], [v1t])
    S.add("pool", lambda e: e.memset(Vc1[:], 0.0), [], [vct])
    S.add("pool", lambda e: e.memset(Vc1[:, :, 64:65], 1.0), [vct], [vct])
    kc_scope = S.scope()
    kc_scope.__enter__()
    KcVcT = sb("n_KcVcT", [128, SL + 16], BF16)
    with S.scope():
        xt = sb("n_xt", [128, 4, 1024], F32)
        junk = sb("n_junk", [128, 1024], F32)
        ss = sb("n_ss", [128, 4], F32)
        rstd = sb("n_rstd", [128, 4], F32)
        hT = sb("n_hT", [128, 8, 512], BF16)
        xtok, jtok, sstok, rtok, htok = [Tok() for _ in range(5)]
        for c in range(SL // 512):
            t0 = c * 512
            cs = slice(t0, t0 + 512)
            load_x_rms(C, x, xin_tok, t0, xt, xtok, junk, jtok, ss, sstok, rstd, rtok, gains, gtok, hT, htok, ident, itok)
            for u in (0, 1):
                wv, wt = load_unit(C, WSs, u)
                for f in range(4 if u == 0 else 1):
                    ft = u * 4 + f
                    ps, pt = next_ps(C)
                    for kt in range(8):
                        S.add("pe", lambda e, ps=ps, wv=wv, f=f, kt=kt: e.matmul(
                            ps[:], lhsT=wv[:, f, kt, :], rhs=hT[:, kt, :], start=(kt == 0), stop=(kt == 7)),
                            [wt, htok], [pt])
                    if ft < 2:
                        norm_evac(C, ps[:], pt, gains, gtok, 8, onesbd, onest, tmps, [(QT[:, ft, cs], 0, 128, qtok)])
                    elif ft == 2:
                        norm_evac(C, ps[:], pt, gains, gtok, 9, onesbd, onest, tmps,
                                  [(Kslo[0:64, cs], 0, 64, kst), (Kshi[64:128, cs], 64, 128, kst)])
                    elif ft == 3:
                        norm_evac(C, ps[:], pt, gains, gtok, 9, onesbd, onest, tmps,
                                  [(Kwlo[0:64, cs], 0, 64, kwt), (Kwhi[64:128, cs], 64, 128, kwt)])
                    else:
                        S.add("act", lambda e, ps=ps, cs=cs: e.activation(out=KcVcT[:, cs], in_=ps[:], func=AF.Copy),
                              [pt], accw=[kcvt])
            wv, wt = load_unit(C, WMs, 2)
            for j in range(4):
                tile_i = c * 4 + j
                ps, pt = next_ps(C)
                for kt in range(8):
                    S.add("pe", lambda e, ps=ps, wv=wv, j=j, kt=kt: e.matmul(
                        ps[:, 0:140], lhsT=hT[:, kt, j * 128:(j + 1) * 128], rhs=wv[:, kt, 0:140],
                        start=(kt == 0), stop=(kt == 7)), [wt, htok], [pt])
                S.add("dve", lambda e, ps=ps, tile_i=tile_i: e.tensor_copy(
                    out=V1[:, tile_i, :, 0:64], in_=ps[:, 0:128].rearrange("p (b d) -> p b d", b=2)), [pt], accw=[v1t])
                S.add("act", lambda e, ps=ps, tile_i=tile_i: e.activation(
                    out=Gt[:, tile_i, :], in_=ps[:, 128:140], func=AF.Sigmoid), [pt], accw=[gtt])
    with S.scope():
        W1b = sb("n_W1b", [128, 32, 256], BF16)
        posT = sb("n_posT", [128, 32], F32)
        w2f = sb("n_w2f", [128, 2, 2, 128], F32)
        w2b = sb("n_w2b", [128, 2, 2, 128], BF16)
        zr = [sb("n_zr%d" % i, [128, 512], BF16) for i in range(4)]
        zrt = [Tok() for _ in range(4)]
        GT = sb("n_GT", [128, 4, 512], BF16)
        ga = sb("n_ga", [128, 512], F32)
        gb = sb("n_gb", [128, 512], F32)
        w1t, post, w2t, w2bt, GTt, gat, gbt = [Tok() for _ in range(7)]
        S.dma(posT[:], pd["posT"], writes=[post])
        S.dma(w2f[:], pd["w2"], writes=[w2t])
        S.add("dve", lambda e: e.tensor_copy(out=w2b[:], in_=w2f[:]), [w2t], [w2bt])
        S.add("dve", lambda e: e.tensor_copy(out=KcVcT[:, SL:SL + 16], in_=KcVcT[:, SL - 1:SL].to_broadcast([128, 16])),
              [kcvt], [kcvt])
        accs = [next_ps(C) for _ in range(4)]
        zn = 0
        for kv, srcw in enumerate((pd["w1k"], pd["w1v"])):
            for r0 in range(0, 32, 2):
                i = C.sn % 3
                C.sn += 1
                st, stt = C.stg[i], C.stgtok[i]
                S.dma(st[:, :512], srcw[:, r0:r0 + 2, :].rearrange("p r h -> p (r h)"), writes=[stt])
                eng = ("dve", "pool")[C.cast_rr % 2]
                C.cast_rr += 1
                S.add(eng, lambda e, st=st, r0=r0: e.tensor_copy(
                    out=W1b[:, r0:r0 + 2, :].rearrange("p r h -> p (r h)"), in_=st[:, :512]), [stt], accw=[w1t])
            for r in range(32):
                z, zt = zr[zn % 4], zrt[zn % 4]
                zn += 1
                if r < 16:
                    src = KcVcT[:, 0:16 * nb].rearrange("p (i s) -> p i s", s=16)[:, :, r]
                else:
                    src = KcVcT[:, 16:16 + 16 * nb].rearrange("p (i s) -> p i s", s=16)[:, :, r - 16]
                eng = ("dve", "pool")[r % 2]
                S.add(eng, lambda e, z=z, src=src, r=r: e.tensor_scalar(
                    out=z[:, :nb], in0=src, scalar1=posT[:, r:r + 1], scalar2=None, op0=ALU.add), [kcvt, post], [zt])
                for hid in range(2):
                    ps, pt = accs[kv * 2 + hid]
                    S.add("pe", lambda e, ps=ps, hid=hid, r=r, z=z: e.matmul(
                        ps[:, :nb], lhsT=W1b[:, r, hid * 128:(hid + 1) * 128], rhs=z[:, :nb],
                        start=(r == 0), stop=(r == 31)), [w1t, zt], [pt])
        for a in range(4):
            ps, pt = accs[a]
            S.add("act", lambda e, ps=ps: e.activation(out=ga[:, :nb], in_=ps[:, :nb], func=AF.Square), [pt], [gat])
            S.add("dve", lambda e: e.tensor_scalar(out=ga[:, :nb], in0=ga[:, :nb], scalar1=0.044715, scalar2=1.0,
                                                   op0=ALU.mult, op1=ALU.add), [gat], [gat])
            S.add("dve", lambda e, ps=ps: e.tensor_tensor(out=ga[:, :nb], in0=ga[:, :nb], in1=ps[:, :nb], op=ALU.mult),
                  [gat, pt], [gat])
            S.add("act", lambda e: e.activation(out=gb[:, :nb], in_=ga[:, :nb], func=AF.Sigmoid, scale=1.5957691216057308),
                  [gat], [gbt])
            S.add("dve", lambda e, ps=ps, a=a: e.tensor_tensor(out=GT[:, a, :nb], in0=gb[:, :nb], in1=ps[:, :nb],
                                                               op=ALU.mult), [gbt, pt], accw=[GTt])
        ps, pt = next_ps(C)
        for t in range(2):
            S.add("pe", lambda e, ps=ps, t=t: e.matmul(ps[:, :nb], lhsT=w2b[:, t, 0, :], rhs=GT[:, t, :nb],
                                                       start=(t == 0), stop=(t == 1)), [w2bt, GTt], [pt])
        norm_evac(C, ps[:, :nb], pt, gains, gtok, 9, onesbd, onest, tmps,
                  [(kclo[0:64, :nb], 0, 64, kct), (kchi[64:128, :nb], 64, 128, kct)])
        for ct in range(NCT):
            ps, pt = next_ps(C)
            wdt = min(128, nb)
            for t in range(2):
                S.add("pe", lambda e, ps=ps, t=t, ct=ct, wdt=wdt: e.matmul(
                    ps[:wdt, 0:64], lhsT=GT[:, 2 + t, ct * 128:ct * 128 + wdt], rhs=w2b[:, t, 1, 0:64],
                    start=(t == 0), stop=(t == 1)), [w2bt, GTt], [pt])
            S.add("dve", lambda e, ps=ps, ct=ct, wdt=wdt: e.tensor_copy(out=Vc1[:wdt, ct, 0:64], in_=ps[:wdt, 0:64]),
                  [pt], accw=[vct])
    kc_scope.__exit__(None, None, None)
    with S.scope():
        wexp = sb("n_wexp", [128, SL], BF16)
        triT4 = sb("n_triT4", [128, 512], BF16)
        triU4 = sb("n_triU4", [128, 512], BF16)
        cmQ = sb("n_cmQ", [128, 16, 128], BF16)
        cmT = sb("n_cmT", [128, 16, 128], BF16)
        wet, trt, trut, cmqt, cmtt = [Tok() for _ in range(5)]
        S.dma(wexp[:], cd["wexp"], writes=[wet])
        S.dma(triT4[:], cd["triT"], writes=[trt])
        S.dma(triU4[:], cd["triU"], writes=[trut])
        S.dma(cmQ[:], cd["cmaskQ"], writes=[cmqt])
        S.dma(cmT[:], cd["cmaskT"], writes=[cmtt])
        E4 = sb("n_E4", [128, 4, 512], F32)
        rsum = sb("n_rsum", [128, 4], F32)
        rinv = sb("n_rinv", [128, 4], F32)
        imp = sb("n_imp", [128, 512], F32)
        ib = sb("n_ib", [128, 128], F32)
        sc = sb("n_sc", [128, 128], F32)
        sc2 = sb("n_sc2", [128, 128], F32)
        m8a = sb("n_m8a", [128, 8], F32)
        m8b = sb("n_m8b", [128, 8], F32)
        nmf = sb("n_nmf", [128, 128], F32)
        nmb = sb("n_nmb", [128, 128], BF16)
        nmT4 = sb("n_nmT4", [128, 512], BF16)
        PT = [sb("n_PT%d" % i, [128, 512], BF16) for i in range(3)]
        PTt = [Tok() for _ in range(3)]
        den = sb("n_den", [128, 4], F32)
        coef = sb("n_coef", [128, 4], F32)
        acc = sb("n_acc", [128, 4, 64], F32)
        ob = [sb("n_ob%d" % i, [128, 256], BF16) for i in range(2)]
        obt = [Tok(), Tok()]
        e4t, rsumt, rinvt, impt, ibt_, sct, sc2t, m8at, m8bt, nmft, nmbt, nmTt, dent, coeft, acct = [Tok() for _ in range(15)]
        npool = len(C.ps)
        Obank = [(C.ps[npool - 2], C.pstok[npool - 2]), (C.ps[npool - 1], C.pstok[npool - 1])]
        C.ps_active = npool - 2
        on = 0
        ptn = 0

        def att_branch(tiles, br, qt, first_branch):
            nonlocal on, ptn
            qs = slice(qt * 128, (qt + 1) * 128)
            O, Ot = Obank[on % 2]
            on += 1
            nt = len(tiles)
            def scores(idx):
                klo, khi, ktoks, v, vtok, masks = tiles[idx]
                psT, ptT = next_ps(C)
                first = True
                for (ml, mlt, mr, mrt, wide) in masks:
                    if wide:
                        S.add("pe", lambda e, psT=psT, ml=ml, mr=mr, first=first: e.matmul(
                            psT[:, 0:512], lhsT=ml, rhs=mr, start=first, stop=False, skip_group_check=True),
                            [mlt, mrt], [ptT])
                        first = False
                    else:
                        for cb in range(4):
                            S.add("pe", lambda e, psT=psT, ml=ml, mr=mr, cb=cb, first=first: e.matmul(
                                psT[:, cb * 128:(cb + 1) * 128], lhsT=ml, rhs=mr, start=first, stop=False,
                                skip_group_check=True), [mlt, mrt], [ptT])
                            first = False
                S.add("pe", lambda e, psT=psT, klo=klo, qs=qs, first=first: e.matmul(
                    psT[:, 0:256], lhsT=klo, rhs=QT[:, :, qs], start=first, stop=False, skip_group_check=True),
                    [ktoks, qtok], [ptT])
                S.add("pe", lambda e, psT=psT, khi=khi, qs=qs: e.matmul(
                    psT[:, 256:512], lhsT=khi, rhs=QT[:, :, qs], start=False, stop=True, skip_group_check=True),
                    [ktoks, qtok], [ptT])
                return psT, ptT

            pend = scores(0)
            for idx in range(nt):
                psT, ptT = pend
                if idx + 1 < nt:
                    pend = scores(idx + 1)
                v, vtok = tiles[idx][3], tiles[idx][4]
                P, Pt_ = PT[ptn % 3], PTt[ptn % 3]
                ptn += 1
                S.add("act", lambda e, psT=psT, P=P: e.activation(out=P[:], in_=psT[:], func=AF.Exp, scale=0.125),
                      [ptT], [Pt_])
                for cb in range(4):
                    S.add("pe", lambda e, O=O, P=P, v=v, cb=cb, idx=idx: e.matmul(
                        O[:, cb * 65:(cb + 1) * 65], lhsT=P[:, cb * 128:(cb + 1) * 128], rhs=v,
                        start=(idx == 0 and cb == 0), stop=(idx == nt - 1), skip_group_check=True), [Pt_, vtok], [Ot])
            Ov = O[:, 0:260].rearrange("p (c d) -> p c d", c=4)
            S.add("dve", lambda e, Ov=Ov: e.tensor_scalar(out=den[:], in0=Ov[:, :, 64], scalar1=1e-30, scalar2=None,
                                                          op0=ALU.max), [Ot], [dent])
            S.add("dve", lambda e: e.reciprocal(out=den[:], in_=den[:]), [dent], [dent])
            gv = Gt[:, qt, br * 4:(br + 1) * 4].rearrange("p (a b) -> p b a", a=2)
            S.add("dve", lambda e, gv=gv: e.tensor_tensor(out=coef[:].rearrange("p (b a) -> p b a", b=2),
                                                          in0=den[:].rearrange("p (b a) -> p b a", b=2), in1=gv,
                                                          op=ALU.mult), [dent, gtt], [coeft])
            for cb in range(4):
                h = HORD[cb]
                if first_branch:
                    S.add("dve", lambda e, Ov=Ov, cb=cb, h=h: e.tensor_scalar(
                        out=acc[:, h, :], in0=Ov[:, cb, 0:64], scalar1=coef[:, cb:cb + 1], scalar2=None,
                        op0=ALU.mult), [Ot, coeft], [acct])
                else:
                    S.add("dve", lambda e, Ov=Ov, cb=cb, h=h: e.scalar_tensor_tensor(
                        out=acc[:, h, :], in0=Ov[:, cb, 0:64], scalar=coef[:, cb:cb + 1], in1=acc[:, h, :],
                        op0=ALU.mult, op1=ALU.add), [Ot, coeft, acct], [acct])

        for qt in range(NT):
            bg = getattr(C, "bg", None)
            if bg is not None:
                for _ in range(C.bg_per_tile):
                    next(bg, None)
            qs = slice(qt * 128, (qt + 1) * 128)
            ctl = (8 * qt + 6) // 128
            ncol = 128 * (ctl + 1)
            r16 = qt % 16
            for cb, (p, Kc) in enumerate(((0, kclo), (1, kclo), (0, kchi), (1, kchi))):
                psS, ptS = next_ps(C)
                first = True
                if ctl > 0:
                    S.add("pe", lambda e, psS=psS, p=p, Kc=Kc, qs=qs, ctl=ctl: e.matmul(
                        psS[:, 0:ctl * 128], lhsT=QT[:, p, qs], rhs=Kc[:, 0:ctl * 128], start=True, stop=False,
                        skip_group_check=True), [qtok, kct], [ptS])
                    first = False
                S.add("pe", lambda e, psS=psS, ctl=ctl, ncol=ncol, r16=r16, first=first: e.matmul(
                    psS[:, ctl * 128:ncol], lhsT=identb[:], rhs=cmQ[:, r16, :], start=first, stop=False,
                    skip_group_check=True), [ibt, cmqt], [ptS])
                S.add("pe", lambda e, psS=psS, p=p, Kc=Kc, qs=qs, ctl=ctl, ncol=ncol: e.matmul(
                    psS[:, ctl * 128:ncol], lhsT=QT[:, p, qs], rhs=Kc[:, ctl * 128:ncol], start=False, stop=True,
                    skip_group_check=True), [qtok, kct], [ptS])
                S.add("act", lambda e, psS=psS, cb=cb, ncol=ncol: e.activation(
                    out=E4[:, cb, :ncol], in_=psS[:, :ncol], func=AF.Exp, scale=0.125, accum_out=rsum[:, cb:cb + 1]),
                    [ptS], accw=[e4t, rsumt])
            S.add("dve", lambda e: e.tensor_scalar(out=rinv[:], in0=rsum[:], scalar1=1e-30, scalar2=None, op0=ALU.max),
                  [rsumt], [rinvt])
            S.add("dve", lambda e: e.reciprocal(out=rinv[:], in_=rinv[:]), [rinvt], [rinvt])
            S.add("dve", lambda e, ncol=ncol: e.tensor_scalar(out=imp[:, :ncol], in0=E4[:, 0, :ncol], scalar1=rinv[:, 0:1],
                                                             scalar2=None, op0=ALU.mult), [e4t, rinvt], [impt])
            for cb in range(1, 4):
                S.add("dve", lambda e, cb=cb, ncol=ncol: e.scalar_tensor_tensor(
                    out=imp[:, :ncol], in0=E4[:, cb, :ncol], scalar=rinv[:, cb:cb + 1], in1=imp[:, :ncol],
                    op0=ALU.mult, op1=ALU.add), [e4t, rinvt, impt], [impt])
            nblk = ncol // 4
            S.add("dve", lambda e, ncol=ncol, nblk=nblk: e.tensor_reduce(
                out=ib[:, :nblk], in_=imp[:, :ncol].rearrange("p (j r) -> p j r", r=4), axis=AX.X, op=ALU.add),
                [impt], [ibt_])
            S.add("dve", lambda e, nblk=nblk: e.tensor_tensor(
                out=ib[:, 1:nblk], in0=ib[:, 1:nblk],
                in1=imp[:, 0:4 * (nblk - 1)].rearrange("p (j r) -> p j r", r=4)[:, :, 3], op=ALU.add),
                [impt, ibt_], [ibt_])
            S.add("pool", lambda e: e.memset(sc[:], -1e30), [], [sct])
            if qt > 0:
                S.add("dve", lambda e, qt=qt: e.tensor_copy(out=sc[:, 0:2 * qt], in_=ib[:, 0:2 * qt]), [ibt_, sct], [sct])
                S.add("dve", lambda e, qt=qt: e.memset(sc[0:64, 2 * qt - 1:2 * qt], 1e4), [sct], [sct])
            S.add("dve", lambda e: e.memset(sc[:, 0:1], 1e4), [sct], [sct])
            S.add("dve", lambda e, qt=qt: e.memset(sc[:, 2 * qt:2 * qt + 1], 1e4), [sct], [sct])
            S.add("dve", lambda e, qt=qt: e.memset(sc[64:128, 2 * qt + 1:2 * qt + 2], 1e4), [sct], [sct])
            S.add("dve", lambda e: e.max(out=m8a[:], in_=sc[:]), [sct], [m8at])
            S.add("dve", lambda e: e.match_replace(out=sc2[:], in_to_replace=m8a[:], in_values=sc[:], imm_value=-1e30),
                  [sct, m8at], [sc2t])
            S.add("dve", lambda e: e.max(out=m8b[:], in_=sc2[:]), [sc2t], [m8bt])
            S.add("dve", lambda e: e.tensor_scalar(out=nmf[:], in0=sc[:], scalar1=m8b[:, 7:8], scalar2=None,
                                                   op0=ALU.is_ge), [sct, m8bt], [nmft])
            S.add("dve", lambda e: e.tensor_scalar(out=nmb[:], in0=nmf[:], scalar1=-1.0, scalar2=-NEG,
                                                   op0=ALU.add, op1=ALU.mult), [nmft], [nmbt])
            tiles = []
            for ct in range(ctl + 1):
                cs = slice(ct * 128, (ct + 1) * 128)
                masks = [(identb[:], ibt, cmT[:, r16, :], cmtt, False)] if ct == ctl else []
                tiles.append((kclo[:, cs], kchi[:, cs], kct, Vc1[:, ct, :], vct, masks))
            att_branch(tiles, 0, qt, True)
            tiles = []
            for kt in range(max(0, qt - 4), qt + 1):
                ks_ = slice(kt * 128, (kt + 1) * 128)
                masks = []
                if kt == qt:
                    masks.append((identb[:], ibt, triT4[:], trt, True))
                if kt == qt - 4:
                    masks.append((identb[:], ibt, triU4[:], trut, True))
                tiles.append((Kwlo[:, ks_], Kwhi[:, ks_], kwt, V1[:, kt, 1, :], v1t, masks))
            att_branch(tiles, 2, qt, False)
            pb, pbt = next_psb(C)
            S.add("pe", lambda e, pb=pb: e.transpose(out=pb[:, 0:128], in_=nmb[:], identity=identb[:]), [nmbt, ibt], [pbt])
            S.add("dve", lambda e, pb=pb: e.tensor_copy(out=nmT4[:, 0:128], in_=pb[:, 0:128]), [pbt], [nmTt])
            S.add("dve", lambda e: e.tensor_copy(out=nmT4[:, 128:256], in_=nmT4[:, 0:128]), [nmTt], [nmTt])
            S.add("dve", lambda e: e.tensor_copy(out=nmT4[:, 256:512], in_=nmT4[:, 0:256]), [nmTt], [nmTt])
            tiles = []
            for kt in range(qt + 1):
                ks_ = slice(kt * 128, (kt + 1) * 128)
                masks = [(wexp[:, ks_], wet, nmT4[:], nmTt, True)]
                if kt == qt:
                    masks.append((identb[:], ibt, triT4[:], trt, True))
                tiles.append((Kslo[:, ks_], Kshi[:, ks_], kst, V1[:, kt, 0, :], v1t, masks))
            att_branch(tiles, 1, qt, False)
            o_, ot_ = ob[qt % 2], obt[qt % 2]
            S.add("act", lambda e, o_=o_: e.activation(out=o_[:], in_=acc[:].rearrange("p h d -> p (h d)"), func=AF.Copy),
                  [acct], [ot_])
            S.dma(ao[qs, 0:256], o_[:], reads=[ot_], accw=[ao_tok], q="act")
        C.ps_active = npool


from concourse.bass_utils import run_bass_kernel_spmd

_NC_CACHE = {}


def _get_nc(key, builder):
    if key not in _NC_CACHE:
        _NC_CACHE[key] = builder()
    return _NC_CACHE[key]


def kernel(**inputs):
    z = {k: np.asarray(v) for k, v in inputs.items()}
    x = np.ascontiguousarray(z["x"], np.float32)
    B, SL, D = x.shape
    depth = z["w_in"].shape[0]
    ncA = _get_nc("A", lambda: build_phaseA(SL))
    ncB = _get_nc("B", lambda: build_phaseB(SL // 2))
    constsA = [hostA_consts(g, SL) for g in range(2)]
    ident = np.eye(128, dtype=np.float32)
    for L in range(depth):
        insA = []
        wsm = [hostA_weights(z["w_in"][L], g) for g in range(2)]
        prm = [hostA_params(z, L, g) for g in range(2)]
        for c in range(8):
            b, g = c // 2, c % 2
            d = dict(x=np.ascontiguousarray(x[b]), WS=wsm[g][0], WM=wsm[g][1])
            d.update(constsA[g])
            d.update(prm[g])
            insA.append(d)
        resA = run_bass_kernel_spmd(ncA, insA, core_ids=list(range(8)))
        ao = [np.asarray(resA.results[c]["ao"]) for c in range(8)]

        def gl(v):
            return np.ascontiguousarray(np.asarray(v, np.float32).reshape(8, 128).T)
        gains = np.ascontiguousarray(np.concatenate(
            [gl(z["norm_mix"][L]), gl(z["norm_mlp"][L]), gl(z["norm_ple"][L])], 1), np.float32)
        w_merge = np.ascontiguousarray(z["w_in"][L][:, 3352:5400], np.float32)
        insB = []
        for c in range(8):
            b, hf = c // 2, c % 2
            sl = slice(hf * (SL // 2), (hf + 1) * (SL // 2))
            attn = np.concatenate([ao[2 * b][sl, :256], ao[2 * b + 1][sl, :256],
                                   ao[2 * b][sl, 256:], ao[2 * b + 1][sl, 256:]], 1)
            insB.append(dict(
                x=np.ascontiguousarray(x[b, sl]), attn=np.ascontiguousarray(attn),
                p=np.ascontiguousarray(z["p"][L, b, sl], np.float32), gains=gains, ident=ident,
                w_merge=w_merge, w_up_nsa=np.ascontiguousarray(z["w_up_nsa"][L], np.float32),
                w_up_ret=np.ascontiguousarray(z["w_up_ret"][L], np.float32),
                w_out=np.ascontiguousarray(z["w_out"][L], np.float32),
                w_ff1=np.ascontiguousarray(z["w_ff1"][L], np.float32),
                w_ff2=np.ascontiguousarray(z["w_ff2"][L], np.float32),
                w_gate=np.ascontiguousarray(z["w_ple_gate"][L], np.float32),
                w_ple=np.ascontiguousarray(z["w_ple"][L], np.float32)))
        resB = run_bass_kernel_spmd(ncB, insB, core_ids=list(range(8)))
        xn = np.empty_like(x)
        for c in range(8):
            b, hf = c // 2, c % 2
            xn[b, hf * (SL // 2):(hf + 1) * (SL // 2)] = np.asarray(resB.results[c]["xo"])
        x = xn
    return x


B_WNAMES = (("w_merge", 1024, 2048), ("w_up_nsa", 512, 1024), ("w_up_ret", 512, 1024), ("w_out", 1024, 1024),
            ("w_ff1", 1024, 4096), ("w_ff2", 4096, 1024), ("w_gate", 1024, 1024), ("w_ple", 256, 1024))
PAIR_GROUPS = [[0, 1], [2, 3], [4, 5], [6, 7]]


def build_fused(SL=8192, depth=2):
    nc = bass.Bass("TRN2", target_bir_lowering=False)
    dt = nc.dram_tensor
    T = SL // 2
    x_full = dt("x", [SL, 1024], F32, kind="ExternalInput").ap()
    xh = dt("xh", [T, 1024], F32, kind="ExternalInput").ap()
    hmask_d = dt("hmask", [128, 2], F32, kind="ExternalInput").ap()
    cd = {k: dt(k, sh, ty, kind="ExternalInput").ap() for k, (sh, ty) in A_CONST_SHAPES(SL).items()}
    WS_d, WM_d, pd, p_d, gB_d, wd = [], [], [], [], [], []
    for L in range(depth):
        WS_d.append(dt("WS%d" % L, [1024, 2048], F32, kind="ExternalInput").ap())
        WM_d.append(dt("WM%d" % L, [1024, 1536], F32, kind="ExternalInput").ap())
        pd.append({k: dt("%s%d" % (k, L), sh, F32, kind="ExternalInput").ap() for k, sh in A_PARAM_SHAPES.items()})
        p_d.append(dt("p%d" % L, [T, 256], F32, kind="ExternalInput").ap())
        gB_d.append(dt("gainsB%d" % L, [128, 24], F32, kind="ExternalInput").ap())
        wd.append({n: dt("%s%d" % (n, L), [K, N], F32, kind="ExternalInput").ap() for n, K, N in B_WNAMES})
    out = dt("xo", [T, 1024], F32, kind="ExternalOutput").ap()
    ao = [dt("ao%d" % L, [SL, 512], BF16, kind="Internal").ap() for L in range(depth)]
    aog = [dt("aog%d" % L, [2 * SL, 512], BF16, kind="Internal").ap() for L in range(depth)]
    xmid = [dt("xmid%d" % L, [T, 1024], F32, kind="Internal").ap() for L in range(depth - 1)]
    xg = [dt("xg%d" % L, [SL, 1024], F32, kind="Internal").ap() for L in range(depth - 1)]
    S = Sched(nc)
    C = make_pools(S, n_wbuf=3)
    xg_tok = None
    xmid_tok = None
    for L in range(depth):
        S.prefix = "L%dB_" % L
        WB = make_B_wspecs(S, wd[L])
        C.bg = prep_B_gen(C, WB)
        C.bg_per_tile = -(-212 // (SL // 128)) + 1
        S.prefix = "L%dA_" % L
        aot, aogt = Tok(), Tok()
        with S.scope():
            XK = min(512, T)
            xmap = None if L == 0 else (lambda t: 2 * ((t % T) // XK) * XK + (t // T) * XK + (t % T) % XK)
            emit_phaseA(C, SL, x_full if L == 0 else xg[L - 1], WS_d[L], WM_d[L], cd, pd[L], ao[L],
                        xin_tok=xg_tok, ao_tok=aot, xmap=xmap)
        for _ in C.bg:
            pass
        C.bg = None
        RK = min(2048, SL)
        for k in range(SL // RK):
            S.collective("AllGather", ao[L][k * RK:(k + 1) * RK, :].opt(), aog[L][2 * k * RK:2 * (k + 1) * RK, :].opt(),
                         PAIR_GROUPS, reads=[aot], accw=[aogt])
        S.prefix = "L%dB_" % L
        with S.scope():
            hm = S.sbuf("hm", [128, 2], F32)
            hmt = Tok()
            S.dma(hm[:], hmask_d, writes=[hmt])
            atAB = [S.sbuf("atAB%d" % i, [128, 4, 1024], BF16) for i in range(2)]
            atABt = [Tok(), Tok()]
            nw = len(C.wbuf)
            C.wbuf = C.wbuf + [S.sbuf("wbufx%d" % i, [128, WU_ELEMS], BF16) for i in range(1)]
            C.wtok = C.wtok + [Tok() for _ in range(1)]
            last = (L == depth - 1)
            xo_tok = Tok()
            emit_phaseB(C, T, xh if L == 0 else xmid[L - 1], aog[L], p_d[L], gB_d[L], cd["ident"], wd[L],
                        out if last else xmid[L], xin_tok=xmid_tok, attn_tok=aogt, xo_tok=xo_tok,
                        gathered=(SL, hm, hmt, atAB, atABt), W=WB)
            C.wbuf = C.wbuf[:nw]
            C.wtok = C.wtok[:nw]
        if not last:
            xmid_tok = xo_tok
            xg_tok = Tok()
            XK = min(512, T)
            for k in range(T // XK):
                S.collective("AllGather", xmid[L][k * XK:(k + 1) * XK, :].opt(),
                             xg[L][2 * k * XK:2 * (k + 1) * XK, :].opt(), PAIR_GROUPS, reads=[xo_tok], accw=[xg_tok])
    S.emit()
    S.close()
    return nc


def fused_inputs(z, SL, depth):
    import ml_dtypes
    x = np.ascontiguousarray(z["x"], np.float32)
    T = SL // 2
    consts = [hostA_consts(g, SL) for g in range(2)]

    def gl(v):
        return np.ascontiguousarray(np.asarray(v, np.float32).reshape(8, 128).T)
    per_layer = []
    for L in range(depth):
        d = {}
        d["wsm"] = [hostA_weights(z["w_in"][L], g) for g in range(2)]
        d["prm"] = [hostA_params(z, L, g) for g in range(2)]
        d["gainsB"] = np.ascontiguousarray(np.concatenate(
            [gl(z["norm_mix"][L]), gl(z["norm_mlp"][L]), gl(z["norm_ple"][L])], 1), np.float32)
        d["w"] = dict(
            w_merge=np.ascontiguousarray(z["w_in"][L][:, 3352:5400], np.float32),
            w_up_nsa=np.ascontiguousarray(z["w_up_nsa"][L], np.float32),
            w_up_ret=np.ascontiguousarray(z["w_up_ret"][L], np.float32),
            w_out=np.ascontiguousarray(z["w_out"][L], np.float32),
            w_ff1=np.ascontiguousarray(z["w_ff1"][L], np.float32),
            w_ff2=np.ascontiguousarray(z["w_ff2"][L], np.float32),
            w_gate=np.ascontiguousarray(z["w_ple_gate"][L], np.float32),
            w_ple=np.ascontiguousarray(z["w_ple"][L], np.float32))
        per_layer.append(d)
    ins = []
    for c in range(8):
        b, r = c // 2, c % 2
        sl = slice(r * T, (r + 1) * T)
        d = dict(x=np.ascontiguousarray(x[b, :SL]), xh=np.ascontiguousarray(x[b, sl]))
        hm = np.zeros((128, 2), np.float32)
        hm[:, r] = 1.0
        d["hmask"] = hm
        d.update(consts[r])
        for L in range(depth):
            pl = per_layer[L]
            d["WS%d" % L], d["WM%d" % L] = pl["wsm"][r]
            for k, v in pl["prm"][r].items():
                d["%s%d" % (k, L)] = v
            d["p%d" % L] = np.ascontiguousarray(z["p"][L, b, sl], np.float32)
            d["gainsB%d" % L] = pl["gainsB"]
            for k, v in pl["w"].items():
                d["%s%d" % (k, L)] = v
        ins.append(d)
    return ins


def kernel(**inputs):
    z = {k: np.asarray(v) for k, v in inputs.items()}
    B, SL, D = z["x"].shape
    depth = z["w_in"].shape[0]
    nc = _get_nc(("F", SL, depth), lambda: build_fused(SL, depth))
    ins = fused_inputs(z, SL, depth)
    res = run_bass_kernel_spmd(nc, ins, core_ids=list(range(8)))
    T = SL // 2
    out = np.empty((B, SL, D), np.float32)
    for c in range(8):
        b, r = c // 2, c % 2
        out[b, r * T:(r + 1) * T] = np.asarray(res.results[c]["xo"])
    return out
```

```python
from contextlib import ExitStack
import numpy as np
import concourse.bass as bass
import concourse.mybir as mybir

F32 = mybir.dt.float32
BF16 = mybir.dt.bfloat16
I32 = mybir.dt.int32
AF = mybir.ActivationFunctionType
ALU = mybir.AluOpType
AX = mybir.AxisListType

ENGS = ("pe", "act", "dve", "pool", "sp")
N_DMA_SEMS = 24


class Tok:
    __slots__ = ("lws", "rs", "base", "name", "excl", "accgrp")

    def __init__(self, name="", excl=False):
        self.excl = excl
        self.accgrp = False
        self.lws = []
        self.rs = []
        self.base = []
        self.name = name


class Op:
    __slots__ = ("eng", "fn", "deps", "dma", "idx", "sig", "dma_n", "cc")

    def __init__(self, eng, fn, deps, dma, idx):
        self.eng = eng
        self.fn = fn
        self.deps = deps
        self.dma = dma
        self.idx = idx
        self.sig = None
        self.dma_n = None
        self.cc = None


class _Scope:
    def __init__(self, S):
        self.S = S

    def __enter__(self):
        self.saved = self.S.stack
        self.S.stack = ExitStack()
        return self

    def __exit__(self, *a):
        self.S.barrier()
        self.S.stack.close()
        self.S.stack = self.saved
        return False


class Sched:
    def __init__(self, nc):
        self.nc = nc
        self.ops = {e: [] for e in ENGS}
        self.ndma = {e: 0 for e in ENGS}
        self.final_waits = []
        self.all_dma = []
        self.ncc = 0
        self.prefix = ""
        self.stack = ExitStack()

    def sbuf(self, name, shape, dtype):
        return self.stack.enter_context(self.nc.sbuf_tensor("sb_" + self.prefix + name, list(shape), dtype))

    def psum(self, name, shape, dtype):
        return self.stack.enter_context(self.nc.psum_tensor("pp_" + name, list(shape), dtype))

    def add(self, eng, fn, reads=(), writes=(), dma=False, accw=(), extra=()):
        deps = []
        seen = set()

        def push(d):
            if d is not None and d not in seen:
                seen.add(d)
                deps.append(d)

        for d in extra:
            push(d)
        for t in reads:
            for w in t.lws:
                push(w)
            if t.excl:
                for r in t.rs:
                    if r[0] != eng:
                        push(r)
        for t in writes:
            for w in t.lws:
                push(w)
            for r in t.rs:
                push(r)
        for t in accw:
            if t.rs or not t.lws or not t.accgrp:
                for w in t.lws:
                    push(w)
                for r in t.rs:
                    push(r)
            else:
                for d in t.base:
                    push(d)
        lst = self.ops[eng]
        op = Op(eng, fn, deps, dma, len(lst))
        if dma:
            op.dma_n = self.ndma[eng]
            self.ndma[eng] += 1
            self.all_dma.append((eng, op.idx))
        lst.append(op)
        me = (eng, op.idx)
        for t in reads:
            t.rs.append(me)
        for t in writes:
            t.lws = [me]
            t.rs = []
            t.base = []
            t.accgrp = False
        for t in accw:
            if t.rs or not t.lws or not t.accgrp:
                t.base = list(t.lws) + list(t.rs)
                t.lws = [me]
                t.rs = []
                t.accgrp = True
            else:
                t.lws.append(me)
        return op

    def collective(self, kind, src, dst, groups, reads=(), writes=(), accw=()):
        op = self.add("pool", lambda e: e.collective_compute(kind, ALU.bypass, replica_groups=groups,
                                                             ins=[src], outs=[dst]), reads, writes, accw=accw)
        self.ncc += 1
        op.cc = self.ncc
        return op

    def barrier(self):
        extra = list(self.all_dma)
        for e in ENGS:
            if self.ops[e]:
                extra.append((e, len(self.ops[e]) - 1))
        self.all_dma = []
        b0 = self.add("sp", lambda e: e.nop(), extra=extra)
        me = ("sp", b0.idx)
        for e in ("pe", "act", "dve", "pool"):
            self.add(e, lambda eng: eng.nop(), extra=[me])

    def dma(self, out, in_, reads=(), writes=(), q="sp", accw=(), **kw):
        return self.add(q, lambda e: e.dma_start(out=out, in_=in_, **kw), reads, writes, dma=True, accw=accw)

    def scope(self):
        return _Scope(self)

    def emit(self):
        nc = self.nc
        ops = self.ops
        needed = {e: set() for e in ENGS}
        waits = {e: [] for e in ENGS}
        for e in ENGS:
            maxw = {d: -1 for d in ENGS}
            dma_waited = set()
            for op in ops[e]:
                keep = []
                for (de, di) in op.deps:
                    dop = ops[de][di]
                    if dop.dma or dop.cc:
                        if (de, di) in dma_waited:
                            continue
                        dma_waited.add((de, di))
                        keep.append((de, di))
                    else:
                        if de == e and e == "pe":
                            continue
                        if de == e and di == op.idx:
                            continue
                        if di <= maxw[de]:
                            continue
                        maxw[de] = di
                        keep.append((de, di))
                        needed[de].add(di)
                waits[e].append(keep)
        for e in ENGS:
            c = 0
            for op in ops[e]:
                if (not op.dma) and (not op.cc) and op.idx in needed[e]:
                    c += 1
                    op.sig = c
        st = self.stack
        csem = {e: st.enter_context(nc.semaphore("c_" + e)) for e in ENGS}
        ccsem = st.enter_context(nc.semaphore("cc_sem"))
        dsem = {e: [st.enter_context(nc.semaphore("d_%s_%d" % (e, i))) for i in range(N_DMA_SEMS)]
                for e in ENGS if self.ndma[e] > 0}
        block = st.enter_context(nc.Block())

        def gen(e, eng):
            for op, keep in zip(ops[e], waits[e]):
                if op.dma:
                    n = op.dma_n
                    if n >= N_DMA_SEMS:
                        eng.wait_ge(dsem[e][n % N_DMA_SEMS], 16 * (n // N_DMA_SEMS))
                for (de, di) in keep:
                    dop = ops[de][di]
                    if dop.dma:
                        n = dop.dma_n
                        eng.wait_ge(dsem[de][n % N_DMA_SEMS], 16 * (n // N_DMA_SEMS + 1))
                    elif dop.cc:
                        eng.wait_ge(ccsem, dop.cc)
                    else:
                        eng.wait_ge(csem[de], dop.sig)
                ins = op.fn(eng)
                if op.dma:
                    n = op.dma_n
                    ins.then_inc(dsem[e][n % N_DMA_SEMS], 16)
                elif op.cc:
                    ins.then_inc(ccsem, 1)
                elif op.sig is not None:
                    ins.then_inc(csem[e], 1)
            if e == "pool" and self.ncc:
                eng.wait_ge(ccsem, self.ncc)
            nd = self.ndma[e]
            for i in range(min(nd, N_DMA_SEMS)):
                cnt = (nd - 1 - i) // N_DMA_SEMS + 1
                eng.wait_ge(dsem[e][i], 16 * cnt)

        @block.tensor
        def _(eng):
            gen("pe", eng)

        @block.scalar
        def _(eng):
            gen("act", eng)

        @block.vector
        def _(eng):
            gen("dve", eng)

        @block.gpsimd
        def _(eng):
            gen("pool", eng)

        @block.sync
        def _(eng):
            gen("sp", eng)

    def close(self):
        self.stack.close()


D_MODEL = 1024
EPS = 1e-6
WU_ELEMS = 4096


class WSpec:
    def __init__(self, S, name, w_ap, K, N, kind):
        self.name, self.K, self.N, self.kind = name, K, N, kind
        self.KT = K // 128
        self.w = w_ap
        nc = S.nc
        if kind == "S":
            assert N % 512 == 0
            self.nunits = N // 512
            self.uelems = 4 * self.KT * 128
        else:
            assert N % 512 == 0
            self.KTU = min(8, self.KT)
            self.nv = self.KT // self.KTU
            self.nunits = (N // 512) * self.nv
            self.uelems = self.KTU * 512
        assert self.uelems <= WU_ELEMS
        self.scr = nc.dram_tensor("scr_" + S.prefix + name, [self.nunits, 128, self.uelems], BF16, kind="Internal").ap()
        self.tok = Tok("scr_" + name)

    def unit_src(self, u):
        return self.scr[u]

    def view(self, buf):
        b = buf[:, : self.uelems]
        if self.kind == "S":
            return b.rearrange("p (f k c) -> p f k c", f=4, k=self.KT)
        return b.rearrange("p (k c) -> p k c", k=self.KTU)


class Ctx:
    pass


def make_pools(S, n_wbuf=5, n_ps=6):
    C = Ctx()
    C.S = S
    C.wbuf = [S.sbuf("wbuf%d" % i, [128, WU_ELEMS], BF16) for i in range(n_wbuf)]
    C.wtok = [Tok("wbuf%d" % i) for i in range(n_wbuf)]
    C.wn = 0
    C.ps = [S.psum("ps%d" % i, [128, 512], F32) for i in range(n_ps)]
    C.pstok = [Tok("ps%d" % i, excl=True) for i in range(n_ps)]
    C.pn = 0
    C.psb = [S.psum("psb%d" % i, [128, 1024], BF16) for i in range(2)]
    C.psbtok = [Tok("psb0", excl=True), Tok("psb1", excl=True)]
    C.pbn = 0
    C.stg = [S.sbuf("stg%d" % i, [128, 512], F32) for i in range(3)]
    C.stgtok = [Tok() for _ in range(3)]
    C.stgb = [S.sbuf("stgb%d" % i, [128, 512], BF16) for i in range(3)]
    C.stgbtok = [Tok() for _ in range(3)]
    C.sn = 0
    C.cast_rr = 0
    return C


def next_ps(C):
    i = C.pn % getattr(C, "ps_active", len(C.ps))
    C.pn += 1
    return C.ps[i], C.pstok[i]


def next_psb(C):
    i = C.pbn % 2
    C.pbn += 1
    return C.psb[i][:, 0:512], C.psbtok[i]


def load_unit(C, ws, u, q="sp"):
    i = C.wn % len(C.wbuf)
    C.wn += 1
    buf, tok = C.wbuf[i], C.wtok[i]
    C.S.dma(buf[:, : ws.uelems], ws.unit_src(u), reads=[ws.tok], writes=[tok], q=q)
    return ws.view(buf), tok


def prep_weight_gen(C, ws, q="act"):
    S = C.S
    K, N, KT = ws.K, ws.N, ws.KT
    for kt in range(KT):
        for c0 in range(0, N, 512):
            cw = min(512, N - c0)
            i = C.sn % 3
            C.sn += 1
            st, stt, sb, sbt = C.stg[i], C.stgtok[i], C.stgb[i], C.stgbtok[i]
            S.dma(st[:, :cw], ws.w[kt * 128:(kt + 1) * 128, c0:c0 + cw], writes=[stt])
            eng = ("dve", "pool")[C.cast_rr % 2] if q == "act" else "pool"
            C.cast_rr += 1
            S.add(eng, lambda e, sb=sb, st=st, cw=cw: e.tensor_copy(out=sb[:, :cw], in_=st[:, :cw]), [stt], [sbt])
            if ws.kind == "S":
                u0, nu = c0 // 512, cw // 512
                dst = ws.scr[u0:u0 + nu].rearrange("u p (f k c) -> p u f k c", f=4, k=KT)[:, :, :, kt, :]
                src = sb[:, :cw].rearrange("p (u f c) -> p u f c", u=nu, f=4)
                for uu in range(nu):
                    S.dma(dst[:, uu], src[:, uu], reads=[sbt], accw=[ws.tok], q=q)
            else:
                v, kk = kt // ws.KTU, kt % ws.KTU
                n0, nn = c0 // 512, cw // 512
                for n in range(nn):
                    u = (n0 + n) * ws.nv + v
                    dst = ws.scr[u].rearrange("p (k c) -> p k c", k=ws.KTU)[:, kk, :]
                    S.dma(dst, sb[:, n * 512:(n + 1) * 512], reads=[sbt], accw=[ws.tok], q=q)
            yield


def prep_weight(C, ws):
    for _ in prep_weight_gen(C, ws):
        pass


def rms_to_featmajor(C, xt, xtok, gains, gtok, gcol0, hT, htok, ident, itok, tmp):
    S = C.S
    ss, sstok, xs, xstok, junk, jtok, rstd, rtok = tmp
    for j in range(4):
        S.add("act", lambda e, j=j: e.activation(
            out=xs[:, j, :], in_=xt[:, j, :], func=AF.Square, accum_out=ss[:, j:j + 1]), [xtok], [xstok, sstok])
    S.add("dve", lambda e: e.tensor_scalar(out=rstd[:], in0=ss[:], scalar1=1.0 / D_MODEL, scalar2=EPS,
                                           op0=ALU.mult, op1=ALU.add), [sstok], [rtok])
    S.add("act", lambda e: e.activation(out=rstd[:], in_=rstd[:], func=AF.Sqrt), [rtok], [rtok])
    S.add("dve", lambda e: e.reciprocal(out=rstd[:], in_=rstd[:]), [rtok], [rtok])
    for j in range(4):
        S.add("act", lambda e, j=j: e.activation(out=xs[:, j, :], in_=xt[:, j, :], func=AF.Copy,
                                                 scale=rstd[:, j:j + 1]), [xtok, rtok], [xstok])
    for kt in range(8):
        ps, pt = next_ps(C)
        for j in range(4):
            S.add("pe", lambda e, ps=ps, j=j, kt=kt: e.transpose(
                out=ps[:, j * 128:(j + 1) * 128], in_=xs[:, j, kt * 128:(kt + 1) * 128], identity=ident[:]),
                [xstok, itok], [pt])
        if kt % 2 == 0:
            S.add("dve", lambda e, ps=ps, kt=kt: e.tensor_scalar(
                out=hT[:, kt, :], in0=ps[:], scalar1=gains[:, gcol0 + kt:gcol0 + kt + 1], scalar2=None,
                op0=ALU.mult), [pt, gtok], accw=[htok])
        else:
            S.add("act", lambda e, ps=ps, kt=kt: e.activation(
                out=hT[:, kt, :], in_=ps[:], func=AF.Copy, scale=gains[:, gcol0 + kt:gcol0 + kt + 1]),
                [pt, gtok], accw=[htok])


def build_phaseB(T=4096):
    nc = bass.Bass("TRN2", target_bir_lowering=False)
    dt = nc.dram_tensor
    x = dt("x", [T, 1024], F32, kind="ExternalInput").ap()
    attn = dt("attn", [T, 1024], BF16, kind="ExternalInput").ap()
    pin = dt("p", [T, 256], F32, kind="ExternalInput").ap()
    gains_d = dt("gains", [128, 24], F32, kind="ExternalInput").ap()
    ident_d = dt("ident", [128, 128], F32, kind="ExternalInput").ap()
    wd = {}
    for name, K, N in (("w_merge", 1024, 2048), ("w_up_nsa", 512, 1024), ("w_up_ret", 512, 1024),
                       ("w_out", 1024, 1024), ("w_ff1", 1024, 4096), ("w_ff2", 4096, 1024),
                       ("w_gate", 1024, 1024), ("w_ple", 256, 1024)):
        wd[name] = dt(name, [K, N], F32, kind="ExternalInput").ap()
    xo = dt("xo", [T, 1024], F32, kind="ExternalOutput").ap()
    S = Sched(nc)
    C = make_pools(S)
    emit_phaseB(C, T, x, attn, pin, gains_d, ident_d, wd, xo)
    S.emit()
    S.close()
    return nc


B_KINDS = {"w_merge": "S", "w_up_nsa": "S", "w_up_ret": "S", "w_out": "M", "w_ff1": "S", "w_ff2": "M",
           "w_gate": "M", "w_ple": "M"}


def make_B_wspecs(S, wd):
    W = {}
    for name, ap in wd.items():
        K, N = ap.shape
        W[name] = WSpec(S, name, ap, K, N, B_KINDS[name])
    return W


def prep_B_gen(C, W):
    for name in ("w_merge", "w_up_nsa", "w_up_ret", "w_out", "w_ff1", "w_ff2", "w_gate", "w_ple"):
        for _ in prep_weight_gen(C, W[name], q="sp"):
            yield


DBG_STAGE = 99
DBG_NQT = None
DBG_R = 99
DBG_SUB = 0
DBG_Q = "act"
DBG_PREP = True


def emit_phaseB(C, T, x, attn, pin, gains_d, ident_d, wd, xo, xin_tok=None, attn_tok=None, xo_tok=None,
                gathered=None, W=None):
    S = C.S
    kinds = {"w_merge": "S", "w_up_nsa": "S", "w_up_ret": "S", "w_out": "M", "w_ff1": "S", "w_ff2": "M",
             "w_gate": "M", "w_ple": "M"}
    preW = W is not None
    if not preW:
        W = make_B_wspecs(S, wd)
    gains = S.sbuf("gains", [128, 24], F32)
    gtok = Tok()
    ident = S.sbuf("ident", [128, 128], F32)
    identb = S.sbuf("identb", [128, 128], BF16)
    itok, ibtok = Tok(), Tok()
    S.dma(gains[:], gains_d, writes=[gtok])
    S.dma(ident[:], ident_d, writes=[itok])
    S.add("dve", lambda e: e.tensor_copy(out=identb[:], in_=ident[:]), [itok], [ibtok])
    for name in ("w_merge", "w_up_nsa", "w_up_ret", "w_out", "w_ff1", "w_ff2", "w_gate", "w_ple"):
        if DBG_PREP and not preW:
            prep_weight(C, W[name])
    xt = S.sbuf("xt", [128, 4, 1024], F32)
    at = S.sbuf("at", [128, 4, 1024], BF16)
    ptm = S.sbuf("ptm", [128, 4, 256], F32)
    xs = S.sbuf("xs", [128, 4, 1024], F32)
    junk = None
    ss = S.sbuf("ss", [128, 4], F32)
    rstd = S.sbuf("rstd", [128, 4], F32)
    hT = S.sbuf("hT", [128, 8, 512], BF16)
    aT = S.sbuf("aT", [128, 8, 512], BF16)
    sgT = S.sbuf("sgT", [128, 16, 512], BF16)
    mixT = S.sbuf("mixT", [128, 8, 512], BF16)
    uT = S.sbuf("uT", [128, 32, 512], BF16)
    pT = S.sbuf("pT", [128, 2, 512], BF16)
    tmpf = [S.sbuf("tmpf%d" % i, [128, 512], F32) for i in range(2)]
    tmpft = [Tok(), Tok()]
    gsb = S.sbuf("gsb", [128, 512], F32)
    xtok, atok, ptok, xstok, jtok, sstok, rtok = [Tok() for _ in range(7)]
    htok, aTtok, sgtok, mixtok, utok, pTtok, gsbtok = [Tok() for _ in range(7)]
    tmp = (ss, sstok, xs, xstok, junk, jtok, rstd, rtok)
    xin_tok = xin_tok or Tok()
    attn_tok = attn_tok or Tok()
    xo_tok = xo_tok or Tok()
    nchunk = T // 512
    tn = 0
    for c in range(nchunk):
        t0 = c * 512
        S.dma(xt[:], x[t0:t0 + 512, :].rearrange("(j p) d -> p j d", p=128), reads=[xin_tok], writes=[xtok])
        if gathered is None:
            S.dma(at[:], attn[t0:t0 + 512, :].rearrange("(j p) d -> p j d", p=128), reads=[attn_tok], writes=[atok])
        else:
            SLg, hm, hmt, atAB, atABt = gathered
            for hf in range(2):
                for g in range(2):
                    RKg = min(2048, SLg)
                    tk_ = hf * T + t0
                    r0 = 2 * (tk_ // RKg) * RKg + g * RKg + tk_ % RKg
                    srcv = attn[r0:r0 + 512, :].rearrange("(j p) d -> p j d", p=128)
                    S.dma(atAB[hf][:, :, g * 256:(g + 1) * 256], srcv[:, :, 0:256], reads=[attn_tok], accw=[atABt[hf]])
                    S.dma(atAB[hf][:, :, 512 + g * 256:512 + (g + 1) * 256], srcv[:, :, 256:512], reads=[attn_tok],
                          accw=[atABt[hf]])
            S.add("dve", lambda e: e.tensor_scalar(out=at[:], in0=atAB[0][:], scalar1=hm[:, 0:1], scalar2=None,
                                                   op0=ALU.mult), [atABt[0], hmt], [atok])
            S.add("dve", lambda e: e.scalar_tensor_tensor(out=at[:], in0=atAB[1][:], scalar=hm[:, 1:2], in1=at[:],
                                                          op0=ALU.mult, op1=ALU.add), [atABt[1], hmt, atok], [atok])
        S.dma(ptm[:], pin[t0:t0 + 512, :].rearrange("(j p) d -> p j d", p=128), writes=[ptok])
        def _store(t0=t0):
            S.dma(xo[t0:t0 + 512, :].rearrange("(j p) d -> p j d", p=128), xt[:], reads=[xtok], writes=[xo_tok],
                  q=DBG_Q)
        if DBG_STAGE <= 0:
            _store()
            continue
        rms_to_featmajor(C, xt, xtok, gains, gtok, 0, hT, htok, ident, itok, tmp)
        if DBG_STAGE <= 1:
            _store()
            continue
        ws = W["w_merge"]
        for u in range(ws.nunits):
            wv, wt = load_unit(C, ws, u)
            for f in range(4):
                ps, pt = next_ps(C)
                if DBG_SUB == 1:
                    continue
                for kt in range(8):
                    S.add("pe", lambda e, ps=ps, wv=wv, f=f, kt=kt: e.matmul(
                        ps[:], lhsT=wv[:, f, kt, :], rhs=hT[:, kt, :], start=(kt == 0), stop=(kt == 7)),
                        [wt, htok], [pt])
                if DBG_SUB == 2:
                    continue
                S.add("act", lambda e, ps=ps, ft=u * 4 + f: e.activation(
                    out=sgT[:, ft, :], in_=ps[:], func=AF.Sigmoid), [pt], accw=[sgtok])
        if DBG_STAGE <= 2:
            _store()
            continue
        for ft in range(8):
            pb, pbt = next_psb(C)
            for j in range(4):
                S.add("pe", lambda e, pb=pb, j=j, ft=ft: e.transpose(
                    out=pb[:, j * 128:(j + 1) * 128], in_=at[:, j, ft * 128:(ft + 1) * 128], identity=identb[:]),
                    [atok, ibtok], [pbt])
            S.add("dve", lambda e, pb=pb, ft=ft: e.tensor_copy(out=aT[:, ft, :], in_=pb), [pbt], accw=[aTtok])
        if DBG_SUB == 3:
            _store()
            continue
        wsa, wsb = W["w_up_nsa"], W["w_up_ret"]
        for u in range(2):
            wva, wta = load_unit(C, wsa, u)
            wvb, wtb = load_unit(C, wsb, u)
            for f in range(4):
                ft = u * 4 + f
                psa, pta = next_ps(C)
                for kt in range(4):
                    S.add("pe", lambda e, psa=psa, wva=wva, f=f, kt=kt: e.matmul(
                        psa[:], lhsT=wva[:, f, kt, :], rhs=aT[:, kt, :], start=(kt == 0), stop=(kt == 3)),
                        [wta, aTtok], [pta])
                psb_, ptb = next_ps(C)
                for kt in range(4):
                    S.add("pe", lambda e, psb_=psb_, wvb=wvb, f=f, kt=kt: e.matmul(
                        psb_[:], lhsT=wvb[:, f, kt, :], rhs=aT[:, 4 + kt, :], start=(kt == 0), stop=(kt == 3)),
                        [wtb, aTtok], [ptb])
                if DBG_SUB == 4:
                    continue
                tf, tft = tmpf[tn % 2], tmpft[tn % 2]
                tn += 1
                S.add("dve", lambda e, tf=tf, psa=psa, ft=ft: e.tensor_tensor(
                    out=tf[:], in0=psa[:], in1=sgT[:, ft, :], op=ALU.mult), [pta, sgtok], [tft])
                tf2, tft2 = tmpf[tn % 2], tmpft[tn % 2]
                tn += 1
                S.add("dve", lambda e, tf2=tf2, psb_=psb_, ft=ft: e.tensor_tensor(
                    out=tf2[:], in0=psb_[:], in1=sgT[:, 8 + ft, :], op=ALU.mult), [ptb, sgtok], [tft2])
                if DBG_SUB == 5:
                    continue
                S.add("pool", lambda e, tf=tf, tf2=tf2, ft=ft: e.tensor_tensor(
                    out=mixT[:, ft, :], in0=tf[:], in1=tf2[:], op=ALU.add), [tft, tft2], accw=[mixtok])
        if DBG_STAGE <= 3:
            _store()
            continue
        ws = W["w_out"]
        for n in range(2):
            wv, wt = load_unit(C, ws, n)
            for j in range(4):
                ps, pt = next_ps(C)
                for kt in range(8):
                    S.add("pe", lambda e, ps=ps, wv=wv, j=j, kt=kt: e.matmul(
                        ps[:], lhsT=mixT[:, kt, j * 128:(j + 1) * 128], rhs=wv[:, kt, :],
                        start=(kt == 0), stop=(kt == 7)), [wt, mixtok], [pt])
                S.add("dve", lambda e, ps=ps, j=j, n=n: e.tensor_tensor(
                    out=xt[:, j, n * 512:(n + 1) * 512], in0=ps[:], in1=xt[:, j, n * 512:(n + 1) * 512],
                    op=ALU.add), [pt, xtok], [xtok])
        if DBG_STAGE <= 4:
            _store()
            continue
        rms_to_featmajor(C, xt, xtok, gains, gtok, 8, hT, htok, ident, itok, tmp)
        ws = W["w_ff1"]
        for u in range(ws.nunits):
            wv, wt = load_unit(C, ws, u)
            for f in range(4):
                ft = u * 4 + f
                ps, pt = next_ps(C)
                for kt in range(8):
                    S.add("pe", lambda e, ps=ps, wv=wv, f=f, kt=kt: e.matmul(
                        ps[:], lhsT=wv[:, f, kt, :], rhs=hT[:, kt, :], start=(kt == 0), stop=(kt == 7)),
                        [wt, htok], [pt])
                tf, tft = tmpf[tn % 2], tmpft[tn % 2]
                tn += 1
                S.add("act", lambda e, ps=ps, tf=tf: e.activation(out=tf[:], in_=ps[:], func=AF.Relu),
                      [pt], [tft])
                S.add("pool", lambda e, tf=tf, ft=ft: e.tensor_tensor(
                    out=uT[:, ft, :], in0=tf[:], in1=tf[:], op=ALU.mult), [tft], accw=[utok])
        ws = W["w_ff2"]
        for n in range(2):
            pss = [next_ps(C) for _ in range(4)]
            for v in range(ws.nv):
                wv, wt = load_unit(C, ws, n * ws.nv + v)
                for j in range(4):
                    ps, pt = pss[j]
                    for kk in range(8):
                        kt = v * 8 + kk
                        S.add("pe", lambda e, ps=ps, wv=wv, j=j, kk=kk, kt=kt: e.matmul(
                            ps[:], lhsT=uT[:, kt, j * 128:(j + 1) * 128], rhs=wv[:, kk, :],
                            start=(kt == 0), stop=(kt == 31)), [wt, utok], [pt])
            for j in range(4):
                ps, pt = pss[j]
                S.add("dve", lambda e, ps=ps, j=j, n=n: e.tensor_tensor(
                    out=xt[:, j, n * 512:(n + 1) * 512], in0=ps[:], in1=xt[:, j, n * 512:(n + 1) * 512],
                    op=ALU.add), [pt, xtok], [xtok])
        if DBG_STAGE <= 5:
            _store()
            continue
        rms_to_featmajor(C, xt, xtok, gains, gtok, 16, hT, htok, ident, itok, tmp)
        for kt in range(2):
            ps, pt = next_ps(C)
            for j in range(4):
                S.add("pe", lambda e, ps=ps, j=j, kt=kt: e.transpose(
                    out=ps[:, j * 128:(j + 1) * 128], in_=ptm[:, j, kt * 128:(kt + 1) * 128], identity=ident[:]),
                    [ptok, itok], [pt])
            S.add("dve", lambda e, ps=ps, kt=kt: e.tensor_copy(out=pT[:, kt, :], in_=ps[:]), [pt], accw=[pTtok])
        wsg, wsp = W["w_gate"], W["w_ple"]
        for n in range(2):
            wvg, wtg = load_unit(C, wsg, n)
            wvp, wtp = load_unit(C, wsp, n)
            for j in range(4):
                ps, pt = next_ps(C)
                for kt in range(8):
                    S.add("pe", lambda e, ps=ps, wvg=wvg, j=j, kt=kt: e.matmul(
                        ps[:], lhsT=hT[:, kt, j * 128:(j + 1) * 128], rhs=wvg[:, kt, :],
                        start=(kt == 0), stop=(kt == 7)), [wtg, htok], [pt])
                S.add("act", lambda e, ps=ps: e.activation(out=gsb[:], in_=ps[:], func=AF.Sigmoid),
                      [pt], [gsbtok])
                ps2, pt2 = next_ps(C)
                for kt in range(2):
                    S.add("pe", lambda e, ps2=ps2, wvp=wvp, j=j, kt=kt: e.matmul(
                        ps2[:], lhsT=pT[:, kt, j * 128:(j + 1) * 128], rhs=wvp[:, kt, :],
                        start=(kt == 0), stop=(kt == 1)), [wtp, pTtok], [pt2])
                tf, tft = tmpf[tn % 2], tmpft[tn % 2]
                tn += 1
                S.add("dve", lambda e, tf=tf, ps2=ps2: e.tensor_tensor(
                    out=tf[:], in0=ps2[:], in1=gsb[:], op=ALU.mult), [pt2, gsbtok], [tft])
                S.add("pool", lambda e, tf=tf, j=j, n=n: e.tensor_tensor(
                    out=xt[:, j, n * 512:(n + 1) * 512], in0=tf[:], in1=xt[:, j, n * 512:(n + 1) * 512],
                    op=ALU.add), [tft, xtok], [xtok])
        _store()


IN_SPLITS = (512, 128, 128, 128, 128, 128, 128, 24, 512, 512, 512, 512, 1024, 1024)
NEG = -30000.0


def hostA_weights(w_in, g):
    offs = np.cumsum([0] + list(IN_SPLITS))

    def col(i, a, b):
        return w_in[:, offs[i] + a: offs[i] + b]

    def swap(x):
        return np.concatenate([x[:, 64:], x[:, :64]], 1)

    q = [col(0, (g * 4 + h) * 64, (g * 4 + h + 1) * 64) for h in range(4)]
    kc, vc = col(1, g * 64, g * 64 + 64), col(2, g * 64, g * 64 + 64)
    ks, vs = col(3, g * 64, g * 64 + 64), col(4, g * 64, g * 64 + 64)
    kw, vw = col(5, g * 64, g * 64 + 64), col(6, g * 64, g * 64 + 64)
    gates = np.stack([w_in[:, offs[7] + br * 8 + g * 4 + h] for br in range(3) for h in range(4)], 1)
    rq = [col(8, (2 * g + h) * 128, (2 * g + h + 1) * 128) for h in range(2)]
    rk = [col(9, (2 * g + h) * 128, (2 * g + h + 1) * 128) for h in range(2)]
    rv = col(10, 2 * g * 128, (2 * g + 2) * 128)
    rg = col(11, 2 * g * 128, (2 * g + 2) * 128)
    z128 = np.zeros((1024, 128), np.float32)
    WS = np.concatenate([q[0], q[1], q[2], q[3], ks, ks, kw, kw, kc, vc, z128, z128, z128,
                         rq[0], swap(rq[0]), rq[1], swap(rq[1]), rk[0], swap(rk[0]), rk[1], swap(rk[1])], 1)
    WM = np.concatenate([rk[0], rk[1], rv, rg, np.zeros((1024, 256), np.float32),
                         vs, vw, gates, np.zeros((1024, 512 - 140), np.float32)], 1)
    return np.ascontiguousarray(WS, np.float32), np.ascontiguousarray(WM, np.float32)


def hostA_consts(g, S):
    import ml_dtypes
    bf = ml_dtypes.bfloat16
    c = {}
    c["ident"] = np.eye(128, dtype=np.float32)
    bd = np.zeros((128, 128), np.float32)
    bd[:64, :64] = 1
    bd[64:, 64:] = 1
    c["onesbd"] = bd.astype(bf)
    half = 64
    inv = (10000.0 ** (-np.arange(half, dtype=np.float32) / half)).astype(np.float32)
    pos = np.arange(S, dtype=np.float32)
    ang = (pos[:, None] * inv[None, :]).astype(np.float32)
    cos, sin = np.cos(ang.astype(np.float64)), np.sin(ang.astype(np.float64))
    cosT = np.concatenate([cos.T, cos.T], 0)
    sinsT = np.concatenate([-sin.T, sin.T], 0)
    ksc = 128.0 ** -0.5
    c["ropeq"] = np.stack([cosT, sinsT], 1).astype(np.float32)
    c["ropek"] = (np.stack([cosT, sinsT], 1) * ksc).astype(np.float32)
    hh = np.array([2 * g, 2 * g + 1], np.float64)
    gamma = 1.0 - 2.0 ** (-5.0 - hh)
    lg = np.log(gamma)
    n = np.arange(128, dtype=np.float64)
    xi = np.exp(lg[:, None] * (n + 1.0))
    zeta = np.exp(lg[:, None] * (127.0 - n))
    c["xi"] = np.broadcast_to(np.tile(xi, (1, 4))[None], (128, 2, 512)).astype(np.float32).copy()
    zt = zeta[:, np.arange(S) % 128]
    c["ctk"] = (cos[:, None, :] * zt.T[:, :, None] * ksc).astype(np.float32)
    c["stk"] = (sin[:, None, :] * zt.T[:, :, None] * ksc).astype(np.float32)
    diff = n[None, :] - n[:, None]
    dec = np.where(diff[None] >= 0, np.exp(lg[:, None, None] * np.maximum(diff[None], 0)), 0.0)
    c["decayT"] = np.ascontiguousarray(dec.transpose(1, 0, 2)).astype(np.float32)
    c["gch"] = np.broadcast_to(np.exp(lg * 128.0)[None], (128, 2)).astype(np.float32).copy()
    kk, qq = np.arange(128)[:, None], np.arange(128)[None, :]
    c["triT"] = np.tile(np.where(kk <= qq, 0.0, NEG), (1, 4)).astype(bf)
    c["triU"] = np.tile(np.where(kk > qq, 0.0, NEG), (1, 4)).astype(bf)
    r = np.arange(16)[None, :, None]
    ql, il = np.arange(128)[:, None, None], np.arange(128)[None, None, :]
    c["cmaskQ"] = np.where(128 * r + ql - 16 * il - 31 >= 0, 0.0, NEG).astype(bf)
    c["cmaskT"] = np.ascontiguousarray(np.transpose(np.where(128 * r + ql - 16 * il - 31 >= 0, 0.0, NEG), (2, 1, 0))).astype(bf)
    c["wexp"] = (np.arange(S)[None, :] // 64 == np.arange(128)[:, None]).astype(bf)
    return c


A_CONST_SHAPES = lambda S: {
    "ident": ([128, 128], F32), "onesbd": ([128, 128], BF16), "ropeq": ([128, 2, S], F32),
    "ropek": ([128, 2, S], F32), "xi": ([128, 2, 512], F32), "ctk": ([S, 2, 64], F32), "stk": ([S, 2, 64], F32),
    "decayT": ([128, 2, 128], F32), "gch": ([128, 2], F32), "triT": ([128, 512], BF16), "triU": ([128, 512], BF16),
    "cmaskQ": ([128, 16, 128], BF16), "cmaskT": ([128, 16, 128], BF16), "wexp": ([128, S], BF16)}


def hostA_params(z, L, g):
    p = {}

    def gl(v):
        return np.ascontiguousarray(v.reshape(8, 128).T)
    qg, kg = z["nsa_q_norm"][L], z["nsa_k_norm"][L]
    p["gainsA"] = np.concatenate([gl(z["norm_mix"][L]), np.tile(qg, 2)[:, None], np.tile(kg, 2)[:, None]], 1).astype(np.float32)
    p["posT"] = np.ascontiguousarray(np.concatenate([z["cmp_pos_k"][L].T, z["cmp_pos_v"][L].T], 0), np.float32)
    w1k = z["cmp_w1_k"][L].reshape(32, 64, 256).transpose(1, 0, 2)
    w1v = z["cmp_w1_v"][L].reshape(32, 64, 256).transpose(1, 0, 2)
    zz = np.zeros_like(w1k)
    p["w1k"] = np.ascontiguousarray(np.concatenate([w1k, zz], 0), np.float32)
    p["w1v"] = np.ascontiguousarray(np.concatenate([zz, w1v], 0), np.float32)
    w2k = z["cmp_w2_k"][L].reshape(2, 128, 64).transpose(1, 0, 2)
    w2v = z["cmp_w2_v"][L].reshape(2, 128, 64).transpose(1, 0, 2)
    p["w2"] = np.ascontiguousarray(np.stack([np.concatenate([w2k, w2k], 2), np.concatenate([w2v, np.zeros_like(w2v)], 2)], 2), np.float32)
    return p


A_PARAM_SHAPES = {"gainsA": [128, 10], "posT": [128, 32], "w1k": [128, 32, 256], "w1v": [128, 32, 256],
                  "w2": [128, 2, 2, 128]}


def build_phaseA(SL=8192, parts=("ret", "nsa")):
    nc = bass.Bass("TRN2", target_bir_lowering=False)
    dt = nc.dram_tensor
    x = dt("x", [SL, 1024], F32, kind="ExternalInput").ap()
    WS_d = dt("WS", [1024, 2048], F32, kind="ExternalInput").ap()
    WM_d = dt("WM", [1024, 1536], F32, kind="ExternalInput").ap()
    cd = {k: dt(k, sh, ty, kind="ExternalInput").ap() for k, (sh, ty) in A_CONST_SHAPES(SL).items()}
    pd = {k: dt(k, sh, F32, kind="ExternalInput").ap() for k, sh in A_PARAM_SHAPES.items()}
    ao = dt("ao", [SL, 512], BF16, kind="ExternalOutput").ap()
    S = Sched(nc)
    C = make_pools(S, n_wbuf=3)
    emit_phaseA(C, SL, x, WS_d, WM_d, cd, pd, ao, parts)
    S.emit()
    S.close()
    return nc


def norm_evac(C, ps, pt, gains, gtok, gcol, onesbd, otok, tmps, dsts):
    S = C.S
    qf, qft, sq, sqt, rs, rst = tmps
    N = ps.shape[-1]
    S.add("act", lambda e: e.activation(out=qf[:, :N], in_=ps, func=AF.Copy), [pt], [qft])
    S.add("act", lambda e: e.activation(out=sq[:, :N], in_=ps, func=AF.Square), [pt], [sqt])
    p2, pt2 = next_ps(C)
    S.add("pe", lambda e: e.matmul(p2[:, :N], lhsT=onesbd[:], rhs=sq[:, :N], start=True, stop=True), [sqt, otok], [pt2])
    S.add("act", lambda e: e.activation(out=rs[:, :N], in_=p2[:, :N], func=AF.Ln, scale=1.0 / 64, bias=C.epsb[:, 0:1]), [pt2, C.epst], [rst])
    S.add("act", lambda e: e.activation(out=rs[:, :N], in_=rs[:, :N], func=AF.Exp, scale=-0.5), [rst], [rst])
    for (dst, lo, hi, tok) in dsts:
        S.add("dve", lambda e, dst=dst, lo=lo, hi=hi: e.scalar_tensor_tensor(
            out=dst, in0=qf[lo:hi, :N], scalar=gains[lo:hi, gcol:gcol + 1], in1=rs[lo:hi, :N],
            op0=ALU.mult, op1=ALU.mult), [qft, rst, gtok], accw=[tok])


def emit_phaseA(C, SL, x, WS_d, WM_d, cd, pd, ao, parts=("ret", "nsa"), xin_tok=None, ao_tok=None, xmap=None):
    C.xmap = xmap or (lambda t: t)
    S = C.S
    nchunk = SL // 512
    xin_tok = xin_tok or Tok()
    ao_tok = ao_tok or Tok()
    WSs = WSpec(S, "WSs", WS_d, 1024, 2048, "S")
    WMs = WSpec(S, "WMs", WM_d, 1024, 1536, "M")
    gains = S.sbuf("gainsA", [128, 10], F32)
    gtok = Tok()
    ident = S.sbuf("identA", [128, 128], F32)
    itok = Tok()
    C.epsb = S.sbuf("epsb", [128, 1], F32)
    C.epst = Tok()
    S.dma(gains[:], pd["gainsA"], writes=[gtok])
    S.dma(ident[:], cd["ident"], writes=[itok])
    S.add("dve", lambda e: e.memset(C.epsb[:], EPS), [], [C.epst])
    prep_weight(C, WSs)
    prep_weight(C, WMs)
    C.hscr = S.nc.dram_tensor("hscr_" + S.prefix, [nchunk, 128, 4096], BF16, kind="Internal").ap()
    C.hscr_tok = [Tok() for _ in range(nchunk)]
    C.share_h = ("ret" in parts) and ("nsa" in parts)
    if "ret" in parts:
        with S.scope():
            emit_ret_pass(C, SL, x, xin_tok, WSs, WMs, cd, gains, gtok, ident, itok, ao, ao_tok)
    if "nsa" in parts:
        emit_nsa(C, SL, x, xin_tok, WSs, WMs, cd, pd, gains, gtok, ident, itok, ao, ao_tok)


def load_x_rms(C, x, xin_tok, t0, xt, xtok, junk, jtok, ss, sstok, rstd, rtok, gains, gtok, hT, htok, ident, itok):
    S = C.S
    xr0 = C.xmap(t0)
    S.dma(xt[:], x[xr0:xr0 + 512, :].rearrange("(j p) d -> p j d", p=128), reads=[xin_tok], writes=[xtok])
    for j in range(4):
        S.add("act", lambda e, j=j: e.activation(out=junk[:], in_=xt[:, j, :], func=AF.Square,
                                                 accum_out=ss[:, j:j + 1]), [xtok], [jtok, sstok])
    S.add("dve", lambda e: e.tensor_scalar(out=rstd[:], in0=ss[:], scalar1=1.0 / D_MODEL, scalar2=EPS,
                                           op0=ALU.mult, op1=ALU.add), [sstok], [rtok])
    S.add("act", lambda e: e.activation(out=rstd[:], in_=rstd[:], func=AF.Sqrt), [rtok], [rtok])
    S.add("dve", lambda e: e.reciprocal(out=rstd[:], in_=rstd[:]), [rtok], [rtok])
    for j in range(4):
        S.add("act", lambda e, j=j: e.activation(out=xt[:, j, :], in_=xt[:, j, :], func=AF.Copy,
                                                 scale=rstd[:, j:j + 1]), [xtok, rtok], [xtok])
    for kt in range(8):
        ps, pt = next_ps(C)
        for j in range(4):
            S.add("pe", lambda e, ps=ps, j=j, kt=kt: e.transpose(
                out=ps[:, j * 128:(j + 1) * 128], in_=xt[:, j, kt * 128:(kt + 1) * 128], identity=ident[:]),
                [xtok, itok], [pt])
        if kt % 2 == 0:
            S.add("dve", lambda e, ps=ps, kt=kt: e.tensor_scalar(
                out=hT[:, kt, :], in0=ps[:], scalar1=gains[:, kt:kt + 1], scalar2=None, op0=ALU.mult),
                [pt, gtok], accw=[htok])
        else:
            S.add("act", lambda e, ps=ps, kt=kt: e.activation(
                out=hT[:, kt, :], in_=ps[:], func=AF.Copy, scale=gains[:, kt:kt + 1]), [pt, gtok], accw=[htok])


def emit_ret_pass(C, SL, x, xin_tok, WSs, WMs, cd, gains, gtok, ident, itok, ao, ao_tok):
    S = C.S
    sb = S.sbuf
    xt2 = [sb("r_xt%d" % i, [128, 4, 1024], F32) for i in range(2)]
    junk = sb("r_junk", [128, 1024], F32)
    ss2 = [sb("r_ss%d" % i, [128, 4], F32) for i in range(2)]
    rstd2 = [sb("r_rstd%d" % i, [128, 4], F32) for i in range(2)]
    hT2 = [sb("r_hT%d" % i, [128, 8, 512], BF16) for i in range(2)]
    xtok2, sstok2, rtok2, htok2 = [[Tok(), Tok()] for _ in range(4)]
    rq_tab = sb("r_rqtab", [128, 2, 512], F32)
    rk_tab = sb("r_rktab", [128, 2, 512], F32)
    ctk_t = sb("r_ctk", [128, 4, 2, 64], F32)
    stk_t = sb("r_stk", [128, 4, 2, 64], F32)
    xi_t = sb("r_xi", [128, 2, 512], F32)
    decT = sb("r_dec", [128, 2, 128], F32)
    gch = sb("r_gch", [128, 2], F32)
    t1 = [sb("r_t1_%d" % i, [128, 512], F32) for i in range(2)]
    t2 = [sb("r_t2_%d" % i, [128, 512], F32) for i in range(2)]
    tmpq = sb("r_tmpq", [128, 512], F32)
    QrT = sb("r_QrT", [128, 2, 512], BF16)
    QrxT = sb("r_QrxT", [128, 2, 512], BF16)
    KrT = sb("r_KrT", [128, 2, 512], BF16)
    Vr = sb("r_Vr", [128, 4, 256], BF16)
    kz = sb("r_kz", [128, 4, 2, 128], BF16)
    sg = sb("r_sg", [128, 4, 256], F32)
    tabcd = [sb("r_tabcd%d" % i, [128, 2, 64], F32) for i in range(4)]
    IT = [sb("r_IT%d" % i, [128, 128], BF16) for i in range(2)]
    yr = sb("r_yr", [128, 4, 2, 128], F32)
    ssr = sb("r_ssr", [128, 8], F32)
    rr = sb("r_rr", [128, 8], F32)
    ro = sb("r_ro", [128, 4, 256], BF16)
    R = sb("r_R", [128, 2, 128], F32)
    Rb = sb("r_Rb", [128, 2, 128], BF16)
    junkb = sb("r_junkb", [128, 128], BF16)
    (xtok, jtok, sstok, rtok, htok, rqt, rkt, ctt, stt, xit, dect, gcht, tmpqt, qrt, qrxt, krt, vrt, kzt, sgt,
     yrt, ssrt, rrt, rot, jbt) = [Tok() for _ in range(24)]
    t1t, t2t = [Tok(), Tok()], [Tok(), Tok()]
    tabt = [Tok() for _ in range(4)]
    ITt = [Tok(), Tok()]
    Rt, Rbt = [Tok(), Tok()], [Tok(), Tok()]
    S.dma(xi_t[:], cd["xi"], writes=[xit])
    S.dma(decT[:], cd["decayT"], writes=[dect])
    S.dma(gch[:], cd["gch"], writes=[gcht])
    for h in range(2):
        S.add("dve", lambda e, h=h: e.memset(R[:, h, :], 0.0), [], [Rt[h]])
        S.add("pool", lambda e, h=h: e.memset(Rb[:, h, :], 0.0), [], [Rbt[h]])
    nchunk = SL // 512
    tn = 0
    itn = 0
    def _lx(c):
        i = c % 2
        load_x_rms(C, x, xin_tok, c * 512, xt2[i], xtok2[i], junk, jtok, ss2[i], sstok2[i], rstd2[i], rtok2[i],
                   gains, gtok, hT2[i], htok2[i], ident, itok)
        if C.share_h:
            S.dma(C.hscr[c], hT2[i][:].rearrange("p k t -> p (k t)"), reads=[htok2[i]], writes=[C.hscr_tok[c]], q="sp")

    _lx(0)
    for c in range(nchunk):
        t0 = c * 512
        hT, htok = hT2[c % 2], htok2[c % 2]
        S.dma(rq_tab[:], cd["ropeq"][:, :, t0:t0 + 512], writes=[rqt])
        S.dma(rk_tab[:], cd["ropek"][:, :, t0:t0 + 512], writes=[rkt])
        S.dma(ctk_t[:], cd["ctk"][t0:t0 + 512].rearrange("(j p) h i -> p j h i", p=128), writes=[ctt])
        S.dma(stk_t[:], cd["stk"][t0:t0 + 512].rearrange("(j p) h i -> p j h i", p=128), writes=[stt])
        if DBG_R <= 1:
            continue
        for u in (2, 3):
            wv, wt = load_unit(C, WSs, u)
            isq = (u == 2)
            tab, tabt_ = (rq_tab, rqt) if isq else (rk_tab, rkt)
            for h in range(2):
                a, at_ = t1[tn % 2], t1t[tn % 2]
                b, bt_ = t2[tn % 2], t2t[tn % 2]
                tn += 1
                for half, (dstb, dtok) in enumerate(((a, at_), (b, bt_))):
                    f = 2 * h + half
                    ps, pt = next_ps(C)
                    for kt in range(8):
                        S.add("pe", lambda e, hT=hT, ps=ps, wv=wv, f=f, kt=kt: e.matmul(
                            ps[:], lhsT=wv[:, f, kt, :], rhs=hT[:, kt, :], start=(kt == 0), stop=(kt == 7)),
                            [wt, htok], [pt])
                    S.add("dve", lambda e, ps=ps, dstb=dstb, half=half, tab=tab: e.tensor_tensor(
                        out=dstb[:], in0=ps[:], in1=tab[:, half, :], op=ALU.mult), [pt, tabt_], [dtok])
                if isq:
                    S.add("pool", lambda e, a=a, b=b: e.tensor_tensor(out=tmpq[:], in0=a[:], in1=b[:], op=ALU.add),
                          [at_, bt_], [tmpqt])
                    S.add("act", lambda e, h=h: e.activation(out=QrT[:, h, :], in_=tmpq[:], func=AF.Copy),
                          [tmpqt], accw=[qrt])
                    S.add("pool", lambda e, h=h: e.tensor_tensor(out=QrxT[:, h, :], in0=tmpq[:], in1=xi_t[:, h, :],
                                                                 op=ALU.mult), [tmpqt, xit], accw=[qrxt])
                else:
                    S.add("pool", lambda e, a=a, b=b, h=h: e.tensor_tensor(out=KrT[:, h, :], in0=a[:], in1=b[:],
                                                                           op=ALU.add), [at_, bt_], accw=[krt])
        if DBG_R <= 2:
            continue
        if c + 1 < nchunk:
            _lx(c + 1)
        wv, wt = load_unit(C, WMs, 0)
        for j in range(4):
            ps, pt = next_ps(C)
            for kt in range(8):
                S.add("pe", lambda e, hT=hT, ps=ps, wv=wv, j=j, kt=kt: e.matmul(
                    ps[:], lhsT=hT[:, kt, j * 128:(j + 1) * 128], rhs=wv[:, kt, :], start=(kt == 0), stop=(kt == 7)),
                    [wt, htok], [pt])
            if DBG_SUB == 1:
                S.add("act", lambda e, ps=ps, j=j: e.activation(out=Vr[:, j, :], in_=ps[:, 256:512], func=AF.Copy),
                      [pt], accw=[vrt])
                continue
            pv = ps[:, 0:256].rearrange("p (h t i) -> p h t i", h=2, t=2)
            x1, x2 = pv[:, :, 0, :], pv[:, :, 1, :]
            kzv = kz[:, j].rearrange("p h (t i) -> p h t i", t=2)
            ta, tb, tc, td = tabcd
            S.add("dve", lambda e, x1=x1, j=j: e.tensor_tensor(out=ta[:], in0=x1, in1=ctk_t[:, j], op=ALU.mult),
                  [pt, ctt], [tabt[0]])
            S.add("dve", lambda e, x2=x2, j=j: e.tensor_tensor(out=tb[:], in0=x2, in1=stk_t[:, j], op=ALU.mult),
                  [pt, stt], [tabt[1]])
            S.add("dve", lambda e, x1=x1, j=j: e.tensor_tensor(out=tc[:], in0=x1, in1=stk_t[:, j], op=ALU.mult),
                  [pt, stt], [tabt[2]])
            S.add("dve", lambda e, x2=x2, j=j: e.tensor_tensor(out=td[:], in0=x2, in1=ctk_t[:, j], op=ALU.mult),
                  [pt, ctt], [tabt[3]])
            if DBG_SUB == 2:
                continue
            S.add("dve", lambda e, kzv=kzv: e.tensor_tensor(out=kzv[:, :, 0, :], in0=ta[:], in1=tb[:], op=ALU.subtract),
                  [tabt[0], tabt[1]] if DBG_SUB != 3 else [], accw=[kzt])
            S.add("dve", lambda e, kzv=kzv: e.tensor_tensor(out=kzv[:, :, 1, :], in0=tc[:], in1=td[:], op=ALU.add),
                  [tabt[2], tabt[3]] if DBG_SUB != 3 else [], accw=[kzt])
            S.add("act", lambda e, ps=ps, j=j: e.activation(out=Vr[:, j, :], in_=ps[:, 256:512], func=AF.Copy),
                  [pt], accw=[vrt])
        if DBG_R <= 3:
            continue
        wv, wt = load_unit(C, WMs, 1)
        for j in range(4):
            ps, pt = next_ps(C)
            for kt in range(8):
                S.add("pe", lambda e, hT=hT, ps=ps, wv=wv, j=j, kt=kt: e.matmul(
                    ps[:, 0:256], lhsT=hT[:, kt, j * 128:(j + 1) * 128], rhs=wv[:, kt, 0:256],
                    start=(kt == 0), stop=(kt == 7)), [wt, htok], [pt])
            S.add("act", lambda e, ps=ps, j=j: e.activation(out=sg[:, j, :], in_=ps[:, 0:256], func=AF.Silu),
                  [pt], accw=[sgt])
        if DBG_R <= 4:
            continue
        for j in range(4):
            js = slice(j * 128, (j + 1) * 128)
            for h in range(2):
                hs = slice(h * 128, (h + 1) * 128)
                psI, ptI = next_ps(C)
                S.add("pe", lambda e, psI=psI, h=h, js=js: e.matmul(
                    psI[:, 0:128], lhsT=KrT[:, h, js], rhs=QrT[:, h, js], start=True, stop=True), [krt, qrt], [ptI])
                it_, itt = IT[itn % 2], ITt[itn % 2]
                itn += 1
                S.add("dve", lambda e, psI=psI, it_=it_, h=h: e.tensor_tensor(
                    out=it_[:], in0=psI[:, 0:128], in1=decT[:, h, :], op=ALU.mult), [ptI, dect], [itt])
                psO, ptO = next_ps(C)
                S.add("pe", lambda e, psO=psO, it_=it_, j=j, hs=hs: e.matmul(
                    psO[:, 0:128], lhsT=it_[:], rhs=Vr[:, j, hs], start=True, stop=False), [itt, vrt], [ptO])
                S.add("pe", lambda e, psO=psO, h=h, js=js: e.matmul(
                    psO[:, 0:128], lhsT=QrxT[:, h, js], rhs=Rb[:, h, :], start=False, stop=True), [qrxt, Rbt[h]], [ptO])
                S.add("act", lambda e, psO=psO, j=j, h=h: e.activation(
                    out=junkb[:], in_=psO[:, 0:128], func=AF.Square, accum_out=ssr[:, j * 2 + h:j * 2 + h + 1]),
                    [ptO], [jbt], accw=[ssrt])
                S.add("dve", lambda e, psO=psO, j=j, h=h: e.tensor_copy(out=yr[:, j, h, :], in_=psO[:, 0:128]),
                      [ptO], accw=[yrt])
                psK, ptK = next_ps(C)
                S.add("pe", lambda e, psK=psK, j=j, h=h, hs=hs: e.matmul(
                    psK[:, 0:128], lhsT=kz[:, j, h, :], rhs=Vr[:, j, hs], start=True, stop=True), [kzt, vrt], [ptK])
                S.add("dve", lambda e, psK=psK, h=h: e.scalar_tensor_tensor(
                    out=R[:, h, :], in0=R[:, h, :], scalar=gch[:, h:h + 1], in1=psK[:, 0:128],
                    op0=ALU.mult, op1=ALU.add), [ptK, gcht, Rt[h]], [Rt[h]])
                S.add("pool", lambda e, h=h: e.tensor_copy(out=Rb[:, h, :], in_=R[:, h, :]), [Rt[h]], [Rbt[h]])
        if DBG_R <= 5:
            continue
        S.add("dve", lambda e: e.tensor_scalar(out=rr[:], in0=ssr[:], scalar1=1.0 / 128, scalar2=EPS,
                                               op0=ALU.mult, op1=ALU.add), [ssrt], [rrt])
        S.add("act", lambda e: e.activation(out=rr[:], in_=rr[:], func=AF.Sqrt), [rrt], [rrt])
        S.add("dve", lambda e: e.reciprocal(out=rr[:], in_=rr[:]), [rrt], [rrt])
        for j in range(4):
            for h in range(2):
                hs = slice(h * 128, (h + 1) * 128)
                S.add("dve", lambda e, j=j, h=h, hs=hs: e.scalar_tensor_tensor(
                    out=ro[:, j, hs], in0=yr[:, j, h, :], scalar=rr[:, j * 2 + h:j * 2 + h + 1], in1=sg[:, j, hs],
                    op0=ALU.mult, op1=ALU.mult), [yrt, rrt, sgt], accw=[rot])
        S.dma(ao[t0:t0 + 512, 256:512].rearrange("(j p) d -> p j d", p=128), ro[:], reads=[rot], accw=[ao_tok], q="act")


HORD = (0, 2, 1, 3)


def emit_nsa(C, SL, x, xin_tok, WSs, WMs, cd, pd, gains, gtok, ident, itok, ao, ao_tok):
    S = C.S
    sb = S.sbuf
    NT = SL // 128
    nb = SL // 16
    assert nb <= 512
    NCT = max(1, nb // 128)
    QT = sb("n_QT", [128, 2, SL], BF16)
    Kslo, Kshi = sb("n_Kslo", [128, SL], BF16), sb("n_Kshi", [128, SL], BF16)
    Kwlo, Kwhi = sb("n_Kwlo", [128, SL], BF16), sb("n_Kwhi", [128, SL], BF16)
    V1 = sb("n_V1", [128, NT, 2, 65], BF16)
    Gt = sb("n_Gt", [128, NT, 12], F32)
    kclo, kchi = sb("n_kclo", [128, 512], BF16), sb("n_kchi", [128, 512], BF16)
    Vc1 = sb("n_Vc1", [128, 4, 65], BF16)
    onesbd = sb("n_onesbd", [128, 128], BF16)
    identb = sb("n_identb", [128, 128], BF16)
    qf = sb("n_qf", [128, 512], F32)
    sq = sb("n_sq", [128, 512], BF16)
    rs = sb("n_rs", [128, 512], F32)
    qtok, kst, kwt, kcvt, v1t, gtt, kct, vct, onest, ibt, qft, sqt, rst = [Tok() for _ in range(13)]
    tmps = (qf, qft, sq, sqt, rs, rst)
    S.dma(onesbd[:], cd["onesbd"], writes=[onest])
    S.add("dve", lambda e: e.tensor_copy(out=identb[:], in_=ident[:]), [itok], [ibt])
    for (t_, tk) in ((Kslo, kst), (Kshi, kst), (Kwlo, kwt), (Kwhi, kwt), (kclo, kct), (kchi, kct)):
        S.add("pool", lambda e, t_=t_: e.memset(t_[:], 0.0), [], [tk])
    S.add("pool", lambda e: e.memset(V1[:], 1.0), [], [v1t])
    S.add("pool", lambda e: e.memset(Vc1[:], 0.0), [], [vct])
    S.add("pool", lambda e: e.memset(Vc1[:, :, 64:65], 1.0), [vct], [vct])
    kc_scope = S.scope()
    kc_scope.__enter__()
    KcVcT = sb("n_KcVcT", [128, SL + 16], BF16)
    with S.scope():
        if C.share_h:
            hT2 = [sb("n_hT%d" % i, [128, 8, 512], BF16) for i in range(2)]
            htok2 = [Tok(), Tok()]

            def _lh(c):
                S.dma(hT2[c % 2][:].rearrange("p k t -> p (k t)"), C.hscr[c], reads=[C.hscr_tok[c]], writes=[htok2[c % 2]])
            _lh(0)
        else:
            xt = sb("n_xt", [128, 4, 1024], F32)
            junk = sb("n_junk", [128, 1024], F32)
            ss = sb("n_ss", [128, 4], F32)
            rstd = sb("n_rstd", [128, 4], F32)
            hT = sb("n_hT", [128, 8, 512], BF16)
            xtok, jtok, sstok, rtok, htok = [Tok() for _ in range(5)]
        for c in range(SL // 512):
            t0 = c * 512
            cs = slice(t0, t0 + 512)
            if C.share_h:
                hT, htok = hT2[c % 2], htok2[c % 2]
                if c + 1 < SL // 512:
                    _lh(c + 1)
            else:
                load_x_rms(C, x, xin_tok, t0, xt, xtok, junk, jtok, ss, sstok, rstd, rtok, gains, gtok, hT, htok, ident, itok)
            for u in (0, 1):
                wv, wt = load_unit(C, WSs, u)
                for f in range(4 if u == 0 else 1):
                    ft = u * 4 + f
                    ps, pt = next_ps(C)
                    for kt in range(8):
                        S.add("pe", lambda e, hT=hT, ps=ps, wv=wv, f=f, kt=kt: e.matmul(
                            ps[:], lhsT=wv[:, f, kt, :], rhs=hT[:, kt, :], start=(kt == 0), stop=(kt == 7)),
                            [wt, htok], [pt])
                    if ft < 2:
                        norm_evac(C, ps[:], pt, gains, gtok, 8, onesbd, onest, tmps, [(QT[:, ft, cs], 0, 128, qtok)])
                    elif ft == 2:
                        norm_evac(C, ps[:], pt, gains, gtok, 9, onesbd, onest, tmps,
                                  [(Kslo[0:64, cs], 0, 64, kst), (Kshi[64:128, cs], 64, 128, kst)])
                    elif ft == 3:
                        norm_evac(C, ps[:], pt, gains, gtok, 9, onesbd, onest, tmps,
                                  [(Kwlo[0:64, cs], 0, 64, kwt), (Kwhi[64:128, cs], 64, 128, kwt)])
                    else:
                        S.add("act", lambda e, ps=ps, cs=cs: e.activation(out=KcVcT[:, cs], in_=ps[:], func=AF.Copy),
                              [pt], accw=[kcvt])
            wv, wt = load_unit(C, WMs, 2)
            for j in range(4):
                tile_i = c * 4 + j
                ps, pt = next_ps(C)
                for kt in range(8):
                    S.add("pe", lambda e, hT=hT, ps=ps, wv=wv, j=j, kt=kt: e.matmul(
                        ps[:, 0:140], lhsT=hT[:, kt, j * 128:(j + 1) * 128], rhs=wv[:, kt, 0:140],
                        start=(kt == 0), stop=(kt == 7)), [wt, htok], [pt])
                S.add("dve", lambda e, ps=ps, tile_i=tile_i: e.tensor_copy(
                    out=V1[:, tile_i, :, 0:64], in_=ps[:, 0:128].rearrange("p (b d) -> p b d", b=2)), [pt], accw=[v1t])
                S.add("act", lambda e, ps=ps, tile_i=tile_i: e.activation(
                    out=Gt[:, tile_i, :], in_=ps[:, 128:140], func=AF.Sigmoid), [pt], accw=[gtt])
    with S.scope():
        W1b = sb("n_W1b", [128, 32, 256], BF16)
        posT = sb("n_posT", [128, 32], F32)
        w2f = sb("n_w2f", [128, 2, 2, 128], F32)
        w2b = sb("n_w2b", [128, 2, 2, 128], BF16)
        zr = [sb("n_zr%d" % i, [128, 512], BF16) for i in range(4)]
        zrt = [Tok() for _ in range(4)]
        GT = sb("n_GT", [128, 4, 512], BF16)
        ga = sb("n_ga", [128, 512], F32)
        gb = sb("n_gb", [128, 512], F32)
        w1t, post, w2t, w2bt, GTt, gat, gbt = [Tok() for _ in range(7)]
        S.dma(posT[:], pd["posT"], writes=[post])
        S.dma(w2f[:], pd["w2"], writes=[w2t])
        S.add("dve", lambda e: e.tensor_copy(out=w2b[:], in_=w2f[:]), [w2t], [w2bt])
        S.add("dve", lambda e: e.tensor_copy(out=KcVcT[:, SL:SL + 16], in_=KcVcT[:, SL - 1:SL].to_broadcast([128, 16])),
              [kcvt], [kcvt])
        accs = [next_ps(C) for _ in range(4)]
        zn = 0
        for kv, srcw in enumerate((pd["w1k"], pd["w1v"])):
            for r0 in range(0, 32, 2):
                i = C.sn % 3
                C.sn += 1
                st, stt = C.stg[i], C.stgtok[i]
                S.dma(st[:, :512], srcw[:, r0:r0 + 2, :].rearrange("p r h -> p (r h)"), writes=[stt])
                eng = ("dve", "pool")[C.cast_rr % 2]
                C.cast_rr += 1
                S.add(eng, lambda e, st=st, r0=r0: e.tensor_copy(
                    out=W1b[:, r0:r0 + 2, :].rearrange("p r h -> p (r h)"), in_=st[:, :512]), [stt], accw=[w1t])
            for r in range(32):
                z, zt = zr[zn % 4], zrt[zn % 4]
                zn += 1
                if r < 16:
                    src = KcVcT[:, 0:16 * nb].rearrange("p (i s) -> p i s", s=16)[:, :, r]
                else:
                    src = KcVcT[:, 16:16 + 16 * nb].rearrange("p (i s) -> p i s", s=16)[:, :, r - 16]
                eng = ("dve", "pool")[r % 2]
                S.add(eng, lambda e, z=z, src=src, r=r: e.tensor_scalar(
                    out=z[:, :nb], in0=src, scalar1=posT[:, r:r + 1], scalar2=None, op0=ALU.add), [kcvt, post], [zt])
                for hid in range(2):
                    ps, pt = accs[kv * 2 + hid]
                    S.add("pe", lambda e, ps=ps, hid=hid, r=r, z=z: e.matmul(
                        ps[:, :nb], lhsT=W1b[:, r, hid * 128:(hid + 1) * 128], rhs=z[:, :nb],
                        start=(r == 0), stop=(r == 31)), [w1t, zt], [pt])
        for a in range(4):
            ps, pt = accs[a]
            S.add("act", lambda e, ps=ps: e.activation(out=ga[:, :nb], in_=ps[:, :nb], func=AF.Square), [pt], [gat])
            S.add("dve", lambda e: e.tensor_scalar(out=ga[:, :nb], in0=ga[:, :nb], scalar1=0.044715, scalar2=1.0,
                                                   op0=ALU.mult, op1=ALU.add), [gat], [gat])
            S.add("dve", lambda e, ps=ps: e.tensor_tensor(out=ga[:, :nb], in0=ga[:, :nb], in1=ps[:, :nb], op=ALU.mult),
                  [gat, pt], [gat])
            S.add("act", lambda e: e.activation(out=gb[:, :nb], in_=ga[:, :nb], func=AF.Sigmoid, scale=1.5957691216057308),
                  [gat], [gbt])
            S.add("dve", lambda e, ps=ps, a=a: e.tensor_tensor(out=GT[:, a, :nb], in0=gb[:, :nb], in1=ps[:, :nb],
                                                               op=ALU.mult), [gbt, pt], accw=[GTt])
        ps, pt = next_ps(C)
        for t in range(2):
            S.add("pe", lambda e, ps=ps, t=t: e.matmul(ps[:, :nb], lhsT=w2b[:, t, 0, :], rhs=GT[:, t, :nb],
                                                       start=(t == 0), stop=(t == 1)), [w2bt, GTt], [pt])
        norm_evac(C, ps[:, :nb], pt, gains, gtok, 9, onesbd, onest, tmps,
                  [(kclo[0:64, :nb], 0, 64, kct), (kchi[64:128, :nb], 64, 128, kct)])
        for ct in range(NCT):
            ps, pt = next_ps(C)
            wdt = min(128, nb)
            for t in range(2):
                S.add("pe", lambda e, ps=ps, t=t, ct=ct, wdt=wdt: e.matmul(
                    ps[:wdt, 0:64], lhsT=GT[:, 2 + t, ct * 128:ct * 128 + wdt], rhs=w2b[:, t, 1, 0:64],
                    start=(t == 0), stop=(t == 1)), [w2bt, GTt], [pt])
            S.add("dve", lambda e, ps=ps, ct=ct, wdt=wdt: e.tensor_copy(out=Vc1[:wdt, ct, 0:64], in_=ps[:wdt, 0:64]),
                  [pt], accw=[vct])
    kc_scope.__exit__(None, None, None)
    with S.scope():
        wexp = sb("n_wexp", [128, SL], BF16)
        triT4 = sb("n_triT4", [128, 512], BF16)
        triU4 = sb("n_triU4", [128, 512], BF16)
        cmQ = sb("n_cmQ", [128, 16, 128], BF16)
        cmT = sb("n_cmT", [128, 16, 128], BF16)
        wet, trt, trut, cmqt, cmtt = [Tok() for _ in range(5)]
        S.dma(wexp[:], cd["wexp"], writes=[wet])
        S.dma(triT4[:], cd["triT"], writes=[trt])
        S.dma(triU4[:], cd["triU"], writes=[trut])
        S.dma(cmQ[:], cd["cmaskQ"], writes=[cmqt])
        S.dma(cmT[:], cd["cmaskT"], writes=[cmtt])
        E4 = sb("n_E4", [128, 4, 512], F32)
        rsum = sb("n_rsum", [128, 4], F32)
        rinv = sb("n_rinv", [128, 4], F32)
        imp = sb("n_imp", [128, 512], F32)
        ib = sb("n_ib", [128, 128], F32)
        sc = sb("n_sc", [128, 128], F32)
        sc2 = sb("n_sc2", [128, 128], F32)
        m8a = sb("n_m8a", [128, 8], F32)
        m8b = sb("n_m8b", [128, 8], F32)
        nmf = sb("n_nmf", [128, 128], F32)
        nmb = sb("n_nmb", [128, 128], BF16)
        nmT4 = sb("n_nmT4", [128, 512], BF16)
        PT = [sb("n_PT%d" % i, [128, 512], BF16) for i in range(3)]
        PTt = [Tok() for _ in range(3)]
        den = sb("n_den", [128, 4], F32)
        coef = sb("n_coef", [128, 4], F32)
        acc = sb("n_acc", [128, 4, 64], F32)
        ob = [sb("n_ob%d" % i, [128, 256], BF16) for i in range(2)]
        obt = [Tok(), Tok()]
        e4t, rsumt, rinvt, impt, ibt_, sct, sc2t, m8at, m8bt, nmft, nmbt, nmTt, dent, coeft, acct = [Tok() for _ in range(15)]
        npool = len(C.ps)
        Obank = [(C.ps[npool - 2], C.pstok[npool - 2]), (C.ps[npool - 1], C.pstok[npool - 1])]
        C.ps_active = npool - 2
        on = 0
        ptn = 0

        def att_branch(tiles, br, qt, first_branch):
            nonlocal on, ptn
            qs = slice(qt * 128, (qt + 1) * 128)
            O, Ot = Obank[on % 2]
            on += 1
            nt = len(tiles)
            def scores(idx):
                klo, khi, ktoks, v, vtok, masks = tiles[idx]
                psT, ptT = next_ps(C)
                first = True
                for (ml, mlt, mr, mrt, wide) in masks:
                    if wide:
                        S.add("pe", lambda e, psT=psT, ml=ml, mr=mr, first=first: e.matmul(
                            psT[:, 0:512], lhsT=ml, rhs=mr, start=first, stop=False, skip_group_check=True),
                            [mlt, mrt], [ptT])
                        first = False
                    else:
                        for cb in range(4):
                            S.add("pe", lambda e, psT=psT, ml=ml, mr=mr, cb=cb, first=first: e.matmul(
                                psT[:, cb * 128:(cb + 1) * 128], lhsT=ml, rhs=mr, start=first, stop=False,
                                skip_group_check=True), [mlt, mrt], [ptT])
                            first = False
                S.add("pe", lambda e, psT=psT, klo=klo, qs=qs, first=first: e.matmul(
                    psT[:, 0:256], lhsT=klo, rhs=QT[:, :, qs], start=first, stop=False, skip_group_check=True),
                    [ktoks, qtok], [ptT])
                S.add("pe", lambda e, psT=psT, khi=khi, qs=qs: e.matmul(
                    psT[:, 256:512], lhsT=khi, rhs=QT[:, :, qs], start=False, stop=True, skip_group_check=True),
                    [ktoks, qtok], [ptT])
                return psT, ptT

            pend = scores(0)
            for idx in range(nt):
                psT, ptT = pend
                if idx + 1 < nt:
                    pend = scores(idx + 1)
                v, vtok = tiles[idx][3], tiles[idx][4]
                P, Pt_ = PT[ptn % 3], PTt[ptn % 3]
                ptn += 1
                S.add("act", lambda e, psT=psT, P=P: e.activation(out=P[:], in_=psT[:], func=AF.Exp, scale=0.125),
                      [ptT], [Pt_])
                for cb in range(4):
                    S.add("pe", lambda e, O=O, P=P, v=v, cb=cb, idx=idx: e.matmul(
                        O[:, cb * 65:(cb + 1) * 65], lhsT=P[:, cb * 128:(cb + 1) * 128], rhs=v,
                        start=(idx == 0 and cb == 0), stop=(idx == nt - 1), skip_group_check=True), [Pt_, vtok], [Ot])
            Ov = O[:, 0:260].rearrange("p (c d) -> p c d", c=4)
            S.add("dve", lambda e, Ov=Ov: e.tensor_scalar(out=den[:], in0=Ov[:, :, 64], scalar1=1e-30, scalar2=None,
                                                          op0=ALU.max), [Ot], [dent])
            S.add("dve", lambda e: e.reciprocal(out=den[:], in_=den[:]), [dent], [dent])
            gv = Gt[:, qt, br * 4:(br + 1) * 4].rearrange("p (a b) -> p b a", a=2)
            S.add("dve", lambda e, gv=gv: e.tensor_tensor(out=coef[:].rearrange("p (b a) -> p b a", b=2),
                                                          in0=den[:].rearrange("p (b a) -> p b a", b=2), in1=gv,
                                                          op=ALU.mult), [dent, gtt], [coeft])
            for cb in range(4):
                h = HORD[cb]
                if first_branch:
                    S.add("dve", lambda e, Ov=Ov, cb=cb, h=h: e.tensor_scalar(
                        out=acc[:, h, :], in0=Ov[:, cb, 0:64], scalar1=coef[:, cb:cb + 1], scalar2=None,
                        op0=ALU.mult), [Ot, coeft], [acct])
                else:
                    S.add("dve", lambda e, Ov=Ov, cb=cb, h=h: e.scalar_tensor_tensor(
                        out=acc[:, h, :], in0=Ov[:, cb, 0:64], scalar=coef[:, cb:cb + 1], in1=acc[:, h, :],
                        op0=ALU.mult, op1=ALU.add), [Ot, coeft, acct], [acct])

        for qt in range(NT if DBG_NQT is None else DBG_NQT):
            bg = getattr(C, "bg", None)
            if bg is not None:
                for _ in range(C.bg_per_tile):
                    next(bg, None)
            qs = slice(qt * 128, (qt + 1) * 128)
            ctl = (8 * qt + 6) // 128
            ncol = 128 * (ctl + 1)
            r16 = qt % 16
            for cb, (p, Kc) in enumerate(((0, kclo), (1, kclo), (0, kchi), (1, kchi))):
                psS, ptS = next_ps(C)
                first = True
                if ctl > 0:
                    S.add("pe", lambda e, psS=psS, p=p, Kc=Kc, qs=qs, ctl=ctl: e.matmul(
                        psS[:, 0:ctl * 128], lhsT=QT[:, p, qs], rhs=Kc[:, 0:ctl * 128], start=True, stop=False,
                        skip_group_check=True), [qtok, kct], [ptS])
                    first = False
                S.add("pe", lambda e, psS=psS, ctl=ctl, ncol=ncol, r16=r16, first=first: e.matmul(
                    psS[:, ctl * 128:ncol], lhsT=identb[:], rhs=cmQ[:, r16, :], start=first, stop=False,
                    skip_group_check=True), [ibt, cmqt], [ptS])
                S.add("pe", lambda e, psS=psS, p=p, Kc=Kc, qs=qs, ctl=ctl, ncol=ncol: e.matmul(
                    psS[:, ctl * 128:ncol], lhsT=QT[:, p, qs], rhs=Kc[:, ctl * 128:ncol], start=False, stop=True,
                    skip_group_check=True), [qtok, kct], [ptS])
                S.add("act", lambda e, psS=psS, cb=cb, ncol=ncol: e.activation(
                    out=E4[:, cb, :ncol], in_=psS[:, :ncol], func=AF.Exp, scale=0.125, accum_out=rsum[:, cb:cb + 1]),
                    [ptS], accw=[e4t, rsumt])
            S.add("dve", lambda e: e.tensor_scalar(out=rinv[:], in0=rsum[:], scalar1=1e-30, scalar2=None, op0=ALU.max),
                  [rsumt], [rinvt])
            S.add("dve", lambda e: e.reciprocal(out=rinv[:], in_=rinv[:]), [rinvt], [rinvt])
            S.add("dve", lambda e, ncol=ncol: e.tensor_scalar(out=imp[:, :ncol], in0=E4[:, 0, :ncol], scalar1=rinv[:, 0:1],
                                                             scalar2=None, op0=ALU.mult), [e4t, rinvt], [impt])
            for cb in range(1, 4):
                S.add("dve", lambda e, cb=cb, ncol=ncol: e.scalar_tensor_tensor(
                    out=imp[:, :ncol], in0=E4[:, cb, :ncol], scalar=rinv[:, cb:cb + 1], in1=imp[:, :ncol],
                    op0=ALU.mult, op1=ALU.add), [e4t, rinvt, impt], [impt])
            nblk = ncol // 4
            S.add("dve", lambda e, ncol=ncol, nblk=nblk: e.tensor_reduce(
                out=ib[:, :nblk], in_=imp[:, :ncol].rearrange("p (j r) -> p j r", r=4), axis=AX.X, op=ALU.add),
                [impt], [ibt_])
            S.add("dve", lambda e, nblk=nblk: e.tensor_tensor(
                out=ib[:, 1:nblk], in0=ib[:, 1:nblk],
                in1=imp[:, 0:4 * (nblk - 1)].rearrange("p (j r) -> p j r", r=4)[:, :, 3], op=ALU.add),
                [impt, ibt_], [ibt_])
            S.add("pool", lambda e: e.memset(sc[:], -1e30), [], [sct])
            if qt > 0:
                S.add("dve", lambda e, qt=qt: e.tensor_copy(out=sc[:, 0:2 * qt], in_=ib[:, 0:2 * qt]), [ibt_, sct], [sct])
                S.add("dve", lambda e, qt=qt: e.memset(sc[0:64, 2 * qt - 1:2 * qt], 1e4), [sct], [sct])
            S.add("dve", lambda e: e.memset(sc[:, 0:1], 1e4), [sct], [sct])
            S.add("dve", lambda e, qt=qt: e.memset(sc[:, 2 * qt:2 * qt + 1], 1e4), [sct], [sct])
            S.add("dve", lambda e, qt=qt: e.memset(sc[64:128, 2 * qt + 1:2 * qt + 2], 1e4), [sct], [sct])
            S.add("dve", lambda e: e.max(out=m8a[:], in_=sc[:]), [sct], [m8at])
            S.add("dve", lambda e: e.match_replace(out=sc2[:], in_to_replace=m8a[:], in_values=sc[:], imm_value=-1e30),
                  [sct, m8at], [sc2t])
            S.add("dve", lambda e: e.max(out=m8b[:], in_=sc2[:]), [sc2t], [m8bt])
            S.add("dve", lambda e: e.tensor_scalar(out=nmf[:], in0=sc[:], scalar1=m8b[:, 7:8], scalar2=None,
                                                   op0=ALU.is_ge), [sct, m8bt], [nmft])
            S.add("dve", lambda e: e.tensor_scalar(out=nmb[:], in0=nmf[:], scalar1=-1.0, scalar2=-NEG,
                                                   op0=ALU.add, op1=ALU.mult), [nmft], [nmbt])
            tiles = []
            for ct in range(ctl + 1):
                cs = slice(ct * 128, (ct + 1) * 128)
                masks = [(identb[:], ibt, cmT[:, r16, :], cmtt, False)] if ct == ctl else []
                tiles.append((kclo[:, cs], kchi[:, cs], kct, Vc1[:, ct, :], vct, masks))
            att_branch(tiles, 0, qt, True)
            tiles = []
            for kt in range(max(0, qt - 4), qt + 1):
                ks_ = slice(kt * 128, (kt + 1) * 128)
                masks = []
                if kt == qt:
                    masks.append((identb[:], ibt, triT4[:], trt, True))
                if kt == qt - 4:
                    masks.append((identb[:], ibt, triU4[:], trut, True))
                tiles.append((Kwlo[:, ks_], Kwhi[:, ks_], kwt, V1[:, kt, 1, :], v1t, masks))
            att_branch(tiles, 2, qt, False)
            pb, pbt = next_psb(C)
            S.add("pe", lambda e, pb=pb: e.transpose(out=pb[:, 0:128], in_=nmb[:], identity=identb[:]), [nmbt, ibt], [pbt])
            S.add("dve", lambda e, pb=pb: e.tensor_copy(out=nmT4[:, 0:128], in_=pb[:, 0:128]), [pbt], [nmTt])
            S.add("dve", lambda e: e.tensor_copy(out=nmT4[:, 128:256], in_=nmT4[:, 0:128]), [nmTt], [nmTt])
            S.add("dve", lambda e: e.tensor_copy(out=nmT4[:, 256:512], in_=nmT4[:, 0:256]), [nmTt], [nmTt])
            tiles = []
            for kt in range(qt + 1):
                ks_ = slice(kt * 128, (kt + 1) * 128)
                masks = [(wexp[:, ks_], wet, nmT4[:], nmTt, True)]
                if kt == qt:
                    masks.append((identb[:], ibt, triT4[:], trt, True))
                tiles.append((Kslo[:, ks_], Kshi[:, ks_], kst, V1[:, kt, 0, :], v1t, masks))
            att_branch(tiles, 1, qt, False)
            o_, ot_ = ob[qt % 2], obt[qt % 2]
            S.add("act", lambda e, o_=o_: e.activation(out=o_[:], in_=acc[:].rearrange("p h d -> p (h d)"), func=AF.Copy),
                  [acct], [ot_])
            S.dma(ao[qs, 0:256], o_[:], reads=[ot_], accw=[ao_tok], q="act")
        C.ps_active = npool


from concourse.bass_utils import run_bass_kernel_spmd

_NC_CACHE = {}


def _get_nc(key, builder):
    if key not in _NC_CACHE:
        _NC_CACHE[key] = builder()
    return _NC_CACHE[key]


def kernel(**inputs):
    z = {k: np.asarray(v) for k, v in inputs.items()}
    x = np.ascontiguousarray(z["x"], np.float32)
    B, SL, D = x.shape
    depth = z["w_in"].shape[0]
    ncA = _get_nc("A", lambda: build_phaseA(SL))
    ncB = _get_nc("B", lambda: build_phaseB(SL // 2))
    constsA = [hostA_consts(g, SL) for g in range(2)]
    ident = np.eye(128, dtype=np.float32)
    for L in range(depth):
        insA = []
        wsm = [hostA_weights(z["w_in"][L], g) for g in range(2)]
        prm = [hostA_params(z, L, g) for g in range(2)]
        for c in range(8):
            b, g = c // 2, c % 2
            d = dict(x=np.ascontiguousarray(x[b]), WS=wsm[g][0], WM=wsm[g][1])
            d.update(constsA[g])
            d.update(prm[g])
            insA.append(d)
        resA = run_bass_kernel_spmd(ncA, insA, core_ids=list(range(8)))
        ao = [np.asarray(resA.results[c]["ao"]) for c in range(8)]

        def gl(v):
            return np.ascontiguousarray(np.asarray(v, np.float32).reshape(8, 128).T)
        gains = np.ascontiguousarray(np.concatenate(
            [gl(z["norm_mix"][L]), gl(z["norm_mlp"][L]), gl(z["norm_ple"][L])], 1), np.float32)
        w_merge = np.ascontiguousarray(z["w_in"][L][:, 3352:5400], np.float32)
        insB = []
        for c in range(8):
            b, hf = c // 2, c % 2
            sl = slice(hf * (SL // 2), (hf + 1) * (SL // 2))
            attn = np.concatenate([ao[2 * b][sl, :256], ao[2 * b + 1][sl, :256],
                                   ao[2 * b][sl, 256:], ao[2 * b + 1][sl, 256:]], 1)
            insB.append(dict(
                x=np.ascontiguousarray(x[b, sl]), attn=np.ascontiguousarray(attn),
                p=np.ascontiguousarray(z["p"][L, b, sl], np.float32), gains=gains, ident=ident,
                w_merge=w_merge, w_up_nsa=np.ascontiguousarray(z["w_up_nsa"][L], np.float32),
                w_up_ret=np.ascontiguousarray(z["w_up_ret"][L], np.float32),
                w_out=np.ascontiguousarray(z["w_out"][L], np.float32),
                w_ff1=np.ascontiguousarray(z["w_ff1"][L], np.float32),
                w_ff2=np.ascontiguousarray(z["w_ff2"][L], np.float32),
                w_gate=np.ascontiguousarray(z["w_ple_gate"][L], np.float32),
                w_ple=np.ascontiguousarray(z["w_ple"][L], np.float32)))
        resB = run_bass_kernel_spmd(ncB, insB, core_ids=list(range(8)))
        xn = np.empty_like(x)
        for c in range(8):
            b, hf = c // 2, c % 2
            xn[b, hf * (SL // 2):(hf + 1) * (SL // 2)] = np.asarray(resB.results[c]["xo"])
        x = xn
    return x


B_WNAMES = (("w_merge", 1024, 2048), ("w_up_nsa", 512, 1024), ("w_up_ret", 512, 1024), ("w_out", 1024, 1024),
            ("w_ff1", 1024, 4096), ("w_ff2", 4096, 1024), ("w_gate", 1024, 1024), ("w_ple", 256, 1024))
PAIR_GROUPS = [[0, 1], [2, 3], [4, 5], [6, 7]]


def build_fused(SL=8192, depth=2):
    nc = bass.Bass("TRN2", target_bir_lowering=False)
    dt = nc.dram_tensor
    T = SL // 2
    x_full = dt("x", [SL, 1024], F32, kind="ExternalInput").ap()
    xh = dt("xh", [T, 1024], F32, kind="ExternalInput").ap()
    hmask_d = dt("hmask", [128, 2], F32, kind="ExternalInput").ap()
    cd = {k: dt(k, sh, ty, kind="ExternalInput").ap() for k, (sh, ty) in A_CONST_SHAPES(SL).items()}
    WS_d, WM_d, pd, p_d, gB_d, wd = [], [], [], [], [], []
    for L in range(depth):
        WS_d.append(dt("WS%d" % L, [1024, 2048], F32, kind="ExternalInput").ap())
        WM_d.append(dt("WM%d" % L, [1024, 1536], F32, kind="ExternalInput").ap())
        pd.append({k: dt("%s%d" % (k, L), sh, F32, kind="ExternalInput").ap() for k, sh in A_PARAM_SHAPES.items()})
        p_d.append(dt("p%d" % L, [T, 256], F32, kind="ExternalInput").ap())
        gB_d.append(dt("gainsB%d" % L, [128, 24], F32, kind="ExternalInput").ap())
        wd.append({n: dt("%s%d" % (n, L), [K, N], F32, kind="ExternalInput").ap() for n, K, N in B_WNAMES})
    out = dt("xo", [T, 1024], F32, kind="ExternalOutput").ap()
    ao = [dt("ao%d" % L, [SL, 512], BF16, kind="Internal").ap() for L in range(depth)]
    aog = [dt("aog%d" % L, [2 * SL, 512], BF16, kind="Internal").ap() for L in range(depth)]
    xmid = [dt("xmid%d" % L, [T, 1024], F32, kind="Internal").ap() for L in range(depth - 1)]
    xg = [dt("xg%d" % L, [SL, 1024], F32, kind="Internal").ap() for L in range(depth - 1)]
    S = Sched(nc)
    C = make_pools(S, n_wbuf=3)
    xg_tok = None
    xmid_tok = None
    for L in range(depth):
        S.prefix = "L%dB_" % L
        WB = make_B_wspecs(S, wd[L])
        C.bg = prep_B_gen(C, WB)
        C.bg_per_tile = -(-212 // (SL // 128)) + 1
        S.prefix = "L%dA_" % L
        aot, aogt = Tok(), Tok()
        with S.scope():
            XK = min(512, T)
            xmap = None if L == 0 else (lambda t: 2 * ((t % T) // XK) * XK + (t // T) * XK + (t % T) % XK)
            emit_phaseA(C, SL, x_full if L == 0 else xg[L - 1], WS_d[L], WM_d[L], cd, pd[L], ao[L],
                        xin_tok=xg_tok, ao_tok=aot, xmap=xmap)
        for _ in C.bg:
            pass
        C.bg = None
        RK = min(2048, SL)
        for k in range(SL // RK):
            S.collective("AllGather", ao[L][k * RK:(k + 1) * RK, :].opt(), aog[L][2 * k * RK:2 * (k + 1) * RK, :].opt(),
                         PAIR_GROUPS, reads=[aot], accw=[aogt])
        S.prefix = "L%dB_" % L
        with S.scope():
            hm = S.sbuf("hm", [128, 2], F32)
            hmt = Tok()
            S.dma(hm[:], hmask_d, writes=[hmt])
            atAB = [S.sbuf("atAB%d" % i, [128, 4, 1024], BF16) for i in range(2)]
            atABt = [Tok(), Tok()]
            nw = len(C.wbuf)
            C.wbuf = C.wbuf + [S.sbuf("wbufx%d" % i, [128, WU_ELEMS], BF16) for i in range(1)]
            C.wtok = C.wtok + [Tok() for _ in range(1)]
            last = (L == depth - 1)
            xo_tok = Tok()
            emit_phaseB(C, T, xh if L == 0 else xmid[L - 1], aog[L], p_d[L], gB_d[L], cd["ident"], wd[L],
                        out if last else xmid[L], xin_tok=xmid_tok, attn_tok=aogt, xo_tok=xo_tok,
                        gathered=(SL, hm, hmt, atAB, atABt), W=WB)
            C.wbuf = C.wbuf[:nw]
            C.wtok = C.wtok[:nw]
        if not last:
            xmid_tok = xo_tok
            xg_tok = Tok()
            XK = min(512, T)
            for k in range(T // XK):
                S.collective("AllGather", xmid[L][k * XK:(k + 1) * XK, :].opt(),
                             xg[L][2 * k * XK:2 * (k + 1) * XK, :].opt(), PAIR_GROUPS, reads=[xo_tok], accw=[xg_tok])
    S.emit()
    S.close()
    return nc


def fused_inputs(z, SL, depth):
    import ml_dtypes
    x = np.ascontiguousarray(z["x"], np.float32)
    T = SL // 2
    consts = [hostA_consts(g, SL) for g in range(2)]

    def gl(v):
        return np.ascontiguousarray(np.asarray(v, np.float32).reshape(8, 128).T)
    per_layer = []
    for L in range(depth):
        d = {}
        d["wsm"] = [hostA_weights(z["w_in"][L], g) for g in range(2)]
        d["prm"] = [hostA_params(z, L, g) for g in range(2)]
        d["gainsB"] = np.ascontiguousarray(np.concatenate(
            [gl(z["norm_mix"][L]), gl(z["norm_mlp"][L]), gl(z["norm_ple"][L])], 1), np.float32)
        d["w"] = dict(
            w_merge=np.ascontiguousarray(z["w_in"][L][:, 3352:5400], np.float32),
            w_up_nsa=np.ascontiguousarray(z["w_up_nsa"][L], np.float32),
            w_up_ret=np.ascontiguousarray(z["w_up_ret"][L], np.float32),
            w_out=np.ascontiguousarray(z["w_out"][L], np.float32),
            w_ff1=np.ascontiguousarray(z["w_ff1"][L], np.float32),
            w_ff2=np.ascontiguousarray(z["w_ff2"][L], np.float32),
            w_gate=np.ascontiguousarray(z["w_ple_gate"][L], np.float32),
            w_ple=np.ascontiguousarray(z["w_ple"][L], np.float32))
        per_layer.append(d)
    ins = []
    for c in range(8):
        b, r = c // 2, c % 2
        sl = slice(r * T, (r + 1) * T)
        d = dict(x=np.ascontiguousarray(x[b, :SL]), xh=np.ascontiguousarray(x[b, sl]))
        hm = np.zeros((128, 2), np.float32)
        hm[:, r] = 1.0
        d["hmask"] = hm
        d.update(consts[r])
        for L in range(depth):
            pl = per_layer[L]
            d["WS%d" % L], d["WM%d" % L] = pl["wsm"][r]
            for k, v in pl["prm"][r].items():
                d["%s%d" % (k, L)] = v
            d["p%d" % L] = np.ascontiguousarray(z["p"][L, b, sl], np.float32)
            d["gainsB%d" % L] = pl["gainsB"]
            for k, v in pl["w"].items():
                d["%s%d" % (k, L)] = v
        ins.append(d)
    return ins


def kernel(**inputs):
    z = {k: np.asarray(v) for k, v in inputs.items()}
    B, SL, D = z["x"].shape
    depth = z["w_in"].shape[0]
    nc = _get_nc(("F", SL, depth), lambda: build_fused(SL, depth))
    ins = fused_inputs(z, SL, depth)
    res = run_bass_kernel_spmd(nc, ins, core_ids=list(range(8)))
    T = SL // 2
    out = np.empty((B, SL, D), np.float32)
    for c in range(8):
        b, r = c // 2, c % 2
        out[b, r * T:(r + 1) * T] = np.asarray(res.results[c]["xo"])
    return out
```

```python
from contextlib import ExitStack
import numpy as np
import concourse.bass as bass
import concourse.mybir as mybir

F32 = mybir.dt.float32
BF16 = mybir.dt.bfloat16
I32 = mybir.dt.int32
AF = mybir.ActivationFunctionType
ALU = mybir.AluOpType
AX = mybir.AxisListType

ENGS = ("pe", "act", "dve", "pool", "sp")
N_DMA_SEMS = 24


class Tok:
    __slots__ = ("lws", "rs", "base", "name", "excl", "accgrp")

    def __init__(self, name="", excl=False):
        self.excl = excl
        self.accgrp = False
        self.lws = []
        self.rs = []
        self.base = []
        self.name = name


class Op:
    __slots__ = ("eng", "fn", "deps", "dma", "idx", "sig", "dma_n", "cc")

    def __init__(self, eng, fn, deps, dma, idx):
        self.eng = eng
        self.fn = fn
        self.deps = deps
        self.dma = dma
        self.idx = idx
        self.sig = None
        self.dma_n = None
        self.cc = None


class _Scope:
    def __init__(self, S):
        self.S = S

    def __enter__(self):
        self.saved = self.S.stack
        self.S.stack = ExitStack()
        return self

    def __exit__(self, *a):
        self.S.barrier()
        self.S.stack.close()
        self.S.stack = self.saved
        return False


class Sched:
    def __init__(self, nc):
        self.nc = nc
        self.ops = {e: [] for e in ENGS}
        self.ndma = {e: 0 for e in ENGS}
        self.final_waits = []
        self.all_dma = []
        self.ncc = 0
        self.prefix = ""
        self.stack = ExitStack()

    def sbuf(self, name, shape, dtype):
        return self.stack.enter_context(self.nc.sbuf_tensor("sb_" + self.prefix + name, list(shape), dtype))

    def psum(self, name, shape, dtype):
        return self.stack.enter_context(self.nc.psum_tensor("pp_" + name, list(shape), dtype))

    def add(self, eng, fn, reads=(), writes=(), dma=False, accw=(), extra=()):
        deps = []
        seen = set()

        def push(d):
            if d is not None and d not in seen:
                seen.add(d)
                deps.append(d)

        for d in extra:
            push(d)
        for t in reads:
            for w in t.lws:
                push(w)
            if t.excl:
                for r in t.rs:
                    if r[0] != eng:
                        push(r)
        for t in writes:
            for w in t.lws:
                push(w)
            for r in t.rs:
                push(r)
        for t in accw:
            if t.rs or not t.lws or not t.accgrp:
                for w in t.lws:
                    push(w)
                for r in t.rs:
                    push(r)
            else:
                for d in t.base:
                    push(d)
        lst = self.ops[eng]
        op = Op(eng, fn, deps, dma, len(lst))
        if dma:
            op.dma_n = self.ndma[eng]
            self.ndma[eng] += 1
            self.all_dma.append((eng, op.idx))
        lst.append(op)
        me = (eng, op.idx)
        for t in reads:
            t.rs.append(me)
        for t in writes:
            t.lws = [me]
            t.rs = []
            t.base = []
            t.accgrp = False
        for t in accw:
            if t.rs or not t.lws or not t.accgrp:
                t.base = list(t.lws) + list(t.rs)
                t.lws = [me]
                t.rs = []
                t.accgrp = True
            else:
                t.lws.append(me)
        return op

    def collective(self, kind, src, dst, groups, reads=(), writes=(), accw=()):
        op = self.add("pool", lambda e: e.collective_compute(kind, ALU.bypass, replica_groups=groups,
                                                             ins=[src], outs=[dst]), reads, writes, accw=accw)
        self.ncc += 1
        op.cc = self.ncc
        return op

    def barrier(self):
        extra = list(self.all_dma)
        for e in ENGS:
            if self.ops[e]:
                extra.append((e, len(self.ops[e]) - 1))
        self.all_dma = []
        b0 = self.add("sp", lambda e: e.nop(), extra=extra)
        me = ("sp", b0.idx)
        for e in ("pe", "act", "dve", "pool"):
            self.add(e, lambda eng: eng.nop(), extra=[me])

    def dma(self, out, in_, reads=(), writes=(), q="sp", accw=(), **kw):
        return self.add(q, lambda e: e.dma_start(out=out, in_=in_, **kw), reads, writes, dma=True, accw=accw)

    def scope(self):
        return _Scope(self)

    def emit(self):
        nc = self.nc
        ops = self.ops
        needed = {e: set() for e in ENGS}
        waits = {e: [] for e in ENGS}
        for e in ENGS:
            maxw = {d: -1 for d in ENGS}
            dma_waited = set()
            for op in ops[e]:
                keep = []
                for (de, di) in op.deps:
                    dop = ops[de][di]
                    if dop.dma or dop.cc:
                        if (de, di) in dma_waited:
                            continue
                        dma_waited.add((de, di))
                        keep.append((de, di))
                    else:
                        if de == e and e == "pe":
                            continue
                        if de == e and di == op.idx:
                            continue
                        if di <= maxw[de]:
                            continue
                        maxw[de] = di
                        keep.append((de, di))
                        needed[de].add(di)
                waits[e].append(keep)
        for e in ENGS:
            c = 0
            for op in ops[e]:
                if (not op.dma) and (not op.cc) and op.idx in needed[e]:
                    c += 1
                    op.sig = c
        st = self.stack
        csem = {e: st.enter_context(nc.semaphore("c_" + e)) for e in ENGS}
        ccsem = st.enter_context(nc.semaphore("cc_sem"))
        dsem = {e: [st.enter_context(nc.semaphore("d_%s_%d" % (e, i))) for i in range(N_DMA_SEMS)]
                for e in ENGS if self.ndma[e] > 0}
        block = st.enter_context(nc.Block())

        def gen(e, eng):
            for op, keep in zip(ops[e], waits[e]):
                if op.dma:
                    n = op.dma_n
                    if n >= N_DMA_SEMS:
                        eng.wait_ge(dsem[e][n % N_DMA_SEMS], 16 * (n // N_DMA_SEMS))
                for (de, di) in keep:
                    dop = ops[de][di]
                    if dop.dma:
                        n = dop.dma_n
                        eng.wait_ge(dsem[de][n % N_DMA_SEMS], 16 * (n // N_DMA_SEMS + 1))
                    elif dop.cc:
                        eng.wait_ge(ccsem, dop.cc)
                    else:
                        eng.wait_ge(csem[de], dop.sig)
                ins = op.fn(eng)
                if op.dma:
                    n = op.dma_n
                    ins.then_inc(dsem[e][n % N_DMA_SEMS], 16)
                elif op.cc:
                    ins.then_inc(ccsem, 1)
                elif op.sig is not None:
                    ins.then_inc(csem[e], 1)
            if e == "pool" and self.ncc:
                eng.wait_ge(ccsem, self.ncc)
            nd = self.ndma[e]
            for i in range(min(nd, N_DMA_SEMS)):
                cnt = (nd - 1 - i) // N_DMA_SEMS + 1
                eng.wait_ge(dsem[e][i], 16 * cnt)

        @block.tensor
        def _(eng):
            gen("pe", eng)

        @block.scalar
        def _(eng):
            gen("act", eng)

        @block.vector
        def _(eng):
            gen("dve", eng)

        @block.gpsimd
        def _(eng):
            gen("pool", eng)

        @block.sync
        def _(eng):
            gen("sp", eng)

    def close(self):
        self.stack.close()


D_MODEL = 1024
EPS = 1e-6
WU_ELEMS = 4096


class WSpec:
    def __init__(self, S, name, w_ap, K, N, kind):
        self.name, self.K, self.N, self.kind = name, K, N, kind
        self.KT = K // 128
        self.w = w_ap
        nc = S.nc
        if kind == "S":
            assert N % 512 == 0
            self.nunits = N // 512
            self.uelems = 4 * self.KT * 128
        else:
            assert N % 512 == 0
            self.KTU = min(8, self.KT)
            self.nv = self.KT // self.KTU
            self.nunits = (N // 512) * self.nv
            self.uelems = self.KTU * 512
        assert self.uelems <= WU_ELEMS
        self.scr = nc.dram_tensor("scr_" + S.prefix + name, [self.nunits, 128, self.uelems], BF16, kind="Internal").ap()
        self.tok = Tok("scr_" + name)

    def unit_src(self, u):
        return self.scr[u]

    def view(self, buf):
        b = buf[:, : self.uelems]
        if self.kind == "S":
            return b.rearrange("p (f k c) -> p f k c", f=4, k=self.KT)
        return b.rearrange("p (k c) -> p k c", k=self.KTU)


class Ctx:
    pass


def make_pools(S, n_wbuf=5, n_ps=6):
    C = Ctx()
    C.S = S
    C.wbuf = [S.sbuf("wbuf%d" % i, [128, WU_ELEMS], BF16) for i in range(n_wbuf)]
    C.wtok = [Tok("wbuf%d" % i) for i in range(n_wbuf)]
    C.wn = 0
    C.ps = [S.psum("ps%d" % i, [128, 512], F32) for i in range(n_ps)]
    C.pstok = [Tok("ps%d" % i, excl=True) for i in range(n_ps)]
    C.pn = 0
    C.psb = [S.psum("psb%d" % i, [128, 1024], BF16) for i in range(2)]
    C.psbtok = [Tok("psb0", excl=True), Tok("psb1", excl=True)]
    C.pbn = 0
    C.stg = [S.sbuf("stg%d" % i, [128, 512], F32) for i in range(3)]
    C.stgtok = [Tok() for _ in range(3)]
    C.stgb = [S.sbuf("stgb%d" % i, [128, 512], BF16) for i in range(3)]
    C.stgbtok = [Tok() for _ in range(3)]
    C.sn = 0
    C.cast_rr = 0
    return C


def next_ps(C):
    i = C.pn % getattr(C, "ps_active", len(C.ps))
    C.pn += 1
    return C.ps[i], C.pstok[i]


def next_psb(C):
    i = C.pbn % 2
    C.pbn += 1
    return C.psb[i][:, 0:512], C.psbtok[i]


def load_unit(C, ws, u, q="sp"):
    i = C.wn % len(C.wbuf)
    C.wn += 1
    buf, tok = C.wbuf[i], C.wtok[i]
    C.S.dma(buf[:, : ws.uelems], ws.unit_src(u), reads=[ws.tok], writes=[tok], q=q)
    return ws.view(buf), tok


def prep_weight_gen(C, ws, q="act"):
    S = C.S
    K, N, KT = ws.K, ws.N, ws.KT
    for kt in range(KT):
        for c0 in range(0, N, 512):
            cw = min(512, N - c0)
            i = C.sn % 3
            C.sn += 1
            st, stt, sb, sbt = C.stg[i], C.stgtok[i], C.stgb[i], C.stgbtok[i]
            S.dma(st[:, :cw], ws.w[kt * 128:(kt + 1) * 128, c0:c0 + cw], writes=[stt])
            eng = ("dve", "pool")[C.cast_rr % 2] if q == "act" else "pool"
            C.cast_rr += 1
            S.add(eng, lambda e, sb=sb, st=st, cw=cw: e.tensor_copy(out=sb[:, :cw], in_=st[:, :cw]), [stt], [sbt])
            if ws.kind == "S":
                u0, nu = c0 // 512, cw // 512
                dst = ws.scr[u0:u0 + nu].rearrange("u p (f k c) -> p u f k c", f=4, k=KT)[:, :, :, kt, :]
                src = sb[:, :cw].rearrange("p (u f c) -> p u f c", u=nu, f=4)
                for uu in range(nu):
                    S.dma(dst[:, uu], src[:, uu], reads=[sbt], accw=[ws.tok], q=q)
            else:
                v, kk = kt // ws.KTU, kt % ws.KTU
                n0, nn = c0 // 512, cw // 512
                for n in range(nn):
                    u = (n0 + n) * ws.nv + v
                    dst = ws.scr[u].rearrange("p (k c) -> p k c", k=ws.KTU)[:, kk, :]
                    S.dma(dst, sb[:, n * 512:(n + 1) * 512], reads=[sbt], accw=[ws.tok], q=q)
            yield


def prep_weight(C, ws):
    for _ in prep_weight_gen(C, ws):
        pass


def rms_to_featmajor(C, xt, xtok, gains, gtok, gcol0, hT, htok, ident, itok, tmp):
    S = C.S
    ss, sstok, xs, xstok, junk, jtok, rstd, rtok = tmp
    for j in range(4):
        S.add("act", lambda e, j=j: e.activation(
            out=xs[:, j, :], in_=xt[:, j, :], func=AF.Square, accum_out=ss[:, j:j + 1]), [xtok], [xstok, sstok])
    S.add("dve", lambda e: e.tensor_scalar(out=rstd[:], in0=ss[:], scalar1=1.0 / D_MODEL, scalar2=EPS,
                                           op0=ALU.mult, op1=ALU.add), [sstok], [rtok])
    S.add("act", lambda e: e.activation(out=rstd[:], in_=rstd[:], func=AF.Sqrt), [rtok], [rtok])
    S.add("dve", lambda e: e.reciprocal(out=rstd[:], in_=rstd[:]), [rtok], [rtok])
    for j in range(4):
        S.add("act", lambda e, j=j: e.activation(out=xs[:, j, :], in_=xt[:, j, :], func=AF.Copy,
                                                 scale=rstd[:, j:j + 1]), [xtok, rtok], [xstok])
    for kt in range(8):
        ps, pt = next_ps(C)
        for j in range(4):
            S.add("pe", lambda e, ps=ps, j=j, kt=kt: e.transpose(
                out=ps[:, j * 128:(j + 1) * 128], in_=xs[:, j, kt * 128:(kt + 1) * 128], identity=ident[:]),
                [xstok, itok], [pt])
        if kt % 2 == 0:
            S.add("dve", lambda e, ps=ps, kt=kt: e.tensor_scalar(
                out=hT[:, kt, :], in0=ps[:], scalar1=gains[:, gcol0 + kt:gcol0 + kt + 1], scalar2=None,
                op0=ALU.mult), [pt, gtok], accw=[htok])
        else:
            S.add("act", lambda e, ps=ps, kt=kt: e.activation(
                out=hT[:, kt, :], in_=ps[:], func=AF.Copy, scale=gains[:, gcol0 + kt:gcol0 + kt + 1]),
                [pt, gtok], accw=[htok])


def build_phaseB(T=4096):
    nc = bass.Bass("TRN2", target_bir_lowering=False)
    dt = nc.dram_tensor
    x = dt("x", [T, 1024], F32, kind="ExternalInput").ap()
    attn = dt("attn", [T, 1024], BF16, kind="ExternalInput").ap()
    pin = dt("p", [T, 256], F32, kind="ExternalInput").ap()
    gains_d = dt("gains", [128, 24], F32, kind="ExternalInput").ap()
    ident_d = dt("ident", [128, 128], F32, kind="ExternalInput").ap()
    wd = {}
    for name, K, N in (("w_merge", 1024, 2048), ("w_up_nsa", 512, 1024), ("w_up_ret", 512, 1024),
                       ("w_out", 1024, 1024), ("w_ff1", 1024, 4096), ("w_ff2", 4096, 1024),
                       ("w_gate", 1024, 1024), ("w_ple", 256, 1024)):
        wd[name] = dt(name, [K, N], F32, kind="ExternalInput").ap()
    xo = dt("xo", [T, 1024], F32, kind="ExternalOutput").ap()
    S = Sched(nc)
    C = make_pools(S)
    emit_phaseB(C, T, x, attn, pin, gains_d, ident_d, wd, xo)
    S.emit()
    S.close()
    return nc


B_KINDS = {"w_merge": "S", "w_up_nsa": "S", "w_up_ret": "S", "w_out": "M", "w_ff1": "S", "w_ff2": "M",
           "w_gate": "M", "w_ple": "M"}


def make_B_wspecs(S, wd):
    W = {}
    for name, ap in wd.items():
        K, N = ap.shape
        W[name] = WSpec(S, name, ap, K, N, B_KINDS[name])
    return W


def prep_B_gen(C, W):
    for name in ("w_merge", "w_up_nsa", "w_up_ret", "w_out", "w_ff1", "w_ff2", "w_gate", "w_ple"):
        for _ in prep_weight_gen(C, W[name], q="sp"):
            yield


DBG_STAGE = 99
DBG_NQT = None
DBG_R = 99
DBG_SUB = 0
DBG_Q = "act"
DBG_PREP = True


def emit_phaseB(C, T, x, attn, pin, gains_d, ident_d, wd, xo, xin_tok=None, attn_tok=None, xo_tok=None,
                gathered=None, W=None):
    S = C.S
    kinds = {"w_merge": "S", "w_up_nsa": "S", "w_up_ret": "S", "w_out": "M", "w_ff1": "S", "w_ff2": "M",
             "w_gate": "M", "w_ple": "M"}
    preW = W is not None
    if not preW:
        W = make_B_wspecs(S, wd)
    gains = S.sbuf("gains", [128, 24], F32)
    gtok = Tok()
    ident = S.sbuf("ident", [128, 128], F32)
    identb = S.sbuf("identb", [128, 128], BF16)
    itok, ibtok = Tok(), Tok()
    S.dma(gains[:], gains_d, writes=[gtok])
    S.dma(ident[:], ident_d, writes=[itok])
    S.add("dve", lambda e: e.tensor_copy(out=identb[:], in_=ident[:]), [itok], [ibtok])
    for name in ("w_merge", "w_up_nsa", "w_up_ret", "w_out", "w_ff1", "w_ff2", "w_gate", "w_ple"):
        if DBG_PREP and not preW:
            prep_weight(C, W[name])
    xt = S.sbuf("xt", [128, 4, 1024], F32)
    at = S.sbuf("at", [128, 4, 1024], BF16)
    ptm = S.sbuf("ptm", [128, 4, 256], F32)
    xs = S.sbuf("xs", [128, 4, 1024], F32)
    junk = None
    ss = S.sbuf("ss", [128, 4], F32)
    rstd = S.sbuf("rstd", [128, 4], F32)
    hT = S.sbuf("hT", [128, 8, 512], BF16)
    aT = S.sbuf("aT", [128, 8, 512], BF16)
    sgT = S.sbuf("sgT", [128, 16, 512], BF16)
    mixT = S.sbuf("mixT", [128, 8, 512], BF16)
    uT = S.sbuf("uT", [128, 32, 512], BF16)
    pT = S.sbuf("pT", [128, 2, 512], BF16)
    tmpf = [S.sbuf("tmpf%d" % i, [128, 512], F32) for i in range(2)]
    tmpft = [Tok(), Tok()]
    gsb = S.sbuf("gsb", [128, 512], F32)
    xtok, atok, ptok, xstok, jtok, sstok, rtok = [Tok() for _ in range(7)]
    htok, aTtok, sgtok, mixtok, utok, pTtok, gsbtok = [Tok() for _ in range(7)]
    tmp = (ss, sstok, xs, xstok, junk, jtok, rstd, rtok)
    xin_tok = xin_tok or Tok()
    attn_tok = attn_tok or Tok()
    xo_tok = xo_tok or Tok()
    nchunk = T // 512
    tn = 0
    for c in range(nchunk):
        t0 = c * 512
        S.dma(xt[:], x[t0:t0 + 512, :].rearrange("(j p) d -> p j d", p=128), reads=[xin_tok], writes=[xtok])
        if gathered is None:
            S.dma(at[:], attn[t0:t0 + 512, :].rearrange("(j p) d -> p j d", p=128), reads=[attn_tok], writes=[atok])
        else:
            SLg, hm, hmt, atAB, atABt = gathered
            for hf in range(2):
                for g in range(2):
                    RKg = min(2048, SLg)
                    tk_ = hf * T + t0
                    r0 = 2 * (tk_ // RKg) * RKg + g * RKg + tk_ % RKg
                    srcv = attn[r0:r0 + 512, :].rearrange("(j p) d -> p j d", p=128)
                    S.dma(atAB[hf][:, :, g * 256:(g + 1) * 256], srcv[:, :, 0:256], reads=[attn_tok], accw=[atABt[hf]])
                    S.dma(atAB[hf][:, :, 512 + g * 256:512 + (g + 1) * 256], srcv[:, :, 256:512], reads=[attn_tok],
                          accw=[atABt[hf]])
            S.add("dve", lambda e: e.tensor_scalar(out=at[:], in0=atAB[0][:], scalar1=hm[:, 0:1], scalar2=None,
                                                   op0=ALU.mult), [atABt[0], hmt], [atok])
            S.add("dve", lambda e: e.scalar_tensor_tensor(out=at[:], in0=atAB[1][:], scalar=hm[:, 1:2], in1=at[:],
                                                          op0=ALU.mult, op1=ALU.add), [atABt[1], hmt, atok], [atok])
        S.dma(ptm[:], pin[t0:t0 + 512, :].rearrange("(j p) d -> p j d", p=128), writes=[ptok])
        def _store(t0=t0):
            S.dma(xo[t0:t0 + 512, :].rearrange("(j p) d -> p j d", p=128), xt[:], reads=[xtok], writes=[xo_tok],
                  q=DBG_Q)
        if DBG_STAGE <= 0:
            _store()
            continue
        rms_to_featmajor(C, xt, xtok, gains, gtok, 0, hT, htok, ident, itok, tmp)
        if DBG_STAGE <= 1:
            _store()
            continue
        ws = W["w_merge"]
        for u in range(ws.nunits):
            wv, wt = load_unit(C, ws, u)
            for f in range(4):
                ps, pt = next_ps(C)
                if DBG_SUB == 1:
                    continue
                for kt in range(8):
                    S.add("pe", lambda e, ps=ps, wv=wv, f=f, kt=kt: e.matmul(
                        ps[:], lhsT=wv[:, f, kt, :], rhs=hT[:, kt, :], start=(kt == 0), stop=(kt == 7)),
                        [wt, htok], [pt])
                if DBG_SUB == 2:
                    continue
                S.add("act", lambda e, ps=ps, ft=u * 4 + f: e.activation(
                    out=sgT[:, ft, :], in_=ps[:], func=AF.Sigmoid), [pt], accw=[sgtok])
        if DBG_STAGE <= 2:
            _store()
            continue
        for ft in range(8):
            pb, pbt = next_psb(C)
            for j in range(4):
                S.add("pe", lambda e, pb=pb, j=j, ft=ft: e.transpose(
                    out=pb[:, j * 128:(j + 1) * 128], in_=at[:, j, ft * 128:(ft + 1) * 128], identity=identb[:]),
                    [atok, ibtok], [pbt])
            S.add("dve", lambda e, pb=pb, ft=ft: e.tensor_copy(out=aT[:, ft, :], in_=pb), [pbt], accw=[aTtok])
        if DBG_SUB == 3:
            _store()
            continue
        wsa, wsb = W["w_up_nsa"], W["w_up_ret"]
        for u in range(2):
            wva, wta = load_unit(C, wsa, u)
            wvb, wtb = load_unit(C, wsb, u)
            for f in range(4):
                ft = u * 4 + f
                psa, pta = next_ps(C)
                for kt in range(4):
                    S.add("pe", lambda e, psa=psa, wva=wva, f=f, kt=kt: e.matmul(
                        psa[:], lhsT=wva[:, f, kt, :], rhs=aT[:, kt, :], start=(kt == 0), stop=(kt == 3)),
                        [wta, aTtok], [pta])
                psb_, ptb = next_ps(C)
                for kt in range(4):
                    S.add("pe", lambda e, psb_=psb_, wvb=wvb, f=f, kt=kt: e.matmul(
                        psb_[:], lhsT=wvb[:, f, kt, :], rhs=aT[:, 4 + kt, :], start=(kt == 0), stop=(kt == 3)),
                        [wtb, aTtok], [ptb])
                if DBG_SUB == 4:
                    continue
                tf, tft = tmpf[tn % 2], tmpft[tn % 2]
                tn += 1
                S.add("dve", lambda e, tf=tf, psa=psa, ft=ft: e.tensor_tensor(
                    out=tf[:], in0=psa[:], in1=sgT[:, ft, :], op=ALU.mult), [pta, sgtok], [tft])
                tf2, tft2 = tmpf[tn % 2], tmpft[tn % 2]
                tn += 1
                S.add("dve", lambda e, tf2=tf2, psb_=psb_, ft=ft: e.tensor_tensor(
                    out=tf2[:], in0=psb_[:], in1=sgT[:, 8 + ft, :], op=ALU.mult), [ptb, sgtok], [tft2])
                if DBG_SUB == 5:
                    continue
                S.add("pool", lambda e, tf=tf, tf2=tf2, ft=ft: e.tensor_tensor(
                    out=mixT[:, ft, :], in0=tf[:], in1=tf2[:], op=ALU.add), [tft, tft2], accw=[mixtok])
        if DBG_STAGE <= 3:
            _store()
            continue
        ws = W["w_out"]
        for n in range(2):
            wv, wt = load_unit(C, ws, n)
            for j in range(4):
                ps, pt = next_ps(C)
                for kt in range(8):
                    S.add("pe", lambda e, ps=ps, wv=wv, j=j, kt=kt: e.matmul(
                        ps[:], lhsT=mixT[:, kt, j * 128:(j + 1) * 128], rhs=wv[:, kt, :],
                        start=(kt == 0), stop=(kt == 7)), [wt, mixtok], [pt])
                S.add("dve", lambda e, ps=ps, j=j, n=n: e.tensor_tensor(
                    out=xt[:, j, n * 512:(n + 1) * 512], in0=ps[:], in1=xt[:, j, n * 512:(n + 1) * 512],
                    op=ALU.add), [pt, xtok], [xtok])
        if DBG_STAGE <= 4:
            _store()
            continue
        rms_to_featmajor(C, xt, xtok, gains, gtok, 8, hT, htok, ident, itok, tmp)
        ws = W["w_ff1"]
        for u in range(ws.nunits):
            wv, wt = load_unit(C, ws, u)
            for f in range(4):
                ft = u * 4 + f
                ps, pt = next_ps(C)
                for kt in range(8):
                    S.add("pe", lambda e, ps=ps, wv=wv, f=f, kt=kt: e.matmul(
                        ps[:], lhsT=wv[:, f, kt, :], rhs=hT[:, kt, :], start=(kt == 0), stop=(kt == 7)),
                        [wt, htok], [pt])
                tf, tft = tmpf[tn % 2], tmpft[tn % 2]
                tn += 1
                S.add("act", lambda e, ps=ps, tf=tf: e.activation(out=tf[:], in_=ps[:], func=AF.Relu),
                      [pt], [tft])
                S.add("pool", lambda e, tf=tf, ft=ft: e.tensor_tensor(
                    out=uT[:, ft, :], in0=tf[:], in1=tf[:], op=ALU.mult), [tft], accw=[utok])
        ws = W["w_ff2"]
        for n in range(2):
            pss = [next_ps(C) for _ in range(4)]
            for v in range(ws.nv):
                wv, wt = load_unit(C, ws, n * ws.nv + v)
                for j in range(4):
                    ps, pt = pss[j]
                    for kk in range(8):
                        kt = v * 8 + kk
                        S.add("pe", lambda e, ps=ps, wv=wv, j=j, kk=kk, kt=kt: e.matmul(
                            ps[:], lhsT=uT[:, kt, j * 128:(j + 1) * 128], rhs=wv[:, kk, :],
                            start=(kt == 0), stop=(kt == 31)), [wt, utok], [pt])
            for j in range(4):
                ps, pt = pss[j]
                S.add("dve", lambda e, ps=ps, j=j, n=n: e.tensor_tensor(
                    out=xt[:, j, n * 512:(n + 1) * 512], in0=ps[:], in1=xt[:, j, n * 512:(n + 1) * 512],
                    op=ALU.add), [pt, xtok], [xtok])
        if DBG_STAGE <= 5:
            _store()
            continue
        rms_to_featmajor(C, xt, xtok, gains, gtok, 16, hT, htok, ident, itok, tmp)
        for kt in range(2):
            ps, pt = next_ps(C)
            for j in range(4):
                S.add("pe", lambda e, ps=ps, j=j, kt=kt: e.transpose(
                    out=ps[:, j * 128:(j + 1) * 128], in_=ptm[:, j, kt * 128:(kt + 1) * 128], identity=ident[:]),
                    [ptok, itok], [pt])
            S.add("dve", lambda e, ps=ps, kt=kt: e.tensor_copy(out=pT[:, kt, :], in_=ps[:]), [pt], accw=[pTtok])
        wsg, wsp = W["w_gate"], W["w_ple"]
        for n in range(2):
            wvg, wtg = load_unit(C, wsg, n)
            wvp, wtp = load_unit(C, wsp, n)
            for j in range(4):
                ps, pt = next_ps(C)
                for kt in range(8):
                    S.add("pe", lambda e, ps=ps, wvg=wvg, j=j, kt=kt: e.matmul(
                        ps[:], lhsT=hT[:, kt, j * 128:(j + 1) * 128], rhs=wvg[:, kt, :],
                        start=(kt == 0), stop=(kt == 7)), [wtg, htok], [pt])
                S.add("act", lambda e, ps=ps: e.activation(out=gsb[:], in_=ps[:], func=AF.Sigmoid),
                      [pt], [gsbtok])
                ps2, pt2 = next_ps(C)
                for kt in range(2):
                    S.add("pe", lambda e, ps2=ps2, wvp=wvp, j=j, kt=kt: e.matmul(
                        ps2[:], lhsT=pT[:, kt, j * 128:(j + 1) * 128], rhs=wvp[:, kt, :],
                        start=(kt == 0), stop=(kt == 1)), [wtp, pTtok], [pt2])
                tf, tft = tmpf[tn % 2], tmpft[tn % 2]
                tn += 1
                S.add("dve", lambda e, tf=tf, ps2=ps2: e.tensor_tensor(
                    out=tf[:], in0=ps2[:], in1=gsb[:], op=ALU.mult), [pt2, gsbtok], [tft])
                S.add("pool", lambda e, tf=tf, j=j, n=n: e.tensor_tensor(
                    out=xt[:, j, n * 512:(n + 1) * 512], in0=tf[:], in1=xt[:, j, n * 512:(n + 1) * 512],
                    op=ALU.add), [tft, xtok], [xtok])
        _store()


IN_SPLITS = (512, 128, 128, 128, 128, 128, 128, 24, 512, 512, 512, 512, 1024, 1024)
NEG = -30000.0


def hostA_weights(w_in, g):
    offs = np.cumsum([0] + list(IN_SPLITS))

    def col(i, a, b):
        return w_in[:, offs[i] + a: offs[i] + b]

    def swap(x):
        return np.concatenate([x[:, 64:], x[:, :64]], 1)

    q = [col(0, (g * 4 + h) * 64, (g * 4 + h + 1) * 64) for h in range(4)]
    kc, vc = col(1, g * 64, g * 64 + 64), col(2, g * 64, g * 64 + 64)
    ks, vs = col(3, g * 64, g * 64 + 64), col(4, g * 64, g * 64 + 64)
    kw, vw = col(5, g * 64, g * 64 + 64), col(6, g * 64, g * 64 + 64)
    gates = np.stack([w_in[:, offs[7] + br * 8 + g * 4 + h] for br in range(3) for h in range(4)], 1)
    rq = [col(8, (2 * g + h) * 128, (2 * g + h + 1) * 128) for h in range(2)]
    rk = [col(9, (2 * g + h) * 128, (2 * g + h + 1) * 128) for h in range(2)]
    rv = col(10, 2 * g * 128, (2 * g + 2) * 128)
    rg = col(11, 2 * g * 128, (2 * g + 2) * 128)
    z128 = np.zeros((1024, 128), np.float32)
    WS = np.concatenate([q[0], q[1], q[2], q[3], ks, ks, kw, kw, kc, vc, z128, z128, z128,
                         rq[0], swap(rq[0]), rq[1], swap(rq[1]), rk[0], swap(rk[0]), rk[1], swap(rk[1])], 1)
    WM = np.concatenate([rk[0], rk[1], rv, rg, np.zeros((1024, 256), np.float32),
                         vs, vw, gates, np.zeros((1024, 512 - 140), np.float32)], 1)
    return np.ascontiguousarray(WS, np.float32), np.ascontiguousarray(WM, np.float32)


def hostA_consts(g, S):
    import ml_dtypes
    bf = ml_dtypes.bfloat16
    c = {}
    c["ident"] = np.eye(128, dtype=np.float32)
    bd = np.zeros((128, 128), np.float32)
    bd[:64, :64] = 1
    bd[64:, 64:] = 1
    c["onesbd"] = bd.astype(bf)
    half = 64
    inv = (10000.0 ** (-np.arange(half, dtype=np.float32) / half)).astype(np.float32)
    pos = np.arange(S, dtype=np.float32)
    ang = (pos[:, None] * inv[None, :]).astype(np.float32)
    cos, sin = np.cos(ang.astype(np.float64)), np.sin(ang.astype(np.float64))
    cosT = np.concatenate([cos.T, cos.T], 0)
    sinsT = np.concatenate([-sin.T, sin.T], 0)
    ksc = 128.0 ** -0.5
    c["ropeq"] = np.stack([cosT, sinsT], 1).astype(np.float32)
    c["ropek"] = (np.stack([cosT, sinsT], 1) * ksc).astype(np.float32)
    hh = np.array([2 * g, 2 * g + 1], np.float64)
    gamma = 1.0 - 2.0 ** (-5.0 - hh)
    lg = np.log(gamma)
    n = np.arange(128, dtype=np.float64)
    xi = np.exp(lg[:, None] * (n + 1.0))
    zeta = np.exp(lg[:, None] * (127.0 - n))
    c["xi"] = np.broadcast_to(np.tile(xi, (1, 4))[None], (128, 2, 512)).astype(np.float32).copy()
    zt = zeta[:, np.arange(S) % 128]
    c["ctk"] = (cos[:, None, :] * zt.T[:, :, None] * ksc).astype(np.float32)
    c["stk"] = (sin[:, None, :] * zt.T[:, :, None] * ksc).astype(np.float32)
    diff = n[None, :] - n[:, None]
    dec = np.where(diff[None] >= 0, np.exp(lg[:, None, None] * np.maximum(diff[None], 0)), 0.0)
    c["decayT"] = np.ascontiguousarray(dec.transpose(1, 0, 2)).astype(np.float32)
    c["gch"] = np.broadcast_to(np.exp(lg * 128.0)[None], (128, 2)).astype(np.float32).copy()
    kk, qq = np.arange(128)[:, None], np.arange(128)[None, :]
    c["triT"] = np.tile(np.where(kk <= qq, 0.0, NEG), (1, 4)).astype(bf)
    c["triU"] = np.tile(np.where(kk > qq, 0.0, NEG), (1, 4)).astype(bf)
    r = np.arange(16)[None, :, None]
    ql, il = np.arange(128)[:, None, None], np.arange(128)[None, None, :]
    c["cmaskQ"] = np.where(128 * r + ql - 16 * il - 31 >= 0, 0.0, NEG).astype(bf)
    c["cmaskT"] = np.ascontiguousarray(np.transpose(np.where(128 * r + ql - 16 * il - 31 >= 0, 0.0, NEG), (2, 1, 0))).astype(bf)
    c["wexp"] = (np.arange(S)[None, :] // 64 == np.arange(128)[:, None]).astype(bf)
    return c


A_CONST_SHAPES = lambda S: {
    "ident": ([128, 128], F32), "onesbd": ([128, 128], BF16), "ropeq": ([128, 2, S], F32),
    "ropek": ([128, 2, S], F32), "xi": ([128, 2, 512], F32), "ctk": ([S, 2, 64], F32), "stk": ([S, 2, 64], F32),
    "decayT": ([128, 2, 128], F32), "gch": ([128, 2], F32), "triT": ([128, 512], BF16), "triU": ([128, 512], BF16),
    "cmaskQ": ([128, 16, 128], BF16), "cmaskT": ([128, 16, 128], BF16), "wexp": ([128, S], BF16)}


def hostA_params(z, L, g):
    p = {}

    def gl(v):
        return np.ascontiguousarray(v.reshape(8, 128).T)
    qg, kg = z["nsa_q_norm"][L], z["nsa_k_norm"][L]
    p["gainsA"] = np.concatenate([gl(z["norm_mix"][L]), np.tile(qg, 2)[:, None], np.tile(kg, 2)[:, None]], 1).astype(np.float32)
    p["posT"] = np.ascontiguousarray(np.concatenate([z["cmp_pos_k"][L].T, z["cmp_pos_v"][L].T], 0), np.float32)
    w1k = z["cmp_w1_k"][L].reshape(32, 64, 256).transpose(1, 0, 2)
    w1v = z["cmp_w1_v"][L].reshape(32, 64, 256).transpose(1, 0, 2)
    zz = np.zeros_like(w1k)
    p["w1k"] = np.ascontiguousarray(np.concatenate([w1k, zz], 0), np.float32)
    p["w1v"] = np.ascontiguousarray(np.concatenate([zz, w1v], 0), np.float32)
    w2k = z["cmp_w2_k"][L].reshape(2, 128, 64).transpose(1, 0, 2)
    w2v = z["cmp_w2_v"][L].reshape(2, 128, 64).transpose(1, 0, 2)
    p["w2"] = np.ascontiguousarray(np.stack([np.concatenate([w2k, w2k], 2), np.concatenate([w2v, np.zeros_like(w2v)], 2)], 2), np.float32)
    return p


A_PARAM_SHAPES = {"gainsA": [128, 10], "posT": [128, 32], "w1k": [128, 32, 256], "w1v": [128, 32, 256],
                  "w2": [128, 2, 2, 128]}


def build_phaseA(SL=8192, parts=("ret", "nsa")):
    nc = bass.Bass("TRN2", target_bir_lowering=False)
    dt = nc.dram_tensor
    x = dt("x", [SL, 1024], F32, kind="ExternalInput").ap()
    WS_d = dt("WS", [1024, 2048], F32, kind="ExternalInput").ap()
    WM_d = dt("WM", [1024, 1536], F32, kind="ExternalInput").ap()
    cd = {k: dt(k, sh, ty, kind="ExternalInput").ap() for k, (sh, ty) in A_CONST_SHAPES(SL).items()}
    pd = {k: dt(k, sh, F32, kind="ExternalInput").ap() for k, sh in A_PARAM_SHAPES.items()}
    ao = dt("ao", [SL, 512], BF16, kind="ExternalOutput").ap()
    S = Sched(nc)
    C = make_pools(S, n_wbuf=3)
    emit_phaseA(C, SL, x, WS_d, WM_d, cd, pd, ao, parts)
    S.emit()
    S.close()
    return nc


def norm_evac(C, ps, pt, gains, gtok, gcol, onesbd, otok, tmps, dsts):
    S = C.S
    qf, qft, sq, sqt, rs, rst = tmps
    N = ps.shape[-1]
    S.add("act", lambda e: e.activation(out=qf[:, :N], in_=ps, func=AF.Copy), [pt], [qft])
    S.add("act", lambda e: e.activation(out=sq[:, :N], in_=ps, func=AF.Square), [pt], [sqt])
    p2, pt2 = next_ps(C)
    S.add("pe", lambda e: e.matmul(p2[:, :N], lhsT=onesbd[:], rhs=sq[:, :N], start=True, stop=True), [sqt, otok], [pt2])
    S.add("act", lambda e: e.activation(out=rs[:, :N], in_=p2[:, :N], func=AF.Ln, scale=1.0 / 64, bias=C.epsb[:, 0:1]), [pt2, C.epst], [rst])
    S.add("act", lambda e: e.activation(out=rs[:, :N], in_=rs[:, :N], func=AF.Exp, scale=-0.5), [rst], [rst])
    for (dst, lo, hi, tok) in dsts:
        S.add("dve", lambda e, dst=dst, lo=lo, hi=hi: e.scalar_tensor_tensor(
            out=dst, in0=qf[lo:hi, :N], scalar=gains[lo:hi, gcol:gcol + 1], in1=rs[lo:hi, :N],
            op0=ALU.mult, op1=ALU.mult), [qft, rst, gtok], accw=[tok])


def emit_phaseA(C, SL, x, WS_d, WM_d, cd, pd, ao, parts=("ret", "nsa"), xin_tok=None, ao_tok=None, xmap=None):
    C.xmap = xmap or (lambda t: t)
    S = C.S
    nchunk = SL // 512
    xin_tok = xin_tok or Tok()
    ao_tok = ao_tok or Tok()
    WSs = WSpec(S, "WSs", WS_d, 1024, 2048, "S")
    WMs = WSpec(S, "WMs", WM_d, 1024, 1536, "M")
    gains = S.sbuf("gainsA", [128, 10], F32)
    gtok = Tok()
    ident = S.sbuf("identA", [128, 128], F32)
    itok = Tok()
    C.epsb = S.sbuf("epsb", [128, 1], F32)
    C.epst = Tok()
    S.dma(gains[:], pd["gainsA"], writes=[gtok])
    S.dma(ident[:], cd["ident"], writes=[itok])
    S.add("dve", lambda e: e.memset(C.epsb[:], EPS), [], [C.epst])
    prep_weight(C, WSs)
    prep_weight(C, WMs)
    C.hscr = S.nc.dram_tensor("hscr_" + S.prefix, [nchunk, 128, 4096], BF16, kind="Internal").ap()
    C.hscr_tok = [Tok() for _ in range(nchunk)]
    C.share_h = ("ret" in parts) and ("nsa" in parts)
    if "ret" in parts:
        with S.scope():
            emit_ret_pass(C, SL, x, xin_tok, WSs, WMs, cd, gains, gtok, ident, itok, ao, ao_tok)
    if "nsa" in parts:
        emit_nsa(C, SL, x, xin_tok, WSs, WMs, cd, pd, gains, gtok, ident, itok, ao, ao_tok)


def load_x_rms(C, x, xin_tok, t0, xt, xtok, junk, jtok, ss, sstok, rstd, rtok, gains, gtok, hT, htok, ident, itok):
    S = C.S
    xr0 = C.xmap(t0)
    S.dma(xt[:], x[xr0:xr0 + 512, :].rearrange("(j p) d -> p j d", p=128), reads=[xin_tok], writes=[xtok])
    for j in range(4):
        S.add("act", lambda e, j=j: e.activation(out=junk[:], in_=xt[:, j, :], func=AF.Square,
                                                 accum_out=ss[:, j:j + 1]), [xtok], [jtok, sstok])
    S.add("dve", lambda e: e.tensor_scalar(out=rstd[:], in0=ss[:], scalar1=1.0 / D_MODEL, scalar2=EPS,
                                           op0=ALU.mult, op1=ALU.add), [sstok], [rtok])
    S.add("act", lambda e: e.activation(out=rstd[:], in_=rstd[:], func=AF.Sqrt), [rtok], [rtok])
    S.add("dve", lambda e: e.reciprocal(out=rstd[:], in_=rstd[:]), [rtok], [rtok])
    for j in range(4):
        S.add("act", lambda e, j=j: e.activation(out=xt[:, j, :], in_=xt[:, j, :], func=AF.Copy,
                                                 scale=rstd[:, j:j + 1]), [xtok, rtok], [xtok])
    for kt in range(8):
        ps, pt = next_ps(C)
        for j in range(4):
            S.add("pe", lambda e, ps=ps, j=j, kt=kt: e.transpose(
                out=ps[:, j * 128:(j + 1) * 128], in_=xt[:, j, kt * 128:(kt + 1) * 128], identity=ident[:]),
                [xtok, itok], [pt])
        if kt % 2 == 0:
            S.add("dve", lambda e, ps=ps, kt=kt: e.tensor_scalar(
                out=hT[:, kt, :], in0=ps[:], scalar1=gains[:, kt:kt + 1], scalar2=None, op0=ALU.mult),
                [pt, gtok], accw=[htok])
        else:
            S.add("act", lambda e, ps=ps, kt=kt: e.activation(
                out=hT[:, kt, :], in_=ps[:], func=AF.Copy, scale=gains[:, kt:kt + 1]), [pt, gtok], accw=[htok])


def emit_ret_pass(C, SL, x, xin_tok, WSs, WMs, cd, gains, gtok, ident, itok, ao, ao_tok):
    S = C.S
    sb = S.sbuf
    xt2 = [sb("r_xt%d" % i, [128, 4, 1024], F32) for i in range(2)]
    junk = sb("r_junk", [128, 1024], F32)
    ss2 = [sb("r_ss%d" % i, [128, 4], F32) for i in range(2)]
    rstd2 = [sb("r_rstd%d" % i, [128, 4], F32) for i in range(2)]
    hT2 = [sb("r_hT%d" % i, [128, 8, 512], BF16) for i in range(2)]
    xtok2, sstok2, rtok2, htok2 = [[Tok(), Tok()] for _ in range(4)]
    rq_tab = sb("r_rqtab", [128, 2, 512], F32)
    rk_tab = sb("r_rktab", [128, 2, 512], F32)
    ctk_t = sb("r_ctk", [128, 4, 2, 64], F32)
    stk_t = sb("r_stk", [128, 4, 2, 64], F32)
    xi_t = sb("r_xi", [128, 2, 512], F32)
    decT = sb("r_dec", [128, 2, 128], F32)
    gch = sb("r_gch", [128, 2], F32)
    t1 = [sb("r_t1_%d" % i, [128, 512], F32) for i in range(2)]
    t2 = [sb("r_t2_%d" % i, [128, 512], F32) for i in range(2)]
    tmpq = sb("r_tmpq", [128, 512], F32)
    QrT = sb("r_QrT", [128, 2, 512], BF16)
    QrxT = sb("r_QrxT", [128, 2, 512], BF16)
    KrT = sb("r_KrT", [128, 2, 512], BF16)
    Vr = sb("r_Vr", [128, 4, 256], BF16)
    kz = sb("r_kz", [128, 4, 2, 128], BF16)
    sg = sb("r_sg", [128, 4, 256], F32)
    tabcd = [sb("r_tabcd%d" % i, [128, 2, 64], F32) for i in range(4)]
    IT = [sb("r_IT%d" % i, [128, 128], BF16) for i in range(2)]
    yr = sb("r_yr", [128, 4, 2, 128], F32)
    ssr = sb("r_ssr", [128, 8], F32)
    rr = sb("r_rr", [128, 8], F32)
    ro = sb("r_ro", [128, 4, 256], BF16)
    R = sb("r_R", [128, 2, 128], F32)
    Rb = sb("r_Rb", [128, 2, 128], BF16)
    junkb = sb("r_junkb", [128, 128], BF16)
    (xtok, jtok, sstok, rtok, htok, rqt, rkt, ctt, stt, xit, dect, gcht, tmpqt, qrt, qrxt, krt, vrt, kzt, sgt,
     yrt, ssrt, rrt, rot, jbt) = [Tok() for _ in range(24)]
    t1t, t2t = [Tok(), Tok()], [Tok(), Tok()]
    tabt = [Tok() for _ in range(4)]
    ITt = [Tok(), Tok()]
    Rt, Rbt = [Tok(), Tok()], [Tok(), Tok()]
    S.dma(xi_t[:], cd["xi"], writes=[xit])
    S.dma(decT[:], cd["decayT"], writes=[dect])
    S.dma(gch[:], cd["gch"], writes=[gcht])
    for h in range(2):
        S.add("dve", lambda e, h=h: e.memset(R[:, h, :], 0.0), [], [Rt[h]])
        S.add("pool", lambda e, h=h: e.memset(Rb[:, h, :], 0.0), [], [Rbt[h]])
    nchunk = SL // 512
    tn = 0
    itn = 0
    def _lx(c):
        i = c % 2
        load_x_rms(C, x, xin_tok, c * 512, xt2[i], xtok2[i], junk, jtok, ss2[i], sstok2[i], rstd2[i], rtok2[i],
                   gains, gtok, hT2[i], htok2[i], ident, itok)
        if C.share_h:
            S.dma(C.hscr[c], hT2[i][:].rearrange("p k t -> p (k t)"), reads=[htok2[i]], writes=[C.hscr_tok[c]], q="sp")

    _lx(0)
    for c in range(nchunk):
        t0 = c * 512
        hT, htok = hT2[c % 2], htok2[c % 2]
        S.dma(rq_tab[:], cd["ropeq"][:, :, t0:t0 + 512], writes=[rqt])
        S.dma(rk_tab[:], cd["ropek"][:, :, t0:t0 + 512], writes=[rkt])
        S.dma(ctk_t[:], cd["ctk"][t0:t0 + 512].rearrange("(j p) h i -> p j h i", p=128), writes=[ctt])
        S.dma(stk_t[:], cd["stk"][t0:t0 + 512].rearrange("(j p) h i -> p j h i", p=128), writes=[stt])
        if DBG_R <= 1:
            continue
        for u in (2, 3):
            wv, wt = load_unit(C, WSs, u)
            isq = (u == 2)
            tab, tabt_ = (rq_tab, rqt) if isq else (rk_tab, rkt)
            for h in range(2):
                a, at_ = t1[tn % 2], t1t[tn % 2]
                b, bt_ = t2[tn % 2], t2t[tn % 2]
                tn += 1
                for half, (dstb, dtok) in enumerate(((a, at_), (b, bt_))):
                    f = 2 * h + half
                    ps, pt = next_ps(C)
                    for kt in range(8):
                        S.add("pe", lambda e, hT=hT, ps=ps, wv=wv, f=f, kt=kt: e.matmul(
                            ps[:], lhsT=wv[:, f, kt, :], rhs=hT[:, kt, :], start=(kt == 0), stop=(kt == 7)),
                            [wt, htok], [pt])
                    S.add("dve", lambda e, ps=ps, dstb=dstb, half=half, tab=tab: e.tensor_tensor(
                        out=dstb[:], in0=ps[:], in1=tab[:, half, :], op=ALU.mult), [pt, tabt_], [dtok])
                if isq:
                    S.add("pool", lambda e, a=a, b=b: e.tensor_tensor(out=tmpq[:], in0=a[:], in1=b[:], op=ALU.add),
                          [at_, bt_], [tmpqt])
                    S.add("act", lambda e, h=h: e.activation(out=QrT[:, h, :], in_=tmpq[:], func=AF.Copy),
                          [tmpqt], accw=[qrt])
                    S.add("pool", lambda e, h=h: e.tensor_tensor(out=QrxT[:, h, :], in0=tmpq[:], in1=xi_t[:, h, :],
                                                                 op=ALU.mult), [tmpqt, xit], accw=[qrxt])
                else:
                    S.add("pool", lambda e, a=a, b=b, h=h: e.tensor_tensor(out=KrT[:, h, :], in0=a[:], in1=b[:],
                                                                           op=ALU.add), [at_, bt_], accw=[krt])
        if DBG_R <= 2:
            continue
        if c + 1 < nchunk:
            _lx(c + 1)
        wv, wt = load_unit(C, WMs, 0)
        for j in range(4):
            ps, pt = next_ps(C)
            for kt in range(8):
                S.add("pe", lambda e, hT=hT, ps=ps, wv=wv, j=j, kt=kt: e.matmul(
                    ps[:], lhsT=hT[:, kt, j * 128:(j + 1) * 128], rhs=wv[:, kt, :], start=(kt == 0), stop=(kt == 7)),
                    [wt, htok], [pt])
            if DBG_SUB == 1:
                S.add("act", lambda e, ps=ps, j=j: e.activation(out=Vr[:, j, :], in_=ps[:, 256:512], func=AF.Copy),
                      [pt], accw=[vrt])
                continue
            pv = ps[:, 0:256].rearrange("p (h t i) -> p h t i", h=2, t=2)
            x1, x2 = pv[:, :, 0, :], pv[:, :, 1, :]
            kzv = kz[:, j].rearrange("p h (t i) -> p h t i", t=2)
            ta, tb, tc, td = tabcd
            S.add("dve", lambda e, x1=x1, j=j: e.tensor_tensor(out=ta[:], in0=x1, in1=ctk_t[:, j], op=ALU.mult),
                  [pt, ctt], [tabt[0]])
            S.add("dve", lambda e, x2=x2, j=j: e.tensor_tensor(out=tb[:], in0=x2, in1=stk_t[:, j], op=ALU.mult),
                  [pt, stt], [tabt[1]])
            S.add("dve", lambda e, x1=x1, j=j: e.tensor_tensor(out=tc[:], in0=x1, in1=stk_t[:, j], op=ALU.mult),
                  [pt, stt], [tabt[2]])
            S.add("dve", lambda e, x2=x2, j=j: e.tensor_tensor(out=td[:], in0=x2, in1=ctk_t[:, j], op=ALU.mult),
                  [pt, ctt], [tabt[3]])
            if DBG_SUB == 2:
                continue
            S.add("dve", lambda e, kzv=kzv: e.tensor_tensor(out=kzv[:, :, 0, :], in0=ta[:], in1=tb[:], op=ALU.subtract),
                  [tabt[0], tabt[1]] if DBG_SUB != 3 else [], accw=[kzt])
            S.add("dve", lambda e, kzv=kzv: e.tensor_tensor(out=kzv[:, :, 1, :], in0=tc[:], in1=td[:], op=ALU.add),
                  [tabt[2], tabt[3]] if DBG_SUB != 3 else [], accw=[kzt])
            S.add("act", lambda e, ps=ps, j=j: e.activation(out=Vr[:, j, :], in_=ps[:, 256:512], func=AF.Copy),
                  [pt], accw=[vrt])
        if DBG_R <= 3:
            continue
        wv, wt = load_unit(C, WMs, 1)
        for j in range(4):
            ps, pt = next_ps(C)
            for kt in range(8):
                S.add("pe", lambda e, hT=hT, ps=ps, wv=wv, j=j, kt=kt: e.matmul(
                    ps[:, 0:256], lhsT=hT[:, kt, j * 128:(j + 1) * 128], rhs=wv[:, kt, 0:256],
                    start=(kt == 0), stop=(kt == 7)), [wt, htok], [pt])
            S.add("act", lambda e, ps=ps, j=j: e.activation(out=sg[:, j, :], in_=ps[:, 0:256], func=AF.Silu),
                  [pt], accw=[sgt])
        if DBG_R <= 4:
            continue
        for j in range(4):
            js = slice(j * 128, (j + 1) * 128)
            for h in range(2):
                hs = slice(h * 128, (h + 1) * 128)
                psI, ptI = next_ps(C)
                S.add("pe", lambda e, psI=psI, h=h, js=js: e.matmul(
                    psI[:, 0:128], lhsT=KrT[:, h, js], rhs=QrT[:, h, js], start=True, stop=True), [krt, qrt], [ptI])
                it_, itt = IT[itn % 2], ITt[itn % 2]
                itn += 1
                S.add("dve", lambda e, psI=psI, it_=it_, h=h: e.tensor_tensor(
                    out=it_[:], in0=psI[:, 0:128], in1=decT[:, h, :], op=ALU.mult), [ptI, dect], [itt])
                psO, ptO = next_ps(C)
                S.add("pe", lambda e, psO=psO, it_=it_, j=j, hs=hs: e.matmul(
                    psO[:, 0:128], lhsT=it_[:], rhs=Vr[:, j, hs], start=True, stop=False), [itt, vrt], [ptO])
                S.add("pe", lambda e, psO=psO, h=h, js=js: e.matmul(
                    psO[:, 0:128], lhsT=QrxT[:, h, js], rhs=Rb[:, h, :], start=False, stop=True), [qrxt, Rbt[h]], [ptO])
                S.add("act", lambda e, psO=psO, j=j, h=h: e.activation(
                    out=junkb[:], in_=psO[:, 0:128], func=AF.Square, accum_out=ssr[:, j * 2 + h:j * 2 + h + 1]),
                    [ptO], [jbt], accw=[ssrt])
                S.add("dve", lambda e, psO=psO, j=j, h=h: e.tensor_copy(out=yr[:, j, h, :], in_=psO[:, 0:128]),
                      [ptO], accw=[yrt])
                psK, ptK = next_ps(C)
                S.add("pe", lambda e, psK=psK, j=j, h=h, hs=hs: e.matmul(
                    psK[:, 0:128], lhsT=kz[:, j, h, :], rhs=Vr[:, j, hs], start=True, stop=True), [kzt, vrt], [ptK])
                S.add("dve", lambda e, psK=psK, h=h: e.scalar_tensor_tensor(
                    out=R[:, h, :], in0=R[:, h, :], scalar=gch[:, h:h + 1], in1=psK[:, 0:128],
                    op0=ALU.mult, op1=ALU.add), [ptK, gcht, Rt[h]], [Rt[h]])
                S.add("pool", lambda e, h=h: e.tensor_copy(out=Rb[:, h, :], in_=R[:, h, :]), [Rt[h]], [Rbt[h]])
        if DBG_R <= 5:
            continue
        S.add("dve", lambda e: e.tensor_scalar(out=rr[:], in0=ssr[:], scalar1=1.0 / 128, scalar2=EPS,
                                               op0=ALU.mult, op1=ALU.add), [ssrt], [rrt])
        S.add("act", lambda e: e.activation(out=rr[:], in_=rr[:], func=AF.Sqrt), [rrt], [rrt])
        S.add("dve", lambda e: e.reciprocal(out=rr[:], in_=rr[:]), [rrt], [rrt])
        for j in range(4):
            for h in range(2):
                hs = slice(h * 128, (h + 1) * 128)
                S.add("dve", lambda e, j=j, h=h, hs=hs: e.scalar_tensor_tensor(
                    out=ro[:, j, hs], in0=yr[:, j, h, :], scalar=rr[:, j * 2 + h:j * 2 + h + 1], in1=sg[:, j, hs],
                    op0=ALU.mult, op1=ALU.mult), [yrt, rrt, sgt], accw=[rot])
        S.dma(ao[t0:t0 + 512, 256:512].rearrange("(j p) d -> p j d", p=128), ro[:], reads=[rot], accw=[ao_tok], q="act")


HORD = (0, 2, 1, 3)


def emit_nsa(C, SL, x, xin_tok, WSs, WMs, cd, pd, gains, gtok, ident, itok, ao, ao_tok):
    S = C.S
    sb = S.sbuf
    NT = SL // 128
    nb = SL // 16
    assert nb <= 512
    NCT = max(1, nb // 128)
    QT = sb("n_QT", [128, 2, SL], BF16)
    Kslo, Kshi = sb("n_Kslo", [128, SL], BF16), sb("n_Kshi", [128, SL], BF16)
    Kwlo, Kwhi = sb("n_Kwlo", [128, SL], BF16), sb("n_Kwhi", [128, SL], BF16)
    V1 = sb("n_V1", [128, NT, 2, 65], BF16)
    Gt = sb("n_Gt", [128, NT, 12], F32)
    kclo, kchi = sb("n_kclo", [128, 512], BF16), sb("n_kchi", [128, 512], BF16)
    Vc1 = sb("n_Vc1", [128, 4, 65], BF16)
    onesbd = sb("n_onesbd", [128, 128], BF16)
    identb = sb("n_identb", [128, 128], BF16)
    qf = sb("n_qf", [128, 512], F32)
    sq = sb("n_sq", [128, 512], BF16)
    rs = sb("n_rs", [128, 512], F32)
    qtok, kst, kwt, kcvt, v1t, gtt, kct, vct, onest, ibt, qft, sqt, rst = [Tok() for _ in range(13)]
    tmps = (qf, qft, sq, sqt, rs, rst)
    S.dma(onesbd[:], cd["onesbd"], writes=[onest])
    S.add("dve", lambda e: e.tensor_copy(out=identb[:], in_=ident[:]), [itok], [ibt])
    for (t_, tk) in ((Kslo, kst), (Kshi, kst), (Kwlo, kwt), (Kwhi, kwt), (kclo, kct), (kchi, kct)):
        S.add("pool", lambda e, t_=t_: e.memset(t_[:], 0.0), [], [tk])
    S.add("pool", lambda e: e.memset(V1[:], 1.0), [], [v1t])
    S.add("pool", lambda e: e.memset(Vc1[:], 0.0), [], [vct])
    S.add("pool", lambda e: e.memset(Vc1[:, :, 64:65], 1.0), [vct], [vct])
    kc_scope = S.scope()
    kc_scope.__enter__()
    KcVcT = sb("n_KcVcT", [128, SL + 16], BF16)
    with S.scope():
        if C.share_h:
            hT2 = [sb("n_hT%d" % i, [128, 8, 512], BF16) for i in range(2)]
            htok2 = [Tok(), Tok()]

            def _lh(c):
                S.dma(hT2[c % 2][:].rearrange("p k t -> p (k t)"), C.hscr[c], reads=[C.hscr_tok[c]], writes=[htok2[c % 2]])
            _lh(0)
        else:
            xt = sb("n_xt", [128, 4, 1024], F32)
            junk = sb("n_junk", [128, 1024], F32)
            ss = sb("n_ss", [128, 4], F32)
            rstd = sb("n_rstd", [128, 4], F32)
            hT = sb("n_hT", [128, 8, 512], BF16)
            xtok, jtok, sstok, rtok, htok = [Tok() for _ in range(5)]
        for c in range(SL // 512):
            t0 = c * 512
            cs = slice(t0, t0 + 512)
            if C.share_h:
                hT, htok = hT2[c % 2], htok2[c % 2]
                if c + 1 < SL // 512:
                    _lh(c + 1)
            else:
                load_x_rms(C, x, xin_tok, t0, xt, xtok, junk, jtok, ss, sstok, rstd, rtok, gains, gtok, hT, htok, ident, itok)
            for u in (0, 1):
                wv, wt = load_unit(C, WSs, u)
                for f in range(4 if u == 0 else 1):
                    ft = u * 4 + f
                    ps, pt = next_ps(C)
                    for kt in range(8):
                        S.add("pe", lambda e, hT=hT, ps=ps, wv=wv, f=f, kt=kt: e.matmul(
                            ps[:], lhsT=wv[:, f, kt, :], rhs=hT[:, kt, :], start=(kt == 0), stop=(kt == 7)),
                            [wt, htok], [pt])
                    if ft < 2:
                        norm_evac(C, ps[:], pt, gains, gtok, 8, onesbd, onest, tmps, [(QT[:, ft, cs], 0, 128, qtok)])
                    elif ft == 2:
                        norm_evac(C, ps[:], pt, gains, gtok, 9, onesbd, onest, tmps,
                                  [(Kslo[0:64, cs], 0, 64, kst), (Kshi[64:128, cs], 64, 128, kst)])
                    elif ft == 3:
                        norm_evac(C, ps[:], pt, gains, gtok, 9, onesbd, onest, tmps,
                                  [(Kwlo[0:64, cs], 0, 64, kwt), (Kwhi[64:128, cs], 64, 128, kwt)])
                    else:
                        S.add("act", lambda e, ps=ps, cs=cs: e.activation(out=KcVcT[:, cs], in_=ps[:], func=AF.Copy),
                              [pt], accw=[kcvt])
            wv, wt = load_unit(C, WMs, 2)
            for j in range(4):
                tile_i = c * 4 + j
                ps, pt = next_ps(C)
                for kt in range(8):
                    S.add("pe", lambda e, hT=hT, ps=ps, wv=wv, j=j, kt=kt: e.matmul(
                        ps[:, 0:140], lhsT=hT[:, kt, j * 128:(j + 1) * 128], rhs=wv[:, kt, 0:140],
                        start=(kt == 0), stop=(kt == 7)), [wt, htok], [pt])
                S.add("dve", lambda e, ps=ps, tile_i=tile_i: e.tensor_copy(
                    out=V1[:, tile_i, :, 0:64], in_=ps[:, 0:128].rearrange("p (b d) -> p b d", b=2)), [pt], accw=[v1t])
                S.add("act", lambda e, ps=ps, tile_i=tile_i: e.activation(
                    out=Gt[:, tile_i, :], in_=ps[:, 128:140], func=AF.Sigmoid), [pt], accw=[gtt])
    with S.scope():
        W1b = sb("n_W1b", [128, 32, 256], BF16)
        posT = sb("n_posT", [128, 32], F32)
        w2f = sb("n_w2f", [128, 2, 2, 128], F32)
        w2b = sb("n_w2b", [128, 2, 2, 128], BF16)
        zr = [sb("n_zr%d" % i, [128, 512], BF16) for i in range(4)]
        zrt = [Tok() for _ in range(4)]
        GT = sb("n_GT", [128, 4, 512], BF16)
        ga = sb("n_ga", [128, 512], F32)
        gb = sb("n_gb", [128, 512], F32)
        w1t, post, w2t, w2bt, GTt, gat, gbt = [Tok() for _ in range(7)]
        S.dma(posT[:], pd["posT"], writes=[post])
        S.dma(w2f[:], pd["w2"], writes=[w2t])
        S.add("dve", lambda e: e.tensor_copy(out=w2b[:], in_=w2f[:]), [w2t], [w2bt])
        S.add("dve", lambda e: e.tensor_copy(out=KcVcT[:, SL:SL + 16], in_=KcVcT[:, SL - 1:SL].to_broadcast([128, 16])),
              [kcvt], [kcvt])
        accs = [next_ps(C) for _ in range(4)]
        zn = 0
        for kv, srcw in enumerate((pd["w1k"], pd["w1v"])):
            for r0 in range(0, 32, 2):
                i = C.sn % 3
                C.sn += 1
                st, stt = C.stg[i], C.stgtok[i]
                S.dma(st[:, :512], srcw[:, r0:r0 + 2, :].rearrange("p r h -> p (r h)"), writes=[stt])
                eng = ("dve", "pool")[C.cast_rr % 2]
                C.cast_rr += 1
                S.add(eng, lambda e, st=st, r0=r0: e.tensor_copy(
                    out=W1b[:, r0:r0 + 2, :].rearrange("p r h -> p (r h)"), in_=st[:, :512]), [stt], accw=[w1t])
            for r in range(32):
                z, zt = zr[zn % 4], zrt[zn % 4]
                zn += 1
                if r < 16:
                    src = KcVcT[:, 0:16 * nb].rearrange("p (i s) -> p i s", s=16)[:, :, r]
                else:
                    src = KcVcT[:, 16:16 + 16 * nb].rearrange("p (i s) -> p i s", s=16)[:, :, r - 16]
                eng = ("dve", "pool")[r % 2]
                S.add(eng, lambda e, z=z, src=src, r=r: e.tensor_scalar(
                    out=z[:, :nb], in0=src, scalar1=posT[:, r:r + 1], scalar2=None, op0=ALU.add), [kcvt, post], [zt])
                for hid in range(2):
                    ps, pt = accs[kv * 2 + hid]
                    S.add("pe", lambda e, ps=ps, hid=hid, r=r, z=z: e.matmul(
                        ps[:, :nb], lhsT=W1b[:, r, hid * 128:(hid + 1) * 128], rhs=z[:, :nb],
                        start=(r == 0), stop=(r == 31)), [w1t, zt], [pt])
        for a in range(4):
            ps, pt = accs[a]
            S.add("act", lambda e, ps=ps: e.activation(out=ga[:, :nb], in_=ps[:, :nb], func=AF.Square), [pt], [gat])
            S.add("dve", lambda e: e.tensor_scalar(out=ga[:, :nb], in0=ga[:, :nb], scalar1=0.044715, scalar2=1.0,
                                                   op0=ALU.mult, op1=ALU.add), [gat], [gat])
            S.add("dve", lambda e, ps=ps: e.tensor_tensor(out=ga[:, :nb], in0=ga[:, :nb], in1=ps[:, :nb], op=ALU.mult),
                  [gat, pt], [gat])
            S.add("act", lambda e: e.activation(out=gb[:, :nb], in_=ga[:, :nb], func=AF.Sigmoid, scale=1.5957691216057308),
                  [gat], [gbt])
            S.add("dve", lambda e, ps=ps, a=a: e.tensor_tensor(out=GT[:, a, :nb], in0=gb[:, :nb], in1=ps[:, :nb],
                                                               op=ALU.mult), [gbt, pt], accw=[GTt])
        ps, pt = next_ps(C)
        for t in range(2):
            S.add("pe", lambda e, ps=ps, t=t: e.matmul(ps[:, :nb], lhsT=w2b[:, t, 0, :], rhs=GT[:, t, :nb],
                                                       start=(t == 0), stop=(t == 1)), [w2bt, GTt], [pt])
        norm_evac(C, ps[:, :nb], pt, gains, gtok, 9, onesbd, onest, tmps,
                  [(kclo[0:64, :nb], 0, 64, kct), (kchi[64:128, :nb], 64, 128, kct)])
        for ct in range(NCT):
            ps, pt = next_ps(C)
            wdt = min(128, nb)
            for t in range(2):
                S.add("pe", lambda e, ps=ps, t=t, ct=ct, wdt=wdt: e.matmul(
                    ps[:wdt, 0:64], lhsT=GT[:, 2 + t, ct * 128:ct * 128 + wdt], rhs=w2b[:, t, 1, 0:64],
                    start=(t == 0), stop=(t == 1)), [w2bt, GTt], [pt])
            S.add("dve", lambda e, ps=ps, ct=ct, wdt=wdt: e.tensor_copy(out=Vc1[:wdt, ct, 0:64], in_=ps[:wdt, 0:64]),
                  [pt], accw=[vct])
    kc_scope.__exit__(None, None, None)
    with S.scope():
        wexp = sb("n_wexp", [128, SL], BF16)
        triT4 = sb("n_triT4", [128, 512], BF16)
        triU4 = sb("n_triU4", [128, 512], BF16)
        cmQ = sb("n_cmQ", [128, 16, 128], BF16)
        cmT = sb("n_cmT", [128, 16, 128], BF16)
        wet, trt, trut, cmqt, cmtt = [Tok() for _ in range(5)]
        S.dma(wexp[:], cd["wexp"], writes=[wet])
        S.dma(triT4[:], cd["triT"], writes=[trt])
        S.dma(triU4[:], cd["triU"], writes=[trut])
        S.dma(cmQ[:], cd["cmaskQ"], writes=[cmqt])
        S.dma(cmT[:], cd["cmaskT"], writes=[cmtt])
        E4 = sb("n_E4", [128, 4, 512], F32)
        rsum = sb("n_rsum", [128, 4], F32)
        rinv = sb("n_rinv", [128, 4], F32)
        imp = sb("n_imp", [128, 512], F32)
        ib = sb("n_ib", [128, 128], F32)
        sc = sb("n_sc", [128, 128], F32)
        sc2 = sb("n_sc2", [128, 128], F32)
        m8a = sb("n_m8a", [128, 8], F32)
        m8b = sb("n_m8b", [128, 8], F32)
        nmf = sb("n_nmf", [128, 128], F32)
        nmb = sb("n_nmb", [128, 128], BF16)
        nmT4 = sb("n_nmT4", [128, 512], BF16)
        PT = [sb("n_PT%d" % i, [128, 512], BF16) for i in range(4)]
        PTt = [Tok() for _ in range(4)]
        den = sb("n_den", [128, 4], F32)
        coef = sb("n_coef", [128, 4], F32)
        acc = sb("n_acc", [128, 4, 64], F32)
        ob = [sb("n_ob%d" % i, [128, 256], BF16) for i in range(2)]
        obt = [Tok(), Tok()]
        e4t, rsumt, rinvt, impt, ibt_, sct, sc2t, m8at, m8bt, nmft, nmbt, nmTt, dent, coeft, acct = [Tok() for _ in range(15)]
        npool = len(C.ps)
        Obank = [(C.ps[npool - 2], C.pstok[npool - 2]), (C.ps[npool - 1], C.pstok[npool - 1])]
        C.ps_active = npool - 2
        on = 0
        ptn = 0

        def att_branch(tiles, br, qt, first_branch):
            nonlocal on, ptn
            qs = slice(qt * 128, (qt + 1) * 128)
            O, Ot = Obank[on % 2]
            on += 1
            nt = len(tiles)
            def scores(idx):
                klo, khi, ktoks, v, vtok, masks = tiles[idx]
                psT, ptT = next_ps(C)
                first = True
                for (ml, mlt, mr, mrt, wide) in masks:
                    if wide:
                        S.add("pe", lambda e, psT=psT, ml=ml, mr=mr, first=first: e.matmul(
                            psT[:, 0:512], lhsT=ml, rhs=mr, start=first, stop=False, skip_group_check=True),
                            [mlt, mrt], [ptT])
                        first = False
                    else:
                        for cb in range(4):
                            S.add("pe", lambda e, psT=psT, ml=ml, mr=mr, cb=cb, first=first: e.matmul(
                                psT[:, cb * 128:(cb + 1) * 128], lhsT=ml, rhs=mr, start=first, stop=False,
                                skip_group_check=True), [mlt, mrt], [ptT])
                            first = False
                S.add("pe", lambda e, psT=psT, klo=klo, qs=qs, first=first: e.matmul(
                    psT[:, 0:256], lhsT=klo, rhs=QT[:, :, qs], start=first, stop=False, skip_group_check=True),
                    [ktoks, qtok], [ptT])
                S.add("pe", lambda e, psT=psT, khi=khi, qs=qs: e.matmul(
                    psT[:, 256:512], lhsT=khi, rhs=QT[:, :, qs], start=False, stop=True, skip_group_check=True),
                    [ktoks, qtok], [ptT])
                return psT, ptT

            DEPTH = 2
            pendq = [scores(i) for i in range(min(DEPTH, nt))]
            for idx in range(nt):
                psT, ptT = pendq.pop(0)
                if idx + DEPTH < nt:
                    pendq.append(scores(idx + DEPTH))
                v, vtok = tiles[idx][3], tiles[idx][4]
                P, Pt_ = PT[ptn % 4], PTt[ptn % 4]
                ptn += 1
                S.add("act", lambda e, psT=psT, P=P: e.activation(out=P[:], in_=psT[:], func=AF.Exp, scale=0.125),
                      [ptT], [Pt_])
                for cb in range(4):
                    S.add("pe", lambda e, O=O, P=P, v=v, cb=cb, idx=idx: e.matmul(
                        O[:, cb * 65:(cb + 1) * 65], lhsT=P[:, cb * 128:(cb + 1) * 128], rhs=v,
                        start=(idx == 0 and cb == 0), stop=(idx == nt - 1), skip_group_check=True), [Pt_, vtok], [Ot])
            Ov = O[:, 0:260].rearrange("p (c d) -> p c d", c=4)
            S.add("dve", lambda e, Ov=Ov: e.tensor_scalar(out=den[:], in0=Ov[:, :, 64], scalar1=1e-30, scalar2=None,
                                                          op0=ALU.max), [Ot], [dent])
            S.add("dve", lambda e: e.reciprocal(out=den[:], in_=den[:]), [dent], [dent])
            gv = Gt[:, qt, br * 4:(br + 1) * 4].rearrange("p (a b) -> p b a", a=2)
            S.add("dve", lambda e, gv=gv: e.tensor_tensor(out=coef[:].rearrange("p (b a) -> p b a", b=2),
                                                          in0=den[:].rearrange("p (b a) -> p b a", b=2), in1=gv,
                                                          op=ALU.mult), [dent, gtt], [coeft])
            for cb in range(4):
                h = HORD[cb]
                if first_branch:
                    S.add("dve", lambda e, Ov=Ov, cb=cb, h=h: e.tensor_scalar(
                        out=acc[:, h, :], in0=Ov[:, cb, 0:64], scalar1=coef[:, cb:cb + 1], scalar2=None,
                        op0=ALU.mult), [Ot, coeft], [acct])
                else:
                    S.add("dve", lambda e, Ov=Ov, cb=cb, h=h: e.scalar_tensor_tensor(
                        out=acc[:, h, :], in0=Ov[:, cb, 0:64], scalar=coef[:, cb:cb + 1], in1=acc[:, h, :],
                        op0=ALU.mult, op1=ALU.add), [Ot, coeft, acct], [acct])

        for qt in range(NT if DBG_NQT is None else DBG_NQT):
            bg = getattr(C, "bg", None)
            if bg is not None:
                for _ in range(C.bg_per_tile):
                    next(bg, None)
            qs = slice(qt * 128, (qt + 1) * 128)
            ctl = (8 * qt + 6) // 128
            ncol = 128 * (ctl + 1)
            r16 = qt % 16
            for cb, (p, Kc) in enumerate(((0, kclo), (1, kclo), (0, kchi), (1, kchi))):
                psS, ptS = next_ps(C)
                first = True
                if ctl > 0:
                    S.add("pe", lambda e, psS=psS, p=p, Kc=Kc, qs=qs, ctl=ctl: e.matmul(
                        psS[:, 0:ctl * 128], lhsT=QT[:, p, qs], rhs=Kc[:, 0:ctl * 128], start=True, stop=False,
                        skip_group_check=True), [qtok, kct], [ptS])
                    first = False
                S.add("pe", lambda e, psS=psS, ctl=ctl, ncol=ncol, r16=r16, first=first: e.matmul(
                    psS[:, ctl * 128:ncol], lhsT=identb[:], rhs=cmQ[:, r16, :], start=first, stop=False,
                    skip_group_check=True), [ibt, cmqt], [ptS])
                S.add("pe", lambda e, psS=psS, p=p, Kc=Kc, qs=qs, ctl=ctl, ncol=ncol: e.matmul(
                    psS[:, ctl * 128:ncol], lhsT=QT[:, p, qs], rhs=Kc[:, ctl * 128:ncol], start=False, stop=True,
                    skip_group_check=True), [qtok, kct], [ptS])
                S.add("act", lambda e, psS=psS, cb=cb, ncol=ncol: e.activation(
                    out=E4[:, cb, :ncol], in_=psS[:, :ncol], func=AF.Exp, scale=0.125, accum_out=rsum[:, cb:cb + 1]),
                    [ptS], accw=[e4t, rsumt])
            S.add("dve", lambda e: e.tensor_scalar(out=rinv[:], in0=rsum[:], scalar1=1e-30, scalar2=None, op0=ALU.max),
                  [rsumt], [rinvt])
            S.add("dve", lambda e: e.reciprocal(out=rinv[:], in_=rinv[:]), [rinvt], [rinvt])
            S.add("dve", lambda e, ncol=ncol: e.tensor_scalar(out=imp[:, :ncol], in0=E4[:, 0, :ncol], scalar1=rinv[:, 0:1],
                                                             scalar2=None, op0=ALU.mult), [e4t, rinvt], [impt])
            for cb in range(1, 4):
                S.add("dve", lambda e, cb=cb, ncol=ncol: e.scalar_tensor_tensor(
                    out=imp[:, :ncol], in0=E4[:, cb, :ncol], scalar=rinv[:, cb:cb + 1], in1=imp[:, :ncol],
                    op0=ALU.mult, op1=ALU.add), [e4t, rinvt, impt], [impt])
            nblk = ncol // 4
            S.add("dve", lambda e, ncol=ncol, nblk=nblk: e.tensor_reduce(
                out=ib[:, :nblk], in_=imp[:, :ncol].rearrange("p (j r) -> p j r", r=4), axis=AX.X, op=ALU.add),
                [impt], [ibt_])
            S.add("dve", lambda e, nblk=nblk: e.tensor_tensor(
                out=ib[:, 1:nblk], in0=ib[:, 1:nblk],
                in1=imp[:, 0:4 * (nblk - 1)].rearrange("p (j r) -> p j r", r=4)[:, :, 3], op=ALU.add),
                [impt, ibt_], [ibt_])
            S.add("pool", lambda e: e.memset(sc[:], -1e30), [], [sct])
            if qt > 0:
                S.add("dve", lambda e, qt=qt: e.tensor_copy(out=sc[:, 0:2 * qt], in_=ib[:, 0:2 * qt]), [ibt_, sct], [sct])
                S.add("dve", lambda e, qt=qt: e.memset(sc[0:64, 2 * qt - 1:2 * qt], 1e4), [sct], [sct])
            S.add("dve", lambda e: e.memset(sc[:, 0:1], 1e4), [sct], [sct])
            S.add("dve", lambda e, qt=qt: e.memset(sc[:, 2 * qt:2 * qt + 1], 1e4), [sct], [sct])
            S.add("dve", lambda e, qt=qt: e.memset(sc[64:128, 2 * qt + 1:2 * qt + 2], 1e4), [sct], [sct])
            S.add("dve", lambda e: e.max(out=m8a[:], in_=sc[:]), [sct], [m8at])
            S.add("dve", lambda e: e.match_replace(out=sc2[:], in_to_replace=m8a[:], in_values=sc[:], imm_value=-1e30),
                  [sct, m8at], [sc2t])
            S.add("dve", lambda e: e.max(out=m8b[:], in_=sc2[:]), [sc2t], [m8bt])
            S.add("dve", lambda e: e.tensor_scalar(out=nmf[:], in0=sc[:], scalar1=m8b[:, 7:8], scalar2=None,
                                                   op0=ALU.is_ge), [sct, m8bt], [nmft])
            S.add("dve", lambda e: e.tensor_scalar(out=nmb[:], in0=nmf[:], scalar1=-1.0, scalar2=-NEG,
                                                   op0=ALU.add, op1=ALU.mult), [nmft], [nmbt])
            tiles = []
            for ct in range(ctl + 1):
                cs = slice(ct * 128, (ct + 1) * 128)
                masks = [(identb[:], ibt, cmT[:, r16, :], cmtt, False)] if ct == ctl else []
                tiles.append((kclo[:, cs], kchi[:, cs], kct, Vc1[:, ct, :], vct, masks))
            att_branch(tiles, 0, qt, True)
            tiles = []
            for kt in range(max(0, qt - 4), qt + 1):
                ks_ = slice(kt * 128, (kt + 1) * 128)
                masks = []
                if kt == qt:
                    masks.append((identb[:], ibt, triT4[:], trt, True))
                if kt == qt - 4:
                    masks.append((identb[:], ibt, triU4[:], trut, True))
                tiles.append((Kwlo[:, ks_], Kwhi[:, ks_], kwt, V1[:, kt, 1, :], v1t, masks))
            att_branch(tiles, 2, qt, False)
            pb, pbt = next_psb(C)
            S.add("pe", lambda e, pb=pb: e.transpose(out=pb[:, 0:128], in_=nmb[:], identity=identb[:]), [nmbt, ibt], [pbt])
            S.add("dve", lambda e, pb=pb: e.tensor_copy(out=nmT4[:, 0:128], in_=pb[:, 0:128]), [pbt], [nmTt])
            S.add("dve", lambda e: e.tensor_copy(out=nmT4[:, 128:256], in_=nmT4[:, 0:128]), [nmTt], [nmTt])
            S.add("dve", lambda e: e.tensor_copy(out=nmT4[:, 256:512], in_=nmT4[:, 0:256]), [nmTt], [nmTt])
            tiles = []
            for kt in range(qt + 1):
                ks_ = slice(kt * 128, (kt + 1) * 128)
                masks = [(wexp[:, ks_], wet, nmT4[:], nmTt, True)]
                if kt == qt:
                    masks.append((identb[:], ibt, triT4[:], trt, True))
                tiles.append((Kslo[:, ks_], Kshi[:, ks_], kst, V1[:, kt, 0, :], v1t, masks))
            att_branch(tiles, 1, qt, False)
            o_, ot_ = ob[qt % 2], obt[qt % 2]
            S.add("act", lambda e, o_=o_: e.activation(out=o_[:], in_=acc[:].rearrange("p h d -> p (h d)"), func=AF.Copy),
                  [acct], [ot_])
            S.dma(ao[qs, 0:256], o_[:], reads=[ot_], accw=[ao_tok], q="act")
        C.ps_active = npool


from concourse.bass_utils import run_bass_kernel_spmd

_NC_CACHE = {}


def _get_nc(key, builder):
    if key not in _NC_CACHE:
        _NC_CACHE[key] = builder()
    return _NC_CACHE[key]


def kernel(**inputs):
    z = {k: np.asarray(v) for k, v in inputs.items()}
    x = np.ascontiguousarray(z["x"], np.float32)
    B, SL, D = x.shape
    depth = z["w_in"].shape[0]
    ncA = _get_nc("A", lambda: build_phaseA(SL))
    ncB = _get_nc("B", lambda: build_phaseB(SL // 2))
    constsA = [hostA_consts(g, SL) for g in range(2)]
    ident = np.eye(128, dtype=np.float32)
    for L in range(depth):
        insA = []
        wsm = [hostA_weights(z["w_in"][L], g) for g in range(2)]
        prm = [hostA_params(z, L, g) for g in range(2)]
        for c in range(8):
            b, g = c // 2, c % 2
            d = dict(x=np.ascontiguousarray(x[b]), WS=wsm[g][0], WM=wsm[g][1])
            d.update(constsA[g])
            d.update(prm[g])
            insA.append(d)
        resA = run_bass_kernel_spmd(ncA, insA, core_ids=list(range(8)))
        ao = [np.asarray(resA.results[c]["ao"]) for c in range(8)]

        def gl(v):
            return np.ascontiguousarray(np.asarray(v, np.float32).reshape(8, 128).T)
        gains = np.ascontiguousarray(np.concatenate(
            [gl(z["norm_mix"][L]), gl(z["norm_mlp"][L]), gl(z["norm_ple"][L])], 1), np.float32)
        w_merge = np.ascontiguousarray(z["w_in"][L][:, 3352:5400], np.float32)
        insB = []
        for c in range(8):
            b, hf = c // 2, c % 2
            sl = slice(hf * (SL // 2), (hf + 1) * (SL // 2))
            attn = np.concatenate([ao[2 * b][sl, :256], ao[2 * b + 1][sl, :256],
                                   ao[2 * b][sl, 256:], ao[2 * b + 1][sl, 256:]], 1)
            insB.append(dict(
                x=np.ascontiguousarray(x[b, sl]), attn=np.ascontiguousarray(attn),
                p=np.ascontiguousarray(z["p"][L, b, sl], np.float32), gains=gains, ident=ident,
                w_merge=w_merge, w_up_nsa=np.ascontiguousarray(z["w_up_nsa"][L], np.float32),
                w_up_ret=np.ascontiguousarray(z["w_up_ret"][L], np.float32),
                w_out=np.ascontiguousarray(z["w_out"][L], np.float32),
                w_ff1=np.ascontiguousarray(z["w_ff1"][L], np.float32),
                w_ff2=np.ascontiguousarray(z["w_ff2"][L], np.float32),
                w_gate=np.ascontiguousarray(z["w_ple_gate"][L], np.float32),
                w_ple=np.ascontiguousarray(z["w_ple"][L], np.float32)))
        resB = run_bass_kernel_spmd(ncB, insB, core_ids=list(range(8)))
        xn = np.empty_like(x)
        for c in range(8):
            b, hf = c // 2, c % 2
            xn[b, hf * (SL // 2):(hf + 1) * (SL // 2)] = np.asarray(resB.results[c]["xo"])
        x = xn
    return x


B_WNAMES = (("w_merge", 1024, 2048), ("w_up_nsa", 512, 1024), ("w_up_ret", 512, 1024), ("w_out", 1024, 1024),
            ("w_ff1", 1024, 4096), ("w_ff2", 4096, 1024), ("w_gate", 1024, 1024), ("w_ple", 256, 1024))
PAIR_GROUPS = [[0, 1], [2, 3], [4, 5], [6, 7]]


def build_fused(SL=8192, depth=2):
    nc = bass.Bass("TRN2", target_bir_lowering=False)
    dt = nc.dram_tensor
    T = SL // 2
    x_full = dt("x", [SL, 1024], F32, kind="ExternalInput").ap()
    xh = dt("xh", [T, 1024], F32, kind="ExternalInput").ap()
    hmask_d = dt("hmask", [128, 2], F32, kind="ExternalInput").ap()
    cd = {k: dt(k, sh, ty, kind="ExternalInput").ap() for k, (sh, ty) in A_CONST_SHAPES(SL).items()}
    WS_d, WM_d, pd, p_d, gB_d, wd = [], [], [], [], [], []
    for L in range(depth):
        WS_d.append(dt("WS%d" % L, [1024, 2048], F32, kind="ExternalInput").ap())
        WM_d.append(dt("WM%d" % L, [1024, 1536], F32, kind="ExternalInput").ap())
        pd.append({k: dt("%s%d" % (k, L), sh, F32, kind="ExternalInput").ap() for k, sh in A_PARAM_SHAPES.items()})
        p_d.append(dt("p%d" % L, [T, 256], F32, kind="ExternalInput").ap())
        gB_d.append(dt("gainsB%d" % L, [128, 24], F32, kind="ExternalInput").ap())
        wd.append({n: dt("%s%d" % (n, L), [K, N], F32, kind="ExternalInput").ap() for n, K, N in B_WNAMES})
    out = dt("xo", [T, 1024], F32, kind="ExternalOutput").ap()
    ao = [dt("ao%d" % L, [SL, 512], BF16, kind="Internal").ap() for L in range(depth)]
    aog = [dt("aog%d" % L, [2 * SL, 512], BF16, kind="Internal").ap() for L in range(depth)]
    xmid = [dt("xmid%d" % L, [T, 1024], F32, kind="Internal").ap() for L in range(depth - 1)]
    xg = [dt("xg%d" % L, [SL, 1024], F32, kind="Internal").ap() for L in range(depth - 1)]
    S = Sched(nc)
    C = make_pools(S, n_wbuf=3)
    xg_tok = None
    xmid_tok = None
    for L in range(depth):
        S.prefix = "L%dB_" % L
        WB = make_B_wspecs(S, wd[L])
        C.bg = prep_B_gen(C, WB)
        C.bg_per_tile = -(-212 // (SL // 128)) + 1
        S.prefix = "L%dA_" % L
        aot, aogt = Tok(), Tok()
        with S.scope():
            XK = min(512, T)
            xmap = None if L == 0 else (lambda t: 2 * ((t % T) // XK) * XK + (t // T) * XK + (t % T) % XK)
            emit_phaseA(C, SL, x_full if L == 0 else xg[L - 1], WS_d[L], WM_d[L], cd, pd[L], ao[L],
                        xin_tok=xg_tok, ao_tok=aot, xmap=xmap)
        for _ in C.bg:
            pass
        C.bg = None
        RK = min(2048, SL)
        for k in range(SL // RK):
            S.collective("AllGather", ao[L][k * RK:(k + 1) * RK, :].opt(), aog[L][2 * k * RK:2 * (k + 1) * RK, :].opt(),
                         PAIR_GROUPS, reads=[aot], accw=[aogt])
        S.prefix = "L%dB_" % L
        with S.scope():
            hm = S.sbuf("hm", [128, 2], F32)
            hmt = Tok()
            S.dma(hm[:], hmask_d, writes=[hmt])
            atAB = [S.sbuf("atAB%d" % i, [128, 4, 1024], BF16) for i in range(2)]
            atABt = [Tok(), Tok()]
            nw = len(C.wbuf)
            C.wbuf = C.wbuf + [S.sbuf("wbufx%d" % i, [128, WU_ELEMS], BF16) for i in range(1)]
            C.wtok = C.wtok + [Tok() for _ in range(1)]
            last = (L == depth - 1)
            xo_tok = Tok()
            emit_phaseB(C, T, xh if L == 0 else xmid[L - 1], aog[L], p_d[L], gB_d[L], cd["ident"], wd[L],
                        out if last else xmid[L], xin_tok=xmid_tok, attn_tok=aogt, xo_tok=xo_tok,
                        gathered=(SL, hm, hmt, atAB, atABt), W=WB)
            C.wbuf = C.wbuf[:nw]
            C.wtok = C.wtok[:nw]
        if not last:
            xmid_tok = xo_tok
            xg_tok = Tok()
            XK = min(512, T)
            for k in range(T // XK):
                S.collective("AllGather", xmid[L][k * XK:(k + 1) * XK, :].opt(),
                             xg[L][2 * k * XK:2 * (k + 1) * XK, :].opt(), PAIR_GROUPS, reads=[xo_tok], accw=[xg_tok])
    S.emit()
    S.close()
    return nc


def fused_inputs(z, SL, depth):
    import ml_dtypes
    x = np.ascontiguousarray(z["x"], np.float32)
    T = SL // 2
    consts = [hostA_consts(g, SL) for g in range(2)]

    def gl(v):
        return np.ascontiguousarray(np.asarray(v, np.float32).reshape(8, 128).T)
    per_layer = []
    for L in range(depth):
        d = {}
        d["wsm"] = [hostA_weights(z["w_in"][L], g) for g in range(2)]
        d["prm"] = [hostA_params(z, L, g) for g in range(2)]
        d["gainsB"] = np.ascontiguousarray(np.concatenate(
            [gl(z["norm_mix"][L]), gl(z["norm_mlp"][L]), gl(z["norm_ple"][L])], 1), np.float32)
        d["w"] = dict(
            w_merge=np.ascontiguousarray(z["w_in"][L][:, 3352:5400], np.float32),
            w_up_nsa=np.ascontiguousarray(z["w_up_nsa"][L], np.float32),
            w_up_ret=np.ascontiguousarray(z["w_up_ret"][L], np.float32),
            w_out=np.ascontiguousarray(z["w_out"][L], np.float32),
            w_ff1=np.ascontiguousarray(z["w_ff1"][L], np.float32),
            w_ff2=np.ascontiguousarray(z["w_ff2"][L], np.float32),
            w_gate=np.ascontiguousarray(z["w_ple_gate"][L], np.float32),
            w_ple=np.ascontiguousarray(z["w_ple"][L], np.float32))
        per_layer.append(d)
    ins = []
    for c in range(8):
        b, r = c // 2, c % 2
        sl = slice(r * T, (r + 1) * T)
        d = dict(x=np.ascontiguousarray(x[b, :SL]), xh=np.ascontiguousarray(x[b, sl]))
        hm = np.zeros((128, 2), np.float32)
        hm[:, r] = 1.0
        d["hmask"] = hm
        d.update(consts[r])
        for L in range(depth):
            pl = per_layer[L]
            d["WS%d" % L], d["WM%d" % L] = pl["wsm"][r]
            for k, v in pl["prm"][r].items():
                d["%s%d" % (k, L)] = v
            d["p%d" % L] = np.ascontiguousarray(z["p"][L, b, sl], np.float32)
            d["gainsB%d" % L] = pl["gainsB"]
            for k, v in pl["w"].items():
                d["%s%d" % (k, L)] = v
        ins.append(d)
    return ins


def kernel(**inputs):
    z = {k: np.asarray(v) for k, v in inputs.items()}
    B, SL, D = z["x"].shape
    depth = z["w_in"].shape[0]
    nc = _get_nc(("F", SL, depth), lambda: build_fused(SL, depth))
    ins = fused_inputs(z, SL, depth)
    res = run_bass_kernel_spmd(nc, ins, core_ids=list(range(8)))
    T = SL // 2
    out = np.empty((B, SL, D), np.float32)
    for c in range(8):
        b, r = c // 2, c % 2
        out[b, r * T:(r + 1) * T] = np.asarray(res.results[c]["xo"])
    return out
```

```python
from contextlib import ExitStack
import numpy as np
import concourse.bass as bass
import concourse.mybir as mybir

F32 = mybir.dt.float32
BF16 = mybir.dt.bfloat16
I32 = mybir.dt.int32
AF = mybir.ActivationFunctionType
ALU = mybir.AluOpType
AX = mybir.AxisListType

ENGS = ("pe", "act", "dve", "pool", "sp")
N_DMA_SEMS = 24


class Tok:
    __slots__ = ("lws", "rs", "base", "name", "excl", "accgrp")

    def __init__(self, name="", excl=False):
        self.excl = excl
        self.accgrp = False
        self.lws = []
        self.rs = []
        self.base = []
        self.name = name


class Op:
    __slots__ = ("eng", "fn", "deps", "dma", "idx", "sig", "dma_n", "cc")

    def __init__(self, eng, fn, deps, dma, idx):
        self.eng = eng
        self.fn = fn
        self.deps = deps
        self.dma = dma
        self.idx = idx
        self.sig = None
        self.dma_n = None
        self.cc = None


class _Scope:
    def __init__(self, S):
        self.S = S

    def __enter__(self):
        self.saved = self.S.stack
        self.S.stack = ExitStack()
        return self

    def __exit__(self, *a):
        self.S.barrier()
        self.S.stack.close()
        self.S.stack = self.saved
        return False


class Sched:
    def __init__(self, nc):
        self.nc = nc
        self.ops = {e: [] for e in ENGS}
        self.ndma = {e: 0 for e in ENGS}
        self.final_waits = []
        self.all_dma = []
        self.ncc = 0
        self.prefix = ""
        self.stack = ExitStack()

    def sbuf(self, name, shape, dtype):
        return self.stack.enter_context(self.nc.sbuf_tensor("sb_" + self.prefix + name, list(shape), dtype))

    def psum(self, name, shape, dtype):
        return self.stack.enter_context(self.nc.psum_tensor("pp_" + name, list(shape), dtype))

    def add(self, eng, fn, reads=(), writes=(), dma=False, accw=(), extra=()):
        deps = []
        seen = set()

        def push(d):
            if d is not None and d not in seen:
                seen.add(d)
                deps.append(d)

        for d in extra:
            push(d)
        for t in reads:
            for w in t.lws:
                push(w)
            if t.excl:
                for r in t.rs:
                    if r[0] != eng:
                        push(r)
        for t in writes:
            for w in t.lws:
                push(w)
            for r in t.rs:
                push(r)
        for t in accw:
            if t.rs or not t.lws or not t.accgrp:
                for w in t.lws:
                    push(w)
                for r in t.rs:
                    push(r)
            else:
                for d in t.base:
                    push(d)
        lst = self.ops[eng]
        op = Op(eng, fn, deps, dma, len(lst))
        if dma:
            op.dma_n = self.ndma[eng]
            self.ndma[eng] += 1
            self.all_dma.append((eng, op.idx))
        lst.append(op)
        me = (eng, op.idx)
        for t in reads:
            t.rs.append(me)
        for t in writes:
            t.lws = [me]
            t.rs = []
            t.base = []
            t.accgrp = False
        for t in accw:
            if t.rs or not t.lws or not t.accgrp:
                t.base = list(t.lws) + list(t.rs)
                t.lws = [me]
                t.rs = []
                t.accgrp = True
            else:
                t.lws.append(me)
        return op

    def collective(self, kind, src, dst, groups, reads=(), writes=(), accw=()):
        op = self.add("pool", lambda e: e.collective_compute(kind, ALU.bypass, replica_groups=groups,
                                                             ins=[src], outs=[dst]), reads, writes, accw=accw)
        self.ncc += 1
        op.cc = self.ncc
        return op

    def barrier(self):
        extra = list(self.all_dma)
        for e in ENGS:
            if self.ops[e]:
                extra.append((e, len(self.ops[e]) - 1))
        self.all_dma = []
        b0 = self.add("sp", lambda e: e.nop(), extra=extra)
        me = ("sp", b0.idx)
        for e in ("pe", "act", "dve", "pool"):
            self.add(e, lambda eng: eng.nop(), extra=[me])

    def dma(self, out, in_, reads=(), writes=(), q="sp", accw=(), **kw):
        return self.add(q, lambda e: e.dma_start(out=out, in_=in_, **kw), reads, writes, dma=True, accw=accw)

    def scope(self):
        return _Scope(self)

    def emit(self):
        nc = self.nc
        ops = self.ops
        needed = {e: set() for e in ENGS}
        waits = {e: [] for e in ENGS}
        for e in ENGS:
            maxw = {d: -1 for d in ENGS}
            dma_waited = set()
            for op in ops[e]:
                keep = []
                for (de, di) in op.deps:
                    dop = ops[de][di]
                    if dop.dma or dop.cc:
                        if (de, di) in dma_waited:
                            continue
                        dma_waited.add((de, di))
                        keep.append((de, di))
                    else:
                        if de == e and e == "pe":
                            continue
                        if de == e and di == op.idx:
                            continue
                        if di <= maxw[de]:
                            continue
                        maxw[de] = di
                        keep.append((de, di))
                        needed[de].add(di)
                waits[e].append(keep)
        for e in ENGS:
            c = 0
            for op in ops[e]:
                if (not op.dma) and (not op.cc) and op.idx in needed[e]:
                    c += 1
                    op.sig = c
        st = self.stack
        csem = {e: st.enter_context(nc.semaphore("c_" + e)) for e in ENGS}
        ccsem = st.enter_context(nc.semaphore("cc_sem"))
        dsem = {e: [st.enter_context(nc.semaphore("d_%s_%d" % (e, i))) for i in range(N_DMA_SEMS)]
                for e in ENGS if self.ndma[e] > 0}
        block = st.enter_context(nc.Block())

        def gen(e, eng):
            for op, keep in zip(ops[e], waits[e]):
                if op.dma:
                    n = op.dma_n
                    if n >= N_DMA_SEMS:
                        eng.wait_ge(dsem[e][n % N_DMA_SEMS], 16 * (n // N_DMA_SEMS))
                for (de, di) in keep:
                    dop = ops[de][di]
                    if dop.dma:
                        n = dop.dma_n
                        eng.wait_ge(dsem[de][n % N_DMA_SEMS], 16 * (n // N_DMA_SEMS + 1))
                    elif dop.cc:
                        eng.wait_ge(ccsem, dop.cc)
                    else:
                        eng.wait_ge(csem[de], dop.sig)
                ins = op.fn(eng)
                if op.dma:
                    n = op.dma_n
                    ins.then_inc(dsem[e][n % N_DMA_SEMS], 16)
                elif op.cc:
                    ins.then_inc(ccsem, 1)
                elif op.sig is not None:
                    ins.then_inc(csem[e], 1)
            if e == "pool" and self.ncc:
                eng.wait_ge(ccsem, self.ncc)
            nd = self.ndma[e]
            for i in range(min(nd, N_DMA_SEMS)):
                cnt = (nd - 1 - i) // N_DMA_SEMS + 1
                eng.wait_ge(dsem[e][i], 16 * cnt)

        @block.tensor
        def _(eng):
            gen("pe", eng)

        @block.scalar
        def _(eng):
            gen("act", eng)

        @block.vector
        def _(eng):
            gen("dve", eng)

        @block.gpsimd
        def _(eng):
            gen("pool", eng)

        @block.sync
        def _(eng):
            gen("sp", eng)

    def close(self):
        self.stack.close()


D_MODEL = 1024
EPS = 1e-6
WU_ELEMS = 4096


class WSpec:
    def __init__(self, S, name, w_ap, K, N, kind):
        self.name, self.K, self.N, self.kind = name, K, N, kind
        self.KT = K // 128
        self.w = w_ap
        nc = S.nc
        if kind == "S":
            assert N % 512 == 0
            self.nunits = N // 512
            self.uelems = 4 * self.KT * 128
        else:
            assert N % 512 == 0
            self.KTU = min(8, self.KT)
            self.nv = self.KT // self.KTU
            self.nunits = (N // 512) * self.nv
            self.uelems = self.KTU * 512
        assert self.uelems <= WU_ELEMS
        self.scr = nc.dram_tensor("scr_" + S.prefix + name, [self.nunits, 128, self.uelems], BF16, kind="Internal").ap()
        self.tok = Tok("scr_" + name)

    def unit_src(self, u):
        return self.scr[u]

    def view(self, buf):
        b = buf[:, : self.uelems]
        if self.kind == "S":
            return b.rearrange("p (f k c) -> p f k c", f=4, k=self.KT)
        return b.rearrange("p (k c) -> p k c", k=self.KTU)


class Ctx:
    pass


def make_pools(S, n_wbuf=5, n_ps=6):
    C = Ctx()
    C.S = S
    C.wbuf = [S.sbuf("wbuf%d" % i, [128, WU_ELEMS], BF16) for i in range(n_wbuf)]
    C.wtok = [Tok("wbuf%d" % i) for i in range(n_wbuf)]
    C.wn = 0
    C.ps = [S.psum("ps%d" % i, [128, 512], F32) for i in range(n_ps)]
    C.pstok = [Tok("ps%d" % i, excl=True) for i in range(n_ps)]
    C.pn = 0
    C.psb = [S.psum("psb%d" % i, [128, 1024], BF16) for i in range(2)]
    C.psbtok = [Tok("psb0", excl=True), Tok("psb1", excl=True)]
    C.pbn = 0
    C.stg = [S.sbuf("stg%d" % i, [128, 512], F32) for i in range(3)]
    C.stgtok = [Tok() for _ in range(3)]
    C.stgb = [S.sbuf("stgb%d" % i, [128, 512], BF16) for i in range(3)]
    C.stgbtok = [Tok() for _ in range(3)]
    C.sn = 0
    C.cast_rr = 0
    return C


def next_ps(C):
    i = C.pn % getattr(C, "ps_active", len(C.ps))
    C.pn += 1
    return C.ps[i], C.pstok[i]


def next_psb(C):
    i = C.pbn % 2
    C.pbn += 1
    return C.psb[i][:, 0:512], C.psbtok[i]


def load_unit(C, ws, u, q="sp"):
    i = C.wn % len(C.wbuf)
    C.wn += 1
    buf, tok = C.wbuf[i], C.wtok[i]
    C.S.dma(buf[:, : ws.uelems], ws.unit_src(u), reads=[ws.tok], writes=[tok], q=q)
    return ws.view(buf), tok


def prep_weight_gen(C, ws, q="act"):
    S = C.S
    K, N, KT = ws.K, ws.N, ws.KT
    for kt in range(KT):
        for c0 in range(0, N, 512):
            cw = min(512, N - c0)
            i = C.sn % 3
            C.sn += 1
            st, stt, sb, sbt = C.stg[i], C.stgtok[i], C.stgb[i], C.stgbtok[i]
            S.dma(st[:, :cw], ws.w[kt * 128:(kt + 1) * 128, c0:c0 + cw], writes=[stt])
            eng = ("dve", "pool")[C.cast_rr % 2] if q == "act" else "pool"
            C.cast_rr += 1
            S.add(eng, lambda e, sb=sb, st=st, cw=cw: e.tensor_copy(out=sb[:, :cw], in_=st[:, :cw]), [stt], [sbt])
            if ws.kind == "S":
                u0, nu = c0 // 512, cw // 512
                dst = ws.scr[u0:u0 + nu].rearrange("u p (f k c) -> p u f k c", f=4, k=KT)[:, :, :, kt, :]
                src = sb[:, :cw].rearrange("p (u f c) -> p u f c", u=nu, f=4)
                for uu in range(nu):
                    S.dma(dst[:, uu], src[:, uu], reads=[sbt], accw=[ws.tok], q=q)
            else:
                v, kk = kt // ws.KTU, kt % ws.KTU
                n0, nn = c0 // 512, cw // 512
                for n in range(nn):
                    u = (n0 + n) * ws.nv + v
                    dst = ws.scr[u].rearrange("p (k c) -> p k c", k=ws.KTU)[:, kk, :]
                    S.dma(dst, sb[:, n * 512:(n + 1) * 512], reads=[sbt], accw=[ws.tok], q=q)
            yield


def prep_weight(C, ws):
    for _ in prep_weight_gen(C, ws):
        pass


def rms_to_featmajor(C, xt, xtok, gains, gtok, gcol0, hT, htok, ident, itok, tmp):
    S = C.S
    ss, sstok, xs, xstok, junk, jtok, rstd, rtok = tmp
    for j in range(4):
        S.add("act", lambda e, j=j: e.activation(
            out=xs[:, j, :], in_=xt[:, j, :], func=AF.Square, accum_out=ss[:, j:j + 1]), [xtok], [xstok, sstok])
    S.add("dve", lambda e: e.tensor_scalar(out=rstd[:], in0=ss[:], scalar1=1.0 / D_MODEL, scalar2=EPS,
                                           op0=ALU.mult, op1=ALU.add), [sstok], [rtok])
    S.add("act", lambda e: e.activation(out=rstd[:], in_=rstd[:], func=AF.Sqrt), [rtok], [rtok])
    S.add("dve", lambda e: e.reciprocal(out=rstd[:], in_=rstd[:]), [rtok], [rtok])
    for j in range(4):
        S.add("act", lambda e, j=j: e.activation(out=xs[:, j, :], in_=xt[:, j, :], func=AF.Copy,
                                                 scale=rstd[:, j:j + 1]), [xtok, rtok], [xstok])
    for kt in range(8):
        ps, pt = next_ps(C)
        for j in range(4):
            S.add("pe", lambda e, ps=ps, j=j, kt=kt: e.transpose(
                out=ps[:, j * 128:(j + 1) * 128], in_=xs[:, j, kt * 128:(kt + 1) * 128], identity=ident[:]),
                [xstok, itok], [pt])
        if kt % 2 == 0:
            S.add("dve", lambda e, ps=ps, kt=kt: e.tensor_scalar(
                out=hT[:, kt, :], in0=ps[:], scalar1=gains[:, gcol0 + kt:gcol0 + kt + 1], scalar2=None,
                op0=ALU.mult), [pt, gtok], accw=[htok])
        else:
            S.add("act", lambda e, ps=ps, kt=kt: e.activation(
                out=hT[:, kt, :], in_=ps[:], func=AF.Copy, scale=gains[:, gcol0 + kt:gcol0 + kt + 1]),
                [pt, gtok], accw=[htok])


def build_phaseB(T=4096):
    nc = bass.Bass("TRN2", target_bir_lowering=False)
    dt = nc.dram_tensor
    x = dt("x", [T, 1024], F32, kind="ExternalInput").ap()
    attn = dt("attn", [T, 1024], BF16, kind="ExternalInput").ap()
    pin = dt("p", [T, 256], F32, kind="ExternalInput").ap()
    gains_d = dt("gains", [128, 24], F32, kind="ExternalInput").ap()
    ident_d = dt("ident", [128, 128], F32, kind="ExternalInput").ap()
    wd = {}
    for name, K, N in (("w_merge", 1024, 2048), ("w_up_nsa", 512, 1024), ("w_up_ret", 512, 1024),
                       ("w_out", 1024, 1024), ("w_ff1", 1024, 4096), ("w_ff2", 4096, 1024),
                       ("w_gate", 1024, 1024), ("w_ple", 256, 1024)):
        wd[name] = dt(name, [K, N], F32, kind="ExternalInput").ap()
    xo = dt("xo", [T, 1024], F32, kind="ExternalOutput").ap()
    S = Sched(nc)
    C = make_pools(S)
    emit_phaseB(C, T, x, attn, pin, gains_d, ident_d, wd, xo)
    S.emit()
    S.close()
    return nc


B_KINDS = {"w_merge": "S", "w_up_nsa": "S", "w_up_ret": "S", "w_out": "M", "w_ff1": "S", "w_ff2": "M",
           "w_gate": "M", "w_ple": "M"}


def make_B_wspecs(S, wd):
    W = {}
    for name, ap in wd.items():
        K, N = ap.shape
        W[name] = WSpec(S, name, ap, K, N, B_KINDS[name])
    return W


def prep_B_gen(C, W):
    for name in ("w_merge", "w_up_nsa", "w_up_ret", "w_out", "w_ff1", "w_ff2", "w_gate", "w_ple"):
        for _ in prep_weight_gen(C, W[name], q="sp"):
            yield


DBG_STAGE = 99
DBG_NQT = None
DBG_R = 99
DBG_SUB = 0
DBG_Q = "act"
DBG_PREP = True


def emit_phaseB(C, T, x, attn, pin, gains_d, ident_d, wd, xo, xin_tok=None, attn_tok=None, xo_tok=None,
                gathered=None, W=None):
    S = C.S
    kinds = {"w_merge": "S", "w_up_nsa": "S", "w_up_ret": "S", "w_out": "M", "w_ff1": "S", "w_ff2": "M",
             "w_gate": "M", "w_ple": "M"}
    preW = W is not None
    if not preW:
        W = make_B_wspecs(S, wd)
    gains = S.sbuf("gains", [128, 24], F32)
    gtok = Tok()
    ident = S.sbuf("ident", [128, 128], F32)
    identb = S.sbuf("identb", [128, 128], BF16)
    itok, ibtok = Tok(), Tok()
    S.dma(gains[:], gains_d, writes=[gtok])
    S.dma(ident[:], ident_d, writes=[itok])
    S.add("dve", lambda e: e.tensor_copy(out=identb[:], in_=ident[:]), [itok], [ibtok])
    for name in ("w_merge", "w_up_nsa", "w_up_ret", "w_out", "w_ff1", "w_ff2", "w_gate", "w_ple"):
        if DBG_PREP and not preW:
            prep_weight(C, W[name])
    xt = S.sbuf("xt", [128, 4, 1024], F32)
    at = S.sbuf("at", [128, 4, 1024], BF16)
    ptm = S.sbuf("ptm", [128, 4, 256], F32)
    xs = S.sbuf("xs", [128, 4, 1024], F32)
    junk = None
    ss = S.sbuf("ss", [128, 4], F32)
    rstd = S.sbuf("rstd", [128, 4], F32)
    hT = S.sbuf("hT", [128, 8, 512], BF16)
    aT = S.sbuf("aT", [128, 8, 512], BF16)
    sgT = S.sbuf("sgT", [128, 16, 512], BF16)
    mixT = S.sbuf("mixT", [128, 8, 512], BF16)
    uT = S.sbuf("uT", [128, 32, 512], BF16)
    pT = S.sbuf("pT", [128, 2, 512], BF16)
    tmpf = [S.sbuf("tmpf%d" % i, [128, 512], F32) for i in range(2)]
    tmpft = [Tok(), Tok()]
    gsb = S.sbuf("gsb", [128, 512], F32)
    xtok, atok, ptok, xstok, jtok, sstok, rtok = [Tok() for _ in range(7)]
    htok, aTtok, sgtok, mixtok, utok, pTtok, gsbtok = [Tok() for _ in range(7)]
    tmp = (ss, sstok, xs, xstok, junk, jtok, rstd, rtok)
    xin_tok = xin_tok or Tok()
    attn_tok = attn_tok or Tok()
    xo_tok = xo_tok or Tok()
    nchunk = T // 512
    tn = 0
    for c in range(nchunk):
        t0 = c * 512
        S.dma(xt[:], x[t0:t0 + 512, :].rearrange("(j p) d -> p j d", p=128), reads=[xin_tok], writes=[xtok])
        if gathered is None:
            S.dma(at[:], attn[t0:t0 + 512, :].rearrange("(j p) d -> p j d", p=128), reads=[attn_tok], writes=[atok])
        else:
            SLg, hm, hmt, atAB, atABt = gathered
            for hf in range(2):
                for g in range(2):
                    RKg = min(2048, SLg)
                    tk_ = hf * T + t0
                    r0 = 2 * (tk_ // RKg) * RKg + g * RKg + tk_ % RKg
                    srcv = attn[r0:r0 + 512, :].rearrange("(j p) d -> p j d", p=128)
                    S.dma(atAB[hf][:, :, g * 256:(g + 1) * 256], srcv[:, :, 0:256], reads=[attn_tok], accw=[atABt[hf]])
                    S.dma(atAB[hf][:, :, 512 + g * 256:512 + (g + 1) * 256], srcv[:, :, 256:512], reads=[attn_tok],
                          accw=[atABt[hf]])
            S.add("dve", lambda e: e.tensor_scalar(out=at[:], in0=atAB[0][:], scalar1=hm[:, 0:1], scalar2=None,
                                                   op0=ALU.mult), [atABt[0], hmt], [atok])
            S.add("dve", lambda e: e.scalar_tensor_tensor(out=at[:], in0=atAB[1][:], scalar=hm[:, 1:2], in1=at[:],
                                                          op0=ALU.mult, op1=ALU.add), [atABt[1], hmt, atok], [atok])
        S.dma(ptm[:], pin[t0:t0 + 512, :].rearrange("(j p) d -> p j d", p=128), writes=[ptok])
        def _store(t0=t0):
            S.dma(xo[t0:t0 + 512, :].rearrange("(j p) d -> p j d", p=128), xt[:], reads=[xtok], writes=[xo_tok],
                  q=DBG_Q)
        if DBG_STAGE <= 0:
            _store()
            continue
        rms_to_featmajor(C, xt, xtok, gains, gtok, 0, hT, htok, ident, itok, tmp)
        if DBG_STAGE <= 1:
            _store()
            continue
        ws = W["w_merge"]
        for u in range(ws.nunits):
            wv, wt = load_unit(C, ws, u)
            for f in range(4):
                ps, pt = next_ps(C)
                if DBG_SUB == 1:
                    continue
                for kt in range(8):
                    S.add("pe", lambda e, ps=ps, wv=wv, f=f, kt=kt: e.matmul(
                        ps[:], lhsT=wv[:, f, kt, :], rhs=hT[:, kt, :], start=(kt == 0), stop=(kt == 7)),
                        [wt, htok], [pt])
                if DBG_SUB == 2:
                    continue
                S.add("act", lambda e, ps=ps, ft=u * 4 + f: e.activation(
                    out=sgT[:, ft, :], in_=ps[:], func=AF.Sigmoid), [pt], accw=[sgtok])
        if DBG_STAGE <= 2:
            _store()
            continue
        for ft in range(8):
            pb, pbt = next_psb(C)
            for j in range(4):
                S.add("pe", lambda e, pb=pb, j=j, ft=ft: e.transpose(
                    out=pb[:, j * 128:(j + 1) * 128], in_=at[:, j, ft * 128:(ft + 1) * 128], identity=identb[:]),
                    [atok, ibtok], [pbt])
            S.add("dve", lambda e, pb=pb, ft=ft: e.tensor_copy(out=aT[:, ft, :], in_=pb), [pbt], accw=[aTtok])
        if DBG_SUB == 3:
            _store()
            continue
        wsa, wsb = W["w_up_nsa"], W["w_up_ret"]
        for u in range(2):
            wva, wta = load_unit(C, wsa, u)
            wvb, wtb = load_unit(C, wsb, u)
            for f in range(4):
                ft = u * 4 + f
                psa, pta = next_ps(C)
                for kt in range(4):
                    S.add("pe", lambda e, psa=psa, wva=wva, f=f, kt=kt: e.matmul(
                        psa[:], lhsT=wva[:, f, kt, :], rhs=aT[:, kt, :], start=(kt == 0), stop=(kt == 3)),
                        [wta, aTtok], [pta])
                psb_, ptb = next_ps(C)
                for kt in range(4):
                    S.add("pe", lambda e, psb_=psb_, wvb=wvb, f=f, kt=kt: e.matmul(
                        psb_[:], lhsT=wvb[:, f, kt, :], rhs=aT[:, 4 + kt, :], start=(kt == 0), stop=(kt == 3)),
                        [wtb, aTtok], [ptb])
                if DBG_SUB == 4:
                    continue
                tf, tft = tmpf[tn % 2], tmpft[tn % 2]
                tn += 1
                S.add("dve", lambda e, tf=tf, psa=psa, ft=ft: e.tensor_tensor(
                    out=tf[:], in0=psa[:], in1=sgT[:, ft, :], op=ALU.mult), [pta, sgtok], [tft])
                tf2, tft2 = tmpf[tn % 2], tmpft[tn % 2]
                tn += 1
                S.add("dve", lambda e, tf2=tf2, psb_=psb_, ft=ft: e.tensor_tensor(
                    out=tf2[:], in0=psb_[:], in1=sgT[:, 8 + ft, :], op=ALU.mult), [ptb, sgtok], [tft2])
                if DBG_SUB == 5:
                    continue
                S.add("pool", lambda e, tf=tf, tf2=tf2, ft=ft: e.tensor_tensor(
                    out=mixT[:, ft, :], in0=tf[:], in1=tf2[:], op=ALU.add), [tft, tft2], accw=[mixtok])
        if DBG_STAGE <= 3:
            _store()
            continue
        ws = W["w_out"]
        for n in range(2):
            wv, wt = load_unit(C, ws, n)
            for j in range(4):
                ps, pt = next_ps(C)
                for kt in range(8):
                    S.add("pe", lambda e, ps=ps, wv=wv, j=j, kt=kt: e.matmul(
                        ps[:], lhsT=mixT[:, kt, j * 128:(j + 1) * 128], rhs=wv[:, kt, :],
                        start=(kt == 0), stop=(kt == 7)), [wt, mixtok], [pt])
                S.add("dve", lambda e, ps=ps, j=j, n=n: e.tensor_tensor(
                    out=xt[:, j, n * 512:(n + 1) * 512], in0=ps[:], in1=xt[:, j, n * 512:(n + 1) * 512],
                    op=ALU.add), [pt, xtok], [xtok])
        if DBG_STAGE <= 4:
            _store()
            continue
        rms_to_featmajor(C, xt, xtok, gains, gtok, 8, hT, htok, ident, itok, tmp)
        ws = W["w_ff1"]
        for u in range(ws.nunits):
            wv, wt = load_unit(C, ws, u)
            for f in range(4):
                ft = u * 4 + f
                ps, pt = next_ps(C)
                for kt in range(8):
                    S.add("pe", lambda e, ps=ps, wv=wv, f=f, kt=kt: e.matmul(
                        ps[:], lhsT=wv[:, f, kt, :], rhs=hT[:, kt, :], start=(kt == 0), stop=(kt == 7)),
                        [wt, htok], [pt])
                tf, tft = tmpf[tn % 2], tmpft[tn % 2]
                tn += 1
                S.add("act", lambda e, ps=ps, tf=tf: e.activation(out=tf[:], in_=ps[:], func=AF.Relu),
                      [pt], [tft])
                S.add("pool", lambda e, tf=tf, ft=ft: e.tensor_tensor(
                    out=uT[:, ft, :], in0=tf[:], in1=tf[:], op=ALU.mult), [tft], accw=[utok])
        ws = W["w_ff2"]
        for n in range(2):
            pss = [next_ps(C) for _ in range(4)]
            for v in range(ws.nv):
                wv, wt = load_unit(C, ws, n * ws.nv + v)
                for j in range(4):
                    ps, pt = pss[j]
                    for kk in range(8):
                        kt = v * 8 + kk
                        S.add("pe", lambda e, ps=ps, wv=wv, j=j, kk=kk, kt=kt: e.matmul(
                            ps[:], lhsT=uT[:, kt, j * 128:(j + 1) * 128], rhs=wv[:, kk, :],
                            start=(kt == 0), stop=(kt == 31)), [wt, utok], [pt])
            for j in range(4):
                ps, pt = pss[j]
                S.add("dve", lambda e, ps=ps, j=j, n=n: e.tensor_tensor(
                    out=xt[:, j, n * 512:(n + 1) * 512], in0=ps[:], in1=xt[:, j, n * 512:(n + 1) * 512],
                    op=ALU.add), [pt, xtok], [xtok])
        if DBG_STAGE <= 5:
            _store()
            continue
        rms_to_featmajor(C, xt, xtok, gains, gtok, 16, hT, htok, ident, itok, tmp)
        for kt in range(2):
            ps, pt = next_ps(C)
            for j in range(4):
                S.add("pe", lambda e, ps=ps, j=j, kt=kt: e.transpose(
                    out=ps[:, j * 128:(j + 1) * 128], in_=ptm[:, j, kt * 128:(kt + 1) * 128], identity=ident[:]),
                    [ptok, itok], [pt])
            S.add("dve", lambda e, ps=ps, kt=kt: e.tensor_copy(out=pT[:, kt, :], in_=ps[:]), [pt], accw=[pTtok])
        wsg, wsp = W["w_gate"], W["w_ple"]
        for n in range(2):
            wvg, wtg = load_unit(C, wsg, n)
            wvp, wtp = load_unit(C, wsp, n)
            for j in range(4):
                ps, pt = next_ps(C)
                for kt in range(8):
                    S.add("pe", lambda e, ps=ps, wvg=wvg, j=j, kt=kt: e.matmul(
                        ps[:], lhsT=hT[:, kt, j * 128:(j + 1) * 128], rhs=wvg[:, kt, :],
                        start=(kt == 0), stop=(kt == 7)), [wtg, htok], [pt])
                S.add("act", lambda e, ps=ps: e.activation(out=gsb[:], in_=ps[:], func=AF.Sigmoid),
                      [pt], [gsbtok])
                ps2, pt2 = next_ps(C)
                for kt in range(2):
                    S.add("pe", lambda e, ps2=ps2, wvp=wvp, j=j, kt=kt: e.matmul(
                        ps2[:], lhsT=pT[:, kt, j * 128:(j + 1) * 128], rhs=wvp[:, kt, :],
                        start=(kt == 0), stop=(kt == 1)), [wtp, pTtok], [pt2])
                tf, tft = tmpf[tn % 2], tmpft[tn % 2]
                tn += 1
                S.add("dve", lambda e, tf=tf, ps2=ps2: e.tensor_tensor(
                    out=tf[:], in0=ps2[:], in1=gsb[:], op=ALU.mult), [pt2, gsbtok], [tft])
                S.add("pool", lambda e, tf=tf, j=j, n=n: e.tensor_tensor(
                    out=xt[:, j, n * 512:(n + 1) * 512], in0=tf[:], in1=xt[:, j, n * 512:(n + 1) * 512],
                    op=ALU.add), [tft, xtok], [xtok])
        _store()


IN_SPLITS = (512, 128, 128, 128, 128, 128, 128, 24, 512, 512, 512, 512, 1024, 1024)
NEG = -30000.0


def hostA_weights(w_in, g):
    offs = np.cumsum([0] + list(IN_SPLITS))

    def col(i, a, b):
        return w_in[:, offs[i] + a: offs[i] + b]

    def swap(x):
        return np.concatenate([x[:, 64:], x[:, :64]], 1)

    q = [col(0, (g * 4 + h) * 64, (g * 4 + h + 1) * 64) for h in range(4)]
    kc, vc = col(1, g * 64, g * 64 + 64), col(2, g * 64, g * 64 + 64)
    ks, vs = col(3, g * 64, g * 64 + 64), col(4, g * 64, g * 64 + 64)
    kw, vw = col(5, g * 64, g * 64 + 64), col(6, g * 64, g * 64 + 64)
    gates = np.stack([w_in[:, offs[7] + br * 8 + g * 4 + h] for br in range(3) for h in range(4)], 1)
    rq = [col(8, (2 * g + h) * 128, (2 * g + h + 1) * 128) for h in range(2)]
    rk = [col(9, (2 * g + h) * 128, (2 * g + h + 1) * 128) for h in range(2)]
    rv = col(10, 2 * g * 128, (2 * g + 2) * 128)
    rg = col(11, 2 * g * 128, (2 * g + 2) * 128)
    z128 = np.zeros((1024, 128), np.float32)
    WS = np.concatenate([q[0], q[1], q[2], q[3], ks, ks, kw, kw, kc, vc, z128, z128, z128,
                         rq[0], swap(rq[0]), rq[1], swap(rq[1]), rk[0], swap(rk[0]), rk[1], swap(rk[1])], 1)
    WM = np.concatenate([rk[0], rk[1], rv, rg, np.zeros((1024, 256), np.float32),
                         vs, vw, gates, np.zeros((1024, 512 - 140), np.float32)], 1)
    return np.ascontiguousarray(WS, np.float32), np.ascontiguousarray(WM, np.float32)


def hostA_consts(g, S):
    import ml_dtypes
    bf = ml_dtypes.bfloat16
    c = {}
    c["ident"] = np.eye(128, dtype=np.float32)
    bd = np.zeros((128, 128), np.float32)
    bd[:64, :64] = 1
    bd[64:, 64:] = 1
    c["onesbd"] = bd.astype(bf)
    half = 64
    inv = (10000.0 ** (-np.arange(half, dtype=np.float32) / half)).astype(np.float32)
    pos = np.arange(S, dtype=np.float32)
    ang = (pos[:, None] * inv[None, :]).astype(np.float32)
    cos, sin = np.cos(ang.astype(np.float64)), np.sin(ang.astype(np.float64))
    cosT = np.concatenate([cos.T, cos.T], 0)
    sinsT = np.concatenate([-sin.T, sin.T], 0)
    ksc = 128.0 ** -0.5
    c["ropeq"] = np.stack([cosT, sinsT], 1).astype(np.float32)
    c["ropek"] = (np.stack([cosT, sinsT], 1) * ksc).astype(np.float32)
    hh = np.array([2 * g, 2 * g + 1], np.float64)
    gamma = 1.0 - 2.0 ** (-5.0 - hh)
    lg = np.log(gamma)
    n = np.arange(128, dtype=np.float64)
    xi = np.exp(lg[:, None] * (n + 1.0))
    zeta = np.exp(lg[:, None] * (127.0 - n))
    c["xi"] = np.broadcast_to(np.tile(xi, (1, 4))[None], (128, 2, 512)).astype(np.float32).copy()
    zt = zeta[:, np.arange(S) % 128]
    c["ctk"] = (cos[:, None, :] * zt.T[:, :, None] * ksc).astype(np.float32)
    c["stk"] = (sin[:, None, :] * zt.T[:, :, None] * ksc).astype(np.float32)
    diff = n[None, :] - n[:, None]
    dec = np.where(diff[None] >= 0, np.exp(lg[:, None, None] * np.maximum(diff[None], 0)), 0.0)
    c["decayT"] = np.ascontiguousarray(dec.transpose(1, 0, 2)).astype(np.float32)
    c["gch"] = np.broadcast_to(np.exp(lg * 128.0)[None], (128, 2)).astype(np.float32).copy()
    kk, qq = np.arange(128)[:, None], np.arange(128)[None, :]
    c["triT"] = np.tile(np.where(kk <= qq, 0.0, NEG), (1, 4)).astype(bf)
    c["triU"] = np.tile(np.where(kk > qq, 0.0, NEG), (1, 4)).astype(bf)
    r = np.arange(16)[None, :, None]
    ql, il = np.arange(128)[:, None, None], np.arange(128)[None, None, :]
    c["cmaskQ"] = np.where(128 * r + ql - 16 * il - 31 >= 0, 0.0, NEG).astype(bf)
    c["cmaskT"] = np.ascontiguousarray(np.transpose(np.where(128 * r + ql - 16 * il - 31 >= 0, 0.0, NEG), (2, 1, 0))).astype(bf)
    c["wexp"] = (((np.arange(S)[None, :] // 64) % 64) == np.arange(64)[:, None]).astype(bf)
    return c


A_CONST_SHAPES = lambda S: {
    "ident": ([128, 128], F32), "onesbd": ([128, 128], BF16), "ropeq": ([128, 2, S], F32),
    "ropek": ([128, 2, S], F32), "xi": ([128, 2, 512], F32), "ctk": ([S, 2, 64], F32), "stk": ([S, 2, 64], F32),
    "decayT": ([128, 2, 128], F32), "gch": ([128, 2], F32), "triT": ([128, 512], BF16), "triU": ([128, 512], BF16),
    "cmaskQ": ([128, 16, 128], BF16), "cmaskT": ([128, 16, 128], BF16), "wexp": ([64, S], BF16)}


def hostA_params(z, L, g):
    p = {}

    def gl(v):
        return np.ascontiguousarray(v.reshape(8, 128).T)
    qg, kg = z["nsa_q_norm"][L], z["nsa_k_norm"][L]
    p["gainsA"] = np.concatenate([gl(z["norm_mix"][L]), np.tile(qg, 2)[:, None], np.tile(kg, 2)[:, None]], 1).astype(np.float32)
    p["posT"] = np.ascontiguousarray(np.concatenate([z["cmp_pos_k"][L].T, z["cmp_pos_v"][L].T], 0), np.float32)
    w1k = z["cmp_w1_k"][L].reshape(32, 64, 256).transpose(1, 0, 2)
    w1v = z["cmp_w1_v"][L].reshape(32, 64, 256).transpose(1, 0, 2)
    zz = np.zeros_like(w1k)
    p["w1k"] = np.ascontiguousarray(np.concatenate([w1k, zz], 0), np.float32)
    p["w1v"] = np.ascontiguousarray(np.concatenate([zz, w1v], 0), np.float32)
    w2k = z["cmp_w2_k"][L].reshape(2, 128, 64).transpose(1, 0, 2)
    w2v = z["cmp_w2_v"][L].reshape(2, 128, 64).transpose(1, 0, 2)
    p["w2"] = np.ascontiguousarray(np.stack([np.concatenate([w2k, w2k], 2), np.concatenate([w2v, np.zeros_like(w2v)], 2)], 2), np.float32)
    return p


A_PARAM_SHAPES = {"gainsA": [128, 10], "posT": [128, 32], "w1k": [128, 32, 256], "w1v": [128, 32, 256],
                  "w2": [128, 2, 2, 128]}


def build_phaseA(SL=8192, parts=("ret", "nsa")):
    nc = bass.Bass("TRN2", target_bir_lowering=False)
    dt = nc.dram_tensor
    x = dt("x", [SL, 1024], F32, kind="ExternalInput").ap()
    WS_d = dt("WS", [1024, 2048], F32, kind="ExternalInput").ap()
    WM_d = dt("WM", [1024, 1536], F32, kind="ExternalInput").ap()
    cd = {k: dt(k, sh, ty, kind="ExternalInput").ap() for k, (sh, ty) in A_CONST_SHAPES(SL).items()}
    pd = {k: dt(k, sh, F32, kind="ExternalInput").ap() for k, sh in A_PARAM_SHAPES.items()}
    ao = dt("ao", [SL, 512], BF16, kind="ExternalOutput").ap()
    S = Sched(nc)
    C = make_pools(S, n_wbuf=3)
    emit_phaseA(C, SL, x, WS_d, WM_d, cd, pd, ao, parts)
    S.emit()
    S.close()
    return nc


def norm_evac(C, ps, pt, gains, gtok, gcol, onesbd, otok, tmps, dsts):
    S = C.S
    qf, qft, sq, sqt, rs, rst = tmps
    N = ps.shape[-1]
    S.add("act", lambda e: e.activation(out=qf[:, :N], in_=ps, func=AF.Copy), [pt], [qft])
    S.add("act", lambda e: e.activation(out=sq[:, :N], in_=ps, func=AF.Square), [pt], [sqt])
    p2, pt2 = next_ps(C)
    S.add("pe", lambda e: e.matmul(p2[:, :N], lhsT=onesbd[:], rhs=sq[:, :N], start=True, stop=True), [sqt, otok], [pt2])
    S.add("act", lambda e: e.activation(out=rs[:, :N], in_=p2[:, :N], func=AF.Ln, scale=1.0 / 64, bias=C.epsb[:, 0:1]), [pt2, C.epst], [rst])
    S.add("act", lambda e: e.activation(out=rs[:, :N], in_=rs[:, :N], func=AF.Exp, scale=-0.5), [rst], [rst])
    for (dst, lo, hi, tok) in dsts:
        S.add("dve", lambda e, dst=dst, lo=lo, hi=hi: e.scalar_tensor_tensor(
            out=dst, in0=qf[lo:hi, :N], scalar=gains[lo:hi, gcol:gcol + 1], in1=rs[lo:hi, :N],
            op0=ALU.mult, op1=ALU.mult), [qft, rst, gtok], accw=[tok])


def emit_phaseA(C, SL, x, WS_d, WM_d, cd, pd, ao, parts=("ret", "nsa"), xin_tok=None, ao_tok=None, xmap=None):
    C.xmap = xmap or (lambda t: t)
    S = C.S
    nchunk = SL // 512
    xin_tok = xin_tok or Tok()
    ao_tok = ao_tok or Tok()
    WSs = WSpec(S, "WSs", WS_d, 1024, 2048, "S")
    WMs = WSpec(S, "WMs", WM_d, 1024, 1536, "M")
    gains = S.sbuf("gainsA", [128, 10], F32)
    gtok = Tok()
    ident = S.sbuf("identA", [128, 128], F32)
    itok = Tok()
    C.epsb = S.sbuf("epsb", [128, 1], F32)
    C.epst = Tok()
    S.dma(gains[:], pd["gainsA"], writes=[gtok])
    S.dma(ident[:], cd["ident"], writes=[itok])
    S.add("dve", lambda e: e.memset(C.epsb[:], EPS), [], [C.epst])
    prep_weight(C, WSs)
    prep_weight(C, WMs)
    C.hscr = S.nc.dram_tensor("hscr_" + S.prefix, [nchunk, 128, 4096], BF16, kind="Internal").ap()
    C.hscr_tok = [Tok() for _ in range(nchunk)]
    C.share_h = ("ret" in parts) and ("nsa" in parts)
    if "ret" in parts:
        with S.scope():
            emit_ret_pass(C, SL, x, xin_tok, WSs, WMs, cd, gains, gtok, ident, itok, ao, ao_tok)
    if "nsa" in parts:
        emit_nsa(C, SL, x, xin_tok, WSs, WMs, cd, pd, gains, gtok, ident, itok, ao, ao_tok)


def load_x_rms(C, x, xin_tok, t0, xt, xtok, junk, jtok, ss, sstok, rstd, rtok, gains, gtok, hT, htok, ident, itok):
    S = C.S
    xr0 = C.xmap(t0)
    S.dma(xt[:], x[xr0:xr0 + 512, :].rearrange("(j p) d -> p j d", p=128), reads=[xin_tok], writes=[xtok])
    for j in range(4):
        S.add("act", lambda e, j=j: e.activation(out=junk[:], in_=xt[:, j, :], func=AF.Square,
                                                 accum_out=ss[:, j:j + 1]), [xtok], [jtok, sstok])
    S.add("dve", lambda e: e.tensor_scalar(out=rstd[:], in0=ss[:], scalar1=1.0 / D_MODEL, scalar2=EPS,
                                           op0=ALU.mult, op1=ALU.add), [sstok], [rtok])
    S.add("act", lambda e: e.activation(out=rstd[:], in_=rstd[:], func=AF.Sqrt), [rtok], [rtok])
    S.add("dve", lambda e: e.reciprocal(out=rstd[:], in_=rstd[:]), [rtok], [rtok])
    for j in range(4):
        S.add("act", lambda e, j=j: e.activation(out=xt[:, j, :], in_=xt[:, j, :], func=AF.Copy,
                                                 scale=rstd[:, j:j + 1]), [xtok, rtok], [xtok])
    for kt in range(8):
        ps, pt = next_ps(C)
        for j in range(4):
            S.add("pe", lambda e, ps=ps, j=j, kt=kt: e.transpose(
                out=ps[:, j * 128:(j + 1) * 128], in_=xt[:, j, kt * 128:(kt + 1) * 128], identity=ident[:]),
                [xtok, itok], [pt])
        if kt % 2 == 0:
            S.add("dve", lambda e, ps=ps, kt=kt: e.tensor_scalar(
                out=hT[:, kt, :], in0=ps[:], scalar1=gains[:, kt:kt + 1], scalar2=None, op0=ALU.mult),
                [pt, gtok], accw=[htok])
        else:
            S.add("act", lambda e, ps=ps, kt=kt: e.activation(
                out=hT[:, kt, :], in_=ps[:], func=AF.Copy, scale=gains[:, kt:kt + 1]), [pt, gtok], accw=[htok])


def emit_ret_pass(C, SL, x, xin_tok, WSs, WMs, cd, gains, gtok, ident, itok, ao, ao_tok):
    S = C.S
    sb = S.sbuf
    xt2 = [sb("r_xt%d" % i, [128, 4, 1024], F32) for i in range(2)]
    junk = sb("r_junk", [128, 1024], F32)
    ss2 = [sb("r_ss%d" % i, [128, 4], F32) for i in range(2)]
    rstd2 = [sb("r_rstd%d" % i, [128, 4], F32) for i in range(2)]
    hT2 = [sb("r_hT%d" % i, [128, 8, 512], BF16) for i in range(2)]
    xtok2, sstok2, rtok2, htok2 = [[Tok(), Tok()] for _ in range(4)]
    rq_tab = sb("r_rqtab", [128, 2, 512], F32)
    rk_tab = sb("r_rktab", [128, 2, 512], F32)
    ctk_t = sb("r_ctk", [128, 4, 2, 64], F32)
    stk_t = sb("r_stk", [128, 4, 2, 64], F32)
    xi_t = sb("r_xi", [128, 2, 512], F32)
    decT = sb("r_dec", [128, 2, 128], F32)
    gch = sb("r_gch", [128, 2], F32)
    t1 = [sb("r_t1_%d" % i, [128, 512], F32) for i in range(2)]
    t2 = [sb("r_t2_%d" % i, [128, 512], F32) for i in range(2)]
    tmpq = sb("r_tmpq", [128, 512], F32)
    QrT = sb("r_QrT", [128, 2, 512], BF16)
    QrxT = sb("r_QrxT", [128, 2, 512], BF16)
    KrT = sb("r_KrT", [128, 2, 512], BF16)
    Vr = sb("r_Vr", [128, 4, 256], BF16)
    kz = sb("r_kz", [128, 4, 2, 128], BF16)
    sg = sb("r_sg", [128, 4, 256], F32)
    tabcd = [sb("r_tabcd%d" % i, [128, 2, 64], F32) for i in range(4)]
    IT = [sb("r_IT%d" % i, [128, 128], BF16) for i in range(2)]
    yr = sb("r_yr", [128, 4, 2, 128], F32)
    ssr = sb("r_ssr", [128, 8], F32)
    rr = sb("r_rr", [128, 8], F32)
    ro = sb("r_ro", [128, 4, 256], BF16)
    R = sb("r_R", [128, 2, 128], F32)
    Rb = sb("r_Rb", [128, 2, 128], BF16)
    junkb = sb("r_junkb", [128, 128], BF16)
    (xtok, jtok, sstok, rtok, htok, rqt, rkt, ctt, stt, xit, dect, gcht, tmpqt, qrt, qrxt, krt, vrt, kzt, sgt,
     yrt, ssrt, rrt, rot, jbt) = [Tok() for _ in range(24)]
    t1t, t2t = [Tok(), Tok()], [Tok(), Tok()]
    tabt = [Tok() for _ in range(4)]
    ITt = [Tok(), Tok()]
    Rt, Rbt = [Tok(), Tok()], [Tok(), Tok()]
    S.dma(xi_t[:], cd["xi"], writes=[xit])
    S.dma(decT[:], cd["decayT"], writes=[dect])
    S.dma(gch[:], cd["gch"], writes=[gcht])
    for h in range(2):
        S.add("dve", lambda e, h=h: e.memset(R[:, h, :], 0.0), [], [Rt[h]])
        S.add("pool", lambda e, h=h: e.memset(Rb[:, h, :], 0.0), [], [Rbt[h]])
    nchunk = SL // 512
    tn = 0
    itn = 0
    def _lx(c):
        i = c % 2
        load_x_rms(C, x, xin_tok, c * 512, xt2[i], xtok2[i], junk, jtok, ss2[i], sstok2[i], rstd2[i], rtok2[i],
                   gains, gtok, hT2[i], htok2[i], ident, itok)
        if C.share_h:
            S.dma(C.hscr[c], hT2[i][:].rearrange("p k t -> p (k t)"), reads=[htok2[i]], writes=[C.hscr_tok[c]], q="sp")

    _lx(0)
    for c in range(nchunk):
        t0 = c * 512
        hT, htok = hT2[c % 2], htok2[c % 2]
        S.dma(rq_tab[:], cd["ropeq"][:, :, t0:t0 + 512], writes=[rqt])
        S.dma(rk_tab[:], cd["ropek"][:, :, t0:t0 + 512], writes=[rkt])
        S.dma(ctk_t[:], cd["ctk"][t0:t0 + 512].rearrange("(j p) h i -> p j h i", p=128), writes=[ctt])
        S.dma(stk_t[:], cd["stk"][t0:t0 + 512].rearrange("(j p) h i -> p j h i", p=128), writes=[stt])
        if DBG_R <= 1:
            continue
        for u in (2, 3):
            wv, wt = load_unit(C, WSs, u)
            isq = (u == 2)
            tab, tabt_ = (rq_tab, rqt) if isq else (rk_tab, rkt)
            for h in range(2):
                a, at_ = t1[tn % 2], t1t[tn % 2]
                b, bt_ = t2[tn % 2], t2t[tn % 2]
                tn += 1
                for half, (dstb, dtok) in enumerate(((a, at_), (b, bt_))):
                    f = 2 * h + half
                    ps, pt = next_ps(C)
                    for kt in range(8):
                        S.add("pe", lambda e, hT=hT, ps=ps, wv=wv, f=f, kt=kt: e.matmul(
                            ps[:], lhsT=wv[:, f, kt, :], rhs=hT[:, kt, :], start=(kt == 0), stop=(kt == 7)),
                            [wt, htok], [pt])
                    S.add("dve", lambda e, ps=ps, dstb=dstb, half=half, tab=tab: e.tensor_tensor(
                        out=dstb[:], in0=ps[:], in1=tab[:, half, :], op=ALU.mult), [pt, tabt_], [dtok])
                if isq:
                    S.add("pool", lambda e, a=a, b=b: e.tensor_tensor(out=tmpq[:], in0=a[:], in1=b[:], op=ALU.add),
                          [at_, bt_], [tmpqt])
                    S.add("act", lambda e, h=h: e.activation(out=QrT[:, h, :], in_=tmpq[:], func=AF.Copy),
                          [tmpqt], accw=[qrt])
                    S.add("pool", lambda e, h=h: e.tensor_tensor(out=QrxT[:, h, :], in0=tmpq[:], in1=xi_t[:, h, :],
                                                                 op=ALU.mult), [tmpqt, xit], accw=[qrxt])
                else:
                    S.add("pool", lambda e, a=a, b=b, h=h: e.tensor_tensor(out=KrT[:, h, :], in0=a[:], in1=b[:],
                                                                           op=ALU.add), [at_, bt_], accw=[krt])
        if DBG_R <= 2:
            continue
        if c + 1 < nchunk:
            _lx(c + 1)
        wv, wt = load_unit(C, WMs, 0)
        for j in range(4):
            ps, pt = next_ps(C)
            for kt in range(8):
                S.add("pe", lambda e, hT=hT, ps=ps, wv=wv, j=j, kt=kt: e.matmul(
                    ps[:], lhsT=hT[:, kt, j * 128:(j + 1) * 128], rhs=wv[:, kt, :], start=(kt == 0), stop=(kt == 7)),
                    [wt, htok], [pt])
            if DBG_SUB == 1:
                S.add("act", lambda e, ps=ps, j=j: e.activation(out=Vr[:, j, :], in_=ps[:, 256:512], func=AF.Copy),
                      [pt], accw=[vrt])
                continue
            pv = ps[:, 0:256].rearrange("p (h t i) -> p h t i", h=2, t=2)
            x1, x2 = pv[:, :, 0, :], pv[:, :, 1, :]
            kzv = kz[:, j].rearrange("p h (t i) -> p h t i", t=2)
            ta, tb, tc, td = tabcd
            S.add("dve", lambda e, x1=x1, j=j: e.tensor_tensor(out=ta[:], in0=x1, in1=ctk_t[:, j], op=ALU.mult),
                  [pt, ctt], [tabt[0]])
            S.add("dve", lambda e, x2=x2, j=j: e.tensor_tensor(out=tb[:], in0=x2, in1=stk_t[:, j], op=ALU.mult),
                  [pt, stt], [tabt[1]])
            S.add("dve", lambda e, x1=x1, j=j: e.tensor_tensor(out=tc[:], in0=x1, in1=stk_t[:, j], op=ALU.mult),
                  [pt, stt], [tabt[2]])
            S.add("dve", lambda e, x2=x2, j=j: e.tensor_tensor(out=td[:], in0=x2, in1=ctk_t[:, j], op=ALU.mult),
                  [pt, ctt], [tabt[3]])
            if DBG_SUB == 2:
                continue
            S.add("dve", lambda e, kzv=kzv: e.tensor_tensor(out=kzv[:, :, 0, :], in0=ta[:], in1=tb[:], op=ALU.subtract),
                  [tabt[0], tabt[1]] if DBG_SUB != 3 else [], accw=[kzt])
            S.add("dve", lambda e, kzv=kzv: e.tensor_tensor(out=kzv[:, :, 1, :], in0=tc[:], in1=td[:], op=ALU.add),
                  [tabt[2], tabt[3]] if DBG_SUB != 3 else [], accw=[kzt])
            S.add("act", lambda e, ps=ps, j=j: e.activation(out=Vr[:, j, :], in_=ps[:, 256:512], func=AF.Copy),
                  [pt], accw=[vrt])
        if DBG_R <= 3:
            continue
        wv, wt = load_unit(C, WMs, 1)
        for j in range(4):
            ps, pt = next_ps(C)
            for kt in range(8):
                S.add("pe", lambda e, hT=hT, ps=ps, wv=wv, j=j, kt=kt: e.matmul(
                    ps[:, 0:256], lhsT=hT[:, kt, j * 128:(j + 1) * 128], rhs=wv[:, kt, 0:256],
                    start=(kt == 0), stop=(kt == 7)), [wt, htok], [pt])
            S.add("act", lambda e, ps=ps, j=j: e.activation(out=sg[:, j, :], in_=ps[:, 0:256], func=AF.Silu),
                  [pt], accw=[sgt])
        if DBG_R <= 4:
            continue
        for j in range(4):
            js = slice(j * 128, (j + 1) * 128)
            for h in range(2):
                hs = slice(h * 128, (h + 1) * 128)
                psI, ptI = next_ps(C)
                S.add("pe", lambda e, psI=psI, h=h, js=js: e.matmul(
                    psI[:, 0:128], lhsT=KrT[:, h, js], rhs=QrT[:, h, js], start=True, stop=True), [krt, qrt], [ptI])
                it_, itt = IT[itn % 2], ITt[itn % 2]
                itn += 1
                S.add("dve", lambda e, psI=psI, it_=it_, h=h: e.tensor_tensor(
                    out=it_[:], in0=psI[:, 0:128], in1=decT[:, h, :], op=ALU.mult), [ptI, dect], [itt])
                psO, ptO = next_ps(C)
                S.add("pe", lambda e, psO=psO, it_=it_, j=j, hs=hs: e.matmul(
                    psO[:, 0:128], lhsT=it_[:], rhs=Vr[:, j, hs], start=True, stop=False), [itt, vrt], [ptO])
                S.add("pe", lambda e, psO=psO, h=h, js=js: e.matmul(
                    psO[:, 0:128], lhsT=QrxT[:, h, js], rhs=Rb[:, h, :], start=False, stop=True), [qrxt, Rbt[h]], [ptO])
                S.add("act", lambda e, psO=psO, j=j, h=h: e.activation(
                    out=junkb[:], in_=psO[:, 0:128], func=AF.Square, accum_out=ssr[:, j * 2 + h:j * 2 + h + 1]),
                    [ptO], [jbt], accw=[ssrt])
                S.add("dve", lambda e, psO=psO, j=j, h=h: e.tensor_copy(out=yr[:, j, h, :], in_=psO[:, 0:128]),
                      [ptO], accw=[yrt])
                psK, ptK = next_ps(C)
                S.add("pe", lambda e, psK=psK, j=j, h=h, hs=hs: e.matmul(
                    psK[:, 0:128], lhsT=kz[:, j, h, :], rhs=Vr[:, j, hs], start=True, stop=True), [kzt, vrt], [ptK])
                S.add("dve", lambda e, psK=psK, h=h: e.scalar_tensor_tensor(
                    out=R[:, h, :], in0=R[:, h, :], scalar=gch[:, h:h + 1], in1=psK[:, 0:128],
                    op0=ALU.mult, op1=ALU.add), [ptK, gcht, Rt[h]], [Rt[h]])
                S.add("pool", lambda e, h=h: e.tensor_copy(out=Rb[:, h, :], in_=R[:, h, :]), [Rt[h]], [Rbt[h]])
        if DBG_R <= 5:
            continue
        S.add("dve", lambda e: e.tensor_scalar(out=rr[:], in0=ssr[:], scalar1=1.0 / 128, scalar2=EPS,
                                               op0=ALU.mult, op1=ALU.add), [ssrt], [rrt])
        S.add("act", lambda e: e.activation(out=rr[:], in_=rr[:], func=AF.Sqrt), [rrt], [rrt])
        S.add("dve", lambda e: e.reciprocal(out=rr[:], in_=rr[:]), [rrt], [rrt])
        for j in range(4):
            for h in range(2):
                hs = slice(h * 128, (h + 1) * 128)
                S.add("dve", lambda e, j=j, h=h, hs=hs: e.scalar_tensor_tensor(
                    out=ro[:, j, hs], in0=yr[:, j, h, :], scalar=rr[:, j * 2 + h:j * 2 + h + 1], in1=sg[:, j, hs],
                    op0=ALU.mult, op1=ALU.mult), [yrt, rrt, sgt], accw=[rot])
        S.dma(ao[t0:t0 + 512, 256:512].rearrange("(j p) d -> p j d", p=128), ro[:], reads=[rot], accw=[ao_tok], q="act")


HORD = (0, 2, 1, 3)


def emit_nsa(C, SL, x, xin_tok, WSs, WMs, cd, pd, gains, gtok, ident, itok, ao, ao_tok):
    S = C.S
    sb = S.sbuf
    NT = SL // 128
    nb = SL // 16
    assert nb <= 512
    NCT = max(1, nb // 128)
    QT = sb("n_QT", [128, 2, SL], BF16)
    Kslo, Kshi = sb("n_Kslo", [128, SL], BF16), sb("n_Kshi", [128, SL], BF16)
    Kwlo, Kwhi = sb("n_Kwlo", [128, SL], BF16), sb("n_Kwhi", [128, SL], BF16)
    V1 = sb("n_V1", [128, NT, 2, 65], BF16)
    Gt = sb("n_Gt", [128, NT, 12], F32)
    kclo, kchi = sb("n_kclo", [128, 512], BF16), sb("n_kchi", [128, 512], BF16)
    Vc1 = sb("n_Vc1", [128, 4, 65], BF16)
    onesbd = sb("n_onesbd", [128, 128], BF16)
    identb = sb("n_identb", [128, 128], BF16)
    qf = sb("n_qf", [128, 512], F32)
    sq = sb("n_sq", [128, 512], BF16)
    rs = sb("n_rs", [128, 512], F32)
    qtok, kst, kwt, kcvt, v1t, gtt, kct, vct, onest, ibt, qft, sqt, rst = [Tok() for _ in range(13)]
    tmps = (qf, qft, sq, sqt, rs, rst)
    S.dma(onesbd[:], cd["onesbd"], writes=[onest])
    S.add("dve", lambda e: e.tensor_copy(out=identb[:], in_=ident[:]), [itok], [ibt])
    for (t_, tk) in ((Kslo, kst), (Kshi, kst), (Kwlo, kwt), (Kwhi, kwt), (kclo, kct), (kchi, kct)):
        S.add("pool", lambda e, t_=t_: e.memset(t_[:], 0.0), [], [tk])
    S.add("pool", lambda e: e.memset(V1[:], 1.0), [], [v1t])
    S.add("pool", lambda e: e.memset(Vc1[:], 0.0), [], [vct])
    S.add("pool", lambda e: e.memset(Vc1[:, :, 64:65], 1.0), [vct], [vct])
    kc_scope = S.scope()
    kc_scope.__enter__()
    KcVcT = sb("n_KcVcT", [128, SL + 16], BF16)
    with S.scope():
        if C.share_h:
            hT2 = [sb("n_hT%d" % i, [128, 8, 512], BF16) for i in range(2)]
            htok2 = [Tok(), Tok()]

            def _lh(c):
                S.dma(hT2[c % 2][:].rearrange("p k t -> p (k t)"), C.hscr[c], reads=[C.hscr_tok[c]], writes=[htok2[c % 2]])
            _lh(0)
        else:
            xt = sb("n_xt", [128, 4, 1024], F32)
            junk = sb("n_junk", [128, 1024], F32)
            ss = sb("n_ss", [128, 4], F32)
            rstd = sb("n_rstd", [128, 4], F32)
            hT = sb("n_hT", [128, 8, 512], BF16)
            xtok, jtok, sstok, rtok, htok = [Tok() for _ in range(5)]
        for c in range(SL // 512):
            t0 = c * 512
            cs = slice(t0, t0 + 512)
            if C.share_h:
                hT, htok = hT2[c % 2], htok2[c % 2]
                if c + 1 < SL // 512:
                    _lh(c + 1)
            else:
                load_x_rms(C, x, xin_tok, t0, xt, xtok, junk, jtok, ss, sstok, rstd, rtok, gains, gtok, hT, htok, ident, itok)
            for u in (0, 1):
                wv, wt = load_unit(C, WSs, u)
                for f in range(4 if u == 0 else 1):
                    ft = u * 4 + f
                    ps, pt = next_ps(C)
                    for kt in range(8):
                        S.add("pe", lambda e, hT=hT, ps=ps, wv=wv, f=f, kt=kt: e.matmul(
                            ps[:], lhsT=wv[:, f, kt, :], rhs=hT[:, kt, :], start=(kt == 0), stop=(kt == 7)),
                            [wt, htok], [pt])
                    if ft < 2:
                        norm_evac(C, ps[:], pt, gains, gtok, 8, onesbd, onest, tmps, [(QT[:, ft, cs], 0, 128, qtok)])
                    elif ft == 2:
                        norm_evac(C, ps[:], pt, gains, gtok, 9, onesbd, onest, tmps,
                                  [(Kslo[0:64, cs], 0, 64, kst), (Kshi[64:128, cs], 64, 128, kst)])
                    elif ft == 3:
                        norm_evac(C, ps[:], pt, gains, gtok, 9, onesbd, onest, tmps,
                                  [(Kwlo[0:64, cs], 0, 64, kwt), (Kwhi[64:128, cs], 64, 128, kwt)])
                    else:
                        S.add("act", lambda e, ps=ps, cs=cs: e.activation(out=KcVcT[:, cs], in_=ps[:], func=AF.Copy),
                              [pt], accw=[kcvt])
            wv, wt = load_unit(C, WMs, 2)
            for j in range(4):
                tile_i = c * 4 + j
                ps, pt = next_ps(C)
                for kt in range(8):
                    S.add("pe", lambda e, hT=hT, ps=ps, wv=wv, j=j, kt=kt: e.matmul(
                        ps[:, 0:140], lhsT=hT[:, kt, j * 128:(j + 1) * 128], rhs=wv[:, kt, 0:140],
                        start=(kt == 0), stop=(kt == 7)), [wt, htok], [pt])
                S.add("dve", lambda e, ps=ps, tile_i=tile_i: e.tensor_copy(
                    out=V1[:, tile_i, :, 0:64], in_=ps[:, 0:128].rearrange("p (b d) -> p b d", b=2)), [pt], accw=[v1t])
                S.add("act", lambda e, ps=ps, tile_i=tile_i: e.activation(
                    out=Gt[:, tile_i, :], in_=ps[:, 128:140], func=AF.Sigmoid), [pt], accw=[gtt])
    with S.scope():
        W1b = sb("n_W1b", [128, 32, 256], BF16)
        posT = sb("n_posT", [128, 32], F32)
        w2f = sb("n_w2f", [128, 2, 2, 128], F32)
        w2b = sb("n_w2b", [128, 2, 2, 128], BF16)
        zr = [sb("n_zr%d" % i, [128, 512], BF16) for i in range(4)]
        zrt = [Tok() for _ in range(4)]
        GT = sb("n_GT", [128, 4, 512], BF16)
        ga = sb("n_ga", [128, 512], F32)
        gb = sb("n_gb", [128, 512], F32)
        w1t, post, w2t, w2bt, GTt, gat, gbt = [Tok() for _ in range(7)]
        S.dma(posT[:], pd["posT"], writes=[post])
        S.dma(w2f[:], pd["w2"], writes=[w2t])
        S.add("dve", lambda e: e.tensor_copy(out=w2b[:], in_=w2f[:]), [w2t], [w2bt])
        S.add("dve", lambda e: e.tensor_copy(out=KcVcT[:, SL:SL + 16], in_=KcVcT[:, SL - 1:SL].to_broadcast([128, 16])),
              [kcvt], [kcvt])
        accs = [next_ps(C) for _ in range(4)]
        zn = 0
        for kv, srcw in enumerate((pd["w1k"], pd["w1v"])):
            for r0 in range(0, 32, 2):
                i = C.sn % 3
                C.sn += 1
                st, stt = C.stg[i], C.stgtok[i]
                S.dma(st[:, :512], srcw[:, r0:r0 + 2, :].rearrange("p r h -> p (r h)"), writes=[stt])
                eng = ("dve", "pool")[C.cast_rr % 2]
                C.cast_rr += 1
                S.add(eng, lambda e, st=st, r0=r0: e.tensor_copy(
                    out=W1b[:, r0:r0 + 2, :].rearrange("p r h -> p (r h)"), in_=st[:, :512]), [stt], accw=[w1t])
            for r in range(32):
                z, zt = zr[zn % 4], zrt[zn % 4]
                zn += 1
                if r < 16:
                    src = KcVcT[:, 0:16 * nb].rearrange("p (i s) -> p i s", s=16)[:, :, r]
                else:
                    src = KcVcT[:, 16:16 + 16 * nb].rearrange("p (i s) -> p i s", s=16)[:, :, r - 16]
                eng = ("dve", "pool")[r % 2]
                S.add(eng, lambda e, z=z, src=src, r=r: e.tensor_scalar(
                    out=z[:, :nb], in0=src, scalar1=posT[:, r:r + 1], scalar2=None, op0=ALU.add), [kcvt, post], [zt])
                for hid in range(2):
                    ps, pt = accs[kv * 2 + hid]
                    S.add("pe", lambda e, ps=ps, hid=hid, r=r, z=z: e.matmul(
                        ps[:, :nb], lhsT=W1b[:, r, hid * 128:(hid + 1) * 128], rhs=z[:, :nb],
                        start=(r == 0), stop=(r == 31)), [w1t, zt], [pt])
        for a in range(4):
            ps, pt = accs[a]
            S.add("act", lambda e, ps=ps: e.activation(out=ga[:, :nb], in_=ps[:, :nb], func=AF.Square), [pt], [gat])
            S.add("dve", lambda e: e.tensor_scalar(out=ga[:, :nb], in0=ga[:, :nb], scalar1=0.044715, scalar2=1.0,
                                                   op0=ALU.mult, op1=ALU.add), [gat], [gat])
            S.add("dve", lambda e, ps=ps: e.tensor_tensor(out=ga[:, :nb], in0=ga[:, :nb], in1=ps[:, :nb], op=ALU.mult),
                  [gat, pt], [gat])
            S.add("act", lambda e: e.activation(out=gb[:, :nb], in_=ga[:, :nb], func=AF.Sigmoid, scale=1.5957691216057308),
                  [gat], [gbt])
            S.add("dve", lambda e, ps=ps, a=a: e.tensor_tensor(out=GT[:, a, :nb], in0=gb[:, :nb], in1=ps[:, :nb],
                                                               op=ALU.mult), [gbt, pt], accw=[GTt])
        ps, pt = next_ps(C)
        for t in range(2):
            S.add("pe", lambda e, ps=ps, t=t: e.matmul(ps[:, :nb], lhsT=w2b[:, t, 0, :], rhs=GT[:, t, :nb],
                                                       start=(t == 0), stop=(t == 1)), [w2bt, GTt], [pt])
        norm_evac(C, ps[:, :nb], pt, gains, gtok, 9, onesbd, onest, tmps,
                  [(kclo[0:64, :nb], 0, 64, kct), (kchi[64:128, :nb], 64, 128, kct)])
        for ct in range(NCT):
            ps, pt = next_ps(C)
            wdt = min(128, nb)
            for t in range(2):
                S.add("pe", lambda e, ps=ps, t=t, ct=ct, wdt=wdt: e.matmul(
                    ps[:wdt, 0:64], lhsT=GT[:, 2 + t, ct * 128:ct * 128 + wdt], rhs=w2b[:, t, 1, 0:64],
                    start=(t == 0), stop=(t == 1)), [w2bt, GTt], [pt])
            S.add("dve", lambda e, ps=ps, ct=ct, wdt=wdt: e.tensor_copy(out=Vc1[:wdt, ct, 0:64], in_=ps[:wdt, 0:64]),
                  [pt], accw=[vct])
    kc_scope.__exit__(None, None, None)
    with S.scope():
        triT4 = sb("n_triT4", [128, 512], BF16)
        triU4 = sb("n_triU4", [128, 512], BF16)
        cmQ = sb("n_cmQ", [128, 16, 128], BF16)
        cmT = sb("n_cmT", [128, 16, 128], BF16)
        wet, trt, trut, cmqt, cmtt = [Tok() for _ in range(5)]
        S.dma(Kslo[64:128, :], cd["wexp"][0:64, :], reads=[kst], writes=[kst])
        S.dma(Kshi[0:64, :], cd["wexp"][0:64, :], reads=[kst], writes=[kst])
        S.dma(triT4[:], cd["triT"], writes=[trt])
        S.dma(triU4[:], cd["triU"], writes=[trut])
        S.dma(cmQ[:], cd["cmaskQ"], writes=[cmqt])
        S.dma(cmT[:], cd["cmaskT"], writes=[cmtt])
        E4 = sb("n_E4", [128, 4, 512], F32)
        rsum = sb("n_rsum", [128, 4], F32)
        rinv = sb("n_rinv", [128, 4], F32)
        imp = sb("n_imp", [128, 512], F32)
        ib = sb("n_ib", [128, 128], F32)
        sc = sb("n_sc", [128, 128], F32)
        sc2 = sb("n_sc2", [128, 128], F32)
        m8a = sb("n_m8a", [128, 8], F32)
        m8b = sb("n_m8b", [128, 8], F32)
        nmf = sb("n_nmf", [128, 128], F32)
        nmb = sb("n_nmb", [128, 128], BF16)
        nmr = sb("n_nmr", [128, 128], BF16)
        Qlo = [sb("n_Qlo%d" % i, [128, 2, 128], BF16) for i in range(2)]
        Qhi = [sb("n_Qhi%d" % i, [128, 2, 128], BF16) for i in range(2)]
        nmrt = Tok()
        Qlot, Qhit = [Tok(), Tok()], [Tok(), Tok()]
        PT = [sb("n_PT%d" % i, [128, 512], BF16) for i in range(4)]
        PTt = [Tok() for _ in range(4)]
        den = sb("n_den", [128, 4], F32)
        coef = sb("n_coef", [128, 4], F32)
        acc = sb("n_acc", [128, 4, 64], F32)
        ob = [sb("n_ob%d" % i, [128, 256], BF16) for i in range(2)]
        obt = [Tok(), Tok()]
        e4t, rsumt, rinvt, impt, ibt_, sct, sc2t, m8at, m8bt, nmft, nmbt, nmTt, dent, coeft, acct = [Tok() for _ in range(15)]
        npool = len(C.ps)
        Obank = [(C.ps[npool - 2], C.pstok[npool - 2]), (C.ps[npool - 1], C.pstok[npool - 1])]
        C.ps_active = npool - 2
        on = 0
        ptn = 0

        def att_branch(tiles, br, qt, first_branch):
            nonlocal on, ptn
            qs = slice(qt * 128, (qt + 1) * 128)
            O, Ot = Obank[on % 2]
            on += 1
            nt = len(tiles)
            def scores(idx):
                klo, khi, ktoks, v, vtok, masks = tiles[idx][:6]
                if len(tiles[idx]) > 6:
                    rlo, rhi, rtoks = tiles[idx][6:9]
                else:
                    rlo, rhi, rtoks = QT[:, :, qs], QT[:, :, qs], [qtok]
                psT, ptT = next_ps(C)
                first = True
                for (ml, mlt, mr, mrt, wide) in masks:
                    if wide:
                        S.add("pe", lambda e, psT=psT, ml=ml, mr=mr, first=first: e.matmul(
                            psT[:, 0:512], lhsT=ml, rhs=mr, start=first, stop=False, skip_group_check=True),
                            [mlt, mrt], [ptT])
                        first = False
                    else:
                        for cb in range(4):
                            S.add("pe", lambda e, psT=psT, ml=ml, mr=mr, cb=cb, first=first: e.matmul(
                                psT[:, cb * 128:(cb + 1) * 128], lhsT=ml, rhs=mr, start=first, stop=False,
                                skip_group_check=True), [mlt, mrt], [ptT])
                            first = False
                S.add("pe", lambda e, psT=psT, klo=klo, rlo=rlo, first=first: e.matmul(
                    psT[:, 0:256], lhsT=klo, rhs=rlo, start=first, stop=False, skip_group_check=True),
                    [ktoks] + rtoks, [ptT])
                S.add("pe", lambda e, psT=psT, khi=khi, rhi=rhi: e.matmul(
                    psT[:, 256:512], lhsT=khi, rhs=rhi, start=False, stop=True, skip_group_check=True),
                    [ktoks] + rtoks, [ptT])
                return psT, ptT

            DEPTH = 2
            pendq = [scores(i) for i in range(min(DEPTH, nt))]
            for idx in range(nt):
                psT, ptT = pendq.pop(0)
                if idx + DEPTH < nt:
                    pendq.append(scores(idx + DEPTH))
                v, vtok = tiles[idx][3], tiles[idx][4]
                P, Pt_ = PT[ptn % 4], PTt[ptn % 4]
                ptn += 1
                S.add("act", lambda e, psT=psT, P=P: e.activation(out=P[:], in_=psT[:], func=AF.Exp, scale=0.125),
                      [ptT], [Pt_])
                for cb in range(4):
                    S.add("pe", lambda e, O=O, P=P, v=v, cb=cb, idx=idx: e.matmul(
                        O[:, cb * 65:(cb + 1) * 65], lhsT=P[:, cb * 128:(cb + 1) * 128], rhs=v,
                        start=(idx == 0 and cb == 0), stop=(idx == nt - 1), skip_group_check=True), [Pt_, vtok], [Ot])
            Ov = O[:, 0:260].rearrange("p (c d) -> p c d", c=4)
            S.add("dve", lambda e, Ov=Ov: e.tensor_scalar(out=den[:], in0=Ov[:, :, 64], scalar1=1e-30, scalar2=None,
                                                          op0=ALU.max), [Ot], [dent])
            S.add("dve", lambda e: e.reciprocal(out=den[:], in_=den[:]), [dent], [dent])
            gv = Gt[:, qt, br * 4:(br + 1) * 4].rearrange("p (a b) -> p b a", a=2)
            S.add("dve", lambda e, gv=gv: e.tensor_tensor(out=coef[:].rearrange("p (b a) -> p b a", b=2),
                                                          in0=den[:].rearrange("p (b a) -> p b a", b=2), in1=gv,
                                                          op=ALU.mult), [dent, gtt], [coeft])
            for cb in range(4):
                h = HORD[cb]
                if first_branch:
                    S.add("dve", lambda e, Ov=Ov, cb=cb, h=h: e.tensor_scalar(
                        out=acc[:, h, :], in0=Ov[:, cb, 0:64], scalar1=coef[:, cb:cb + 1], scalar2=None,
                        op0=ALU.mult), [Ot, coeft], [acct])
                else:
                    S.add("dve", lambda e, Ov=Ov, cb=cb, h=h: e.scalar_tensor_tensor(
                        out=acc[:, h, :], in0=Ov[:, cb, 0:64], scalar=coef[:, cb:cb + 1], in1=acc[:, h, :],
                        op0=ALU.mult, op1=ALU.add), [Ot, coeft, acct], [acct])

        for qt in range(NT if DBG_NQT is None else DBG_NQT):
            bg = getattr(C, "bg", None)
            if bg is not None:
                for _ in range(C.bg_per_tile):
                    next(bg, None)
            qs = slice(qt * 128, (qt + 1) * 128)
            ctl = (8 * qt + 6) // 128
            ncol = 128 * (ctl + 1)
            r16 = qt % 16
            for cb, (p, Kc) in enumerate(((0, kclo), (1, kclo), (0, kchi), (1, kchi))):
                psS, ptS = next_ps(C)
                first = True
                if ctl > 0:
                    S.add("pe", lambda e, psS=psS, p=p, Kc=Kc, qs=qs, ctl=ctl: e.matmul(
                        psS[:, 0:ctl * 128], lhsT=QT[:, p, qs], rhs=Kc[:, 0:ctl * 128], start=True, stop=False,
                        skip_group_check=True), [qtok, kct], [ptS])
                    first = False
                S.add("pe", lambda e, psS=psS, ctl=ctl, ncol=ncol, r16=r16, first=first: e.matmul(
                    psS[:, ctl * 128:ncol], lhsT=identb[:], rhs=cmQ[:, r16, :], start=first, stop=False,
                    skip_group_check=True), [ibt, cmqt], [ptS])
                S.add("pe", lambda e, psS=psS, p=p, Kc=Kc, qs=qs, ctl=ctl, ncol=ncol: e.matmul(
                    psS[:, ctl * 128:ncol], lhsT=QT[:, p, qs], rhs=Kc[:, ctl * 128:ncol], start=False, stop=True,
                    skip_group_check=True), [qtok, kct], [ptS])
                S.add("act", lambda e, psS=psS, cb=cb, ncol=ncol: e.activation(
                    out=E4[:, cb, :ncol], in_=psS[:, :ncol], func=AF.Exp, scale=0.125, accum_out=rsum[:, cb:cb + 1]),
                    [ptS], accw=[e4t, rsumt])
            S.add("dve", lambda e: e.tensor_scalar(out=rinv[:], in0=rsum[:], scalar1=1e-30, scalar2=None, op0=ALU.max),
                  [rsumt], [rinvt])
            S.add("dve", lambda e: e.reciprocal(out=rinv[:], in_=rinv[:]), [rinvt], [rinvt])
            S.add("dve", lambda e, ncol=ncol: e.tensor_scalar(out=imp[:, :ncol], in0=E4[:, 0, :ncol], scalar1=rinv[:, 0:1],
                                                             scalar2=None, op0=ALU.mult), [e4t, rinvt], [impt])
            for cb in range(1, 4):
                S.add("dve", lambda e, cb=cb, ncol=ncol: e.scalar_tensor_tensor(
                    out=imp[:, :ncol], in0=E4[:, cb, :ncol], scalar=rinv[:, cb:cb + 1], in1=imp[:, :ncol],
                    op0=ALU.mult, op1=ALU.add), [e4t, rinvt, impt], [impt])
            nblk = ncol // 4
            S.add("dve", lambda e, ncol=ncol, nblk=nblk: e.tensor_reduce(
                out=ib[:, :nblk], in_=imp[:, :ncol].rearrange("p (j r) -> p j r", r=4), axis=AX.X, op=ALU.add),
                [impt], [ibt_])
            S.add("dve", lambda e, nblk=nblk: e.tensor_tensor(
                out=ib[:, 1:nblk], in0=ib[:, 1:nblk],
                in1=imp[:, 0:4 * (nblk - 1)].rearrange("p (j r) -> p j r", r=4)[:, :, 3], op=ALU.add),
                [impt, ibt_], [ibt_])
            S.add("pool", lambda e: e.memset(sc[:], -1e30), [], [sct])
            if qt > 0:
                S.add("dve", lambda e, qt=qt: e.tensor_copy(out=sc[:, 0:2 * qt], in_=ib[:, 0:2 * qt]), [ibt_, sct], [sct])
                S.add("dve", lambda e, qt=qt: e.memset(sc[0:64, 2 * qt - 1:2 * qt], 1e4), [sct], [sct])
            S.add("dve", lambda e: e.memset(sc[:, 0:1], 1e4), [sct], [sct])
            S.add("dve", lambda e, qt=qt: e.memset(sc[:, 2 * qt:2 * qt + 1], 1e4), [sct], [sct])
            S.add("dve", lambda e, qt=qt: e.memset(sc[64:128, 2 * qt + 1:2 * qt + 2], 1e4), [sct], [sct])
            S.add("dve", lambda e: e.max(out=m8a[:], in_=sc[:]), [sct], [m8at])
            S.add("dve", lambda e: e.match_replace(out=sc2[:], in_to_replace=m8a[:], in_values=sc[:], imm_value=-1e30),
                  [sct, m8at], [sc2t])
            S.add("dve", lambda e: e.max(out=m8b[:], in_=sc2[:]), [sc2t], [m8bt])
            S.add("dve", lambda e: e.tensor_scalar(out=nmf[:], in0=sc[:], scalar1=m8b[:, 7:8], scalar2=None,
                                                   op0=ALU.is_ge), [sct, m8bt], [nmft])
            S.add("dve", lambda e: e.tensor_scalar(out=nmb[:], in0=nmf[:], scalar1=-1.0, scalar2=-NEG,
                                                   op0=ALU.add, op1=ALU.mult), [nmft], [nmbt])
            tiles = []
            for ct in range(ctl + 1):
                cs = slice(ct * 128, (ct + 1) * 128)
                masks = [(identb[:], ibt, cmT[:, r16, :], cmtt, False)] if ct == ctl else []
                tiles.append((kclo[:, cs], kchi[:, cs], kct, Vc1[:, ct, :], vct, masks))
            att_branch(tiles, 0, qt, True)
            tiles = []
            for kt in range(max(0, qt - 4), qt + 1):
                ks_ = slice(kt * 128, (kt + 1) * 128)
                masks = []
                if kt == qt:
                    masks.append((identb[:], ibt, triT4[:], trt, True))
                if kt == qt - 4:
                    masks.append((identb[:], ibt, triU4[:], trut, True))
                tiles.append((Kwlo[:, ks_], Kwhi[:, ks_], kwt, V1[:, kt, 1, :], v1t, masks))
            att_branch(tiles, 2, qt, False)
            S.add("dve", lambda e: e.tensor_copy(out=nmr[:, 0:64], in_=nmb[:, 64:128]), [nmbt], [nmrt])
            S.add("dve", lambda e: e.tensor_copy(out=nmr[:, 64:128], in_=nmb[:, 0:64]), [nmbt, nmrt], [nmrt])
            pb1, pbt1 = next_psb(C)
            S.add("pe", lambda e, pb1=pb1: e.transpose(out=pb1[:, 0:128], in_=nmb[:], identity=identb[:]), [nmbt, ibt], [pbt1])
            pb2, pbt2 = next_psb(C)
            S.add("pe", lambda e, pb2=pb2: e.transpose(out=pb2[:, 0:128], in_=nmr[:], identity=identb[:]), [nmrt, ibt], [pbt2])
            nrng = 1 if qt < 32 else 2
            for rg_ in range(nrng):
                srcl, srclt = (pb2, pbt2) if rg_ == 0 else (pb1, pbt1)
                srch, srcht = (pb1, pbt1) if rg_ == 0 else (pb2, pbt2)
                S.add("act", lambda e, rg_=rg_, qs=qs: e.activation(out=Qlo[rg_][0:64, :, :], in_=QT[0:64, :, qs], func=AF.Copy),
                      [qtok], accw=[Qlot[rg_]])
                S.add("act", lambda e, rg_=rg_, qs=qs: e.activation(out=Qhi[rg_][64:128, :, :], in_=QT[64:128, :, qs], func=AF.Copy),
                      [qtok], accw=[Qhit[rg_]])
                for hh in range(2):
                    S.add("dve", lambda e, rg_=rg_, hh=hh, srcl=srcl: e.tensor_copy(
                        out=Qlo[rg_][64:128, hh, :], in_=srcl[64:128, 0:128]), [srclt], accw=[Qlot[rg_]])
                    S.add("dve", lambda e, rg_=rg_, hh=hh, srch=srch: e.tensor_copy(
                        out=Qhi[rg_][0:64, hh, :], in_=srch[0:64, 0:128]), [srcht], accw=[Qhit[rg_]])
            tiles = []
            for kt in range(qt + 1):
                ks_ = slice(kt * 128, (kt + 1) * 128)
                masks = []
                if kt == qt:
                    masks.append((identb[:], ibt, triT4[:], trt, True))
                rg_ = 0 if kt < 32 else 1
                tiles.append((Kslo[:, ks_], Kshi[:, ks_], kst, V1[:, kt, 0, :], v1t, masks,
                              Qlo[rg_][:], Qhi[rg_][:], [Qlot[rg_], Qhit[rg_]]))
            att_branch(tiles, 1, qt, False)
            o_, ot_ = ob[qt % 2], obt[qt % 2]
            S.add("act", lambda e, o_=o_: e.activation(out=o_[:], in_=acc[:].rearrange("p h d -> p (h d)"), func=AF.Copy),
                  [acct], [ot_])
            S.dma(ao[qs, 0:256], o_[:], reads=[ot_], accw=[ao_tok], q="act")
        C.ps_active = npool


from concourse.bass_utils import run_bass_kernel_spmd

_NC_CACHE = {}


def _get_nc(key, builder):
    if key not in _NC_CACHE:
        _NC_CACHE[key] = builder()
    return _NC_CACHE[key]


def kernel(**inputs):
    z = {k: np.asarray(v) for k, v in inputs.items()}
    x = np.ascontiguousarray(z["x"], np.float32)
    B, SL, D = x.shape
    depth = z["w_in"].shape[0]
    ncA = _get_nc("A", lambda: build_phaseA(SL))
    ncB = _get_nc("B", lambda: build_phaseB(SL // 2))
    constsA = [hostA_consts(g, SL) for g in range(2)]
    ident = np.eye(128, dtype=np.float32)
    for L in range(depth):
        insA = []
        wsm = [hostA_weights(z["w_in"][L], g) for g in range(2)]
        prm = [hostA_params(z, L, g) for g in range(2)]
        for c in range(8):
            b, g = c // 2, c % 2
            d = dict(x=np.ascontiguousarray(x[b]), WS=wsm[g][0], WM=wsm[g][1])
            d.update(constsA[g])
            d.update(prm[g])
            insA.append(d)
        resA = run_bass_kernel_spmd(ncA, insA, core_ids=list(range(8)))
        ao = [np.asarray(resA.results[c]["ao"]) for c in range(8)]

        def gl(v):
            return np.ascontiguousarray(np.asarray(v, np.float32).reshape(8, 128).T)
        gains = np.ascontiguousarray(np.concatenate(
            [gl(z["norm_mix"][L]), gl(z["norm_mlp"][L]), gl(z["norm_ple"][L])], 1), np.float32)
        w_merge = np.ascontiguousarray(z["w_in"][L][:, 3352:5400], np.float32)
        insB = []
        for c in range(8):
            b, hf = c // 2, c % 2
            sl = slice(hf * (SL // 2), (hf + 1) * (SL // 2))
            attn = np.concatenate([ao[2 * b][sl, :256], ao[2 * b + 1][sl, :256],
                                   ao[2 * b][sl, 256:], ao[2 * b + 1][sl, 256:]], 1)
            insB.append(dict(
                x=np.ascontiguousarray(x[b, sl]), attn=np.ascontiguousarray(attn),
                p=np.ascontiguousarray(z["p"][L, b, sl], np.float32), gains=gains, ident=ident,
                w_merge=w_merge, w_up_nsa=np.ascontiguousarray(z["w_up_nsa"][L], np.float32),
                w_up_ret=np.ascontiguousarray(z["w_up_ret"][L], np.float32),
                w_out=np.ascontiguousarray(z["w_out"][L], np.float32),
                w_ff1=np.ascontiguousarray(z["w_ff1"][L], np.float32),
                w_ff2=np.ascontiguousarray(z["w_ff2"][L], np.float32),
                w_gate=np.ascontiguousarray(z["w_ple_gate"][L], np.float32),
                w_ple=np.ascontiguousarray(z["w_ple"][L], np.float32)))
        resB = run_bass_kernel_spmd(ncB, insB, core_ids=list(range(8)))
        xn = np.empty_like(x)
        for c in range(8):
            b, hf = c // 2, c % 2
            xn[b, hf * (SL // 2):(hf + 1) * (SL // 2)] = np.asarray(resB.results[c]["xo"])
        x = xn
    return x


B_WNAMES = (("w_merge", 1024, 2048), ("w_up_nsa", 512, 1024), ("w_up_ret", 512, 1024), ("w_out", 1024, 1024),
            ("w_ff1", 1024, 4096), ("w_ff2", 4096, 1024), ("w_gate", 1024, 1024), ("w_ple", 256, 1024))
PAIR_GROUPS = [[0, 1], [2, 3], [4, 5], [6, 7]]


def build_fused(SL=8192, depth=2):
    nc = bass.Bass("TRN2", target_bir_lowering=False)
    dt = nc.dram_tensor
    T = SL // 2
    x_full = dt("x", [SL, 1024], F32, kind="ExternalInput").ap()
    xh = dt("xh", [T, 1024], F32, kind="ExternalInput").ap()
    hmask_d = dt("hmask", [128, 2], F32, kind="ExternalInput").ap()
    cd = {k: dt(k, sh, ty, kind="ExternalInput").ap() for k, (sh, ty) in A_CONST_SHAPES(SL).items()}
    WS_d, WM_d, pd, p_d, gB_d, wd = [], [], [], [], [], []
    for L in range(depth):
        WS_d.append(dt("WS%d" % L, [1024, 2048], F32, kind="ExternalInput").ap())
        WM_d.append(dt("WM%d" % L, [1024, 1536], F32, kind="ExternalInput").ap())
        pd.append({k: dt("%s%d" % (k, L), sh, F32, kind="ExternalInput").ap() for k, sh in A_PARAM_SHAPES.items()})
        p_d.append(dt("p%d" % L, [T, 256], F32, kind="ExternalInput").ap())
        gB_d.append(dt("gainsB%d" % L, [128, 24], F32, kind="ExternalInput").ap())
        wd.append({n: dt("%s%d" % (n, L), [K, N], F32, kind="ExternalInput").ap() for n, K, N in B_WNAMES})
    out = dt("xo", [T, 1024], F32, kind="ExternalOutput").ap()
    ao = [dt("ao%d" % L, [SL, 512], BF16, kind="Internal").ap() for L in range(depth)]
    aog = [dt("aog%d" % L, [2 * SL, 512], BF16, kind="Internal").ap() for L in range(depth)]
    xmid = [dt("xmid%d" % L, [T, 1024], F32, kind="Internal").ap() for L in range(depth - 1)]
    xg = [dt("xg%d" % L, [SL, 1024], F32, kind="Internal").ap() for L in range(depth - 1)]
    S = Sched(nc)
    C = make_pools(S, n_wbuf=3)
    xg_tok = None
    xmid_tok = None
    for L in range(depth):
        S.prefix = "L%dB_" % L
        WB = make_B_wspecs(S, wd[L])
        C.bg = prep_B_gen(C, WB)
        C.bg_per_tile = -(-212 // (SL // 128)) + 1
        S.prefix = "L%dA_" % L
        aot, aogt = Tok(), Tok()
        with S.scope():
            XK = min(512, T)
            xmap = None if L == 0 else (lambda t: 2 * ((t % T) // XK) * XK + (t // T) * XK + (t % T) % XK)
            emit_phaseA(C, SL, x_full if L == 0 else xg[L - 1], WS_d[L], WM_d[L], cd, pd[L], ao[L],
                        xin_tok=xg_tok, ao_tok=aot, xmap=xmap)
        for _ in C.bg:
            pass
        C.bg = None
        RK = min(2048, SL)
        for k in range(SL // RK):
            S.collective("AllGather", ao[L][k * RK:(k + 1) * RK, :].opt(), aog[L][2 * k * RK:2 * (k + 1) * RK, :].opt(),
                         PAIR_GROUPS, reads=[aot], accw=[aogt])
        S.prefix = "L%dB_" % L
        with S.scope():
            hm = S.sbuf("hm", [128, 2], F32)
            hmt = Tok()
            S.dma(hm[:], hmask_d, writes=[hmt])
            atAB = [S.sbuf("atAB%d" % i, [128, 4, 1024], BF16) for i in range(2)]
            atABt = [Tok(), Tok()]
            nw = len(C.wbuf)
            C.wbuf = C.wbuf + [S.sbuf("wbufx%d" % i, [128, WU_ELEMS], BF16) for i in range(1)]
            C.wtok = C.wtok + [Tok() for _ in range(1)]
            last = (L == depth - 1)
            xo_tok = Tok()
            emit_phaseB(C, T, xh if L == 0 else xmid[L - 1], aog[L], p_d[L], gB_d[L], cd["ident"], wd[L],
                        out if last else xmid[L], xin_tok=xmid_tok, attn_tok=aogt, xo_tok=xo_tok,
                        gathered=(SL, hm, hmt, atAB, atABt), W=WB)
            C.wbuf = C.wbuf[:nw]
            C.wtok = C.wtok[:nw]
        if not last:
            xmid_tok = xo_tok
            xg_tok = Tok()
            XK = min(512, T)
            for k in range(T // XK):
                S.collective("AllGather", xmid[L][k * XK:(k + 1) * XK, :].opt(),
                             xg[L][2 * k * XK:2 * (k + 1) * XK, :].opt(), PAIR_GROUPS, reads=[xo_tok], accw=[xg_tok])
    S.emit()
    S.close()
    return nc


def fused_inputs(z, SL, depth):
    import ml_dtypes
    x = np.ascontiguousarray(z["x"], np.float32)
    T = SL // 2
    consts = [hostA_consts(g, SL) for g in range(2)]

    def gl(v):
        return np.ascontiguousarray(np.asarray(v, np.float32).reshape(8, 128).T)
    per_layer = []
    for L in range(depth):
        d = {}
        d["wsm"] = [hostA_weights(z["w_in"][L], g) for g in range(2)]
        d["prm"] = [hostA_params(z, L, g) for g in range(2)]
        d["gainsB"] = np.ascontiguousarray(np.concatenate(
            [gl(z["norm_mix"][L]), gl(z["norm_mlp"][L]), gl(z["norm_ple"][L])], 1), np.float32)
        d["w"] = dict(
            w_merge=np.ascontiguousarray(z["w_in"][L][:, 3352:5400], np.float32),
            w_up_nsa=np.ascontiguousarray(z["w_up_nsa"][L], np.float32),
            w_up_ret=np.ascontiguousarray(z["w_up_ret"][L], np.float32),
            w_out=np.ascontiguousarray(z["w_out"][L], np.float32),
            w_ff1=np.ascontiguousarray(z["w_ff1"][L], np.float32),
            w_ff2=np.ascontiguousarray(z["w_ff2"][L], np.float32),
            w_gate=np.ascontiguousarray(z["w_ple_gate"][L], np.float32),
            w_ple=np.ascontiguousarray(z["w_ple"][L], np.float32))
        per_layer.append(d)
    ins = []
    for c in range(8):
        b, r = c // 2, c % 2
        sl = slice(r * T, (r + 1) * T)
        d = dict(x=np.ascontiguousarray(x[b, :SL]), xh=np.ascontiguousarray(x[b, sl]))
        hm = np.zeros((128, 2), np.float32)
        hm[:, r] = 1.0
        d["hmask"] = hm
        d.update(consts[r])
        for L in range(depth):
            pl = per_layer[L]
            d["WS%d" % L], d["WM%d" % L] = pl["wsm"][r]
            for k, v in pl["prm"][r].items():
                d["%s%d" % (k, L)] = v
            d["p%d" % L] = np.ascontiguousarray(z["p"][L, b, sl], np.float32)
            d["gainsB%d" % L] = pl["gainsB"]
            for k, v in pl["w"].items():
                d["%s%d" % (k, L)] = v
        ins.append(d)
    return ins


def kernel(**inputs):
    z = {k: np.asarray(v) for k, v in inputs.items()}
    B, SL, D = z["x"].shape
    depth = z["w_in"].shape[0]
    nc = _get_nc(("F", SL, depth), lambda: build_fused(SL, depth))
    ins = fused_inputs(z, SL, depth)
    res = run_bass_kernel_spmd(nc, ins, core_ids=list(range(8)))
    T = SL // 2
    out = np.empty((B, SL, D), np.float32)
    for c in range(8):
        b, r = c // 2, c % 2
        out[b, r * T:(r + 1) * T] = np.asarray(res.results[c]["xo"])
    return out
```

```python
from contextlib import ExitStack
import numpy as np
import concourse.bass as bass
import concourse.mybir as mybir

F32 = mybir.dt.float32
BF16 = mybir.dt.bfloat16
I32 = mybir.dt.int32
AF = mybir.ActivationFunctionType
ALU = mybir.AluOpType
AX = mybir.AxisListType

ENGS = ("pe", "act", "dve", "pool", "sp")
N_DMA_SEMS = 24


class Tok:
    __slots__ = ("lws", "rs", "base", "name", "excl", "accgrp")

    def __init__(self, name="", excl=False):
        self.excl = excl
        self.accgrp = False
        self.lws = []
        self.rs = []
        self.base = []
        self.name = name


class Op:
    __slots__ = ("eng", "fn", "deps", "dma", "idx", "sig", "dma_n", "cc")

    def __init__(self, eng, fn, deps, dma, idx):
        self.eng = eng
        self.fn = fn
        self.deps = deps
        self.dma = dma
        self.idx = idx
        self.sig = None
        self.dma_n = None
        self.cc = None


class _Scope:
    def __init__(self, S):
        self.S = S

    def __enter__(self):
        self.saved = self.S.stack
        self.S.stack = ExitStack()
        return self

    def __exit__(self, *a):
        self.S.barrier()
        self.S.stack.close()
        self.S.stack = self.saved
        return False


class Sched:
    def __init__(self, nc):
        self.nc = nc
        self.ops = {e: [] for e in ENGS}
        self.ndma = {e: 0 for e in ENGS}
        self.final_waits = []
        self.all_dma = []
        self.ncc = 0
        self.prefix = ""
        self.stack = ExitStack()

    def sbuf(self, name, shape, dtype):
        return self.stack.enter_context(self.nc.sbuf_tensor("sb_" + self.prefix + name, list(shape), dtype))

    def psum(self, name, shape, dtype):
        return self.stack.enter_context(self.nc.psum_tensor("pp_" + name, list(shape), dtype))

    def add(self, eng, fn, reads=(), writes=(), dma=False, accw=(), extra=()):
        deps = []
        seen = set()

        def push(d):
            if d is not None and d not in seen:
                seen.add(d)
                deps.append(d)

        for d in extra:
            push(d)
        for t in reads:
            for w in t.lws:
                push(w)
            if t.excl:
                for r in t.rs:
                    if r[0] != eng:
                        push(r)
        for t in writes:
            for w in t.lws:
                push(w)
            for r in t.rs:
                push(r)
        for t in accw:
            if t.rs or not t.lws or not t.accgrp:
                for w in t.lws:
                    push(w)
                for r in t.rs:
                    push(r)
            else:
                for d in t.base:
                    push(d)
        lst = self.ops[eng]
        op = Op(eng, fn, deps, dma, len(lst))
        if dma:
            op.dma_n = self.ndma[eng]
            self.ndma[eng] += 1
            self.all_dma.append((eng, op.idx))
        lst.append(op)
        me = (eng, op.idx)
        for t in reads:
            t.rs.append(me)
        for t in writes:
            t.lws = [me]
            t.rs = []
            t.base = []
            t.accgrp = False
        for t in accw:
            if t.rs or not t.lws or not t.accgrp:
                t.base = list(t.lws) + list(t.rs)
                t.lws = [me]
                t.rs = []
                t.accgrp = True
            else:
                t.lws.append(me)
        return op

    def collective(self, kind, src, dst, groups, reads=(), writes=(), accw=()):
        op = self.add("pool", lambda e: e.collective_compute(kind, ALU.bypass, replica_groups=groups,
                                                             ins=[src], outs=[dst]), reads, writes, accw=accw)
        self.ncc += 1
        op.cc = self.ncc
        return op

    def barrier(self):
        extra = list(self.all_dma)
        for e in ENGS:
            if self.ops[e]:
                extra.append((e, len(self.ops[e]) - 1))
        self.all_dma = []
        b0 = self.add("sp", lambda e: e.nop(), extra=extra)
        me = ("sp", b0.idx)
        for e in ("pe", "act", "dve", "pool"):
            self.add(e, lambda eng: eng.nop(), extra=[me])

    def dma(self, out, in_, reads=(), writes=(), q="sp", accw=(), **kw):
        return self.add(q, lambda e: e.dma_start(out=out, in_=in_, **kw), reads, writes, dma=True, accw=accw)

    def scope(self):
        return _Scope(self)

    def emit(self):
        nc = self.nc
        ops = self.ops
        needed = {e: set() for e in ENGS}
        waits = {e: [] for e in ENGS}
        for e in ENGS:
            maxw = {d: -1 for d in ENGS}
            dma_waited = set()
            for op in ops[e]:
                keep = []
                for (de, di) in op.deps:
                    dop = ops[de][di]
                    if dop.dma or dop.cc:
                        if (de, di) in dma_waited:
                            continue
                        dma_waited.add((de, di))
                        keep.append((de, di))
                    else:
                        if de == e and e == "pe":
                            continue
                        if de == e and di == op.idx:
                            continue
                        if di <= maxw[de]:
                            continue
                        maxw[de] = di
                        keep.append((de, di))
                        needed[de].add(di)
                waits[e].append(keep)
        for e in ENGS:
            c = 0
            for op in ops[e]:
                if (not op.dma) and (not op.cc) and op.idx in needed[e]:
                    c += 1
                    op.sig = c
        st = self.stack
        csem = {e: st.enter_context(nc.semaphore("c_" + e)) for e in ENGS}
        ccsem = st.enter_context(nc.semaphore("cc_sem"))
        dsem = {e: [st.enter_context(nc.semaphore("d_%s_%d" % (e, i))) for i in range(N_DMA_SEMS)]
                for e in ENGS if self.ndma[e] > 0}
        block = st.enter_context(nc.Block())

        def gen(e, eng):
            for op, keep in zip(ops[e], waits[e]):
                if op.dma:
                    n = op.dma_n
                    if n >= N_DMA_SEMS:
                        eng.wait_ge(dsem[e][n % N_DMA_SEMS], 16 * (n // N_DMA_SEMS))
                for (de, di) in keep:
                    dop = ops[de][di]
                    if dop.dma:
                        n = dop.dma_n
                        eng.wait_ge(dsem[de][n % N_DMA_SEMS], 16 * (n // N_DMA_SEMS + 1))
                    elif dop.cc:
                        eng.wait_ge(ccsem, dop.cc)
                    else:
                        eng.wait_ge(csem[de], dop.sig)
                ins = op.fn(eng)
                if op.dma:
                    n = op.dma_n
                    ins.then_inc(dsem[e][n % N_DMA_SEMS], 16)
                elif op.cc:
                    ins.then_inc(ccsem, 1)
                elif op.sig is not None:
                    ins.then_inc(csem[e], 1)
            if e == "pool" and self.ncc:
                eng.wait_ge(ccsem, self.ncc)
            nd = self.ndma[e]
            for i in range(min(nd, N_DMA_SEMS)):
                cnt = (nd - 1 - i) // N_DMA_SEMS + 1
                eng.wait_ge(dsem[e][i], 16 * cnt)

        @block.tensor
        def _(eng):
            gen("pe", eng)

        @block.scalar
        def _(eng):
            gen("act", eng)

        @block.vector
        def _(eng):
            gen("dve", eng)

        @block.gpsimd
        def _(eng):
            gen("pool", eng)

        @block.sync
        def _(eng):
            gen("sp", eng)

    def close(self):
        self.stack.close()


D_MODEL = 1024
EPS = 1e-6
WU_ELEMS = 4096


class WSpec:
    def __init__(self, S, name, w_ap, K, N, kind):
        self.name, self.K, self.N, self.kind = name, K, N, kind
        self.KT = K // 128
        self.w = w_ap
        nc = S.nc
        if kind == "S":
            assert N % 512 == 0
            self.nunits = N // 512
            self.uelems = 4 * self.KT * 128
        else:
            assert N % 512 == 0
            self.KTU = min(8, self.KT)
            self.nv = self.KT // self.KTU
            self.nunits = (N // 512) * self.nv
            self.uelems = self.KTU * 512
        assert self.uelems <= WU_ELEMS
        self.scr = nc.dram_tensor("scr_" + S.prefix + name, [self.nunits, 128, self.uelems], BF16, kind="Internal").ap()
        self.tok = Tok("scr_" + name)

    def unit_src(self, u):
        return self.scr[u]

    def view(self, buf):
        b = buf[:, : self.uelems]
        if self.kind == "S":
            return b.rearrange("p (f k c) -> p f k c", f=4, k=self.KT)
        return b.rearrange("p (k c) -> p k c", k=self.KTU)


class Ctx:
    pass


def make_pools(S, n_wbuf=5, n_ps=6):
    C = Ctx()
    C.S = S
    C.wbuf = [S.sbuf("wbuf%d" % i, [128, WU_ELEMS], BF16) for i in range(n_wbuf)]
    C.wtok = [Tok("wbuf%d" % i) for i in range(n_wbuf)]
    C.wn = 0
    C.ps = [S.psum("ps%d" % i, [128, 512], F32) for i in range(n_ps)]
    C.pstok = [Tok("ps%d" % i, excl=True) for i in range(n_ps)]
    C.pn = 0
    C.psb = [S.psum("psb%d" % i, [128, 1024], BF16) for i in range(2)]
    C.psbtok = [Tok("psb0", excl=True), Tok("psb1", excl=True)]
    C.pbn = 0
    C.stg = [S.sbuf("stg%d" % i, [128, 512], F32) for i in range(3)]
    C.stgtok = [Tok() for _ in range(3)]
    C.stgb = [S.sbuf("stgb%d" % i, [128, 512], BF16) for i in range(3)]
    C.stgbtok = [Tok() for _ in range(3)]
    C.sn = 0
    C.cast_rr = 0
    return C


def next_ps(C):
    i = C.pn % getattr(C, "ps_active", len(C.ps))
    C.pn += 1
    return C.ps[i], C.pstok[i]


def next_psb(C):
    i = C.pbn % 2
    C.pbn += 1
    return C.psb[i][:, 0:512], C.psbtok[i]


def load_unit(C, ws, u, q="sp"):
    i = C.wn % len(C.wbuf)
    C.wn += 1
    buf, tok = C.wbuf[i], C.wtok[i]
    C.S.dma(buf[:, : ws.uelems], ws.unit_src(u), reads=[ws.tok], writes=[tok], q=q)
    return ws.view(buf), tok


def prep_weight_gen(C, ws, q="act"):
    S = C.S
    K, N, KT = ws.K, ws.N, ws.KT
    for kt in range(KT):
        for c0 in range(0, N, 512):
            cw = min(512, N - c0)
            i = C.sn % 3
            C.sn += 1
            st, stt, sb, sbt = C.stg[i], C.stgtok[i], C.stgb[i], C.stgbtok[i]
            S.dma(st[:, :cw], ws.w[kt * 128:(kt + 1) * 128, c0:c0 + cw], writes=[stt])
            eng = ("dve", "pool")[C.cast_rr % 2] if q == "act" else "pool"
            C.cast_rr += 1
            S.add(eng, lambda e, sb=sb, st=st, cw=cw: e.tensor_copy(out=sb[:, :cw], in_=st[:, :cw]), [stt], [sbt])
            if ws.kind == "S":
                u0, nu = c0 // 512, cw // 512
                dst = ws.scr[u0:u0 + nu].rearrange("u p (f k c) -> p u f k c", f=4, k=KT)[:, :, :, kt, :]
                src = sb[:, :cw].rearrange("p (u f c) -> p u f c", u=nu, f=4)
                for uu in range(nu):
                    S.dma(dst[:, uu], src[:, uu], reads=[sbt], accw=[ws.tok], q=q)
            else:
                v, kk = kt // ws.KTU, kt % ws.KTU
                n0, nn = c0 // 512, cw // 512
                for n in range(nn):
                    u = (n0 + n) * ws.nv + v
                    dst = ws.scr[u].rearrange("p (k c) -> p k c", k=ws.KTU)[:, kk, :]
                    S.dma(dst, sb[:, n * 512:(n + 1) * 512], reads=[sbt], accw=[ws.tok], q=q)
            yield


def prep_weight(C, ws):
    for _ in prep_weight_gen(C, ws):
        pass


def rms_to_featmajor(C, xt, xtok, gains, gtok, gcol0, hT, htok, ident, itok, tmp, unscaled=False):
    S = C.S
    ss, sstok, xs, xstok, junk, jtok, rstd, rtok = tmp
    for j in range(4):
        S.add("act", lambda e, j=j: e.activation(
            out=xs[:, j, :], in_=xt[:, j, :], func=AF.Square, accum_out=ss[:, j:j + 1]), [xtok], [xstok, sstok])
    S.add("dve", lambda e: e.tensor_scalar(out=rstd[:], in0=ss[:], scalar1=1.0 / D_MODEL, scalar2=EPS,
                                           op0=ALU.mult, op1=ALU.add), [sstok], [rtok])
    S.add("act", lambda e: e.activation(out=rstd[:], in_=rstd[:], func=AF.Sqrt), [rtok], [rtok])
    S.add("dve", lambda e: e.reciprocal(out=rstd[:], in_=rstd[:]), [rtok], [rtok])
    if unscaled:
        src, srctok = xt, xtok
    else:
        src, srctok = xs, xstok
        for j in range(4):
            S.add("act", lambda e, j=j: e.activation(out=xs[:, j, :], in_=xt[:, j, :], func=AF.Copy,
                                                     scale=rstd[:, j:j + 1]), [xtok, rtok], [xstok])
    for kt in range(8):
        ps, pt = next_ps(C)
        for j in range(4):
            S.add("pe", lambda e, ps=ps, j=j, kt=kt, src=src: e.transpose(
                out=ps[:, j * 128:(j + 1) * 128], in_=src[:, j, kt * 128:(kt + 1) * 128], identity=ident[:]),
                [srctok, itok], [pt])
        if kt % 2 == 0:
            S.add("dve", lambda e, ps=ps, kt=kt: e.tensor_scalar(
                out=hT[:, kt, :], in0=ps[:], scalar1=gains[:, gcol0 + kt:gcol0 + kt + 1], scalar2=None,
                op0=ALU.mult), [pt, gtok], accw=[htok])
        else:
            S.add("act", lambda e, ps=ps, kt=kt: e.activation(
                out=hT[:, kt, :], in_=ps[:], func=AF.Copy, scale=gains[:, gcol0 + kt:gcol0 + kt + 1]),
                [pt, gtok], accw=[htok])


def build_phaseB(T=4096):
    nc = bass.Bass("TRN2", target_bir_lowering=False)
    dt = nc.dram_tensor
    x = dt("x", [T, 1024], F32, kind="ExternalInput").ap()
    attn = dt("attn", [T, 1024], BF16, kind="ExternalInput").ap()
    pin = dt("p", [T, 256], F32, kind="ExternalInput").ap()
    gains_d = dt("gains", [128, 24], F32, kind="ExternalInput").ap()
    ident_d = dt("ident", [128, 128], F32, kind="ExternalInput").ap()
    wd = {}
    for name, K, N in (("w_merge", 1024, 2048), ("w_up_nsa", 512, 1024), ("w_up_ret", 512, 1024),
                       ("w_out", 1024, 1024), ("w_ff1", 1024, 4096), ("w_ff2", 4096, 1024),
                       ("w_gate", 1024, 1024), ("w_ple", 256, 1024)):
        wd[name] = dt(name, [K, N], F32, kind="ExternalInput").ap()
    xo = dt("xo", [T, 1024], F32, kind="ExternalOutput").ap()
    S = Sched(nc)
    C = make_pools(S)
    emit_phaseB(C, T, x, attn, pin, gains_d, ident_d, wd, xo)
    S.emit()
    S.close()
    return nc


B_KINDS = {"w_merge": "S", "w_up_nsa": "S", "w_up_ret": "S", "w_out": "M", "w_ff1": "S", "w_ff2": "M",
           "w_gate": "M", "w_ple": "M"}


def make_B_wspecs(S, wd):
    W = {}
    for name, ap in wd.items():
        K, N = ap.shape
        W[name] = WSpec(S, name, ap, K, N, B_KINDS[name])
    return W


def prep_B_gen(C, W):
    for name in ("w_merge", "w_up_nsa", "w_up_ret", "w_out", "w_ff1", "w_ff2", "w_gate", "w_ple"):
        for _ in prep_weight_gen(C, W[name], q="sp"):
            yield


DBG_STAGE = 99
DBG_NQT = None
DBG_R = 99
DBG_SUB = 0
DBG_Q = "act"
DBG_PREP = True


def emit_phaseB(C, T, x, attn, pin, gains_d, ident_d, wd, xo, xin_tok=None, attn_tok=None, xo_tok=None,
                gathered=None, W=None):
    S = C.S
    kinds = {"w_merge": "S", "w_up_nsa": "S", "w_up_ret": "S", "w_out": "M", "w_ff1": "S", "w_ff2": "M",
             "w_gate": "M", "w_ple": "M"}
    preW = W is not None
    if not preW:
        W = make_B_wspecs(S, wd)
    gains = S.sbuf("gains", [128, 24], F32)
    gtok = Tok()
    ident = S.sbuf("ident", [128, 128], F32)
    identb = S.sbuf("identb", [128, 128], BF16)
    itok, ibtok = Tok(), Tok()
    S.dma(gains[:], gains_d, writes=[gtok])
    S.dma(ident[:], ident_d, writes=[itok])
    S.add("dve", lambda e: e.tensor_copy(out=identb[:], in_=ident[:]), [itok], [ibtok])
    for name in ("w_merge", "w_up_nsa", "w_up_ret", "w_out", "w_ff1", "w_ff2", "w_gate", "w_ple"):
        if DBG_PREP and not preW:
            prep_weight(C, W[name])
    xt = S.sbuf("xt", [128, 4, 1024], F32)
    at = S.sbuf("at", [128, 4, 1024], BF16)
    ptm = S.sbuf("ptm", [128, 4, 256], F32)
    xs = S.sbuf("xs", [128, 4, 1024], F32)
    junk = None
    ss = S.sbuf("ss", [128, 4], F32)
    rstd = S.sbuf("rstd", [128, 4], F32)
    rstd2 = S.sbuf("rstd2", [128, 4], F32)
    r2tok = Tok()
    hT = S.sbuf("hT", [128, 8, 512], BF16)
    aT = S.sbuf("aT", [128, 8, 512], BF16)
    sgT = S.sbuf("sgT", [128, 16, 512], BF16)
    mixT = S.sbuf("mixT", [128, 8, 512], BF16)
    uT = S.sbuf("uT", [128, 32, 512], BF16)
    pT = S.sbuf("pT", [128, 2, 512], BF16)
    tmpf = [S.sbuf("tmpf%d" % i, [128, 512], F32) for i in range(2)]
    tmpft = [Tok(), Tok()]
    gsb = S.sbuf("gsb", [128, 512], F32)
    xtok, atok, ptok, xstok, jtok, sstok, rtok = [Tok() for _ in range(7)]
    htok, aTtok, sgtok, mixtok, utok, pTtok, gsbtok = [Tok() for _ in range(7)]
    tmp = (ss, sstok, xs, xstok, junk, jtok, rstd, rtok)
    xin_tok = xin_tok or Tok()
    attn_tok = attn_tok or Tok()
    xo_tok = xo_tok or Tok()
    nchunk = T // 512
    tn = 0
    for c in range(nchunk):
        t0 = c * 512
        S.dma(xt[:], x[t0:t0 + 512, :].rearrange("(j p) d -> p j d", p=128), reads=[xin_tok], writes=[xtok])
        if gathered is None:
            S.dma(at[:], attn[t0:t0 + 512, :].rearrange("(j p) d -> p j d", p=128), reads=[attn_tok], writes=[atok])
        else:
            SLg, hm, hmt, atAB, atABt = gathered
            for hf in range(2):
                for g in range(2):
                    RKg = min(2048, SLg)
                    tk_ = hf * T + t0
                    r0 = 2 * (tk_ // RKg) * RKg + g * RKg + tk_ % RKg
                    srcv = attn[r0:r0 + 512, :].rearrange("(j p) d -> p j d", p=128)
                    S.dma(atAB[hf][:, :, g * 256:(g + 1) * 256], srcv[:, :, 0:256], reads=[attn_tok], accw=[atABt[hf]])
                    S.dma(atAB[hf][:, :, 512 + g * 256:512 + (g + 1) * 256], srcv[:, :, 256:512], reads=[attn_tok],
                          accw=[atABt[hf]])
            S.add("dve", lambda e: e.tensor_scalar(out=at[:], in0=atAB[0][:], scalar1=hm[:, 0:1], scalar2=None,
                                                   op0=ALU.mult), [atABt[0], hmt], [atok])
            S.add("dve", lambda e: e.scalar_tensor_tensor(out=at[:], in0=atAB[1][:], scalar=hm[:, 1:2], in1=at[:],
                                                          op0=ALU.mult, op1=ALU.add), [atABt[1], hmt, atok], [atok])
        S.dma(ptm[:], pin[t0:t0 + 512, :].rearrange("(j p) d -> p j d", p=128), writes=[ptok])
        def _store(t0=t0):
            S.dma(xo[t0:t0 + 512, :].rearrange("(j p) d -> p j d", p=128), xt[:], reads=[xtok], writes=[xo_tok],
                  q=DBG_Q)
        if DBG_STAGE <= 0:
            _store()
            continue
        rms_to_featmajor(C, xt, xtok, gains, gtok, 0, hT, htok, ident, itok, tmp)
        if DBG_STAGE <= 1:
            _store()
            continue
        ws = W["w_merge"]
        for u in range(ws.nunits):
            wv, wt = load_unit(C, ws, u)
            for f in range(4):
                ps, pt = next_ps(C)
                if DBG_SUB == 1:
                    continue
                for kt in range(8):
                    S.add("pe", lambda e, ps=ps, wv=wv, f=f, kt=kt: e.matmul(
                        ps[:], lhsT=wv[:, f, kt, :], rhs=hT[:, kt, :], start=(kt == 0), stop=(kt == 7)),
                        [wt, htok], [pt])
                if DBG_SUB == 2:
                    continue
                S.add("act", lambda e, ps=ps, ft=u * 4 + f: e.activation(
                    out=sgT[:, ft, :], in_=ps[:], func=AF.Sigmoid), [pt], accw=[sgtok])
        if DBG_STAGE <= 2:
            _store()
            continue
        for ft in range(8):
            pb, pbt = next_psb(C)
            for j in range(4):
                S.add("pe", lambda e, pb=pb, j=j, ft=ft: e.transpose(
                    out=pb[:, j * 128:(j + 1) * 128], in_=at[:, j, ft * 128:(ft + 1) * 128], identity=identb[:]),
                    [atok, ibtok], [pbt])
            S.add("dve", lambda e, pb=pb, ft=ft: e.tensor_copy(out=aT[:, ft, :], in_=pb), [pbt], accw=[aTtok])
        if DBG_SUB == 3:
            _store()
            continue
        wsa, wsb = W["w_up_nsa"], W["w_up_ret"]
        for u in range(2):
            wva, wta = load_unit(C, wsa, u)
            wvb, wtb = load_unit(C, wsb, u)
            for f in range(4):
                ft = u * 4 + f
                psa, pta = next_ps(C)
                for kt in range(4):
                    S.add("pe", lambda e, psa=psa, wva=wva, f=f, kt=kt: e.matmul(
                        psa[:], lhsT=wva[:, f, kt, :], rhs=aT[:, kt, :], start=(kt == 0), stop=(kt == 3)),
                        [wta, aTtok], [pta])
                psb_, ptb = next_ps(C)
                for kt in range(4):
                    S.add("pe", lambda e, psb_=psb_, wvb=wvb, f=f, kt=kt: e.matmul(
                        psb_[:], lhsT=wvb[:, f, kt, :], rhs=aT[:, 4 + kt, :], start=(kt == 0), stop=(kt == 3)),
                        [wtb, aTtok], [ptb])
                if DBG_SUB == 4:
                    continue
                tf, tft = tmpf[tn % 2], tmpft[tn % 2]
                tn += 1
                S.add("dve", lambda e, tf=tf, psa=psa, ft=ft: e.tensor_tensor(
                    out=tf[:], in0=psa[:], in1=sgT[:, ft, :], op=ALU.mult), [pta, sgtok], [tft])
                tf2, tft2 = tmpf[tn % 2], tmpft[tn % 2]
                tn += 1
                S.add("dve", lambda e, tf2=tf2, psb_=psb_, ft=ft: e.tensor_tensor(
                    out=tf2[:], in0=psb_[:], in1=sgT[:, 8 + ft, :], op=ALU.mult), [ptb, sgtok], [tft2])
                if DBG_SUB == 5:
                    continue
                S.add("pool", lambda e, tf=tf, tf2=tf2, ft=ft: e.tensor_tensor(
                    out=mixT[:, ft, :], in0=tf[:], in1=tf2[:], op=ALU.add), [tft, tft2], accw=[mixtok])
        if DBG_STAGE <= 3:
            _store()
            continue
        ws = W["w_out"]
        for n in range(2):
            wv, wt = load_unit(C, ws, n)
            for j in range(4):
                ps, pt = next_ps(C)
                for kt in range(8):
                    S.add("pe", lambda e, ps=ps, wv=wv, j=j, kt=kt: e.matmul(
                        ps[:], lhsT=mixT[:, kt, j * 128:(j + 1) * 128], rhs=wv[:, kt, :],
                        start=(kt == 0), stop=(kt == 7)), [wt, mixtok], [pt])
                S.add("dve", lambda e, ps=ps, j=j, n=n: e.tensor_tensor(
                    out=xt[:, j, n * 512:(n + 1) * 512], in0=ps[:], in1=xt[:, j, n * 512:(n + 1) * 512],
                    op=ALU.add), [pt, xtok], [xtok])
        if DBG_STAGE <= 4:
            _store()
            continue
        rms_to_featmajor(C, xt, xtok, gains, gtok, 8, hT, htok, ident, itok, tmp, unscaled=True)
        S.add("dve", lambda e: e.tensor_tensor(out=rstd2[:], in0=rstd[:], in1=rstd[:], op=ALU.mult), [rtok], [r2tok])
        ws = W["w_ff1"]
        for u in range(ws.nunits):
            wv, wt = load_unit(C, ws, u)
            for f in range(4):
                ft = u * 4 + f
                ps, pt = next_ps(C)
                for kt in range(8):
                    S.add("pe", lambda e, ps=ps, wv=wv, f=f, kt=kt: e.matmul(
                        ps[:], lhsT=wv[:, f, kt, :], rhs=hT[:, kt, :], start=(kt == 0), stop=(kt == 7)),
                        [wt, htok], [pt])
                tf, tft = tmpf[tn % 2], tmpft[tn % 2]
                tn += 1
                S.add("act", lambda e, ps=ps, tf=tf: e.activation(out=tf[:], in_=ps[:], func=AF.Relu),
                      [pt], [tft])
                S.add("pool", lambda e, tf=tf, ft=ft: e.tensor_tensor(
                    out=uT[:, ft, :], in0=tf[:], in1=tf[:], op=ALU.mult), [tft], accw=[utok])
        ws = W["w_ff2"]
        for n in range(2):
            pss = [next_ps(C) for _ in range(4)]
            for v in range(ws.nv):
                wv, wt = load_unit(C, ws, n * ws.nv + v)
                for j in range(4):
                    ps, pt = pss[j]
                    for kk in range(8):
                        kt = v * 8 + kk
                        S.add("pe", lambda e, ps=ps, wv=wv, j=j, kk=kk, kt=kt: e.matmul(
                            ps[:], lhsT=uT[:, kt, j * 128:(j + 1) * 128], rhs=wv[:, kk, :],
                            start=(kt == 0), stop=(kt == 31)), [wt, utok], [pt])
            for j in range(4):
                ps, pt = pss[j]
                S.add("dve", lambda e, ps=ps, j=j, n=n: e.scalar_tensor_tensor(
                    out=xt[:, j, n * 512:(n + 1) * 512], in0=ps[:], scalar=rstd2[:, j:j + 1],
                    in1=xt[:, j, n * 512:(n + 1) * 512], op0=ALU.mult, op1=ALU.add), [pt, xtok, r2tok], [xtok])
        if DBG_STAGE <= 5:
            _store()
            continue
        rms_to_featmajor(C, xt, xtok, gains, gtok, 16, hT, htok, ident, itok, tmp, unscaled=True)
        for kt in range(2):
            ps, pt = next_ps(C)
            for j in range(4):
                S.add("pe", lambda e, ps=ps, j=j, kt=kt: e.transpose(
                    out=ps[:, j * 128:(j + 1) * 128], in_=ptm[:, j, kt * 128:(kt + 1) * 128], identity=ident[:]),
                    [ptok, itok], [pt])
            S.add("dve", lambda e, ps=ps, kt=kt: e.tensor_copy(out=pT[:, kt, :], in_=ps[:]), [pt], accw=[pTtok])
        wsg, wsp = W["w_gate"], W["w_ple"]
        for n in range(2):
            wvg, wtg = load_unit(C, wsg, n)
            wvp, wtp = load_unit(C, wsp, n)
            for j in range(4):
                ps, pt = next_ps(C)
                for kt in range(8):
                    S.add("pe", lambda e, ps=ps, wvg=wvg, j=j, kt=kt: e.matmul(
                        ps[:], lhsT=hT[:, kt, j * 128:(j + 1) * 128], rhs=wvg[:, kt, :],
                        start=(kt == 0), stop=(kt == 7)), [wtg, htok], [pt])
                S.add("act", lambda e, ps=ps, j=j: e.activation(out=gsb[:], in_=ps[:], func=AF.Sigmoid,
                                                                scale=rstd[:, j:j + 1]), [pt, rtok], [gsbtok])
                ps2, pt2 = next_ps(C)
                for kt in range(2):
                    S.add("pe", lambda e, ps2=ps2, wvp=wvp, j=j, kt=kt: e.matmul(
                        ps2[:], lhsT=pT[:, kt, j * 128:(j + 1) * 128], rhs=wvp[:, kt, :],
                        start=(kt == 0), stop=(kt == 1)), [wtp, pTtok], [pt2])
                tf, tft = tmpf[tn % 2], tmpft[tn % 2]
                tn += 1
                S.add("dve", lambda e, tf=tf, ps2=ps2: e.tensor_tensor(
                    out=tf[:], in0=ps2[:], in1=gsb[:], op=ALU.mult), [pt2, gsbtok], [tft])
                S.add("pool", lambda e, tf=tf, j=j, n=n: e.tensor_tensor(
                    out=xt[:, j, n * 512:(n + 1) * 512], in0=tf[:], in1=xt[:, j, n * 512:(n + 1) * 512],
                    op=ALU.add), [tft, xtok], [xtok])
        _store()


IN_SPLITS = (512, 128, 128, 128, 128, 128, 128, 24, 512, 512, 512, 512, 1024, 1024)
NEG = -30000.0


def hostA_weights(w_in, g):
    offs = np.cumsum([0] + list(IN_SPLITS))

    def col(i, a, b):
        return w_in[:, offs[i] + a: offs[i] + b]

    def swap(x):
        return np.concatenate([x[:, 64:], x[:, :64]], 1)

    q = [col(0, (g * 4 + h) * 64, (g * 4 + h + 1) * 64) for h in range(4)]
    kc, vc = col(1, g * 64, g * 64 + 64), col(2, g * 64, g * 64 + 64)
    ks, vs = col(3, g * 64, g * 64 + 64), col(4, g * 64, g * 64 + 64)
    kw, vw = col(5, g * 64, g * 64 + 64), col(6, g * 64, g * 64 + 64)
    gates = np.stack([w_in[:, offs[7] + br * 8 + g * 4 + h] for br in range(3) for h in range(4)], 1)
    rq = [col(8, (2 * g + h) * 128, (2 * g + h + 1) * 128) for h in range(2)]
    rk = [col(9, (2 * g + h) * 128, (2 * g + h + 1) * 128) for h in range(2)]
    rv = col(10, 2 * g * 128, (2 * g + 2) * 128)
    rg = col(11, 2 * g * 128, (2 * g + 2) * 128)
    z128 = np.zeros((1024, 128), np.float32)
    WS = np.concatenate([q[0], q[1], q[2], q[3], ks, ks, kw, kw, kc, vc, z128, z128, z128,
                         rq[0], swap(rq[0]), rq[1], swap(rq[1]), rk[0], swap(rk[0]), rk[1], swap(rk[1])], 1)
    WM = np.concatenate([rk[0], rk[1], rv, rg, np.zeros((1024, 256), np.float32),
                         vs, vw, gates, np.zeros((1024, 512 - 140), np.float32)], 1)
    return np.ascontiguousarray(WS, np.float32), np.ascontiguousarray(WM, np.float32)


def hostA_consts(g, S):
    import ml_dtypes
    bf = ml_dtypes.bfloat16
    c = {}
    c["ident"] = np.eye(128, dtype=np.float32)
    bd = np.zeros((128, 128), np.float32)
    bd[:64, :64] = 1
    bd[64:, 64:] = 1
    c["onesbd"] = bd.astype(bf)
    half = 64
    inv = (10000.0 ** (-np.arange(half, dtype=np.float32) / half)).astype(np.float32)
    pos = np.arange(S, dtype=np.float32)
    ang = (pos[:, None] * inv[None, :]).astype(np.float32)
    cos, sin = np.cos(ang.astype(np.float64)), np.sin(ang.astype(np.float64))
    cosT = np.concatenate([cos.T, cos.T], 0)
    sinsT = np.concatenate([-sin.T, sin.T], 0)
    ksc = 128.0 ** -0.5
    c["ropeq"] = np.stack([cosT, sinsT], 1).astype(np.float32)
    c["ropek"] = (np.stack([cosT, sinsT], 1) * ksc).astype(np.float32)
    hh = np.array([2 * g, 2 * g + 1], np.float64)
    gamma = 1.0 - 2.0 ** (-5.0 - hh)
    lg = np.log(gamma)
    n = np.arange(128, dtype=np.float64)
    xi = np.exp(lg[:, None] * (n + 1.0))
    zeta = np.exp(lg[:, None] * (127.0 - n))
    c["xi"] = np.broadcast_to(np.tile(xi, (1, 4))[None], (128, 2, 512)).astype(np.float32).copy()
    zt = zeta[:, np.arange(S) % 128]
    c["ctk"] = (cos[:, None, :] * zt.T[:, :, None] * ksc).astype(np.float32)
    c["stk"] = (sin[:, None, :] * zt.T[:, :, None] * ksc).astype(np.float32)
    diff = n[None, :] - n[:, None]
    dec = np.where(diff[None] >= 0, np.exp(lg[:, None, None] * np.maximum(diff[None], 0)), 0.0)
    c["decayT"] = np.ascontiguousarray(dec.transpose(1, 0, 2)).astype(np.float32)
    c["gch"] = np.broadcast_to(np.exp(lg * 128.0)[None], (128, 2)).astype(np.float32).copy()
    kk, qq = np.arange(128)[:, None], np.arange(128)[None, :]
    c["triT"] = np.tile(np.where(kk <= qq, 0.0, NEG), (1, 4)).astype(bf)
    c["triU"] = np.tile(np.where(kk > qq, 0.0, NEG), (1, 4)).astype(bf)
    r = np.arange(16)[None, :, None]
    ql, il = np.arange(128)[:, None, None], np.arange(128)[None, None, :]
    c["cmaskQ"] = np.where(128 * r + ql - 16 * il - 31 >= 0, 0.0, NEG).astype(bf)
    c["cmaskT"] = np.ascontiguousarray(np.transpose(np.where(128 * r + ql - 16 * il - 31 >= 0, 0.0, NEG), (2, 1, 0))).astype(bf)
    c["wexp"] = (((np.arange(S)[None, :] // 64) % 64) == np.arange(64)[:, None]).astype(bf)
    return c


A_CONST_SHAPES = lambda S: {
    "ident": ([128, 128], F32), "onesbd": ([128, 128], BF16), "ropeq": ([128, 2, S], F32),
    "ropek": ([128, 2, S], F32), "xi": ([128, 2, 512], F32), "ctk": ([S, 2, 64], F32), "stk": ([S, 2, 64], F32),
    "decayT": ([128, 2, 128], F32), "gch": ([128, 2], F32), "triT": ([128, 512], BF16), "triU": ([128, 512], BF16),
    "cmaskQ": ([128, 16, 128], BF16), "cmaskT": ([128, 16, 128], BF16), "wexp": ([64, S], BF16)}


def hostA_params(z, L, g):
    p = {}

    def gl(v):
        return np.ascontiguousarray(v.reshape(8, 128).T)
    qg, kg = z["nsa_q_norm"][L], z["nsa_k_norm"][L]
    p["gainsA"] = np.concatenate([gl(z["norm_mix"][L]), np.tile(qg, 2)[:, None], np.tile(kg, 2)[:, None]], 1).astype(np.float32)
    p["posT"] = np.ascontiguousarray(np.concatenate([z["cmp_pos_k"][L].T, z["cmp_pos_v"][L].T], 0), np.float32)
    w1k = z["cmp_w1_k"][L].reshape(32, 64, 256).transpose(1, 0, 2)
    w1v = z["cmp_w1_v"][L].reshape(32, 64, 256).transpose(1, 0, 2)
    zz = np.zeros_like(w1k)
    p["w1k"] = np.ascontiguousarray(np.concatenate([w1k, zz], 0), np.float32)
    p["w1v"] = np.ascontiguousarray(np.concatenate([zz, w1v], 0), np.float32)
    w2k = z["cmp_w2_k"][L].reshape(2, 128, 64).transpose(1, 0, 2)
    w2v = z["cmp_w2_v"][L].reshape(2, 128, 64).transpose(1, 0, 2)
    p["w2"] = np.ascontiguousarray(np.stack([np.concatenate([w2k, w2k], 2), np.concatenate([w2v, np.zeros_like(w2v)], 2)], 2), np.float32)
    return p


A_PARAM_SHAPES = {"gainsA": [128, 10], "posT": [128, 32], "w1k": [128, 32, 256], "w1v": [128, 32, 256],
                  "w2": [128, 2, 2, 128]}


def build_phaseA(SL=8192, parts=("ret", "nsa")):
    nc = bass.Bass("TRN2", target_bir_lowering=False)
    dt = nc.dram_tensor
    x = dt("x", [SL, 1024], F32, kind="ExternalInput").ap()
    WS_d = dt("WS", [1024, 2048], F32, kind="ExternalInput").ap()
    WM_d = dt("WM", [1024, 1536], F32, kind="ExternalInput").ap()
    cd = {k: dt(k, sh, ty, kind="ExternalInput").ap() for k, (sh, ty) in A_CONST_SHAPES(SL).items()}
    pd = {k: dt(k, sh, F32, kind="ExternalInput").ap() for k, sh in A_PARAM_SHAPES.items()}
    ao = dt("ao", [SL, 512], BF16, kind="ExternalOutput").ap()
    S = Sched(nc)
    C = make_pools(S, n_wbuf=3)
    emit_phaseA(C, SL, x, WS_d, WM_d, cd, pd, ao, parts)
    S.emit()
    S.close()
    return nc


def norm_evac(C, ps, pt, gains, gtok, gcol, onesbd, otok, tmps, dsts):
    S = C.S
    qf, qft, sq, sqt, rs, rst = tmps
    N = ps.shape[-1]
    S.add("act", lambda e: e.activation(out=qf[:, :N], in_=ps, func=AF.Copy), [pt], [qft])
    S.add("act", lambda e: e.activation(out=sq[:, :N], in_=ps, func=AF.Square), [pt], [sqt])
    p2, pt2 = next_ps(C)
    S.add("pe", lambda e: e.matmul(p2[:, :N], lhsT=onesbd[:], rhs=sq[:, :N], start=True, stop=True), [sqt, otok], [pt2])
    S.add("act", lambda e: e.activation(out=rs[:, :N], in_=p2[:, :N], func=AF.Ln, scale=1.0 / 64, bias=C.epsb[:, 0:1]), [pt2, C.epst], [rst])
    S.add("act", lambda e: e.activation(out=rs[:, :N], in_=rs[:, :N], func=AF.Exp, scale=-0.5), [rst], [rst])
    for (dst, lo, hi, tok) in dsts:
        S.add("dve", lambda e, dst=dst, lo=lo, hi=hi: e.scalar_tensor_tensor(
            out=dst, in0=qf[lo:hi, :N], scalar=gains[lo:hi, gcol:gcol + 1], in1=rs[lo:hi, :N],
            op0=ALU.mult, op1=ALU.mult), [qft, rst, gtok], accw=[tok])


def emit_phaseA(C, SL, x, WS_d, WM_d, cd, pd, ao, parts=("ret", "nsa"), xin_tok=None, ao_tok=None, xmap=None):
    C.xmap = xmap or (lambda t: t)
    S = C.S
    nchunk = SL // 512
    xin_tok = xin_tok or Tok()
    ao_tok = ao_tok or Tok()
    WSs = WSpec(S, "WSs", WS_d, 1024, 2048, "S")
    WMs = WSpec(S, "WMs", WM_d, 1024, 1536, "M")
    gains = S.sbuf("gainsA", [128, 10], F32)
    gtok = Tok()
    ident = S.sbuf("identA", [128, 128], F32)
    itok = Tok()
    C.epsb = S.sbuf("epsb", [128, 1], F32)
    C.epst = Tok()
    S.dma(gains[:], pd["gainsA"], writes=[gtok])
    S.dma(ident[:], cd["ident"], writes=[itok])
    S.add("dve", lambda e: e.memset(C.epsb[:], EPS), [], [C.epst])
    prep_weight(C, WSs)
    prep_weight(C, WMs)
    C.hscr = S.nc.dram_tensor("hscr_" + S.prefix, [nchunk, 128, 4096], BF16, kind="Internal").ap()
    C.hscr_tok = [Tok() for _ in range(nchunk)]
    C.share_h = ("ret" in parts) and ("nsa" in parts)
    if "ret" in parts:
        with S.scope():
            emit_ret_pass(C, SL, x, xin_tok, WSs, WMs, cd, gains, gtok, ident, itok, ao, ao_tok)
    if "nsa" in parts:
        emit_nsa(C, SL, x, xin_tok, WSs, WMs, cd, pd, gains, gtok, ident, itok, ao, ao_tok)


def load_x_rms(C, x, xin_tok, t0, xt, xtok, junk, jtok, ss, sstok, rstd, rtok, gains, gtok, hT, htok, ident, itok):
    S = C.S
    xr0 = C.xmap(t0)
    S.dma(xt[:], x[xr0:xr0 + 512, :].rearrange("(j p) d -> p j d", p=128), reads=[xin_tok], writes=[xtok])
    for j in range(4):
        S.add("act", lambda e, j=j: e.activation(out=junk[:], in_=xt[:, j, :], func=AF.Square,
                                                 accum_out=ss[:, j:j + 1]), [xtok], [jtok, sstok])
    S.add("dve", lambda e: e.tensor_scalar(out=rstd[:], in0=ss[:], scalar1=1.0 / D_MODEL, scalar2=EPS,
                                           op0=ALU.mult, op1=ALU.add), [sstok], [rtok])
    S.add("act", lambda e: e.activation(out=rstd[:], in_=rstd[:], func=AF.Sqrt), [rtok], [rtok])
    S.add("dve", lambda e: e.reciprocal(out=rstd[:], in_=rstd[:]), [rtok], [rtok])
    for j in range(4):
        S.add("act", lambda e, j=j: e.activation(out=xt[:, j, :], in_=xt[:, j, :], func=AF.Copy,
                                                 scale=rstd[:, j:j + 1]), [xtok, rtok], [xtok])
    for kt in range(8):
        ps, pt = next_ps(C)
        for j in range(4):
            S.add("pe", lambda e, ps=ps, j=j, kt=kt: e.transpose(
                out=ps[:, j * 128:(j + 1) * 128], in_=xt[:, j, kt * 128:(kt + 1) * 128], identity=ident[:]),
                [xtok, itok], [pt])
        if kt % 2 == 0:
            S.add("dve", lambda e, ps=ps, kt=kt: e.tensor_scalar(
                out=hT[:, kt, :], in0=ps[:], scalar1=gains[:, kt:kt + 1], scalar2=None, op0=ALU.mult),
                [pt, gtok], accw=[htok])
        else:
            S.add("act", lambda e, ps=ps, kt=kt: e.activation(
                out=hT[:, kt, :], in_=ps[:], func=AF.Copy, scale=gains[:, kt:kt + 1]), [pt, gtok], accw=[htok])


def emit_ret_pass(C, SL, x, xin_tok, WSs, WMs, cd, gains, gtok, ident, itok, ao, ao_tok):
    S = C.S
    sb = S.sbuf
    xt2 = [sb("r_xt%d" % i, [128, 4, 1024], F32) for i in range(2)]
    junk = sb("r_junk", [128, 1024], F32)
    ss2 = [sb("r_ss%d" % i, [128, 4], F32) for i in range(2)]
    rstd2 = [sb("r_rstd%d" % i, [128, 4], F32) for i in range(2)]
    hT2 = [sb("r_hT%d" % i, [128, 8, 512], BF16) for i in range(2)]
    xtok2, sstok2, rtok2, htok2 = [[Tok(), Tok()] for _ in range(4)]
    rq_tab = sb("r_rqtab", [128, 2, 512], F32)
    rk_tab = sb("r_rktab", [128, 2, 512], F32)
    ctk_t = sb("r_ctk", [128, 4, 2, 64], F32)
    stk_t = sb("r_stk", [128, 4, 2, 64], F32)
    xi_t = sb("r_xi", [128, 2, 512], F32)
    decT = sb("r_dec", [128, 2, 128], F32)
    gch = sb("r_gch", [128, 2], F32)
    t1 = [sb("r_t1_%d" % i, [128, 512], F32) for i in range(2)]
    t2 = [sb("r_t2_%d" % i, [128, 512], F32) for i in range(2)]
    tmpq = sb("r_tmpq", [128, 512], F32)
    QrT = sb("r_QrT", [128, 2, 512], BF16)
    QrxT = sb("r_QrxT", [128, 2, 512], BF16)
    KrT = sb("r_KrT", [128, 2, 512], BF16)
    Vr = sb("r_Vr", [128, 4, 256], BF16)
    kz = sb("r_kz", [128, 4, 2, 128], BF16)
    sg = sb("r_sg", [128, 4, 256], F32)
    tabcd = [sb("r_tabcd%d" % i, [128, 2, 64], F32) for i in range(4)]
    IT = [sb("r_IT%d" % i, [128, 128], BF16) for i in range(2)]
    yr = sb("r_yr", [128, 4, 2, 128], F32)
    ssr = sb("r_ssr", [128, 8], F32)
    rr = sb("r_rr", [128, 8], F32)
    ro = sb("r_ro", [128, 4, 256], BF16)
    R = sb("r_R", [128, 2, 128], F32)
    Rb = sb("r_Rb", [128, 2, 128], BF16)
    junkb = sb("r_junkb", [128, 128], BF16)
    (xtok, jtok, sstok, rtok, htok, rqt, rkt, ctt, stt, xit, dect, gcht, tmpqt, qrt, qrxt, krt, vrt, kzt, sgt,
     yrt, ssrt, rrt, rot, jbt) = [Tok() for _ in range(24)]
    t1t, t2t = [Tok(), Tok()], [Tok(), Tok()]
    tabt = [Tok() for _ in range(4)]
    ITt = [Tok(), Tok()]
    Rt, Rbt = [Tok(), Tok()], [Tok(), Tok()]
    S.dma(xi_t[:], cd["xi"], writes=[xit])
    S.dma(decT[:], cd["decayT"], writes=[dect])
    S.dma(gch[:], cd["gch"], writes=[gcht])
    for h in range(2):
        S.add("dve", lambda e, h=h: e.memset(R[:, h, :], 0.0), [], [Rt[h]])
        S.add("pool", lambda e, h=h: e.memset(Rb[:, h, :], 0.0), [], [Rbt[h]])
    nchunk = SL // 512
    tn = 0
    itn = 0
    def _lx(c):
        i = c % 2
        load_x_rms(C, x, xin_tok, c * 512, xt2[i], xtok2[i], junk, jtok, ss2[i], sstok2[i], rstd2[i], rtok2[i],
                   gains, gtok, hT2[i], htok2[i], ident, itok)
        if C.share_h:
            S.dma(C.hscr[c], hT2[i][:].rearrange("p k t -> p (k t)"), reads=[htok2[i]], writes=[C.hscr_tok[c]], q="sp")

    _lx(0)
    for c in range(nchunk):
        t0 = c * 512
        hT, htok = hT2[c % 2], htok2[c % 2]
        S.dma(rq_tab[:], cd["ropeq"][:, :, t0:t0 + 512], writes=[rqt])
        S.dma(rk_tab[:], cd["ropek"][:, :, t0:t0 + 512], writes=[rkt])
        S.dma(ctk_t[:], cd["ctk"][t0:t0 + 512].rearrange("(j p) h i -> p j h i", p=128), writes=[ctt])
        S.dma(stk_t[:], cd["stk"][t0:t0 + 512].rearrange("(j p) h i -> p j h i", p=128), writes=[stt])
        if DBG_R <= 1:
            continue
        for u in (2, 3):
            wv, wt = load_unit(C, WSs, u)
            isq = (u == 2)
            tab, tabt_ = (rq_tab, rqt) if isq else (rk_tab, rkt)
            for h in range(2):
                a, at_ = t1[tn % 2], t1t[tn % 2]
                b, bt_ = t2[tn % 2], t2t[tn % 2]
                tn += 1
                for half, (dstb, dtok) in enumerate(((a, at_), (b, bt_))):
                    f = 2 * h + half
                    ps, pt = next_ps(C)
                    for kt in range(8):
                        S.add("pe", lambda e, hT=hT, ps=ps, wv=wv, f=f, kt=kt: e.matmul(
                            ps[:], lhsT=wv[:, f, kt, :], rhs=hT[:, kt, :], start=(kt == 0), stop=(kt == 7)),
                            [wt, htok], [pt])
                    S.add("dve", lambda e, ps=ps, dstb=dstb, half=half, tab=tab: e.tensor_tensor(
                        out=dstb[:], in0=ps[:], in1=tab[:, half, :], op=ALU.mult), [pt, tabt_], [dtok])
                if isq:
                    S.add("pool", lambda e, a=a, b=b: e.tensor_tensor(out=tmpq[:], in0=a[:], in1=b[:], op=ALU.add),
                          [at_, bt_], [tmpqt])
                    S.add("act", lambda e, h=h: e.activation(out=QrT[:, h, :], in_=tmpq[:], func=AF.Copy),
                          [tmpqt], accw=[qrt])
                    S.add("pool", lambda e, h=h: e.tensor_tensor(out=QrxT[:, h, :], in0=tmpq[:], in1=xi_t[:, h, :],
                                                                 op=ALU.mult), [tmpqt, xit], accw=[qrxt])
                else:
                    S.add("pool", lambda e, a=a, b=b, h=h: e.tensor_tensor(out=KrT[:, h, :], in0=a[:], in1=b[:],
                                                                           op=ALU.add), [at_, bt_], accw=[krt])
        if DBG_R <= 2:
            continue
        if c + 1 < nchunk:
            _lx(c + 1)
        wv, wt = load_unit(C, WMs, 0)
        for j in range(4):
            ps, pt = next_ps(C)
            for kt in range(8):
                S.add("pe", lambda e, hT=hT, ps=ps, wv=wv, j=j, kt=kt: e.matmul(
                    ps[:], lhsT=hT[:, kt, j * 128:(j + 1) * 128], rhs=wv[:, kt, :], start=(kt == 0), stop=(kt == 7)),
                    [wt, htok], [pt])
            if DBG_SUB == 1:
                S.add("act", lambda e, ps=ps, j=j: e.activation(out=Vr[:, j, :], in_=ps[:, 256:512], func=AF.Copy),
                      [pt], accw=[vrt])
                continue
            pv = ps[:, 0:256].rearrange("p (h t i) -> p h t i", h=2, t=2)
            x1, x2 = pv[:, :, 0, :], pv[:, :, 1, :]
            kzv = kz[:, j].rearrange("p h (t i) -> p h t i", t=2)
            ta, tb, tc, td = tabcd
            S.add("dve", lambda e, x1=x1, j=j: e.tensor_tensor(out=ta[:], in0=x1, in1=ctk_t[:, j], op=ALU.mult),
                  [pt, ctt], [tabt[0]])
            S.add("dve", lambda e, x2=x2, j=j: e.tensor_tensor(out=tb[:], in0=x2, in1=stk_t[:, j], op=ALU.mult),
                  [pt, stt], [tabt[1]])
            S.add("dve", lambda e, x1=x1, j=j: e.tensor_tensor(out=tc[:], in0=x1, in1=stk_t[:, j], op=ALU.mult),
                  [pt, stt], [tabt[2]])
            S.add("dve", lambda e, x2=x2, j=j: e.tensor_tensor(out=td[:], in0=x2, in1=ctk_t[:, j], op=ALU.mult),
                  [pt, ctt], [tabt[3]])
            if DBG_SUB == 2:
                continue
            S.add("dve", lambda e, kzv=kzv: e.tensor_tensor(out=kzv[:, :, 0, :], in0=ta[:], in1=tb[:], op=ALU.subtract),
                  [tabt[0], tabt[1]] if DBG_SUB != 3 else [], accw=[kzt])
            S.add("dve", lambda e, kzv=kzv: e.tensor_tensor(out=kzv[:, :, 1, :], in0=tc[:], in1=td[:], op=ALU.add),
                  [tabt[2], tabt[3]] if DBG_SUB != 3 else [], accw=[kzt])
            S.add("act", lambda e, ps=ps, j=j: e.activation(out=Vr[:, j, :], in_=ps[:, 256:512], func=AF.Copy),
                  [pt], accw=[vrt])
        if DBG_R <= 3:
            continue
        wv, wt = load_unit(C, WMs, 1)
        for j in range(4):
            ps, pt = next_ps(C)
            for kt in range(8):
                S.add("pe", lambda e, hT=hT, ps=ps, wv=wv, j=j, kt=kt: e.matmul(
                    ps[:, 0:256], lhsT=hT[:, kt, j * 128:(j + 1) * 128], rhs=wv[:, kt, 0:256],
                    start=(kt == 0), stop=(kt == 7)), [wt, htok], [pt])
            S.add("act", lambda e, ps=ps, j=j: e.activation(out=sg[:, j, :], in_=ps[:, 0:256], func=AF.Silu),
                  [pt], accw=[sgt])
        if DBG_R <= 4:
            continue
        for j in range(4):
            js = slice(j * 128, (j + 1) * 128)
            for h in range(2):
                hs = slice(h * 128, (h + 1) * 128)
                psI, ptI = next_ps(C)
                S.add("pe", lambda e, psI=psI, h=h, js=js: e.matmul(
                    psI[:, 0:128], lhsT=KrT[:, h, js], rhs=QrT[:, h, js], start=True, stop=True), [krt, qrt], [ptI])
                it_, itt = IT[itn % 2], ITt[itn % 2]
                itn += 1
                S.add("dve", lambda e, psI=psI, it_=it_, h=h: e.tensor_tensor(
                    out=it_[:], in0=psI[:, 0:128], in1=decT[:, h, :], op=ALU.mult), [ptI, dect], [itt])
                psO, ptO = next_ps(C)
                S.add("pe", lambda e, psO=psO, it_=it_, j=j, hs=hs: e.matmul(
                    psO[:, 0:128], lhsT=it_[:], rhs=Vr[:, j, hs], start=True, stop=False), [itt, vrt], [ptO])
                S.add("pe", lambda e, psO=psO, h=h, js=js: e.matmul(
                    psO[:, 0:128], lhsT=QrxT[:, h, js], rhs=Rb[:, h, :], start=False, stop=True), [qrxt, Rbt[h]], [ptO])
                S.add("act", lambda e, psO=psO, j=j, h=h: e.activation(
                    out=junkb[:], in_=psO[:, 0:128], func=AF.Square, accum_out=ssr[:, j * 2 + h:j * 2 + h + 1]),
                    [ptO], [jbt], accw=[ssrt])
                S.add("dve", lambda e, psO=psO, j=j, h=h: e.tensor_copy(out=yr[:, j, h, :], in_=psO[:, 0:128]),
                      [ptO], accw=[yrt])
                psK, ptK = next_ps(C)
                S.add("pe", lambda e, psK=psK, j=j, h=h, hs=hs: e.matmul(
                    psK[:, 0:128], lhsT=kz[:, j, h, :], rhs=Vr[:, j, hs], start=True, stop=True), [kzt, vrt], [ptK])
                S.add("dve", lambda e, psK=psK, h=h: e.scalar_tensor_tensor(
                    out=R[:, h, :], in0=R[:, h, :], scalar=gch[:, h:h + 1], in1=psK[:, 0:128],
                    op0=ALU.mult, op1=ALU.add), [ptK, gcht, Rt[h]], [Rt[h]])
                S.add("pool", lambda e, h=h: e.tensor_copy(out=Rb[:, h, :], in_=R[:, h, :]), [Rt[h]], [Rbt[h]])
        if DBG_R <= 5:
            continue
        S.add("dve", lambda e: e.tensor_scalar(out=rr[:], in0=ssr[:], scalar1=1.0 / 128, scalar2=EPS,
                                               op0=ALU.mult, op1=ALU.add), [ssrt], [rrt])
        S.add("act", lambda e: e.activation(out=rr[:], in_=rr[:], func=AF.Sqrt), [rrt], [rrt])
        S.add("dve", lambda e: e.reciprocal(out=rr[:], in_=rr[:]), [rrt], [rrt])
        for j in range(4):
            for h in range(2):
                hs = slice(h * 128, (h + 1) * 128)
                S.add("dve", lambda e, j=j, h=h, hs=hs: e.scalar_tensor_tensor(
                    out=ro[:, j, hs], in0=yr[:, j, h, :], scalar=rr[:, j * 2 + h:j * 2 + h + 1], in1=sg[:, j, hs],
                    op0=ALU.mult, op1=ALU.mult), [yrt, rrt, sgt], accw=[rot])
        S.dma(ao[t0:t0 + 512, 256:512].rearrange("(j p) d -> p j d", p=128), ro[:], reads=[rot], accw=[ao_tok], q="act")


HORD = (0, 2, 1, 3)


def emit_nsa(C, SL, x, xin_tok, WSs, WMs, cd, pd, gains, gtok, ident, itok, ao, ao_tok):
    S = C.S
    sb = S.sbuf
    NT = SL // 128
    nb = SL // 16
    assert nb <= 512
    NCT = max(1, nb // 128)
    QT = sb("n_QT", [128, 2, SL], BF16)
    Kslo, Kshi = sb("n_Kslo", [128, SL], BF16), sb("n_Kshi", [128, SL], BF16)
    Kwlo, Kwhi = sb("n_Kwlo", [128, SL], BF16), sb("n_Kwhi", [128, SL], BF16)
    V1 = sb("n_V1", [128, NT, 2, 65], BF16)
    Gt = sb("n_Gt", [128, NT, 12], F32)
    kclo, kchi = sb("n_kclo", [128, 512], BF16), sb("n_kchi", [128, 512], BF16)
    Vc1 = sb("n_Vc1", [128, 4, 65], BF16)
    onesbd = sb("n_onesbd", [128, 128], BF16)
    identb = sb("n_identb", [128, 128], BF16)
    qf = sb("n_qf", [128, 512], F32)
    sq = sb("n_sq", [128, 512], BF16)
    rs = sb("n_rs", [128, 512], F32)
    qtok, kst, kwt, kcvt, v1t, gtt, kct, vct, onest, ibt, qft, sqt, rst = [Tok() for _ in range(13)]
    tmps = (qf, qft, sq, sqt, rs, rst)
    S.dma(onesbd[:], cd["onesbd"], writes=[onest])
    S.add("dve", lambda e: e.tensor_copy(out=identb[:], in_=ident[:]), [itok], [ibt])
    for (t_, tk) in ((Kslo, kst), (Kshi, kst), (Kwlo, kwt), (Kwhi, kwt), (kclo, kct), (kchi, kct)):
        S.add("pool", lambda e, t_=t_: e.memset(t_[:], 0.0), [], [tk])
    S.add("pool", lambda e: e.memset(V1[:], 1.0), [], [v1t])
    S.add("pool", lambda e: e.memset(Vc1[:], 0.0), [], [vct])
    S.add("pool", lambda e: e.memset(Vc1[:, :, 64:65], 1.0), [vct], [vct])
    kc_scope = S.scope()
    kc_scope.__enter__()
    KcVcT = sb("n_KcVcT", [128, SL + 16], BF16)
    with S.scope():
        if C.share_h:
            hT2 = [sb("n_hT%d" % i, [128, 8, 512], BF16) for i in range(2)]
            htok2 = [Tok(), Tok()]

            def _lh(c):
                S.dma(hT2[c % 2][:].rearrange("p k t -> p (k t)"), C.hscr[c], reads=[C.hscr_tok[c]], writes=[htok2[c % 2]])
            _lh(0)
        else:
            xt = sb("n_xt", [128, 4, 1024], F32)
            junk = sb("n_junk", [128, 1024], F32)
            ss = sb("n_ss", [128, 4], F32)
            rstd = sb("n_rstd", [128, 4], F32)
            hT = sb("n_hT", [128, 8, 512], BF16)
            xtok, jtok, sstok, rtok, htok = [Tok() for _ in range(5)]
        for c in range(SL // 512):
            t0 = c * 512
            cs = slice(t0, t0 + 512)
            if C.share_h:
                hT, htok = hT2[c % 2], htok2[c % 2]
                if c + 1 < SL // 512:
                    _lh(c + 1)
            else:
                load_x_rms(C, x, xin_tok, t0, xt, xtok, junk, jtok, ss, sstok, rstd, rtok, gains, gtok, hT, htok, ident, itok)
            for u in (0, 1):
                wv, wt = load_unit(C, WSs, u)
                for f in range(4 if u == 0 else 1):
                    ft = u * 4 + f
                    ps, pt = next_ps(C)
                    for kt in range(8):
                        S.add("pe", lambda e, hT=hT, ps=ps, wv=wv, f=f, kt=kt: e.matmul(
                            ps[:], lhsT=wv[:, f, kt, :], rhs=hT[:, kt, :], start=(kt == 0), stop=(kt == 7)),
                            [wt, htok], [pt])
                    if ft < 2:
                        norm_evac(C, ps[:], pt, gains, gtok, 8, onesbd, onest, tmps, [(QT[:, ft, cs], 0, 128, qtok)])
                    elif ft == 2:
                        norm_evac(C, ps[:], pt, gains, gtok, 9, onesbd, onest, tmps,
                                  [(Kslo[0:64, cs], 0, 64, kst), (Kshi[64:128, cs], 64, 128, kst)])
                    elif ft == 3:
                        norm_evac(C, ps[:], pt, gains, gtok, 9, onesbd, onest, tmps,
                                  [(Kwlo[0:64, cs], 0, 64, kwt), (Kwhi[64:128, cs], 64, 128, kwt)])
                    else:
                        S.add("act", lambda e, ps=ps, cs=cs: e.activation(out=KcVcT[:, cs], in_=ps[:], func=AF.Copy),
                              [pt], accw=[kcvt])
            wv, wt = load_unit(C, WMs, 2)
            for j in range(4):
                tile_i = c * 4 + j
                ps, pt = next_ps(C)
                for kt in range(8):
                    S.add("pe", lambda e, hT=hT, ps=ps, wv=wv, j=j, kt=kt: e.matmul(
                        ps[:, 0:140], lhsT=hT[:, kt, j * 128:(j + 1) * 128], rhs=wv[:, kt, 0:140],
                        start=(kt == 0), stop=(kt == 7)), [wt, htok], [pt])
                S.add("dve", lambda e, ps=ps, tile_i=tile_i: e.tensor_copy(
                    out=V1[:, tile_i, :, 0:64], in_=ps[:, 0:128].rearrange("p (b d) -> p b d", b=2)), [pt], accw=[v1t])
                S.add("act", lambda e, ps=ps, tile_i=tile_i: e.activation(
                    out=Gt[:, tile_i, :], in_=ps[:, 128:140], func=AF.Sigmoid), [pt], accw=[gtt])
    with S.scope():
        W1b = sb("n_W1b", [128, 32, 256], BF16)
        posT = sb("n_posT", [128, 32], F32)
        w2f = sb("n_w2f", [128, 2, 2, 128], F32)
        w2b = sb("n_w2b", [128, 2, 2, 128], BF16)
        zr = [sb("n_zr%d" % i, [128, 512], BF16) for i in range(4)]
        zrt = [Tok() for _ in range(4)]
        GT = sb("n_GT", [128, 4, 512], BF16)
        ga = sb("n_ga", [128, 512], F32)
        gb = sb("n_gb", [128, 512], F32)
        w1t, post, w2t, w2bt, GTt, gat, gbt = [Tok() for _ in range(7)]
        S.dma(posT[:], pd["posT"], writes=[post])
        S.dma(w2f[:], pd["w2"], writes=[w2t])
        S.add("dve", lambda e: e.tensor_copy(out=w2b[:], in_=w2f[:]), [w2t], [w2bt])
        S.add("dve", lambda e: e.tensor_copy(out=KcVcT[:, SL:SL + 16], in_=KcVcT[:, SL - 1:SL].to_broadcast([128, 16])),
              [kcvt], [kcvt])
        accs = [next_ps(C) for _ in range(4)]
        zn = 0
        for kv, srcw in enumerate((pd["w1k"], pd["w1v"])):
            for r0 in range(0, 32, 2):
                i = C.sn % 3
                C.sn += 1
                st, stt = C.stg[i], C.stgtok[i]
                S.dma(st[:, :512], srcw[:, r0:r0 + 2, :].rearrange("p r h -> p (r h)"), writes=[stt])
                eng = ("dve", "pool")[C.cast_rr % 2]
                C.cast_rr += 1
                S.add(eng, lambda e, st=st, r0=r0: e.tensor_copy(
                    out=W1b[:, r0:r0 + 2, :].rearrange("p r h -> p (r h)"), in_=st[:, :512]), [stt], accw=[w1t])
            for r in range(32):
                z, zt = zr[zn % 4], zrt[zn % 4]
                zn += 1
                if r < 16:
                    src = KcVcT[:, 0:16 * nb].rearrange("p (i s) -> p i s", s=16)[:, :, r]
                else:
                    src = KcVcT[:, 16:16 + 16 * nb].rearrange("p (i s) -> p i s", s=16)[:, :, r - 16]
                eng = ("dve", "pool")[r % 2]
                S.add(eng, lambda e, z=z, src=src, r=r: e.tensor_scalar(
                    out=z[:, :nb], in0=src, scalar1=posT[:, r:r + 1], scalar2=None, op0=ALU.add), [kcvt, post], [zt])
                for hid in range(2):
                    ps, pt = accs[kv * 2 + hid]
                    S.add("pe", lambda e, ps=ps, hid=hid, r=r, z=z: e.matmul(
                        ps[:, :nb], lhsT=W1b[:, r, hid * 128:(hid + 1) * 128], rhs=z[:, :nb],
                        start=(r == 0), stop=(r == 31)), [w1t, zt], [pt])
        for a in range(4):
            ps, pt = accs[a]
            S.add("act", lambda e, ps=ps: e.activation(out=ga[:, :nb], in_=ps[:, :nb], func=AF.Square), [pt], [gat])
            S.add("dve", lambda e: e.tensor_scalar(out=ga[:, :nb], in0=ga[:, :nb], scalar1=0.044715, scalar2=1.0,
                                                   op0=ALU.mult, op1=ALU.add), [gat], [gat])
            S.add("dve", lambda e, ps=ps: e.tensor_tensor(out=ga[:, :nb], in0=ga[:, :nb], in1=ps[:, :nb], op=ALU.mult),
                  [gat, pt], [gat])
            S.add("act", lambda e: e.activation(out=gb[:, :nb], in_=ga[:, :nb], func=AF.Sigmoid, scale=1.5957691216057308),
                  [gat], [gbt])
            S.add("dve", lambda e, ps=ps, a=a: e.tensor_tensor(out=GT[:, a, :nb], in0=gb[:, :nb], in1=ps[:, :nb],
                                                               op=ALU.mult), [gbt, pt], accw=[GTt])
        ps, pt = next_ps(C)
        for t in range(2):
            S.add("pe", lambda e, ps=ps, t=t: e.matmul(ps[:, :nb], lhsT=w2b[:, t, 0, :], rhs=GT[:, t, :nb],
                                                       start=(t == 0), stop=(t == 1)), [w2bt, GTt], [pt])
        norm_evac(C, ps[:, :nb], pt, gains, gtok, 9, onesbd, onest, tmps,
                  [(kclo[0:64, :nb], 0, 64, kct), (kchi[64:128, :nb], 64, 128, kct)])
        for ct in range(NCT):
            ps, pt = next_ps(C)
            wdt = min(128, nb)
            for t in range(2):
                S.add("pe", lambda e, ps=ps, t=t, ct=ct, wdt=wdt: e.matmul(
                    ps[:wdt, 0:64], lhsT=GT[:, 2 + t, ct * 128:ct * 128 + wdt], rhs=w2b[:, t, 1, 0:64],
                    start=(t == 0), stop=(t == 1)), [w2bt, GTt], [pt])
            S.add("dve", lambda e, ps=ps, ct=ct, wdt=wdt: e.tensor_copy(out=Vc1[:wdt, ct, 0:64], in_=ps[:wdt, 0:64]),
                  [pt], accw=[vct])
    kc_scope.__exit__(None, None, None)
    with S.scope():
        triT4 = sb("n_triT4", [128, 512], BF16)
        triU4 = sb("n_triU4", [128, 512], BF16)
        cmQ = sb("n_cmQ", [128, 16, 128], BF16)
        cmT = sb("n_cmT", [128, 16, 128], BF16)
        wet, trt, trut, cmqt, cmtt = [Tok() for _ in range(5)]
        S.dma(Kslo[64:128, :], cd["wexp"][0:64, :], reads=[kst], writes=[kst])
        S.dma(Kshi[0:64, :], cd["wexp"][0:64, :], reads=[kst], writes=[kst])
        S.dma(triT4[:], cd["triT"], writes=[trt])
        S.dma(triU4[:], cd["triU"], writes=[trut])
        S.dma(cmQ[:], cd["cmaskQ"], writes=[cmqt])
        S.dma(cmT[:], cd["cmaskT"], writes=[cmtt])
        E4 = sb("n_E4", [128, 4, 512], F32)
        rsum = sb("n_rsum", [128, 4], F32)
        rinv = sb("n_rinv", [128, 4], F32)
        imp = sb("n_imp", [128, 512], F32)
        ib = sb("n_ib", [128, 128], F32)
        sc = sb("n_sc", [128, 128], F32)
        sc2 = sb("n_sc2", [128, 128], F32)
        m8a = sb("n_m8a", [128, 8], F32)
        m8b = sb("n_m8b", [128, 8], F32)
        nmf = sb("n_nmf", [128, 128], F32)
        nmb = sb("n_nmb", [128, 128], BF16)
        nmr = sb("n_nmr", [128, 128], BF16)
        Qlo = [sb("n_Qlo%d" % i, [128, 2, 128], BF16) for i in range(2)]
        Qhi = [sb("n_Qhi%d" % i, [128, 2, 128], BF16) for i in range(2)]
        nmrt = Tok()
        Qlot, Qhit = [Tok(), Tok()], [Tok(), Tok()]
        PT = [sb("n_PT%d" % i, [128, 512], BF16) for i in range(4)]
        PTt = [Tok() for _ in range(4)]
        den = sb("n_den", [128, 4], F32)
        coef = sb("n_coef", [128, 4], F32)
        acc = sb("n_acc", [128, 4, 64], F32)
        ob = [sb("n_ob%d" % i, [128, 256], BF16) for i in range(2)]
        obt = [Tok(), Tok()]
        e4t, rsumt, rinvt, impt, ibt_, sct, sc2t, m8at, m8bt, nmft, nmbt, nmTt, dent, coeft, acct = [Tok() for _ in range(15)]
        npool = len(C.ps)
        Obank = [(C.ps[npool - 2], C.pstok[npool - 2]), (C.ps[npool - 1], C.pstok[npool - 1])]
        C.ps_active = npool - 2
        on = 0
        ptn = 0

        def att_branch(tiles, br, qt, first_branch):
            nonlocal on, ptn
            qs = slice(qt * 128, (qt + 1) * 128)
            O, Ot = Obank[on % 2]
            on += 1
            nt = len(tiles)
            def scores(idx):
                klo, khi, ktoks, v, vtok, masks = tiles[idx][:6]
                if len(tiles[idx]) > 6:
                    rlo, rhi, rtoks = tiles[idx][6:9]
                else:
                    rlo, rhi, rtoks = QT[:, :, qs], QT[:, :, qs], [qtok]
                psT, ptT = next_ps(C)
                first = True
                for (ml, mlt, mr, mrt, wide) in masks:
                    if wide:
                        S.add("pe", lambda e, psT=psT, ml=ml, mr=mr, first=first: e.matmul(
                            psT[:, 0:512], lhsT=ml, rhs=mr, start=first, stop=False, skip_group_check=True),
                            [mlt, mrt], [ptT])
                        first = False
                    else:
                        for cb in range(4):
                            S.add("pe", lambda e, psT=psT, ml=ml, mr=mr, cb=cb, first=first: e.matmul(
                                psT[:, cb * 128:(cb + 1) * 128], lhsT=ml, rhs=mr, start=first, stop=False,
                                skip_group_check=True), [mlt, mrt], [ptT])
                            first = False
                S.add("pe", lambda e, psT=psT, klo=klo, rlo=rlo, first=first: e.matmul(
                    psT[:, 0:256], lhsT=klo, rhs=rlo, start=first, stop=False, skip_group_check=True),
                    [ktoks] + rtoks, [ptT])
                S.add("pe", lambda e, psT=psT, khi=khi, rhi=rhi: e.matmul(
                    psT[:, 256:512], lhsT=khi, rhs=rhi, start=False, stop=True, skip_group_check=True),
                    [ktoks] + rtoks, [ptT])
                return psT, ptT

            DEPTH = 2
            pendq = [scores(i) for i in range(min(DEPTH, nt))]
            for idx in range(nt):
                psT, ptT = pendq.pop(0)
                if idx + DEPTH < nt:
                    pendq.append(scores(idx + DEPTH))
                v, vtok = tiles[idx][3], tiles[idx][4]
                P, Pt_ = PT[ptn % 4], PTt[ptn % 4]
                ptn += 1
                S.add("act", lambda e, psT=psT, P=P: e.activation(out=P[:], in_=psT[:], func=AF.Exp, scale=0.125),
                      [ptT], [Pt_])
                for cb in range(4):
                    S.add("pe", lambda e, O=O, P=P, v=v, cb=cb, idx=idx: e.matmul(
                        O[:, cb * 65:(cb + 1) * 65], lhsT=P[:, cb * 128:(cb + 1) * 128], rhs=v,
                        start=(idx == 0 and cb == 0), stop=(idx == nt - 1), skip_group_check=True), [Pt_, vtok], [Ot])
            Ov = O[:, 0:260].rearrange("p (c d) -> p c d", c=4)
            S.add("dve", lambda e, Ov=Ov: e.tensor_scalar(out=den[:], in0=Ov[:, :, 64], scalar1=1e-30, scalar2=None,
                                                          op0=ALU.max), [Ot], [dent])
            S.add("dve", lambda e: e.reciprocal(out=den[:], in_=den[:]), [dent], [dent])
            gv = Gt[:, qt, br * 4:(br + 1) * 4].rearrange("p (a b) -> p b a", a=2)
            S.add("dve", lambda e, gv=gv: e.tensor_tensor(out=coef[:].rearrange("p (b a) -> p b a", b=2),
                                                          in0=den[:].rearrange("p (b a) -> p b a", b=2), in1=gv,
                                                          op=ALU.mult), [dent, gtt], [coeft])
            for cb in range(4):
                h = HORD[cb]
                if first_branch:
                    S.add("dve", lambda e, Ov=Ov, cb=cb, h=h: e.tensor_scalar(
                        out=acc[:, h, :], in0=Ov[:, cb, 0:64], scalar1=coef[:, cb:cb + 1], scalar2=None,
                        op0=ALU.mult), [Ot, coeft], [acct])
                else:
                    S.add("dve", lambda e, Ov=Ov, cb=cb, h=h: e.scalar_tensor_tensor(
                        out=acc[:, h, :], in0=Ov[:, cb, 0:64], scalar=coef[:, cb:cb + 1], in1=acc[:, h, :],
                        op0=ALU.mult, op1=ALU.add), [Ot, coeft, acct], [acct])

        for qt in range(NT if DBG_NQT is None else DBG_NQT):
            bg = getattr(C, "bg", None)
            if bg is not None:
                for _ in range(C.bg_per_tile):
                    next(bg, None)
            qs = slice(qt * 128, (qt + 1) * 128)
            ctl = (8 * qt + 6) // 128
            ncol = 128 * (ctl + 1)
            r16 = qt % 16
            for cb, (p, Kc) in enumerate(((0, kclo), (1, kclo), (0, kchi), (1, kchi))):
                psS, ptS = next_ps(C)
                first = True
                if ctl > 0:
                    S.add("pe", lambda e, psS=psS, p=p, Kc=Kc, qs=qs, ctl=ctl: e.matmul(
                        psS[:, 0:ctl * 128], lhsT=QT[:, p, qs], rhs=Kc[:, 0:ctl * 128], start=True, stop=False,
                        skip_group_check=True), [qtok, kct], [ptS])
                    first = False
                S.add("pe", lambda e, psS=psS, ctl=ctl, ncol=ncol, r16=r16, first=first: e.matmul(
                    psS[:, ctl * 128:ncol], lhsT=identb[:], rhs=cmQ[:, r16, :], start=first, stop=False,
                    skip_group_check=True), [ibt, cmqt], [ptS])
                S.add("pe", lambda e, psS=psS, p=p, Kc=Kc, qs=qs, ctl=ctl, ncol=ncol: e.matmul(
                    psS[:, ctl * 128:ncol], lhsT=QT[:, p, qs], rhs=Kc[:, ctl * 128:ncol], start=False, stop=True,
                    skip_group_check=True), [qtok, kct], [ptS])
                S.add("act", lambda e, psS=psS, cb=cb, ncol=ncol: e.activation(
                    out=E4[:, cb, :ncol], in_=psS[:, :ncol], func=AF.Exp, scale=0.125, accum_out=rsum[:, cb:cb + 1]),
                    [ptS], accw=[e4t, rsumt])
            S.add("dve", lambda e: e.tensor_scalar(out=rinv[:], in0=rsum[:], scalar1=1e-30, scalar2=None, op0=ALU.max),
                  [rsumt], [rinvt])
            S.add("dve", lambda e: e.reciprocal(out=rinv[:], in_=rinv[:]), [rinvt], [rinvt])
            S.add("dve", lambda e, ncol=ncol: e.tensor_scalar(out=imp[:, :ncol], in0=E4[:, 0, :ncol], scalar1=rinv[:, 0:1],
                                                             scalar2=None, op0=ALU.mult), [e4t, rinvt], [impt])
            for cb in range(1, 4):
                S.add("dve", lambda e, cb=cb, ncol=ncol: e.scalar_tensor_tensor(
                    out=imp[:, :ncol], in0=E4[:, cb, :ncol], scalar=rinv[:, cb:cb + 1], in1=imp[:, :ncol],
                    op0=ALU.mult, op1=ALU.add), [e4t, rinvt, impt], [impt])
            nblk = ncol // 4
            S.add("dve", lambda e, ncol=ncol, nblk=nblk: e.tensor_reduce(
                out=ib[:, :nblk], in_=imp[:, :ncol].rearrange("p (j r) -> p j r", r=4), axis=AX.X, op=ALU.add),
                [impt], [ibt_])
            S.add("dve", lambda e, nblk=nblk: e.tensor_tensor(
                out=ib[:, 1:nblk], in0=ib[:, 1:nblk],
                in1=imp[:, 0:4 * (nblk - 1)].rearrange("p (j r) -> p j r", r=4)[:, :, 3], op=ALU.add),
                [impt, ibt_], [ibt_])
            S.add("pool", lambda e: e.memset(sc[:], -1e30), [], [sct])
            if qt > 0:
                S.add("dve", lambda e, qt=qt: e.tensor_copy(out=sc[:, 0:2 * qt], in_=ib[:, 0:2 * qt]), [ibt_, sct], [sct])
                S.add("dve", lambda e, qt=qt: e.memset(sc[0:64, 2 * qt - 1:2 * qt], 1e4), [sct], [sct])
            S.add("dve", lambda e: e.memset(sc[:, 0:1], 1e4), [sct], [sct])
            S.add("dve", lambda e, qt=qt: e.memset(sc[:, 2 * qt:2 * qt + 1], 1e4), [sct], [sct])
            S.add("dve", lambda e, qt=qt: e.memset(sc[64:128, 2 * qt + 1:2 * qt + 2], 1e4), [sct], [sct])
            S.add("dve", lambda e: e.max(out=m8a[:], in_=sc[:]), [sct], [m8at])
            S.add("dve", lambda e: e.match_replace(out=sc2[:], in_to_replace=m8a[:], in_values=sc[:], imm_value=-1e30),
                  [sct, m8at], [sc2t])
            S.add("dve", lambda e: e.max(out=m8b[:], in_=sc2[:]), [sc2t], [m8bt])
            S.add("dve", lambda e: e.tensor_scalar(out=nmf[:], in0=sc[:], scalar1=m8b[:, 7:8], scalar2=None,
                                                   op0=ALU.is_ge), [sct, m8bt], [nmft])
            S.add("dve", lambda e: e.tensor_scalar(out=nmb[:], in0=nmf[:], scalar1=-1.0, scalar2=-NEG,
                                                   op0=ALU.add, op1=ALU.mult), [nmft], [nmbt])
            tiles = []
            for ct in range(ctl + 1):
                cs = slice(ct * 128, (ct + 1) * 128)
                masks = [(identb[:], ibt, cmT[:, r16, :], cmtt, False)] if ct == ctl else []
                tiles.append((kclo[:, cs], kchi[:, cs], kct, Vc1[:, ct, :], vct, masks))
            att_branch(tiles, 0, qt, True)
            tiles = []
            for kt in range(max(0, qt - 4), qt + 1):
                ks_ = slice(kt * 128, (kt + 1) * 128)
                masks = []
                if kt == qt:
                    masks.append((identb[:], ibt, triT4[:], trt, True))
                if kt == qt - 4:
                    masks.append((identb[:], ibt, triU4[:], trut, True))
                tiles.append((Kwlo[:, ks_], Kwhi[:, ks_], kwt, V1[:, kt, 1, :], v1t, masks))
            att_branch(tiles, 2, qt, False)
            S.add("dve", lambda e: e.tensor_copy(out=nmr[:, 0:64], in_=nmb[:, 64:128]), [nmbt], [nmrt])
            S.add("dve", lambda e: e.tensor_copy(out=nmr[:, 64:128], in_=nmb[:, 0:64]), [nmbt, nmrt], [nmrt])
            pb1, pbt1 = next_psb(C)
            S.add("pe", lambda e, pb1=pb1: e.transpose(out=pb1[:, 0:128], in_=nmb[:], identity=identb[:]), [nmbt, ibt], [pbt1])
            pb2, pbt2 = next_psb(C)
            S.add("pe", lambda e, pb2=pb2: e.transpose(out=pb2[:, 0:128], in_=nmr[:], identity=identb[:]), [nmrt, ibt], [pbt2])
            nrng = 1 if qt < 32 else 2
            for rg_ in range(nrng):
                srcl, srclt = (pb2, pbt2) if rg_ == 0 else (pb1, pbt1)
                srch, srcht = (pb1, pbt1) if rg_ == 0 else (pb2, pbt2)
                S.add("act", lambda e, rg_=rg_, qs=qs: e.activation(out=Qlo[rg_][0:64, :, :], in_=QT[0:64, :, qs], func=AF.Copy),
                      [qtok], accw=[Qlot[rg_]])
                S.add("act", lambda e, rg_=rg_, qs=qs: e.activation(out=Qhi[rg_][64:128, :, :], in_=QT[64:128, :, qs], func=AF.Copy),
                      [qtok], accw=[Qhit[rg_]])
                for hh in range(2):
                    S.add("dve", lambda e, rg_=rg_, hh=hh, srcl=srcl: e.tensor_copy(
                        out=Qlo[rg_][64:128, hh, :], in_=srcl[64:128, 0:128]), [srclt], accw=[Qlot[rg_]])
                    S.add("dve", lambda e, rg_=rg_, hh=hh, srch=srch: e.tensor_copy(
                        out=Qhi[rg_][0:64, hh, :], in_=srch[0:64, 0:128]), [srcht], accw=[Qhit[rg_]])
            tiles = []
            for kt in range(qt + 1):
                ks_ = slice(kt * 128, (kt + 1) * 128)
                masks = []
                if kt == qt:
                    masks.append((identb[:], ibt, triT4[:], trt, True))
                rg_ = 0 if kt < 32 else 1
                tiles.append((Kslo[:, ks_], Kshi[:, ks_], kst, V1[:, kt, 0, :], v1t, masks,
                              Qlo[rg_][:], Qhi[rg_][:], [Qlot[rg_], Qhit[rg_]]))
            att_branch(tiles, 1, qt, False)
            o_, ot_ = ob[qt % 2], obt[qt % 2]
            S.add("act", lambda e, o_=o_: e.activation(out=o_[:], in_=acc[:].rearrange("p h d -> p (h d)"), func=AF.Copy),
                  [acct], [ot_])
            S.dma(ao[qs, 0:256], o_[:], reads=[ot_], accw=[ao_tok], q="act")
        C.ps_active = npool


from concourse.bass_utils import run_bass_kernel_spmd

_NC_CACHE = {}


def _get_nc(key, builder):
    if key not in _NC_CACHE:
        _NC_CACHE[key] = builder()
    return _NC_CACHE[key]


def kernel(**inputs):
    z = {k: np.asarray(v) for k, v in inputs.items()}
    x = np.ascontiguousarray(z["x"], np.float32)
    B, SL, D = x.shape
    depth = z["w_in"].shape[0]
    ncA = _get_nc("A", lambda: build_phaseA(SL))
    ncB = _get_nc("B", lambda: build_phaseB(SL // 2))
    constsA = [hostA_consts(g, SL) for g in range(2)]
    ident = np.eye(128, dtype=np.float32)
    for L in range(depth):
        insA = []
        wsm = [hostA_weights(z["w_in"][L], g) for g in range(2)]
        prm = [hostA_params(z, L, g) for g in range(2)]
        for c in range(8):
            b, g = c // 2, c % 2
            d = dict(x=np.ascontiguousarray(x[b]), WS=wsm[g][0], WM=wsm[g][1])
            d.update(constsA[g])
            d.update(prm[g])
            insA.append(d)
        resA = run_bass_kernel_spmd(ncA, insA, core_ids=list(range(8)))
        ao = [np.asarray(resA.results[c]["ao"]) for c in range(8)]

        def gl(v):
            return np.ascontiguousarray(np.asarray(v, np.float32).reshape(8, 128).T)
        gains = np.ascontiguousarray(np.concatenate(
            [gl(z["norm_mix"][L]), gl(z["norm_mlp"][L]), gl(z["norm_ple"][L])], 1), np.float32)
        w_merge = np.ascontiguousarray(z["w_in"][L][:, 3352:5400], np.float32)
        insB = []
        for c in range(8):
            b, hf = c // 2, c % 2
            sl = slice(hf * (SL // 2), (hf + 1) * (SL // 2))
            attn = np.concatenate([ao[2 * b][sl, :256], ao[2 * b + 1][sl, :256],
                                   ao[2 * b][sl, 256:], ao[2 * b + 1][sl, 256:]], 1)
            insB.append(dict(
                x=np.ascontiguousarray(x[b, sl]), attn=np.ascontiguousarray(attn),
                p=np.ascontiguousarray(z["p"][L, b, sl], np.float32), gains=gains, ident=ident,
                w_merge=w_merge, w_up_nsa=np.ascontiguousarray(z["w_up_nsa"][L], np.float32),
                w_up_ret=np.ascontiguousarray(z["w_up_ret"][L], np.float32),
                w_out=np.ascontiguousarray(z["w_out"][L], np.float32),
                w_ff1=np.ascontiguousarray(z["w_ff1"][L], np.float32),
                w_ff2=np.ascontiguousarray(z["w_ff2"][L], np.float32),
                w_gate=np.ascontiguousarray(z["w_ple_gate"][L], np.float32),
                w_ple=np.ascontiguousarray(z["w_ple"][L], np.float32)))
        resB = run_bass_kernel_spmd(ncB, insB, core_ids=list(range(8)))
        xn = np.empty_like(x)
        for c in range(8):
            b, hf = c // 2, c % 2
            xn[b, hf * (SL // 2):(hf + 1) * (SL // 2)] = np.asarray(resB.results[c]["xo"])
        x = xn
    return x


B_WNAMES = (("w_merge", 1024, 2048), ("w_up_nsa", 512, 1024), ("w_up_ret", 512, 1024), ("w_out", 1024, 1024),
            ("w_ff1", 1024, 4096), ("w_ff2", 4096, 1024), ("w_gate", 1024, 1024), ("w_ple", 256, 1024))
PAIR_GROUPS = [[0, 1], [2, 3], [4, 5], [6, 7]]


def build_fused(SL=8192, depth=2):
    nc = bass.Bass("TRN2", target_bir_lowering=False)
    dt = nc.dram_tensor
    T = SL // 2
    x_full = dt("x", [SL, 1024], F32, kind="ExternalInput").ap()
    xh = dt("xh", [T, 1024], F32, kind="ExternalInput").ap()
    hmask_d = dt("hmask", [128, 2], F32, kind="ExternalInput").ap()
    cd = {k: dt(k, sh, ty, kind="ExternalInput").ap() for k, (sh, ty) in A_CONST_SHAPES(SL).items()}
    WS_d, WM_d, pd, p_d, gB_d, wd = [], [], [], [], [], []
    for L in range(depth):
        WS_d.append(dt("WS%d" % L, [1024, 2048], F32, kind="ExternalInput").ap())
        WM_d.append(dt("WM%d" % L, [1024, 1536], F32, kind="ExternalInput").ap())
        pd.append({k: dt("%s%d" % (k, L), sh, F32, kind="ExternalInput").ap() for k, sh in A_PARAM_SHAPES.items()})
        p_d.append(dt("p%d" % L, [T, 256], F32, kind="ExternalInput").ap())
        gB_d.append(dt("gainsB%d" % L, [128, 24], F32, kind="ExternalInput").ap())
        wd.append({n: dt("%s%d" % (n, L), [K, N], F32, kind="ExternalInput").ap() for n, K, N in B_WNAMES})
    out = dt("xo", [T, 1024], F32, kind="ExternalOutput").ap()
    ao = [dt("ao%d" % L, [SL, 512], BF16, kind="Internal").ap() for L in range(depth)]
    aog = [dt("aog%d" % L, [2 * SL, 512], BF16, kind="Internal").ap() for L in range(depth)]
    xmid = [dt("xmid%d" % L, [T, 1024], F32, kind="Internal").ap() for L in range(depth - 1)]
    xg = [dt("xg%d" % L, [SL, 1024], F32, kind="Internal").ap() for L in range(depth - 1)]
    S = Sched(nc)
    C = make_pools(S, n_wbuf=3)
    xg_tok = None
    xmid_tok = None
    for L in range(depth):
        S.prefix = "L%dB_" % L
        WB = make_B_wspecs(S, wd[L])
        C.bg = prep_B_gen(C, WB)
        C.bg_per_tile = -(-212 // (SL // 128)) + 1
        S.prefix = "L%dA_" % L
        aot, aogt = Tok(), Tok()
        with S.scope():
            XK = min(512, T)
            xmap = None if L == 0 else (lambda t: 2 * ((t % T) // XK) * XK + (t // T) * XK + (t % T) % XK)
            emit_phaseA(C, SL, x_full if L == 0 else xg[L - 1], WS_d[L], WM_d[L], cd, pd[L], ao[L],
                        xin_tok=xg_tok, ao_tok=aot, xmap=xmap)
        for _ in C.bg:
            pass
        C.bg = None
        RK = min(2048, SL)
        for k in range(SL // RK):
            S.collective("AllGather", ao[L][k * RK:(k + 1) * RK, :].opt(), aog[L][2 * k * RK:2 * (k + 1) * RK, :].opt(),
                         PAIR_GROUPS, reads=[aot], accw=[aogt])
        S.prefix = "L%dB_" % L
        with S.scope():
            hm = S.sbuf("hm", [128, 2], F32)
            hmt = Tok()
            S.dma(hm[:], hmask_d, writes=[hmt])
            atAB = [S.sbuf("atAB%d" % i, [128, 4, 1024], BF16) for i in range(2)]
            atABt = [Tok(), Tok()]
            nw = len(C.wbuf)
            C.wbuf = C.wbuf + [S.sbuf("wbufx%d" % i, [128, WU_ELEMS], BF16) for i in range(1)]
            C.wtok = C.wtok + [Tok() for _ in range(1)]
            last = (L == depth - 1)
            xo_tok = Tok()
            emit_phaseB(C, T, xh if L == 0 else xmid[L - 1], aog[L], p_d[L], gB_d[L], cd["ident"], wd[L],
                        out if last else xmid[L], xin_tok=xmid_tok, attn_tok=aogt, xo_tok=xo_tok,
                        gathered=(SL, hm, hmt, atAB, atABt), W=WB)
            C.wbuf = C.wbuf[:nw]
            C.wtok = C.wtok[:nw]
        if not last:
            xmid_tok = xo_tok
            xg_tok = Tok()
            XK = min(512, T)
            for k in range(T // XK):
                S.collective("AllGather", xmid[L][k * XK:(k + 1) * XK, :].opt(),
                             xg[L][2 * k * XK:2 * (k + 1) * XK, :].opt(), PAIR_GROUPS, reads=[xo_tok], accw=[xg_tok])
    S.emit()
    S.close()
    return nc


def fused_inputs(z, SL, depth):
    import ml_dtypes
    x = np.ascontiguousarray(z["x"], np.float32)
    T = SL // 2
    consts = [hostA_consts(g, SL) for g in range(2)]

    def gl(v):
        return np.ascontiguousarray(np.asarray(v, np.float32).reshape(8, 128).T)
    per_layer = []
    for L in range(depth):
        d = {}
        d["wsm"] = [hostA_weights(z["w_in"][L], g) for g in range(2)]
        d["prm"] = [hostA_params(z, L, g) for g in range(2)]
        d["gainsB"] = np.ascontiguousarray(np.concatenate(
            [gl(z["norm_mix"][L]), gl(z["norm_mlp"][L]), gl(z["norm_ple"][L])], 1), np.float32)
        d["w"] = dict(
            w_merge=np.ascontiguousarray(z["w_in"][L][:, 3352:5400], np.float32),
            w_up_nsa=np.ascontiguousarray(z["w_up_nsa"][L], np.float32),
            w_up_ret=np.ascontiguousarray(z["w_up_ret"][L], np.float32),
            w_out=np.ascontiguousarray(z["w_out"][L], np.float32),
            w_ff1=np.ascontiguousarray(z["w_ff1"][L], np.float32),
            w_ff2=np.ascontiguousarray(z["w_ff2"][L], np.float32),
            w_gate=np.ascontiguousarray(z["w_ple_gate"][L], np.float32),
            w_ple=np.ascontiguousarray(z["w_ple"][L], np.float32))
        per_layer.append(d)
    ins = []
    for c in range(8):
        b, r = c // 2, c % 2
        sl = slice(r * T, (r + 1) * T)
        d = dict(x=np.ascontiguousarray(x[b, :SL]), xh=np.ascontiguousarray(x[b, sl]))
        hm = np.zeros((128, 2), np.float32)
        hm[:, r] = 1.0
        d["hmask"] = hm
        d.update(consts[r])
        for L in range(depth):
            pl = per_layer[L]
            d["WS%d" % L], d["WM%d" % L] = pl["wsm"][r]
            for k, v in pl["prm"][r].items():
                d["%s%d" % (k, L)] = v
            d["p%d" % L] = np.ascontiguousarray(z["p"][L, b, sl], np.float32)
            d["gainsB%d" % L] = pl["gainsB"]
            for k, v in pl["w"].items():
                d["%s%d" % (k, L)] = v
        ins.append(d)
    return ins


def kernel(**inputs):
    z = {k: np.asarray(v) for k, v in inputs.items()}
    B, SL, D = z["x"].shape
    depth = z["w_in"].shape[0]
    nc = _get_nc(("F", SL, depth), lambda: build_fused(SL, depth))
    ins = fused_inputs(z, SL, depth)
    res = run_bass_kernel_spmd(nc, ins, core_ids=list(range(8)))
    T = SL // 2
    out = np.empty((B, SL, D), np.float32)
    for c in range(8):
        b, r = c // 2, c % 2
        out[b, r * T:(r + 1) * T] = np.asarray(res.results[c]["xo"])
    return out
```
